# Optimizing a Trainium2 kernel written in Bass

```python
import math
import jax, jax.numpy as jnp
from jax import lax
import numpy as np

D_MODEL = 2048
BATCH = 4
SEQ = 2048
DEPTH = 1

N_HEADS = 32
N_KV_HEADS = 4
HEAD_DIM = 64
Q_PER_KV = N_HEADS // N_KV_HEADS
WINDOW = 128
ATTN_BLOCK = 128
N_BUCKETS = 32
MAX_DISTANCE = 128
SSM_EXPAND = 2
D_INNER = SSM_EXPAND * D_MODEL
SSM_HEAD_DIM = 64
N_SSM_HEADS = D_INNER // SSM_HEAD_DIM
N_SSM_GROUPS = 8
SSM_HEADS_PER_GROUP = N_SSM_HEADS // N_SSM_GROUPS
D_STATE = 128
SSM_CONV = 4
SSM_CHUNK = 128
D_FF = ((8 * D_MODEL // 3 + 255) // 256) * 256
FFN_CONV = 3
PLE_DIM = 256
EPS = 1e-6

Q_DIM = N_HEADS * HEAD_DIM
KV_DIM = N_KV_HEADS * HEAD_DIM
BC_DIM = N_SSM_GROUPS * D_STATE
XBC_DIM = D_INNER + 2 * BC_DIM
IN_SPLIT_SIZES = (Q_DIM, KV_DIM, KV_DIM, D_INNER, XBC_DIM, N_SSM_HEADS, D_MODEL, D_MODEL)
IN_DIM = sum(IN_SPLIT_SIZES)

kernel_name = "hybrid_swa_ssd_convffn_block"


def rms_norm(x, gain):
    xf = x.astype(jnp.float32)
    y = xf * lax.rsqrt(jnp.mean(xf * xf, axis=-1, keepdims=True) + EPS)
    return (y * gain.astype(jnp.float32)).astype(x.dtype)


def causal_depthwise_conv(x, w, b):
    width = w.shape[0]
    y = lax.conv_general_dilated(x, w[:, None, :].astype(x.dtype), window_strides=(1,),
                                 padding=[(width - 1, 0)],
                                 dimension_numbers=('NWC', 'WIO', 'NWC'),
                                 feature_group_count=x.shape[-1])
    return y + b.astype(x.dtype)


def t5_causal_bucket(dist):
    max_exact = N_BUCKETS // 2
    d = jnp.maximum(dist, 0)
    log_ratio = jnp.log(jnp.maximum(d, 1).astype(jnp.float32) / max_exact) / math.log(MAX_DISTANCE / max_exact)
    large = max_exact + (log_ratio * (N_BUCKETS - max_exact)).astype(jnp.int32)
    large = jnp.minimum(large, N_BUCKETS - 1)
    return jnp.where(d < max_exact, d, large)


def sliding_window_attention(q, k, v, sinks, bias_table):
    bsz, seq = q.shape[0], q.shape[1]
    L = ATTN_BLOCK
    nb = seq // L
    qb = q.reshape(bsz, nb, L, N_KV_HEADS, Q_PER_KV, HEAD_DIM)
    pad = ((0, 0), (L, 0), (0, 0), (0, 0))
    kp = jnp.pad(k, pad).reshape(bsz, nb + 1, L, N_KV_HEADS, HEAD_DIM)
    vp = jnp.pad(v, pad).reshape(bsz, nb + 1, L, N_KV_HEADS, HEAD_DIM)
    kb = jnp.concatenate([kp[:, :-1], kp[:, 1:]], axis=2)
    vb = jnp.concatenate([vp[:, :-1], vp[:, 1:]], axis=2)
    logits = jnp.einsum('bnqhgd,bnkhd->bnhgqk', qb, kb,
                        preferred_element_type=jnp.float32) * (HEAD_DIM ** -0.5)
    qi = jnp.arange(L)[:, None]
    kj = jnp.arange(2 * L)[None, :]
    dist = qi + L - kj
    band = (dist >= 0) & (dist < WINDOW)
    key_pos = jnp.arange(nb)[:, None] * L - L + jnp.arange(2 * L)[None, :]
    mask = band[None] & (key_pos >= 0)[:, None, :]
    bias = bias_table.astype(jnp.float32)[t5_causal_bucket(dist)]
    bias = jnp.transpose(bias, (2, 0, 1)).reshape(N_KV_HEADS, Q_PER_KV, L, 2 * L)
    logits = jnp.where(mask[None, :, None, None], logits + bias[None, None], -jnp.inf)
    sink = sinks.astype(jnp.float32).reshape(N_KV_HEADS, Q_PER_KV)[None, None, :, :, None, None]
    m = jnp.maximum(jnp.max(logits, axis=-1, keepdims=True), sink)
    e = jnp.exp(logits - m)
    probs = e / (jnp.sum(e, axis=-1, keepdims=True) + jnp.exp(sink - m))
    out = jnp.einsum('bnhgqk,bnkhd->bnqhgd', probs.astype(v.dtype), vb)
    return out.reshape(bsz, seq, Q_DIM)


def ssd_chunked(x, dt, A, Bm, Cm):
    bsz, seq = x.shape[0], x.shape[1]
    L = SSM_CHUNK
    nc = seq // L
    x = x.reshape(bsz, nc, L, N_SSM_GROUPS, SSM_HEADS_PER_GROUP, SSM_HEAD_DIM)
    dt = dt.reshape(bsz, nc, L, N_SSM_GROUPS, SSM_HEADS_PER_GROUP)
    Bm = Bm.reshape(bsz, nc, L, N_SSM_GROUPS, D_STATE)
    Cm = Cm.reshape(bsz, nc, L, N_SSM_GROUPS, D_STATE)
    a = jnp.moveaxis(dt * A, 2, -1)
    a_cum = jnp.cumsum(a, axis=-1)
    xdt = x * dt[..., None]
    causal = jnp.tril(jnp.ones((L, L), dtype=bool))
    seg = jnp.exp(jnp.where(causal, a_cum[..., :, None] - a_cum[..., None, :], -jnp.inf))
    cb = jnp.einsum('bclgn,bcsgn->bcgls', Cm, Bm)
    y_diag = jnp.einsum('bcgjls,bcsgjp->bclgjp', cb[:, :, :, None] * seg, xdt)
    decay_to_end = jnp.exp(a_cum[..., -1:] - a_cum)
    chunk_states = jnp.einsum('bclgn,bcgjl,bclgjp->bcgjpn', Bm, decay_to_end, xdt)
    chunk_decay = jnp.exp(a_cum[..., -1])

    def step(h, inp):
        dec, st = inp
        return h * dec[..., None, None] + st, h

    h0 = jnp.zeros((bsz, N_SSM_GROUPS, SSM_HEADS_PER_GROUP, SSM_HEAD_DIM, D_STATE), jnp.float32)
    _, states_in = lax.scan(step, h0, (jnp.moveaxis(chunk_decay, 1, 0), jnp.moveaxis(chunk_states, 1, 0)))
    states_in = jnp.moveaxis(states_in, 0, 1)
    y_off = jnp.einsum('bclgn,bcgjpn,bcgjl->bclgjp', Cm, states_in, jnp.exp(a_cum))
    return (y_diag + y_off).reshape(bsz, seq, N_SSM_GROUPS, SSM_HEADS_PER_GROUP, SSM_HEAD_DIM)


def mamba2_mixer(z, xbc, dt_raw, conv_w, conv_b, A_log, dt_bias, D_skip, norm_w):
    bsz, seq = z.shape[0], z.shape[1]
    xbc = jax.nn.silu(causal_depthwise_conv(xbc, conv_w, conv_b))
    xs, Bm, Cm = jnp.split(xbc, [D_INNER, D_INNER + BC_DIM], axis=-1)
    xs = xs.astype(jnp.float32).reshape(bsz, seq, N_SSM_GROUPS, SSM_HEADS_PER_GROUP, SSM_HEAD_DIM)
    Bm = Bm.astype(jnp.float32).reshape(bsz, seq, N_SSM_GROUPS, D_STATE)
    Cm = Cm.astype(jnp.float32).reshape(bsz, seq, N_SSM_GROUPS, D_STATE)
    dt = jax.nn.softplus(dt_raw.astype(jnp.float32) + dt_bias.astype(jnp.float32))
    dt = dt.reshape(bsz, seq, N_SSM_GROUPS, SSM_HEADS_PER_GROUP)
    A = -jnp.exp(A_log.astype(jnp.float32)).reshape(N_SSM_GROUPS, SSM_HEADS_PER_GROUP)
    y = ssd_chunked(xs, dt, A, Bm, Cm)
    y = y + D_skip.astype(jnp.float32).reshape(N_SSM_GROUPS, SSM_HEADS_PER_GROUP)[..., None] * xs
    y = y.reshape(bsz, seq, D_INNER) * jax.nn.silu(z.astype(jnp.float32))
    yg = y.reshape(bsz, seq, N_SSM_GROUPS, D_INNER // N_SSM_GROUPS)
    yg = yg * lax.rsqrt(jnp.mean(yg * yg, axis=-1, keepdims=True) + EPS)
    y = yg.reshape(bsz, seq, D_INNER) * norm_w.astype(jnp.float32)
    return y.astype(z.dtype)


def hybrid_layer(x, p_i, norm_mix_w, w_in, q_norm_w, k_norm_w, attn_sinks, rel_bias_table,
                 w_attn_out, ssm_conv_w, ssm_conv_b, ssm_A_log, ssm_dt_bias, ssm_D, ssm_norm_w,
                 w_ssm_out, w_out, norm_ffn_w, w_ffn_up, ffn_conv_w, ffn_conv_b, w_ffn_down,
                 ple_norm_w, w_ple_gate, w_ple_proj):
    bsz, seq = x.shape[0], x.shape[1]
    h = rms_norm(x, norm_mix_w)
    proj = jnp.einsum('bsd,de->bse', h, w_in)
    splits = np.cumsum(IN_SPLIT_SIZES)[:-1].tolist()
    q, k, v, z, xbc, dt_raw, gate_a, gate_s = jnp.split(proj, splits, axis=-1)
    q = rms_norm(q.reshape(bsz, seq, N_HEADS, HEAD_DIM), q_norm_w)
    k = rms_norm(k.reshape(bsz, seq, N_KV_HEADS, HEAD_DIM), k_norm_w)
    v = v.reshape(bsz, seq, N_KV_HEADS, HEAD_DIM)
    attn = sliding_window_attention(q, k, v, attn_sinks, rel_bias_table) @ w_attn_out
    ssm = mamba2_mixer(z, xbc, dt_raw, ssm_conv_w, ssm_conv_b, ssm_A_log, ssm_dt_bias,
                       ssm_D, ssm_norm_w) @ w_ssm_out
    mixed = jax.nn.sigmoid(gate_a) * attn + jax.nn.sigmoid(gate_s) * ssm
    x = x + mixed @ w_out
    hf = rms_norm(x, norm_ffn_w)
    u = causal_depthwise_conv(hf @ w_ffn_up, ffn_conv_w, ffn_conv_b)
    g, up = jnp.split(u, 2, axis=-1)
    x = x + (jax.nn.gelu(g, approximate=True) * up) @ w_ffn_down
    ple_gate = jax.nn.sigmoid(rms_norm(x, ple_norm_w) @ w_ple_gate)
    x = x + ple_gate * (p_i @ w_ple_proj)
    return x


def setup_inputs(seed: int = 0) -> dict:
    key = jax.random.key(seed)
    ks = jax.random.split(key, 32)
    f32 = jnp.float32

    def nrm(k, shape, scale):
        return jax.random.normal(k, shape, f32) * scale

    def gain(k, shape):
        return 1.0 + 0.05 * jax.random.normal(k, shape, f32)

    dt_init = jnp.exp(jax.random.uniform(ks[12], (DEPTH, N_SSM_HEADS), f32,
                                         math.log(1e-3), math.log(1e-1)))
    return {
        "x": nrm(ks[0], (BATCH, SEQ, D_MODEL), 1.0),
        "p": nrm(ks[1], (DEPTH, BATCH, SEQ, PLE_DIM), 1.0),
        "norm_mix_w": gain(ks[2], (DEPTH, D_MODEL)),
        "w_in": nrm(ks[3], (DEPTH, D_MODEL, IN_DIM), D_MODEL ** -0.5),
        "q_norm_w": gain(ks[4], (DEPTH, HEAD_DIM)),
        "k_norm_w": gain(ks[5], (DEPTH, HEAD_DIM)),
        "attn_sinks": nrm(ks[6], (DEPTH, N_HEADS), 1.0),
        "rel_bias_table": nrm(ks[7], (N_BUCKETS, N_HEADS), 0.5),
        "w_attn_out": nrm(ks[8], (DEPTH, Q_DIM, D_MODEL), Q_DIM ** -0.5),
        "ssm_conv_w": nrm(ks[9], (DEPTH, SSM_CONV, XBC_DIM), SSM_CONV ** -0.5),
        "ssm_conv_b": nrm(ks[10], (DEPTH, XBC_DIM), 0.02),
        "ssm_A_log": jnp.log(jax.random.uniform(ks[11], (DEPTH, N_SSM_HEADS), f32, 1.0, 16.0)),
        "ssm_dt_bias": dt_init + jnp.log(-jnp.expm1(-dt_init)),
        "ssm_D": 1.0 + 0.1 * jax.random.normal(ks[13], (DEPTH, N_SSM_HEADS), f32),
        "ssm_norm_w": gain(ks[14], (DEPTH, D_INNER)),
        "w_ssm_out": nrm(ks[15], (DEPTH, D_INNER, D_MODEL), D_INNER ** -0.5),
        "w_out": nrm(ks[16], (DEPTH, D_MODEL, D_MODEL), D_MODEL ** -0.5),
        "norm_ffn_w": gain(ks[17], (DEPTH, D_MODEL)),
        "w_ffn_up": nrm(ks[18], (DEPTH, D_MODEL, 2 * D_FF), D_MODEL ** -0.5),
        "ffn_conv_w": nrm(ks[19], (DEPTH, FFN_CONV, 2 * D_FF), FFN_CONV ** -0.5),
        "ffn_conv_b": nrm(ks[20], (DEPTH, 2 * D_FF), 0.02),
        "w_ffn_down": nrm(ks[21], (DEPTH, D_FF, D_MODEL), D_FF ** -0.5),
        "ple_norm_w": gain(ks[22], (DEPTH, D_MODEL)),
        "w_ple_gate": nrm(ks[23], (DEPTH, D_MODEL, D_MODEL), D_MODEL ** -0.5),
        "w_ple_proj": nrm(ks[24], (DEPTH, PLE_DIM, D_MODEL), PLE_DIM ** -0.5),
    }


def reference(x, p, norm_mix_w, w_in, q_norm_w, k_norm_w, attn_sinks, rel_bias_table,
              w_attn_out, ssm_conv_w, ssm_conv_b, ssm_A_log, ssm_dt_bias, ssm_D, ssm_norm_w,
              w_ssm_out, w_out, norm_ffn_w, w_ffn_up, ffn_conv_w, ffn_conv_b, w_ffn_down,
              ple_norm_w, w_ple_gate, w_ple_proj):
    for i in range(DEPTH):
        x = hybrid_layer(x, p[i], norm_mix_w[i], w_in[i], q_norm_w[i], k_norm_w[i], attn_sinks[i],
                         rel_bias_table, w_attn_out[i], ssm_conv_w[i], ssm_conv_b[i], ssm_A_log[i],
                         ssm_dt_bias[i], ssm_D[i], ssm_norm_w[i], w_ssm_out[i], w_out[i],
                         norm_ffn_w[i], w_ffn_up[i], ffn_conv_w[i], ffn_conv_b[i], w_ffn_down[i],
                         ple_norm_w[i], w_ple_gate[i], w_ple_proj[i])
    return x
```

```python
import numpy as np
import concourse.bass as bass
import concourse.mybir as mybir
from concourse.bass_utils import run_bass_kernel_spmd

F32 = mybir.dt.float32
BF16 = mybir.dt.bfloat16
AF = mybir.ActivationFunctionType
ALU = mybir.AluOpType
AX = mybir.AxisListType

D = 2048
SEQ = 2048
BATCH = 4
NH = 32
NKV = 4
DH = 64
DI = 4096
NSH = 64
NG = 8
DS = 128
DFF = 5632
PLE = 256
EPS = 1e-6
Q0 = 0
K0 = 2048
V0 = 2304
Z0 = 2560
XBC0 = 6656
DT0 = 12800
GA0 = 12864
GS0 = 14912
IN_DIM = 16960

NMC = 9
NT = NMC * 128
TP = 768
TM = 1280
NEGM = -30000.0


class Buf:
    __slots__ = ("name", "w", "r")

    def __init__(self, name, base=None):
        self.name = name
        self.w = None
        self.r = dict(base) if base else {}


class Sched:
    ENG = ("pe", "act", "dve", "pool", "sp")

    def __init__(self):
        self.streams = {e: [] for e in self.ENG}
        self.cnt = {e: 0 for e in self.ENG}
        self.dcnt = {}
        self.waited = {e: {} for e in self.ENG}
        self.dma_sems = []

    def new_dma_sem(self, name):
        self.dma_sems.append(name)
        self.dcnt[name] = 0
        return name

    def _waits(self, eng, reads, writes):
        deps = {}

        def add(s, v):
            if deps.get(s, 0) < v:
                deps[s] = v

        for b in reads:
            if b.w is not None:
                add(*b.w)
        for b in writes:
            if b.w is not None:
                add(*b.w)
            for s, v in b.r.items():
                add(s, v)
        wd = self.waited[eng]
        st = self.streams[eng]
        for s, v in deps.items():
            if wd.get(s, 0) >= v:
                continue
            wd[s] = v
            st.append(("wait", s, v))

    def op(self, eng, fns, reads=(), writes=()):
        self._waits(eng, reads, writes)
        self.cnt[eng] += 1
        c = self.cnt[eng]
        if not isinstance(fns, (list, tuple)):
            fns = [fns]
        st = self.streams[eng]
        for f in fns[:-1]:
            st.append(("inst", f, None, 0))
        st.append(("inst", fns[-1], eng, 1))
        for b in reads:
            b.r[eng] = c
        for b in writes:
            b.w = (eng, c)
            b.r = {}

    def dma(self, eng, fn, sem, reads=(), writes=()):
        self._waits(eng, reads, writes)
        self.dcnt[sem] += 16
        c = self.dcnt[sem]
        self.streams[eng].append(("inst", fn, sem, 16))
        for b in reads:
            b.r[sem] = c
        for b in writes:
            b.w = (sem, c)
            b.r = {}

    def final_wait(self, eng, bufs):
        self._waits(eng, bufs, bufs)


def collect_tokens(bufs):
    r = {}
    for b in bufs:
        if b.w is not None and r.get(b.w[0], 0) < b.w[1]:
            r[b.w[0]] = b.w[1]
        for s, v in b.r.items():
            if r.get(s, 0) < v:
                r[s] = v
    return r


class Region:
    def __init__(self, name, handle, nwords):
        self.name = name
        self.h = handle
        self.nwords = nwords
        self.off = 0
        self.bufs = []
        self.base = {}

    def recarve(self):
        self.base = collect_tokens(self.bufs)
        self.bufs = []
        self.off = 0

    def buf(self, name):
        b = Buf(name, self.base)
        self.bufs.append(b)
        return b

    def alloc(self, shape, dtype):
        nel = 1
        for s in shape:
            nel *= s
        nbytes = nel * (2 if dtype == BF16 else 4)
        nw = (nbytes + 3) // 4
        assert self.off + nw <= self.nwords, (self.name, self.off, nw, self.nwords)
        ap = self.h[:, self.off:self.off + nw]
        self.off += nw
        if dtype == BF16:
            ap = ap.bitcast(BF16)
            if nel != nw * 2:
                ap = ap[:, 0:nel]
        if len(shape) == 2:
            return ap.rearrange("p (a b) -> p a b", b=shape[1])
        if len(shape) == 3:
            return ap.rearrange("p (a b c) -> p a b c", b=shape[1], c=shape[2])
        return ap


class Builder:
    def __init__(self, debug=()):
        self.debug = set(debug)
        self.nc = bass.Bass("TRN2", target_bir_lowering=False)
        self.S = Sched()
        self.dbg_out = {}

    def dram_in(self, name, shape):
        return self.nc.dram_tensor(name, list(shape), F32, kind="ExternalInput").ap()

    def bank(self):
        i = self.bank_i
        self.bank_i = (i + 1) % 8
        return self.pbuf[i], self.ps[i]

    def wslot(self):
        i = self.w_i
        self.w_i = (i + 1) % len(self.wt)
        return i

    def wload(self, srcs, kcn):
        i = self.wslot()
        tot = sum(s.shape[1] for s in srcs)
        assert kcn * tot <= 4096
        view = self.wt[i][:, 0:kcn * tot].rearrange("p (k c) -> p k c", c=tot)
        c0 = 0
        for s in srcs:
            n = s.shape[1]
            src = s.rearrange("(k p) e -> p k e", p=128)
            dst = view[:, :, c0:c0 + n]
            self.S.dma("pool", lambda e, d=dst, s_=src: e.dma_start(out=d, in_=s_), self.wsem[i],
                       reads=(), writes=(self.wbuf[i],))
            c0 += n
        return view, self.wbuf[i]

    def dump(self, name, ap, bufs, shape, dtype=F32):
        if name not in self.debug:
            return
        t = self.nc.dram_tensor("dbg_" + name, list(shape), dtype, kind="ExternalOutput").ap()
        sem = self.S.new_dma_sem("dbgsem_" + name)
        db = Buf("dbg_" + name)
        self.S.dma("sp", lambda e, t=t, ap=ap: e.dma_start(out=t, in_=ap), sem, reads=bufs, writes=(db,))
        self.final_bufs.append(db)
        self.dbg_out[name] = "dbg_" + name

    def build(self):
        nc = self.nc
        S = self.S
        d = {}
        d["xm"] = self.dram_in("xm", [NT, D])
        d["xp"] = self.dram_in("xp", [896, D])
        d["pp"] = self.dram_in("pp", [1024, PLE])
        d["flag"] = self.dram_in("flag", [128, 1])
        d["w_in"] = self.dram_in("w_in", [D, IN_DIM])
        d["w_attn_out"] = self.dram_in("w_attn_out", [D, D])
        d["w_ssm_out"] = self.dram_in("w_ssm_out", [DI, D])
        d["w_out"] = self.dram_in("w_out", [D, D])
        d["w_ffn_up"] = self.dram_in("w_ffn_up", [D, 2 * DFF])
        d["w_ffn_down"] = self.dram_in("w_ffn_down", [DFF, D])
        d["w_ple_gate"] = self.dram_in("w_ple_gate", [D, D])
        d["w_ple_proj"] = self.dram_in("w_ple_proj", [PLE, D])
        d["norm_mix_w"] = self.dram_in("norm_mix_w", [1, D])
        d["norm_ffn_w"] = self.dram_in("norm_ffn_w", [1, D])
        d["ple_norm_w"] = self.dram_in("ple_norm_w", [1, D])
        d["ssm_norm_w"] = self.dram_in("ssm_norm_w", [1, DI])
        d["q_norm_w"] = self.dram_in("q_norm_w", [1, DH])
        d["k_norm_w"] = self.dram_in("k_norm_w", [1, DH])
        d["attn_sinks"] = self.dram_in("attn_sinks", [1, NH])
        d["ssm_A_log"] = self.dram_in("ssm_A_log", [1, NSH])
        d["ssm_dt_bias"] = self.dram_in("ssm_dt_bias", [1, NSH])
        d["ssm_D"] = self.dram_in("ssm_D", [1, NSH])
        d["cw_ssm"] = self.dram_in("cw_ssm", [128, 48 * 4])
        d["cb_ssm"] = self.dram_in("cb_ssm", [128, 48])
        d["cw_ffn"] = self.dram_in("cw_ffn", [128, 88 * 3])
        d["cb_ffn"] = self.dram_in("cb_ffn", [128, 88])
        d["biasT"] = self.dram_in("biasT", [NKV, 128, 8 * 2 * 128])
        d["cm_f"] = self.dram_in("cm_f", [128, 256])
        d["cm_b"] = self.dram_in("cm_b", [128, 128 + 512 + 1024])
        self.d = d
        self.out = nc.dram_tensor("out", [1024, D], F32, kind="ExternalOutput").ap()
        self.final_bufs = []

        R1W, R2W, R3W, R4W, CW = 10240, 6144, 18432, 9216, 4200
        import contextlib
        with contextlib.ExitStack() as es:
            def sb(name, shape, dt):
                return es.enter_context(nc.sbuf_tensor(name, shape, dt))
            r1 = Region("R1", sb("R1", [128, R1W], F32), R1W)
            r2 = Region("R2", sb("R2", [128, R2W], F32), R2W)
            r3 = Region("R3", sb("R3", [128, R3W], F32), R3W)
            r4 = Region("R4", sb("R4", [128, R4W], F32), R4W)
            rc = Region("RC", sb("RC", [128, CW], F32), CW)
            self.r1, self.r2, self.r3, self.r4, self.rc = r1, r2, r3, r4, rc
            self.wt = [sb("wt%d" % i, [128, 4096], BF16) for i in range(2)]
            self.wbuf = [Buf("wbuf%d" % i) for i in range(2)]
            self.wsem = [S.new_dma_sem("wsem%d" % i) for i in range(2)]
            self.w_i = 0
            self.ps = [es.enter_context(nc.psum_tensor("ps%d" % i, [128, 512], F32)) for i in range(8)]
            self.pbuf = [Buf("psum%d" % i) for i in range(8)]
            self.bank_i = 0

            self.setup_consts()
            self.phase0()
            self.dt_proj()
            self.ssm_prefix()
            self.ssm_main()
            self.ssm_out()
            self.attention()
            self.attn_out()
            self.wout_residual()
            self.ffn()
            self.ple()

            S.final_wait("sp", self.final_bufs)

            sem_names = list(Sched.ENG) + S.dma_sems
            sems = {}
            for n in sem_names:
                sems[n] = es.enter_context(nc.semaphore(n))
            block = es.enter_context(nc.Block())

            def replay(engname):
                def run(e):
                    for it in S.streams[engname]:
                        if it[0] == "wait":
                            e.wait_ge(sems[it[1]], it[2])
                        else:
                            ins = it[1](e)
                            if it[2] is not None:
                                ins.then_inc(sems[it[2]], it[3])
                return run

            block.sync(replay("sp"))
            block.gpsimd(replay("pool"))
            block.scalar(replay("act"))
            block.vector(replay("dve"))
            block.tensor(replay("pe"))
        return nc

    def setup_consts(self):
        S, d, rc = self.S, self.d, self.rc
        csem = S.new_dma_sem("csem")
        self.cb = Buf("consts")
        cb = self.cb

        def cload(dst, src):
            S.dma("sp", lambda e, dst=dst, src=src: e.dma_start(out=dst, in_=src), csem, writes=(cb,))

        cmf = rc.alloc([256], F32)
        cload(cmf, d["cm_f"])
        self.tri_f = cmf[:, 0:128]
        self.ones_f = cmf[:, 128:256]
        self.r3.recarve()
        cm = self.r3.alloc([128 + 512 + 1024], F32)
        cmb = self.r3.buf("cm_stage")
        S.dma("sp", lambda e: e.dma_start(out=cm, in_=d["cm_b"]), csem, writes=(cmb,))
        self.ident_bf = rc.alloc([128], BF16)
        self.neg4 = rc.alloc([512], BF16)
        self.sel = rc.alloc([1024], BF16)
        self._cm = cm
        self.Ab = rc.alloc([64], F32)
        self.Db = rc.alloc([64], F32)
        self.dtb = rc.alloc([64], F32)
        self.flag = rc.alloc([1], F32)
        self.cw_ssm = rc.alloc([48, 4], F32)
        self.cb_ssm = rc.alloc([48], F32)
        self.cw_ffn = rc.alloc([88, 3], F32)
        self.cb_ffn = rc.alloc([88], F32)
        self.qg = rc.alloc([64], F32)
        self.kg = rc.alloc([64], F32)
        self.esink = rc.alloc([32], F32)
        self.dt_all = rc.alloc([16, 64], F32)
        self.a_all = rc.alloc([16, 64], F32)
        self.ss16 = rc.alloc([16], F32)
        self.sd16 = rc.alloc([16], F32)
        self.rs16 = rc.alloc([16], F32)
        cload(self.Ab, d["ssm_A_log"].partition_broadcast(128))
        cload(self.Db, d["ssm_D"].partition_broadcast(128))
        cload(self.dtb, d["ssm_dt_bias"].partition_broadcast(128))
        cload(self.flag, d["flag"])
        cload(self.cw_ssm, d["cw_ssm"].rearrange("p (b k) -> p b k", k=4))
        cload(self.cb_ssm, d["cb_ssm"])
        cload(self.cw_ffn, d["cw_ffn"].rearrange("p (b k) -> p b k", k=3))
        cload(self.cb_ffn, d["cb_ffn"])
        cload(self.qg, d["q_norm_w"].partition_broadcast(128))
        cload(self.kg, d["k_norm_w"].partition_broadcast(128))
        cload(self.esink, d["attn_sinks"].partition_broadcast(128))
        S.op("act", [
            lambda e: e.activation(out=self.ident_bf, in_=cm[:, 0:128], func=AF.Copy),
            lambda e: e.activation(out=self.neg4, in_=cm[:, 128:640], func=AF.Copy),
            lambda e: e.activation(out=self.sel, in_=cm[:, 640:1664], func=AF.Copy),
            lambda e: e.activation(out=self.esink, in_=self.esink, func=AF.Exp),
            lambda e: e.activation(out=self.Ab, in_=self.Ab, func=AF.Exp),
        ], reads=(cb, cmb), writes=(cb,))
        S.op("dve", [
            lambda e: e.tensor_scalar(out=self.Ab, in0=self.Ab, scalar1=-1.0, scalar2=None, op0=ALU.mult),
            lambda e: e.tensor_scalar(out=self.qg, in0=self.qg, scalar1=0.125, scalar2=None, op0=ALU.mult),
            lambda e: e.tensor_scalar(out=self.cw_ssm, in0=self.cw_ssm, scalar1=0.5, scalar2=None, op0=ALU.mult),
            lambda e: e.tensor_scalar(out=self.cb_ssm, in0=self.cb_ssm, scalar1=0.5, scalar2=None, op0=ALU.mult),
        ], reads=(cb,), writes=(cb,))

    def hT(self, kc, t0, n):
        if t0 < TP:
            assert t0 + n <= TP
            return self.hTp[:, kc, t0:t0 + n]
        return self.hTm[:, kc, t0 - TP:t0 - TP + n]

    def hT_bufs(self, t0, n):
        c0 = t0 // 128
        c1 = (t0 + n - 1) // 128
        return [self.hTb[c] for c in range(c0, c1 + 1)]

    def rms_tile(self, x_ap, xbufs, gain_bc, gbuf, hb, hbbuf, sidx, scratch_bf, eps=EPS, n=D):
        S = self.S
        ssb = self.small_b[sidx]
        ss, sd, rs = self.ss16[:, sidx:sidx + 1], self.sd16[:, sidx:sidx + 1], self.rs16[:, sidx:sidx + 1]
        S.op("act", lambda e: e.activation(out=scratch_bf, in_=x_ap, func=AF.Square, accum_out=ss),
             reads=list(xbufs), writes=(ssb, hbbuf))
        S.op("act", lambda e: e.activation(out=sd, in_=ss, func=AF.Sqrt, bias=eps, scale=1.0 / n),
             reads=(ssb,), writes=(ssb,))
        S.op("dve", lambda e: e.reciprocal(out=rs, in_=sd), reads=(ssb,), writes=(ssb,))
        S.op("dve", lambda e: e.scalar_tensor_tensor(out=hb, in0=x_ap, scalar=rs, in1=gain_bc,
                                                      op0=ALU.mult, op1=ALU.mult),
             reads=list(xbufs) + [ssb, gbuf], writes=(hbbuf,))

    def transpose_to(self, src_tiles, src_bufs, dst_ap_fn, dst_bufs, evac="act"):
        S = self.S
        n = len(src_tiles)
        pb, ps = self.bank()
        psb = ps[:].bitcast(BF16).rearrange("p (a b) -> p a b", b=128)
        fns = []
        for i, t in enumerate(src_tiles):
            fns.append(lambda e, i=i, t=t: e.transpose(out=psb[:, i, :], in_=t, identity=self.ident_bf))
        S.op("pe", fns, reads=list(src_bufs) + [self.cb], writes=(pb,))
        dst = dst_ap_fn(n)
        if evac == "act":
            S.op("act", lambda e: e.activation(out=dst, in_=psb[:, 0:n, :], func=AF.Copy),
                 reads=(pb,), writes=list(dst_bufs))
        else:
            S.op("dve", lambda e: e.tensor_copy(out=dst, in_=psb[:, 0:n, :]), reads=(pb,), writes=list(dst_bufs))

    def proj_fm(self, wview, wb, e0, kcn, act_fn, act_bufs, ranges):
        S = self.S
        banks = [self.bank() for _ in ranges]
        fns = []
        for kc in range(kcn):
            for (pb, ps), (t0, n) in zip(banks, ranges):
                fns.append(lambda e, kc=kc, ps=ps, t0=t0, n=n: e.matmul(
                    ps[:, 0:n], lhsT=wview[:, kc, e0:e0 + 128], rhs=act_fn(kc, t0, n),
                    start=(kc == 0), stop=(kc == kcn - 1)))
        S.op("pe", fns, reads=[wb] + list(act_bufs), writes=[pb for pb, _ in banks])
        return [(pb, ps[:, 0:n]) for (pb, ps), (t0, n) in zip(banks, ranges)]

    def proj_tm(self, wview, wb, c0, ncols, kcn, lhs_fn, act_bufs):
        S = self.S
        pb, ps = self.bank()
        fns = []
        for kc in range(kcn):
            fns.append(lambda e, kc=kc: e.matmul(ps[:, 0:ncols], lhsT=lhs_fn(kc), rhs=wview[:, kc, c0:c0 + ncols],
                                                 start=(kc == 0), stop=(kc == kcn - 1)))
        S.op("pe", fns, reads=[wb] + list(act_bufs), writes=(pb,))
        return pb, ps[:, 0:ncols]

    def phase0(self):
        S, d = self.S, self.d
        r1, r2, r4 = self.r1, self.r2, self.r4
        self.hTm = r1.alloc([16, TM], BF16)
        self.hTp = r2.alloc([16, TP], BF16)
        self.hTb = [Buf("hT%d" % c) for c in range(16)]
        r1.bufs += self.hTb[6:]
        r2.bufs += self.hTb[:6]
        self.small_b = [Buf("small%d" % i) for i in range(16)]
        r4.recarve()
        xt = [r4.alloc([D], F32) for _ in range(2)]
        xb = [r4.buf("xt%d" % i) for i in range(2)]
        xs = [S.new_dma_sem("xsem%d" % i) for i in range(2)]
        self.xt_sems = xs
        hb = [r4.alloc([D], BF16) for _ in range(2)]
        hbb = [r4.buf("hb%d" % i) for i in range(2)]
        gain = r4.alloc([D], F32)
        gb = r4.buf("gain")
        S.dma("sp", lambda e: e.dma_start(out=gain, in_=d["norm_mix_w"].partition_broadcast(128)), S.new_dma_sem("gsem0"), writes=(gb,))
        for ci in range(16):
            s = ci % 2
            src = d["xp"][ci * 128:(ci + 1) * 128, :] if ci < 7 else d["xm"][(ci - 7) * 128:(ci - 6) * 128, :]
            S.dma("sp", lambda e, s=s, src=src: e.dma_start(out=xt[s], in_=src), xs[s], writes=(xb[s],))
            self.rms_tile(xt[s], [xb[s]], gain, gb, hb[s], hbb[s], ci, hb[s])
            t0 = ci * 128
            for half in range(2):
                tiles = [hb[s][:, (half * 8 + i) * 128:(half * 8 + i + 1) * 128] for i in range(8)]
                self.transpose_to(tiles, [hbb[s]],
                                  lambda n, half=half, t0=t0: (self.hTp[:, half * 8:half * 8 + 8, t0:t0 + 128] if t0 < TP
                                                               else self.hTm[:, half * 8:half * 8 + 8, t0 - TP:t0 - TP + 128]),
                                  [self.hTb[ci]])
        self.dump("hTm", self.hTm, self.hTb[6:], [128, 16, TM], BF16)

    def dt_proj(self):
        S, d = self.S, self.d
        r4 = self.r4
        r4.recarve()
        raw = r4.alloc([16, 64], F32)
        rawb = r4.buf("dtraw")
        wv, wb = self.wload([d["w_in"][:, DT0:DT0 + 64]], 16)
        for ci in range(16):
            t0 = ci * 128
            pb, ps = self.proj_tm(wv, wb, 0, 64, 16, lambda kc, t0=t0: self.hT(kc, t0, 128), [self.hTb[ci]])
            S.op("dve", lambda e, ci=ci, ps=ps: e.tensor_tensor(out=raw[:, ci, :], in0=ps, in1=self.dtb, op=ALU.add),
                 reads=(pb, self.cb), writes=(rawb,))
        dtb_ = Buf("dt_all")
        self.dt_buf = dtb_
        S.op("act", lambda e: e.activation(out=raw, in_=raw, func=AF.Exp), reads=(rawb,), writes=(rawb,))
        S.op("act", lambda e: e.activation(out=self.dt_all, in_=raw, func=AF.Ln, bias=1.0, scale=1.0),
             reads=(rawb,), writes=(dtb_,))
        S.op("dve", lambda e: e.tensor_tensor(out=self.a_all, in0=self.dt_all,
                                              in1=self.Ab.unsqueeze(1).to_broadcast([128, 16, 64]), op=ALU.mult),
             reads=(dtb_, self.cb), writes=(dtb_,))
        self.dump("dt_all", self.dt_all, [dtb_], [128, 16, 64])

    def conv_silu(self, pre, preb, n_out, blk, acc, accb, tt, ttb, out_bf, outb, lo=0):
        S = self.S
        w = self.cw_ssm
        S.op("dve", [
            lambda e: e.tensor_scalar(out=acc[:, 0:n_out], in0=pre[:, lo + 3:lo + 3 + n_out], scalar1=w[:, blk, 3:4],
                                      scalar2=self.cb_ssm[:, blk:blk + 1], op0=ALU.mult, op1=ALU.add)],
             reads=(preb, self.cb), writes=(accb,))
        for k in (2, 1, 0):
            S.op("dve", lambda e, k=k: e.scalar_tensor_tensor(out=acc[:, 0:n_out], in0=pre[:, lo + k:lo + k + n_out],
                                                             scalar=w[:, blk, k:k + 1], in1=acc[:, 0:n_out],
                                                             op0=ALU.mult, op1=ALU.add),
                 reads=(preb, self.cb, accb), writes=(accb,))
        S.op("act", lambda e: e.activation(out=tt[:, 0:n_out], in_=acc[:, 0:n_out], func=AF.Tanh),
             reads=(accb,), writes=(ttb,))
        S.op("dve", lambda e: e.scalar_tensor_tensor(out=out_bf, in0=tt[:, 0:n_out], scalar=1.0, in1=acc[:, 0:n_out],
                                                      op0=ALU.add, op1=ALU.mult),
             reads=(ttb, accb), writes=(outb,))

    def chunk_smalls(self, ci, g, sm, smb):
        S = self.S
        a = self.a_all[:, ci, g * 8:(g + 1) * 8]
        dt = self.dt_all[:, ci, g * 8:(g + 1) * 8]
        pb, ps = self.bank()
        S.op("pe", [
            lambda e: e.matmul(ps[:, 0:8], lhsT=self.tri_f, rhs=a, start=True, stop=True),
            lambda e: e.matmul(ps[:, 8:16], lhsT=self.ones_f, rhs=a, start=True, stop=True),
        ], reads=(self.dt_buf, self.cb), writes=(pb,))
        S.op("act", [
            lambda e: e.activation(out=sm["s16"], in_=ps[:, 0:16], func=AF.Copy),
            lambda e: e.activation(out=sm["e16"], in_=ps[:, 0:16], func=AF.Exp),
        ], reads=(pb,), writes=(smb["s16"],))
        S.op("dve", [
            lambda e: e.tensor_tensor(out=sm["dwl"], in0=sm["s16"][:, 8:16], in1=sm["s16"][:, 0:8], op=ALU.subtract),
            lambda e: e.tensor_scalar(out=sm["nacum"], in0=sm["s16"][:, 0:8], scalar1=-1.0, scalar2=None, op0=ALU.mult),
        ], reads=(smb["s16"],), writes=(smb["dwl"],))
        S.op("act", lambda e: e.activation(out=sm["w"], in_=sm["dwl"], func=AF.Exp), reads=(smb["dwl"],), writes=(smb["w"],))
        S.op("dve", lambda e: e.tensor_tensor(out=sm["dtw"], in0=sm["w"], in1=dt, op=ALU.mult),
             reads=(smb["w"], self.dt_buf), writes=(smb["dtw"],))
        return a, dt

    def alloc_smalls(self, reg):
        sm, smb = {}, {}
        for nm, n in (("s16", 16), ("e16", 16), ("dwl", 8), ("nacum", 8), ("w", 8), ("dtw", 8)):
            sm[nm] = reg.alloc([n], F32)
        for nm in ("s16", "dwl", "w", "dtw"):
            smb[nm] = reg.buf("sm_" + nm)
        return sm, smb

    def state_update(self, ci, g, sm, smb, xs_c, xsb, Btm_c, Btmb, xw, xwb, Scur, Sb, tmpS, tmpSb, apply_flag):
        S = self.S
        S.op("dve", lambda e: e.tensor_tensor(out=xw.rearrange("p (j q) -> p j q", q=64),
                                              in0=xs_c.rearrange("p (j q) -> p j q", q=64),
                                              in1=sm["dtw"].unsqueeze(2).to_broadcast([128, 8, 64]), op=ALU.mult),
             reads=(xsb, smb["dtw"]), writes=(xwb,))
        pb, ps = self.bank()
        S.op("pe", lambda e: e.matmul(ps[:, 0:512], lhsT=Btm_c, rhs=xw, start=True, stop=True),
             reads=(Btmb, xwb), writes=(pb,))
        S.op("dve", lambda e: e.tensor_tensor(out=tmpS.rearrange("p (j q) -> p j q", q=64),
                                              in0=Scur.rearrange("p (j q) -> p j q", q=64),
                                              in1=sm["e16"][:, 8:16].unsqueeze(2).to_broadcast([128, 8, 64]), op=ALU.mult),
             reads=(Sb, smb["s16"]), writes=(tmpSb,))
        S.op("dve", lambda e: e.tensor_tensor(out=Scur, in0=tmpS, in1=ps[:, 0:512], op=ALU.add),
             reads=(tmpSb, pb), writes=(Sb,))
        if apply_flag:
            S.op("dve", lambda e: e.tensor_scalar(out=Scur, in0=Scur, scalar1=self.flag[:, 0:1], scalar2=None, op0=ALU.mult),
                 reads=(Sb, self.cb), writes=(Sb,))

    def ssm_prefix(self):
        S, d = self.S, self.d
        r3, r4 = self.r3, self.r4
        r3.recarve()
        self.stash = r3.h[:, :].rearrange("p (g x) -> p g x", g=8)
        self.yTb = [r3.buf("yT%d" % g) for g in range(8)]
        r4.recarve()
        NP = 896
        pre = r4.alloc([3 + NP], F32); preb = r4.buf("pre")
        acc = r4.alloc([NP], F32); accb = r4.buf("acc")
        tt = r4.alloc([NP], F32); ttb = r4.buf("tt")
        cvo = r4.alloc([NP], BF16); cvob = r4.buf("cvo")
        xs_tm = r4.alloc([7, 512], BF16); xsb = r4.buf("xs_tm")
        B_tm = r4.alloc([7, 128], BF16); Btmb = r4.buf("B_tm")
        xw = r4.alloc([512], BF16); xwb = r4.buf("xw")
        Scur = r4.alloc([512], F32); Sb = r4.buf("Scur")
        tmpS = r4.alloc([512], F32); tmpSb = r4.buf("tmpS")
        sm, smb = self.alloc_smalls(r4)
        S.op("dve", lambda e: e.memset(pre[:, 0:3], 0.0), writes=(preb,))
        ranges = [(0, 384), (384, 384), (768, 128)]
        abufs = self.hTb[0:7]
        for g in range(NG):
            S.op("dve", lambda e: e.memset(Scur, 0.0), writes=(Sb,))
            cols = [XBC0 + g * 512 + j * 128 for j in range(4)] + [XBC0 + DI + g * 128]
            wv = None
            for bi, c0 in enumerate(cols):
                blk = (c0 - XBC0) // 128
                if bi in (0, 2):
                    wv, wb = self.wload([d["w_in"][:, c0:c0 + 256]], 16)
                    e0 = 0
                elif bi == 4:
                    wv, wb = self.wload([d["w_in"][:, c0:c0 + 128]], 16)
                    e0 = 0
                else:
                    e0 = 128
                outs = self.proj_fm(wv, wb, e0, 16, self.hT, abufs, ranges)
                for (pb, ps), (t0, n) in zip(outs, ranges):
                    S.op("act", lambda e, ps=ps, t0=t0, n=n: e.activation(out=pre[:, 3 + t0:3 + t0 + n], in_=ps, func=AF.Copy),
                         reads=(pb,), writes=(preb,))
                self.conv_silu(pre, preb, NP, blk, acc, accb, tt, ttb, cvo, cvob)
                tiles = [cvo[:, c * 128:(c + 1) * 128] for c in range(7)]
                if bi < 4:
                    self.transpose_to(tiles, [cvob], lambda n, bi=bi: xs_tm[:, 0:7, bi * 128:(bi + 1) * 128], [xsb])
                else:
                    self.transpose_to(tiles, [cvob], lambda n: B_tm[:, 0:7, :], [Btmb])
            for c in range(7):
                self.chunk_smalls(c, g, sm, smb)
                self.state_update(c, g, sm, smb, xs_tm[:, c, :], xsb, B_tm[:, c, :], Btmb, xw, xwb,
                                  Scur, Sb, tmpS, tmpSb, False)
            S.op("act", lambda e, g=g: e.activation(out=self.stash[:, g, 0:512], in_=Scur, func=AF.Copy),
                 reads=(Sb,), writes=(self.yTb[g],))
        self.dump("stash", self.stash[:, :, 0:512], self.yTb, [128, 8, 512])

    def ssm_main(self):
        S, d = self.S, self.d
        r2, r3, r4 = self.r2, self.r3, self.r4
        yT = r3.h[:, :].bitcast(BF16).rearrange("p (k t) -> p k t", t=NT)
        self.yT = yT
        r2.recarve(); r4.recarve()
        NPRE = NT + 3
        pre = r4.alloc([NPRE], F32); preb = r4.buf("pre")
        acc = r4.alloc([384], F32); accb = r4.buf("acc")
        tt = r4.alloc([384], F32); ttb = r4.buf("tt")
        cvo = r4.alloc([384], BF16); cvob = r4.buf("cvo")
        BT = r4.alloc([NT], BF16); BTb = r4.buf("BT")
        CT = r4.alloc([NT], BF16); CTb = r4.buf("CT")
        xs_tm = r4.alloc([NMC, 512], BF16); xsb = r4.buf("xs_tm")
        B_tm = r4.alloc([NMC, 128], BF16); Btmb = r4.buf("B_tm")
        zs = r4.alloc([NMC, 512], BF16); zsb = r4.buf("zs")
        ztmp = r4.alloc([256], F32); ztb = r4.buf("ztmp")
        normw = r2.alloc([512], F32); nwb = r2.buf("normw")
        Scur = r2.alloc([512], F32); Sb = r2.buf("Scur")
        tmpS = r2.alloc([512], F32); tmpSb = r2.buf("tmpS")
        xdt = r2.alloc([512], BF16); xdtb = r2.buf("xdt")
        xw = r2.alloc([512], BF16); xwb = r2.buf("xw")
        Sbf = r2.alloc([512], BF16); Sbfb = r2.buf("Sbf")
        segT = r2.alloc([8, 128], BF16); segb = r2.buf("segT")
        MT = r2.alloc([8, 128], BF16); MTb = r2.buf("MT")
        cbT = r2.alloc([128], BF16); cbTb = r2.buf("cbT")
        hi = r2.alloc([128], BF16); lo = r2.alloc([128], BF16); hib = r2.buf("hilo")
        hl8 = r2.alloc([2, 8], BF16); hl8b = r2.buf("hl8")
        t1 = r2.alloc([512], F32); t1b = r2.buf("t1")
        yy = r2.alloc([512], F32); yb = r2.buf("y")
        yn = r2.alloc([512], BF16); ynb = r2.buf("yn")
        sq = r2.alloc([512], BF16); sqb = r2.buf("sq")
        ysm = r2.alloc([4], F32); ysmb = r2.buf("ysm")
        sm, smb = self.alloc_smalls(r2)
        nsem = S.new_dma_sem("nwsem")
        ranges = [(893 + i * 385, 385) for i in range(3)]
        abufs = self.hTb[6:16]
        mainbufs = self.hTb[7:16]
        for g in range(NG):
            S.op("act", lambda e, g=g: e.activation(out=Scur, in_=self.stash[:, g, 0:512], func=AF.Copy),
                 reads=(self.yTb[g],), writes=(Sb,))
            S.dma("sp", lambda e, g=g: e.dma_start(out=normw, in_=d["ssm_norm_w"][:, g * 512:(g + 1) * 512].partition_broadcast(128)),
                  nsem, writes=(nwb,))
            for hv in range(2):
                c0 = Z0 + g * 512 + hv * 256
                wv, wb = self.wload([d["w_in"][:, c0:c0 + 256]], 16)
                for mc in range(NMC):
                    t0 = 896 + mc * 128
                    pb, ps = self.proj_tm(wv, wb, 0, 256, 16, lambda kc, t0=t0: self.hT(kc, t0, 128), [self.hTb[7 + mc]])
                    S.op("act", lambda e, ps=ps: e.activation(out=ztmp, in_=ps, func=AF.Tanh, scale=0.5),
                         reads=(pb,), writes=(ztb,))
                    S.op("dve", lambda e, ps=ps, mc=mc, hv=hv: e.scalar_tensor_tensor(
                        out=zs[:, mc, hv * 256:(hv + 1) * 256], in0=ztmp, scalar=1.0, in1=ps, op0=ALU.add, op1=ALU.mult),
                        reads=(ztb, pb), writes=(zsb,))
            cols = [XBC0 + g * 512 + j * 128 for j in range(4)] + [XBC0 + DI + g * 128, XBC0 + DI + 1024 + g * 128]
            for bi, c0 in enumerate(cols):
                blk = (c0 - XBC0) // 128
                if bi in (0, 2):
                    wv, wb = self.wload([d["w_in"][:, c0:c0 + 256]], 16)
                    e0 = 0
                elif bi == 4:
                    wv, wb = self.wload([d["w_in"][:, c0:c0 + 128], d["w_in"][:, cols[5]:cols[5] + 128]], 16)
                    e0 = 0
                else:
                    e0 = 128
                outs = self.proj_fm(wv, wb, e0, 16, self.hT, abufs, ranges)
                for i, (pb, ps) in enumerate(outs):
                    S.op("act", lambda e, ps=ps, i=i: e.activation(out=pre[:, i * 385:(i + 1) * 385], in_=ps, func=AF.Copy),
                         reads=(pb,), writes=(preb,))
                for th in range(3):
                    if bi < 4:
                        dst, dstb = cvo, cvob
                    elif bi == 4:
                        dst, dstb = BT[:, th * 384:(th + 1) * 384], BTb
                    else:
                        dst, dstb = CT[:, th * 384:(th + 1) * 384], CTb
                    self.conv_silu(pre, preb, 384, blk, acc, accb, tt, ttb, dst, dstb, lo=th * 384)
                    if bi < 4:
                        tiles = [cvo[:, c * 128:(c + 1) * 128] for c in range(3)]
                        self.transpose_to(tiles, [cvob],
                                          lambda n, bi=bi, th=th: xs_tm[:, th * 3:th * 3 + 3, bi * 128:(bi + 1) * 128], [xsb])
                    elif bi == 4:
                        tiles = [BT[:, (th * 3 + c) * 128:(th * 3 + c + 1) * 128] for c in range(3)]
                        self.transpose_to(tiles, [BTb], lambda n, th=th: B_tm[:, th * 3:th * 3 + 3, :], [Btmb])
            Dg = self.Db[:, g * 8:(g + 1) * 8]
            for mc in range(NMC):
                ci = 7 + mc
                tc = slice(mc * 128, (mc + 1) * 128)
                a, dt = self.chunk_smalls(ci, g, sm, smb)
                S.op("dve", [
                    lambda e: e.tensor_copy(out=hl8[:, 0, :], in_=sm["s16"][:, 0:8]),
                ], reads=(smb["s16"],), writes=(hl8b,))
                S.op("dve", [
                    lambda e: e.tensor_tensor(out=hl8[:, 1, :], in0=sm["s16"][:, 0:8], in1=hl8[:, 0, :], op=ALU.subtract),
                ], reads=(smb["s16"], hl8b), writes=(hl8b,))
                pb, ps = self.bank()
                psb = ps[:].bitcast(BF16).rearrange("p (a b) -> p a b", b=128)
                S.op("pe", [
                    lambda e, psb=psb: e.transpose(out=psb[0:8, 0, :], in_=hl8[:, 0, :], identity=self.ident_bf),
                    lambda e, psb=psb: e.transpose(out=psb[0:8, 1, :], in_=hl8[:, 1, :], identity=self.ident_bf),
                ], reads=(hl8b, self.cb), writes=(pb,))
                S.op("act", [
                    lambda e, psb=psb: e.activation(out=hi[0:8, :], in_=psb[0:8, 0, :], func=AF.Copy),
                    lambda e, psb=psb: e.activation(out=lo[0:8, :], in_=psb[0:8, 1, :], func=AF.Copy),
                ], reads=(pb,), writes=(hib,))
                dbanks = [self.bank() for _ in range(2)]
                fns = []
                for bq, (pbq, psq) in enumerate(dbanks):
                    fns.append(lambda e, psq=psq: e.matmul(psq[:, 0:512], lhsT=self.ident_bf, rhs=self.neg4,
                                                           start=True, stop=False))
                    for jj in range(4):
                        j = bq * 4 + jj
                        fns.append(lambda e, psq=psq, jj=jj, j=j: e.matmul(
                            psq[:, jj * 128:(jj + 1) * 128], lhsT=self.sel[0:8, j * 128:(j + 1) * 128], rhs=hi[0:8, :],
                            start=False, stop=False, skip_group_check=True))
                        fns.append(lambda e, psq=psq, jj=jj, j=j: e.matmul(
                            psq[:, jj * 128:(jj + 1) * 128], lhsT=self.sel[0:8, j * 128:(j + 1) * 128], rhs=lo[0:8, :],
                            start=False, stop=(jj == 3), skip_group_check=True))
                S.op("pe", fns, reads=(hib, self.cb), writes=[p for p, _ in dbanks])
                fns = []
                for j in range(8):
                    psq = dbanks[j // 4][1]
                    fns.append(lambda e, psq=psq, j=j: e.activation(out=segT[:, j, :], in_=psq[:, (j % 4) * 128:(j % 4 + 1) * 128],
                                                                    func=AF.Exp, bias=sm["nacum"][:, j:j + 1], scale=1.0))
                S.op("act", fns, reads=[p for p, _ in dbanks] + [smb["dwl"]], writes=(segb,))
                pb, ps = self.bank()
                S.op("pe", lambda e, ps=ps, tc=tc: e.matmul(ps[:, 0:128], lhsT=BT[:, tc], rhs=CT[:, tc], start=True, stop=True),
                     reads=(BTb, CTb), writes=(pb,))
                S.op("act", lambda e, ps=ps: e.activation(out=cbT, in_=ps[:, 0:128], func=AF.Copy), reads=(pb,), writes=(cbTb,))
                S.op("dve", lambda e: e.tensor_tensor(out=MT, in0=segT, in1=cbT.unsqueeze(1).to_broadcast([128, 8, 128]), op=ALU.mult),
                     reads=(segb, cbTb), writes=(MTb,))
                xs_c = xs_tm[:, mc, :]
                S.op("dve", lambda e, xs_c=xs_c, dt=dt: e.tensor_tensor(out=xdt.rearrange("p (j q) -> p j q", q=64),
                                                                       in0=xs_c.rearrange("p (j q) -> p j q", q=64),
                                                                       in1=dt.unsqueeze(2).to_broadcast([128, 8, 64]), op=ALU.mult),
                     reads=(xsb, self.dt_buf), writes=(xdtb,))
                S.op("act", lambda e: e.activation(out=Sbf, in_=Scur, func=AF.Copy), reads=(Sb,), writes=(Sbfb,))
                pbd, psd = self.bank()
                fns = [lambda e, j=j, psd=psd: e.matmul(psd[:, j * 64:(j + 1) * 64], lhsT=MT[:, j, :], rhs=xdt[:, j * 64:(j + 1) * 64],
                                                        start=True, stop=True) for j in range(8)]
                S.op("pe", fns, reads=(MTb, xdtb), writes=(pbd,))
                pbo, pso = self.bank()
                S.op("pe", lambda e, pso=pso, tc=tc: e.matmul(pso[:, 0:512], lhsT=CT[:, tc], rhs=Sbf, start=True, stop=True),
                     reads=(CTb, Sbfb), writes=(pbo,))
                S.op("dve", lambda e, pso=pso: e.tensor_tensor(out=t1.rearrange("p (j q) -> p j q", q=64),
                                                              in0=pso[:, 0:512].rearrange("p (j q) -> p j q", q=64),
                                                              in1=sm["e16"][:, 0:8].unsqueeze(2).to_broadcast([128, 8, 64]), op=ALU.mult),
                     reads=(pbo, smb["s16"]), writes=(t1b,))
                S.op("dve", lambda e, psd=psd: e.tensor_tensor(out=yy, in0=t1, in1=psd[:, 0:512], op=ALU.add),
                     reads=(t1b, pbd), writes=(yb,))
                S.op("dve", lambda e, xs_c=xs_c, Dg=Dg: e.tensor_tensor(out=t1.rearrange("p (j q) -> p j q", q=64),
                                                                       in0=xs_c.rearrange("p (j q) -> p j q", q=64),
                                                                       in1=Dg.unsqueeze(2).to_broadcast([128, 8, 64]), op=ALU.mult),
                     reads=(xsb, self.cb, yb), writes=(t1b,))
                S.op("dve", lambda e: e.tensor_tensor(out=yy, in0=yy, in1=t1, op=ALU.add), reads=(t1b, yb), writes=(yb,))
                S.op("dve", lambda e, mc=mc: e.tensor_tensor(out=yy, in0=yy, in1=zs[:, mc, :], op=ALU.mult), reads=(yb, zsb), writes=(yb,))
                self.state_update(ci, g, sm, smb, xs_c, xsb, B_tm[:, mc, :], Btmb, xw, xwb, Scur, Sb, tmpS, tmpSb, mc == 0)
                S.op("act", lambda e: e.activation(out=sq, in_=yy, func=AF.Square, accum_out=ysm[:, 0:1]), reads=(yb,), writes=(sqb, ysmb))
                S.op("act", lambda e: e.activation(out=ysm[:, 1:2], in_=ysm[:, 0:1], func=AF.Sqrt, bias=4 * EPS, scale=1.0 / 512),
                     reads=(ysmb,), writes=(ysmb,))
                S.op("dve", lambda e: e.reciprocal(out=ysm[:, 2:3], in_=ysm[:, 1:2]), reads=(ysmb,), writes=(ysmb,))
                S.op("dve", lambda e: e.scalar_tensor_tensor(out=yn, in0=yy, scalar=ysm[:, 2:3], in1=normw, op0=ALU.mult, op1=ALU.mult),
                     reads=(yb, ysmb, nwb), writes=(ynb,))
                tiles = [yn[:, i * 128:(i + 1) * 128] for i in range(4)]
                self.transpose_to(tiles, [ynb], lambda n, g=g, tc=tc: yT[:, 4 * g:4 * g + 4, tc], [self.yTb[g]])
        self.dump("yT", yT, self.yTb, [128, 32, NT], BF16)

    def ssm_out(self):
        S, d = self.S, self.d
        r2, r4 = self.r2, self.r4
        r2.recarve(); r4.recarve()
        self.mixedT = r4.alloc([16, NT], BF16)
        self.mixb = [r4.buf("mixedT%d" % i) for i in range(16)]
        tg = r2.alloc([3, 384], F32); tgb = r2.buf("tg")
        self.tg, self.tgb = tg, tgb
        ranges = [(896 + i * 384, 384) for i in range(3)]
        mainbufs = self.hTb[7:16]
        yT = self.yT
        for db in range(16):
            if db % 2 == 0:
                wvg, wbg = self.wload([d["w_in"][:, GS0 + db * 128:GS0 + db * 128 + 256]], 16)
            outs = self.proj_fm(wvg, wbg, (db % 2) * 128, 16, self.hT, mainbufs, ranges)
            for i, (pb, ps) in enumerate(outs):
                S.op("act", lambda e, ps=ps, i=i: e.activation(out=tg[:, i, :], in_=ps, func=AF.Tanh, scale=0.5),
                     reads=(pb,), writes=(tgb,))
            wv, wb = self.wload([d["w_ssm_out"][:, db * 128:(db + 1) * 128]], 32)
            outs = self.proj_fm(wv, wb, 0, 32, lambda kc, t0, n: yT[:, kc, t0 - 896:t0 - 896 + n], self.yTb, ranges)
            for i, (pb, ps) in enumerate(outs):
                S.op("dve", lambda e, ps=ps, i=i, db=db: e.scalar_tensor_tensor(
                    out=self.mixedT[:, db, i * 384:(i + 1) * 384], in0=tg[:, i, :], scalar=1.0, in1=ps, op0=ALU.add, op1=ALU.mult),
                    reads=(tgb, pb), writes=(self.mixb[db],))
        self.dump("mixS", self.mixedT, self.mixb, [128, 16, NT], BF16)

    def attention(self):
        S, d = self.S, self.d
        r2, r3 = self.r2, self.r3
        r2.recarve(); r3.recarve()
        self.aoT = r3.alloc([16, NT], BF16)
        self.aoTb = [r3.buf("aoT%d" % i) for i in range(NKV)]
        qT = r3.alloc([4, NT], BF16); qTb = r3.buf("qT")
        kT2 = r3.alloc([TM], BF16); kTb = r3.buf("kT2")
        v1 = r3.alloc([10, 65], BF16); v1b = r3.buf("v1")
        biasT = r3.alloc([8, 2, 128], F32); biasb = r3.buf("biasT")
        q32 = r3.alloc([512], F32); q32b = r3.buf("q32")
        qtmp = r3.alloc([512], F32); qtmpb = r3.buf("qtmp")
        qn = r3.alloc([512], BF16); qnb = r3.buf("qn")
        kv32 = r3.alloc([128], F32); kvb = r3.buf("kv32")
        ktmp = r3.alloc([64], F32); ktmpb = r3.buf("ktmp")
        kdup = r3.alloc([2, 64], BF16); kdupb = r3.buf("kdup")
        qs = r3.alloc([32], F32); qsb = r3.buf("qs")
        ltmp = r2.alloc([512], F32); ltb = r2.buf("ltmp")
        eT = [r2.alloc([2, 2, 128], BF16) for _ in range(2)]; eTb = [r2.buf("eT%d" % i) for i in range(2)]
        den = r2.alloc([16], F32); denb = r2.buf("den")
        ao = r2.alloc([8, 64], BF16); aob = r2.buf("ao")
        bsem = S.new_dma_sem("biassem")
        S.op("dve", lambda e: e.memset(v1[:, :, 64:65], 1.0), writes=(v1b,))
        for kg in range(NKV):
            S.dma("sp", lambda e, kg=kg: e.dma_start(out=biasT, in_=d["biasT"][kg].rearrange("p (h b q) -> p h b q", h=8, b=2)),
                  bsem, writes=(biasb,))
            wv, wb = self.wload([d["w_in"][:, K0 + kg * 64:K0 + kg * 64 + 64], d["w_in"][:, V0 + kg * 64:V0 + kg * 64 + 64]], 16)
            for cj in range(10):
                t0 = 768 + cj * 128
                pb, ps = self.proj_tm(wv, wb, 0, 128, 16, lambda kc, t0=t0: self.hT(kc, t0, 128), [self.hTb[6 + cj]])
                S.op("act", lambda e, ps=ps: e.activation(out=kv32, in_=ps, func=AF.Copy), reads=(pb,), writes=(kvb,))
                S.op("act", lambda e: e.activation(out=ktmp, in_=kv32[:, 0:64], func=AF.Square, accum_out=qs[:, 0:1]),
                     reads=(kvb,), writes=(ktmpb, qsb))
                S.op("act", lambda e: e.activation(out=qs[:, 1:2], in_=qs[:, 0:1], func=AF.Sqrt, bias=EPS, scale=1.0 / 64),
                     reads=(qsb,), writes=(qsb,))
                S.op("dve", lambda e: e.reciprocal(out=qs[:, 2:3], in_=qs[:, 1:2]), reads=(qsb,), writes=(qsb,))
                S.op("dve", [
                    lambda e: e.scalar_tensor_tensor(out=kdup[:, 0, :], in0=kv32[:, 0:64], scalar=qs[:, 2:3], in1=self.kg,
                                                     op0=ALU.mult, op1=ALU.mult),
                    lambda e: e.scalar_tensor_tensor(out=kdup[:, 1, :], in0=kv32[:, 0:64], scalar=qs[:, 2:3], in1=self.kg,
                                                     op0=ALU.mult, op1=ALU.mult),
                ], reads=(kvb, qsb, self.cb), writes=(kdupb,))
                S.op("act", lambda e, cj=cj: e.activation(out=v1[:, cj, 0:64], in_=kv32[:, 64:128], func=AF.Copy),
                     reads=(kvb,), writes=(v1b,))
                self.transpose_to([kdup.rearrange("p a b -> p (a b)")], [kdupb],
                                  lambda n, cj=cj: kT2[:, cj * 128:(cj + 1) * 128].unsqueeze(1), [kTb])
            wvs = []
            for hv in range(2):
                c0 = Q0 + kg * 512 + hv * 256
                wvs.append(self.wload([d["w_in"][:, c0:c0 + 256]], 16))
            for mc in range(NMC):
                t0 = 896 + mc * 128
                for hv in range(2):
                    pb, ps = self.proj_tm(wvs[hv][0], wvs[hv][1], 0, 256, 16, lambda kc, t0=t0: self.hT(kc, t0, 128), [self.hTb[7 + mc]])
                    S.op("act", lambda e, ps=ps, hv=hv: e.activation(out=q32[:, hv * 256:(hv + 1) * 256], in_=ps, func=AF.Copy),
                         reads=(pb,), writes=(q32b,))
                S.op("dve", lambda e: e.tensor_tensor(out=qtmp, in0=q32, in1=q32, op=ALU.mult), reads=(q32b,), writes=(qtmpb,))
                S.op("dve", lambda e: e.tensor_reduce(out=qs[:, 8:16], in_=qtmp.rearrange("p (h x) -> p h x", x=64), axis=AX.X, op=ALU.add),
                     reads=(qtmpb,), writes=(qsb,))
                S.op("act", lambda e: e.activation(out=qs[:, 16:24], in_=qs[:, 8:16], func=AF.Sqrt, bias=EPS, scale=1.0 / 64),
                     reads=(qsb,), writes=(qsb,))
                S.op("dve", lambda e: e.reciprocal(out=qs[:, 24:32], in_=qs[:, 16:24]), reads=(qsb,), writes=(qsb,))
                S.op("dve", lambda e: e.tensor_tensor(out=qtmp.rearrange("p (h x) -> p h x", x=64), in0=q32.rearrange("p (h x) -> p h x", x=64),
                                                      in1=qs[:, 24:32].unsqueeze(2).to_broadcast([128, 8, 64]), op=ALU.mult),
                     reads=(q32b, qsb), writes=(qtmpb,))
                S.op("dve", lambda e: e.tensor_tensor(out=qn.rearrange("p (h x) -> p h x", x=64), in0=qtmp.rearrange("p (h x) -> p h x", x=64),
                                                      in1=self.qg.unsqueeze(1).to_broadcast([128, 8, 64]), op=ALU.mult),
                     reads=(qtmpb, self.cb), writes=(qnb,))
                tiles = [qn[:, i * 128:(i + 1) * 128] for i in range(4)]
                self.transpose_to(tiles, [qnb], lambda n, mc=mc: qT[:, 0:4, mc * 128:(mc + 1) * 128], [qTb])
            for mc in range(NMC):
                pvb = [self.bank() for _ in range(2)]
                for qp in range(2):
                    lb = [self.bank() for _ in range(2)]
                    fns = []
                    for hh in range(2):
                        psv = lb[hh][1][:, 0:512].rearrange("p (a b q) -> p a b q", a=2, b=2)
                        for qq in range(2):
                            qt = qp * 2 + qq
                            for blk in range(2):
                                cj = mc + blk
                                fns.append(lambda e, hh=hh, blk=blk, cj=cj, psv=psv, qt=qt, qq=qq, mc=mc: e.matmul(
                                    psv[:, qq, blk, :], lhsT=kT2[hh * 64:(hh + 1) * 64, cj * 128:(cj + 1) * 128],
                                    rhs=qT[hh * 64:(hh + 1) * 64, qt, mc * 128:(mc + 1) * 128], start=True, stop=True))
                    S.op("pe", fns, reads=(kTb, qTb), writes=[lb[0][0], lb[1][0]])
                    for hh in range(2):
                        pb, ps = lb[hh]
                        psv = ps[:, 0:512].rearrange("p (a b q) -> p a b q", a=2, b=2)
                        h0 = qp * 4 + hh
                        S.op("dve", lambda e, psv=psv, h0=h0: e.tensor_tensor(out=ltmp.rearrange("p (a b q) -> p a b q", a=2, b=2), in0=psv,
                                                                             in1=biasT[:, h0:h0 + 3:2, :, :], op=ALU.add),
                             reads=(pb, biasb), writes=(ltb,))
                        S.op("act", lambda e, hh=hh: e.activation(out=eT[hh].rearrange("p a b q -> p (a b q)"), in_=ltmp, func=AF.Exp),
                             reads=(ltb,), writes=(eTb[hh],))
                        if mc == 1:
                            S.op("dve", lambda e, hh=hh: e.tensor_scalar(out=eT[hh][:, :, 0, :], in0=eT[hh][:, :, 0, :],
                                                                         scalar1=self.flag[:, 0:1], scalar2=None, op0=ALU.mult),
                                 reads=(eTb[hh], self.cb), writes=(eTb[hh],))
                        fns = []
                        for qq in range(2):
                            h8 = h0 + 2 * qq
                            pvp, pvs = pvb[h8 // 4]
                            slot = h8 % 4
                            for blk in range(2):
                                cj = mc + blk
                                fns.append(lambda e, hh=hh, qq=qq, blk=blk, cj=cj, pvs=pvs, slot=slot: e.matmul(
                                    pvs[:, slot * 65:(slot + 1) * 65], lhsT=eT[hh][:, qq, blk, :], rhs=v1[:, cj, :],
                                    start=(blk == 0), stop=(blk == 1)))
                        S.op("pe", fns, reads=(eTb[hh], v1b), writes=[pvb[qp][0]])
                for hb_ in range(2):
                    pvp, pvs = pvb[hb_]
                    pv3 = pvs[:, 0:260].rearrange("p (s x) -> p s x", x=65)
                    S.op("dve", lambda e, pv3=pv3, hb_=hb_, kg=kg: e.tensor_tensor(
                        out=den[:, hb_ * 4:hb_ * 4 + 4].unsqueeze(2), in0=pv3[:, :, 64:65],
                        in1=self.esink[:, kg * 8 + hb_ * 4:kg * 8 + hb_ * 4 + 4].unsqueeze(2), op=ALU.add),
                        reads=(pvp, self.cb), writes=(denb,))
                    S.op("dve", lambda e, hb_=hb_: e.reciprocal(out=den[:, 8 + hb_ * 4:8 + hb_ * 4 + 4], in_=den[:, hb_ * 4:hb_ * 4 + 4]),
                         reads=(denb,), writes=(denb,))
                    S.op("dve", lambda e, pv3=pv3, hb_=hb_: e.tensor_tensor(
                        out=ao[:, hb_ * 4:hb_ * 4 + 4, :], in0=pv3[:, :, 0:64],
                        in1=den[:, 8 + hb_ * 4:8 + hb_ * 4 + 4].unsqueeze(2).to_broadcast([128, 4, 64]), op=ALU.mult),
                        reads=(pvp, denb), writes=(aob,))
                aof = ao.rearrange("p h x -> p (h x)")
                tiles = [aof[:, i * 128:(i + 1) * 128] for i in range(4)]
                self.transpose_to(tiles, [aob], lambda n, kg=kg, mc=mc: self.aoT[:, kg * 4:kg * 4 + 4, mc * 128:(mc + 1) * 128],
                                  [self.aoTb[kg]])
        self.dump("aoT", self.aoT, self.aoTb, [128, 16, NT], BF16)

    def attn_out(self):
        S, d = self.S, self.d
        r2 = self.r2
        r2.recarve()
        tg = r2.alloc([3, 384], F32); tgb = r2.buf("tg")
        mt = r2.alloc([384], F32); mtb = r2.buf("mtmp")
        ranges = [(896 + i * 384, 384) for i in range(3)]
        mainbufs = self.hTb[7:16]
        for db in range(16):
            if db % 2 == 0:
                wvg, wbg = self.wload([d["w_in"][:, GA0 + db * 128:GA0 + db * 128 + 256]], 16)
                wva, wba = self.wload([d["w_attn_out"][:, db * 128:db * 128 + 256]], 16)
            outs = self.proj_fm(wvg, wbg, (db % 2) * 128, 16, self.hT, mainbufs, ranges)
            for i, (pb, ps) in enumerate(outs):
                S.op("act", lambda e, ps=ps, i=i: e.activation(out=tg[:, i, :], in_=ps, func=AF.Tanh, scale=0.5),
                     reads=(pb,), writes=(tgb,))
            outs = self.proj_fm(wva, wba, (db % 2) * 128, 16, lambda kc, t0, n: self.aoT[:, kc, t0 - 896:t0 - 896 + n],
                                self.aoTb, ranges)
            for i, (pb, ps) in enumerate(outs):
                S.op("dve", lambda e, ps=ps, i=i: e.scalar_tensor_tensor(out=mt, in0=tg[:, i, :], scalar=1.0, in1=ps,
                                                                        op0=ALU.add, op1=ALU.mult),
                     reads=(tgb, pb), writes=(mtb,))
                S.op("dve", lambda e, i=i, db=db: e.tensor_tensor(out=self.mixedT[:, db, i * 384:(i + 1) * 384],
                                                                  in0=self.mixedT[:, db, i * 384:(i + 1) * 384], in1=mt, op=ALU.add),
                     reads=(mtb, self.mixb[db]), writes=(self.mixb[db],))
        self.dump("mixed", self.mixedT, self.mixb, [128, 16, NT], BF16)

    def wout_residual(self):
        S, d = self.S, self.d
        r1, r2, r3 = self.r1, self.r2, self.r3
        r3.recarve()
        self.x1 = r3.alloc([NMC, D], F32)
        self.x1b = [r3.buf("x1_%d" % i) for i in range(NMC)]
        x1, x1b = self.x1, self.x1b
        xsem = S.new_dma_sem("x1sem")
        for mc in range(NMC):
            S.dma("sp", lambda e, mc=mc: e.dma_start(out=x1[:, mc, :], in_=d["xm"][mc * 128:(mc + 1) * 128, :]), xsem,
                  writes=(x1b[mc],))
        for mc in range(NMC):
            x1b[mc].w = (xsem, S.dcnt[xsem])
        for ct in range(8):
            wv, wb = self.wload([d["w_out"][:, ct * 256:(ct + 1) * 256]], 16)
            for mc in range(NMC):
                pb, ps = self.proj_tm(wv, wb, 0, 256, 16, lambda kc, mc=mc: self.mixedT[:, kc, mc * 128:(mc + 1) * 128], self.mixb)
                S.op("dve", lambda e, ps=ps, mc=mc, ct=ct: e.scalar_tensor_tensor(
                    out=x1[:, mc, ct * 256:(ct + 1) * 256], in0=ps, scalar=0.5, in1=x1[:, mc, ct * 256:(ct + 1) * 256],
                    op0=ALU.mult, op1=ALU.add), reads=(pb, x1b[mc]), writes=(x1b[mc],))
        self.dump("x1", x1, x1b, [128, NMC, D])
        r1.recarve(); r2.recarve()
        self.hfT = r1.alloc([16, NT], BF16)
        self.hfTb = [r1.buf("hfT%d" % i) for i in range(NMC)]
        gain = r2.alloc([D], F32); gb = r2.buf("gain")
        hb = [r2.alloc([D], BF16) for _ in range(2)]
        hbb = [r2.buf("hb%d" % i) for i in range(2)]
        S.dma("sp", lambda e: e.dma_start(out=gain, in_=d["norm_ffn_w"].partition_broadcast(128)), S.new_dma_sem("gsem1"), writes=(gb,))
        for mc in range(NMC):
            s = mc % 2
            self.rms_tile(x1[:, mc, :], [x1b[mc]], gain, gb, hb[s], hbb[s], mc, hb[s])
            for half in range(2):
                tiles = [hb[s][:, (half * 8 + i) * 128:(half * 8 + i + 1) * 128] for i in range(8)]
                self.transpose_to(tiles, [hbb[s]],
                                  lambda n, half=half, mc=mc: self.hfT[:, half * 8:half * 8 + 8, mc * 128:(mc + 1) * 128],
                                  [self.hfTb[mc]])
        self.gain_ap, self.gain_b, self.hb2, self.hbb2 = gain, gb, hb, hbb

    def ffn(self):
        S, d = self.S, self.d
        r2, r4 = self.r2, self.r4
        r4.recarve()
        actT = r4.alloc([11, 1024], BF16); actb = r4.buf("actT")
        pre = [r4.alloc([2 + NT], F32) for _ in range(2)]
        preb = [r4.buf("fpre%d" % i) for i in range(2)]
        acc = [r2.alloc([1024], F32) for _ in range(2)]
        accb = [r2.buf("facc%d" % i) for i in range(2)]
        gl = r4.alloc([1024], F32); glb = r4.buf("gl")
        x1, x1b = self.x1, self.x1b
        ranges = [(i * 384, 384) for i in range(3)]
        for i in range(2):
            S.op("dve", lambda e, i=i: e.memset(pre[i][:, 0:2], 0.0), writes=(preb[i],))
        for fg in range(4):
            for jb in range(11):
                b = fg * 11 + jb
                c0 = b * 128
                wv, wb = self.wload([d["w_ffn_up"][:, c0:c0 + 128], d["w_ffn_up"][:, DFF + c0:DFF + c0 + 128]], 16)
                for gu in range(2):
                    blk = b + gu * 44
                    outs = self.proj_fm(wv, wb, gu * 128, 16, lambda kc, t0, n: self.hfT[:, kc, t0:t0 + n], self.hfTb, ranges)
                    for i, (pb, ps) in enumerate(outs):
                        S.op("act", lambda e, ps=ps, i=i, gu=gu: e.activation(out=pre[gu][:, 2 + i * 384:2 + (i + 1) * 384], in_=ps, func=AF.Copy),
                             reads=(pb,), writes=(preb[gu],))
                    S.op("dve", lambda e, gu=gu: e.tensor_scalar(out=pre[gu][:, 128:130], in0=pre[gu][:, 128:130],
                                                                 scalar1=self.flag[:, 0:1], scalar2=None, op0=ALU.mult),
                         reads=(preb[gu], self.cb), writes=(preb[gu],))
                    w = self.cw_ffn
                    S.op("dve", lambda e, gu=gu, blk=blk: e.tensor_scalar(out=acc[gu], in0=pre[gu][:, 130:130 + 1024], scalar1=w[:, blk, 2:3],
                                                                          scalar2=self.cb_ffn[:, blk:blk + 1], op0=ALU.mult, op1=ALU.add),
                         reads=(preb[gu], self.cb), writes=(accb[gu],))
                    for k in (1, 0):
                        S.op("dve", lambda e, gu=gu, blk=blk, k=k: e.scalar_tensor_tensor(
                            out=acc[gu], in0=pre[gu][:, 128 + k:128 + k + 1024], scalar=w[:, blk, k:k + 1], in1=acc[gu],
                            op0=ALU.mult, op1=ALU.add), reads=(preb[gu], self.cb, accb[gu]), writes=(accb[gu],))
                S.op("act", lambda e: e.activation(out=gl, in_=acc[0], func=AF.Gelu_apprx_tanh), reads=(accb[0],), writes=(glb,))
                S.op("dve", lambda e, jb=jb: e.tensor_tensor(out=actT[:, jb, :], in0=gl, in1=acc[1], op=ALU.mult),
                     reads=(glb, accb[1]), writes=(actb,))
            if fg == 0:
                self.dump("actT0", actT, [actb], [128, 11, 1024], BF16)
            for ct in range(8):
                wv, wb = self.wload([d["w_ffn_down"][fg * 1408:(fg + 1) * 1408, ct * 256:(ct + 1) * 256]], 11)
                for mc in range(1, NMC):
                    pb, ps = self.proj_tm(wv, wb, 0, 256, 11, lambda kc, mc=mc: actT[:, kc, (mc - 1) * 128:mc * 128], [actb])
                    S.op("dve", lambda e, ps=ps, mc=mc, ct=ct: e.tensor_tensor(
                        out=x1[:, mc, ct * 256:(ct + 1) * 256], in0=ps, in1=x1[:, mc, ct * 256:(ct + 1) * 256], op=ALU.add),
                        reads=(pb, x1b[mc]), writes=(x1b[mc],))
        self.dump("x2", x1, x1b, [128, NMC, D])

    def ple(self):
        S, d = self.S, self.d
        r1, r2, r4 = self.r1, self.r2, self.r4
        x1, x1b = self.x1, self.x1b
        r1.recarve(); r4.recarve()
        nT = r1.alloc([16, 1024], BF16)
        nTb = [r1.buf("nT%d" % i) for i in range(8)]
        gain, gb, hb, hbb = self.gain_ap, self.gain_b, self.hb2, self.hbb2
        S.dma("sp", lambda e: e.dma_start(out=gain, in_=d["ple_norm_w"].partition_broadcast(128)), S.new_dma_sem("gsem2"), writes=(gb,))
        pT = r4.alloc([2, 1024], BF16); pTb = r4.buf("pT")
        pt = [r4.alloc([PLE], F32) for _ in range(2)]; ptb = [r4.buf("pt%d" % i) for i in range(2)]
        pbf = [r4.alloc([PLE], BF16) for _ in range(2)]; pbfb = [r4.buf("pbf%d" % i) for i in range(2)]
        psem = [S.new_dma_sem("psem%d" % i) for i in range(2)]
        tgp = r4.alloc([256], F32); tgpb = r4.buf("tgp")
        up = r4.alloc([256], F32); upb = r4.buf("up")
        for mc in range(1, NMC):
            s = mc % 2
            o = mc - 1
            self.rms_tile(x1[:, mc, :], [x1b[mc]], gain, gb, hb[s], hbb[s], mc, hb[s])
            for half in range(2):
                tiles = [hb[s][:, (half * 8 + i) * 128:(half * 8 + i + 1) * 128] for i in range(8)]
                self.transpose_to(tiles, [hbb[s]],
                                  lambda n, half=half, o=o: nT[:, half * 8:half * 8 + 8, o * 128:(o + 1) * 128], [nTb[o]])
            S.dma("sp", lambda e, s=s, o=o: e.dma_start(out=pt[s], in_=d["pp"][o * 128:(o + 1) * 128, :]), psem[s], writes=(ptb[s],))
            S.op("act", lambda e, s=s: e.activation(out=pbf[s], in_=pt[s], func=AF.Copy), reads=(ptb[s],), writes=(pbfb[s],))
            tiles = [pbf[s][:, i * 128:(i + 1) * 128] for i in range(2)]
            self.transpose_to(tiles, [pbfb[s]], lambda n, o=o: pT[:, 0:2, o * 128:(o + 1) * 128], [pTb])
        for ct in range(8):
            wv, wb = self.wload([d["w_ple_gate"][:, ct * 256:(ct + 1) * 256]], 16)
            wv2, wb2 = self.wload([d["w_ple_proj"][:, ct * 256:(ct + 1) * 256]], 2)
            for mc in range(1, NMC):
                o = mc - 1
                pb, ps = self.bank()
                fns = []
                for kc in range(16):
                    fns.append(lambda e, kc=kc, ps=ps, o=o: e.matmul(ps[:, 0:256], lhsT=nT[:, kc, o * 128:(o + 1) * 128], rhs=wv[:, kc, 0:256],
                                                                     start=(kc == 0), stop=(kc == 15)))
                for kc in range(2):
                    fns.append(lambda e, kc=kc, ps=ps, o=o: e.matmul(ps[:, 256:512], lhsT=pT[:, kc, o * 128:(o + 1) * 128], rhs=wv2[:, kc, 0:256],
                                                                     start=(kc == 0), stop=(kc == 1)))
                S.op("pe", fns, reads=(wb, wb2, nTb[o], pTb), writes=(pb,))
                S.op("act", lambda e, ps=ps: e.activation(out=tgp, in_=ps[:, 0:256], func=AF.Tanh, scale=0.5), reads=(pb,), writes=(tgpb,))
                S.op("dve", lambda e, ps=ps: e.scalar_tensor_tensor(out=up, in0=tgp, scalar=1.0, in1=ps[:, 256:512], op0=ALU.add, op1=ALU.mult),
                     reads=(tgpb, pb), writes=(upb,))
                S.op("dve", lambda e, mc=mc, ct=ct: e.scalar_tensor_tensor(
                    out=x1[:, mc, ct * 256:(ct + 1) * 256], in0=up, scalar=0.5, in1=x1[:, mc, ct * 256:(ct + 1) * 256],
                    op0=ALU.mult, op1=ALU.add), reads=(upb, x1b[mc]), writes=(x1b[mc],))
        osem = S.new_dma_sem("osem")
        for mc in range(1, NMC):
            ob = Buf("out%d" % mc)
            S.dma("sp", lambda e, mc=mc: e.dma_start(out=self.out[(mc - 1) * 128:mc * 128, :], in_=x1[:, mc, :]), osem,
                  reads=(x1b[mc],), writes=(ob,))
            self.final_bufs.append(ob)


def _t5_bucket(dist):
    nb, md = 32, 128
    me = nb // 2
    dd = np.maximum(dist, 0)
    lr = np.log(np.maximum(dd, 1).astype(np.float32) / me) / np.log(md / me)
    large = me + (lr * (nb - me)).astype(np.int32)
    large = np.minimum(large, nb - 1)
    return np.where(dd < me, dd, large)


def _const_mats():
    ident = np.eye(128, dtype=np.float32)
    tri = (np.arange(128)[:, None] <= np.arange(128)[None, :]).astype(np.float32)
    ones = np.ones((128, 128), np.float32)
    neg = np.where(np.arange(128)[:, None] > np.arange(128)[None, :], -32768.0, 0.0).astype(np.float32)
    neg4 = np.tile(neg, (1, 4))
    sel = np.zeros((128, 8, 128), np.float32)
    for j in range(8):
        sel[j, j, :] = 1.0
    return (np.ascontiguousarray(np.concatenate([tri, ones], axis=1)),
            np.ascontiguousarray(np.concatenate([ident, neg4, sel.reshape(128, 1024)], axis=1)))


def _bias_tables(table):
    L = 128
    qi = np.arange(L)[:, None]
    kj = np.arange(2 * L)[None, :]
    dist = qi + L - kj
    band = (dist >= 0) & (dist < 128)
    bk = _t5_bucket(dist)
    b = table[bk]
    b = np.where(band[:, :, None], b, np.float32(NEGM)).astype(np.float32)
    b = b.reshape(L, 2, L, NKV, 8)
    b = np.transpose(b, (3, 2, 4, 1, 0))
    return np.ascontiguousarray(b).reshape(NKV, 128, 8 * 2 * 128)


_CACHE = {}


def kernel(**inputs):
    x = np.asarray(inputs["x"], np.float32)
    p = np.asarray(inputs["p"], np.float32)[0]
    g = lambda k: np.ascontiguousarray(np.asarray(inputs[k], np.float32)[0])
    shared = {
        "w_in": g("w_in"), "w_attn_out": g("w_attn_out"), "w_ssm_out": g("w_ssm_out"), "w_out": g("w_out"),
        "w_ffn_up": g("w_ffn_up"), "w_ffn_down": g("w_ffn_down"), "w_ple_gate": g("w_ple_gate"),
        "w_ple_proj": g("w_ple_proj"),
        "norm_mix_w": g("norm_mix_w")[None], "norm_ffn_w": g("norm_ffn_w")[None], "ple_norm_w": g("ple_norm_w")[None],
        "ssm_norm_w": g("ssm_norm_w")[None], "q_norm_w": g("q_norm_w")[None], "k_norm_w": g("k_norm_w")[None],
        "attn_sinks": g("attn_sinks")[None], "ssm_A_log": g("ssm_A_log")[None], "ssm_dt_bias": g("ssm_dt_bias")[None],
        "ssm_D": g("ssm_D")[None],
        "cw_ssm": np.ascontiguousarray(g("ssm_conv_w").T.reshape(48, 128, 4).transpose(1, 0, 2)).reshape(128, 192),
        "cb_ssm": np.ascontiguousarray(g("ssm_conv_b").reshape(48, 128).T),
        "cw_ffn": np.ascontiguousarray(g("ffn_conv_w").T.reshape(88, 128, 3).transpose(1, 0, 2)).reshape(128, 264),
        "cb_ffn": np.ascontiguousarray(g("ffn_conv_b").reshape(88, 128).T),
        "biasT": _bias_tables(np.asarray(inputs["rel_bias_table"], np.float32)),
    }
    shared["cm_f"], shared["cm_b"] = _const_mats()
    in_maps = []
    for core in range(8):
        b, hf = core // 2, core % 2
        s0 = hf * 1024
        xm = np.zeros((NT, D), np.float32)
        xp = np.zeros((896, D), np.float32)
        if hf == 1:
            xm[:] = x[b, s0 - 128:s0 + 1024]
            xp[:] = x[b, 0:896]
        else:
            xm[128:] = x[b, 0:1024]
        m = dict(shared)
        m["xm"] = xm
        m["xp"] = xp
        m["pp"] = np.ascontiguousarray(p[b, s0:s0 + 1024])
        m["flag"] = np.full((128, 1), float(hf), np.float32)
        in_maps.append(m)
    dbg = tuple(inputs.get("_debug", ())) if isinstance(inputs.get("_debug", ()), (list, tuple)) else ()
    bld = Builder(debug=dbg)
    nc = bld.build()
    cores = list(range(8))
    if dbg and inputs.get("_cores"):
        cores = list(inputs["_cores"])
    res = run_bass_kernel_spmd(nc, [in_maps[c] for c in cores], core_ids=list(range(len(cores))))
    out = np.zeros((BATCH, SEQ, D), np.float32)
    for i, core in enumerate(cores):
        b, hf = core // 2, core % 2
        out[b, hf * 1024:(hf + 1) * 1024] = res.results[i]["out"]
    if dbg:
        kernel.last_debug = [{k: r[v] for k, v in bld.dbg_out.items()} for r in res.results]
    return out
```

```python
import numpy as np
import concourse.bass as bass
import concourse.mybir as mybir
from concourse.bass_utils import run_bass_kernel_spmd

F32 = mybir.dt.float32
BF16 = mybir.dt.bfloat16
AF = mybir.ActivationFunctionType
ALU = mybir.AluOpType
AX = mybir.AxisListType

D = 2048
SEQ = 2048
BATCH = 4
NH = 32
NKV = 4
DH = 64
DI = 4096
NSH = 64
NG = 8
DS = 128
DFF = 5632
PLE = 256
EPS = 1e-6
Q0 = 0
K0 = 2048
V0 = 2304
Z0 = 2560
XBC0 = 6656
DT0 = 12800
GA0 = 12864
GS0 = 14912
IN_DIM = 16960

NMC = 9
NT = NMC * 128
TP = 768
TM = 1280
NEGM = -30000.0


class Buf:
    __slots__ = ("name", "w", "r")

    def __init__(self, name, base=None):
        self.name = name
        self.w = None
        self.r = dict(base) if base else {}


class Sched:
    ENG = ("pe", "act", "dve", "pool", "sp")

    def __init__(self):
        self.streams = {e: [] for e in self.ENG}
        self.cnt = {e: 0 for e in self.ENG}
        self.dcnt = {}
        self.waited = {e: {} for e in self.ENG}
        self.dma_sems = []

    def new_dma_sem(self, name):
        self.dma_sems.append(name)
        self.dcnt[name] = 0
        return name

    def _waits(self, eng, reads, writes):
        deps = {}

        def add(s, v):
            if deps.get(s, 0) < v:
                deps[s] = v

        for b in reads:
            if b.w is not None:
                add(*b.w)
        for b in writes:
            if b.w is not None:
                add(*b.w)
            for s, v in b.r.items():
                add(s, v)
        wd = self.waited[eng]
        st = self.streams[eng]
        for s, v in deps.items():
            if wd.get(s, 0) >= v:
                continue
            wd[s] = v
            st.append(("wait", s, v))

    def op(self, eng, fns, reads=(), writes=()):
        self._waits(eng, reads, writes)
        self.cnt[eng] += 1
        c = self.cnt[eng]
        if not isinstance(fns, (list, tuple)):
            fns = [fns]
        st = self.streams[eng]
        for f in fns[:-1]:
            st.append(("inst", f, None, 0))
        st.append(("inst", fns[-1], eng, 1))
        for b in reads:
            b.r[eng] = c
        for b in writes:
            b.w = (eng, c)
            b.r = {}

    def dma(self, eng, fn, sem, reads=(), writes=()):
        self._waits(eng, reads, writes)
        self.dcnt[sem] += 16
        c = self.dcnt[sem]
        self.streams[eng].append(("inst", fn, sem, 16))
        for b in reads:
            b.r[sem] = c
        for b in writes:
            b.w = (sem, c)
            b.r = {}

    def final_wait(self, eng, bufs):
        self._waits(eng, bufs, bufs)


def collect_tokens(bufs):
    r = {}
    for b in bufs:
        if b.w is not None and r.get(b.w[0], 0) < b.w[1]:
            r[b.w[0]] = b.w[1]
        for s, v in b.r.items():
            if r.get(s, 0) < v:
                r[s] = v
    return r


class Region:
    def __init__(self, name, handle, nwords):
        self.name = name
        self.h = handle
        self.nwords = nwords
        self.off = 0
        self.bufs = []
        self.base = {}

    def recarve(self):
        self.base = collect_tokens(self.bufs)
        self.bufs = []
        self.off = 0

    def buf(self, name):
        b = Buf(name, self.base)
        self.bufs.append(b)
        return b

    def alloc(self, shape, dtype):
        nel = 1
        for s in shape:
            nel *= s
        nbytes = nel * (2 if dtype == BF16 else 4)
        nw = (nbytes + 3) // 4
        assert self.off + nw <= self.nwords, (self.name, self.off, nw, self.nwords)
        ap = self.h[:, self.off:self.off + nw]
        self.off += nw
        if dtype == BF16:
            ap = ap.bitcast(BF16)
            if nel != nw * 2:
                ap = ap[:, 0:nel]
        if len(shape) == 2:
            return ap.rearrange("p (a b) -> p a b", b=shape[1])
        if len(shape) == 3:
            return ap.rearrange("p (a b c) -> p a b c", b=shape[1], c=shape[2])
        return ap


class Builder:
    def __init__(self, debug=(), nphases=99):
        self.nphases = nphases
        self.debug = set(debug)
        self.nc = bass.Bass("TRN2", target_bir_lowering=False)
        self.S = Sched()
        self.dbg_out = {}

    def dram_in(self, name, shape):
        return self.nc.dram_tensor(name, list(shape), F32, kind="ExternalInput").ap()

    def bank(self):
        i = self.bank_i
        self.bank_i = (i + 1) % 8
        return self.pbuf[i], self.ps[i]

    def wslot(self):
        i = self.w_i
        self.w_i = (i + 1) % len(self.wt)
        return i

    def wload(self, srcs, kcn):
        i = self.wslot()
        tot = sum(s.shape[1] for s in srcs)
        assert kcn * tot <= 4096
        view = self.wt[i][:, 0:kcn * tot].rearrange("p (k c) -> p k c", c=tot)
        c0 = 0
        for s in srcs:
            n = s.shape[1]
            src = s.rearrange("(k p) e -> p k e", p=128)
            dst = view[:, :, c0:c0 + n]
            self.S.dma("pool", lambda e, d=dst, s_=src: e.dma_start(out=d, in_=s_), self.wsem[i],
                       reads=(), writes=(self.wbuf[i],))
            c0 += n
        return view, self.wbuf[i]

    def dump(self, name, ap, bufs, shape, dtype=F32):
        if name not in self.debug:
            return
        t = self.nc.dram_tensor("dbg_" + name, list(shape), dtype, kind="ExternalOutput").ap()
        sem = self.S.new_dma_sem("dbgsem_" + name)
        db = Buf("dbg_" + name)
        self.S.dma("sp", lambda e, t=t, ap=ap: e.dma_start(out=t, in_=ap), sem, reads=bufs, writes=(db,))
        self.final_bufs.append(db)
        self.dbg_out[name] = "dbg_" + name

    def build(self):
        nc = self.nc
        S = self.S
        d = {}
        d["xm"] = self.dram_in("xm", [NT, D])
        d["xp"] = self.dram_in("xp", [896, D])
        d["pp"] = self.dram_in("pp", [1024, PLE])
        d["flag"] = self.dram_in("flag", [128, 1])
        d["w_in"] = self.dram_in("w_in", [D, IN_DIM])
        d["w_attn_out"] = self.dram_in("w_attn_out", [D, D])
        d["w_ssm_out"] = self.dram_in("w_ssm_out", [DI, D])
        d["w_out"] = self.dram_in("w_out", [D, D])
        d["w_ffn_up"] = self.dram_in("w_ffn_up", [D, 2 * DFF])
        d["w_ffn_down"] = self.dram_in("w_ffn_down", [DFF, D])
        d["w_ple_gate"] = self.dram_in("w_ple_gate", [D, D])
        d["w_ple_proj"] = self.dram_in("w_ple_proj", [PLE, D])
        d["norm_mix_w"] = self.dram_in("norm_mix_w", [1, D])
        d["norm_ffn_w"] = self.dram_in("norm_ffn_w", [1, D])
        d["ple_norm_w"] = self.dram_in("ple_norm_w", [1, D])
        d["ssm_norm_w"] = self.dram_in("ssm_norm_w", [1, DI])
        d["q_norm_w"] = self.dram_in("q_norm_w", [1, DH])
        d["k_norm_w"] = self.dram_in("k_norm_w", [1, DH])
        d["attn_sinks"] = self.dram_in("attn_sinks", [1, NH])
        d["ssm_A_log"] = self.dram_in("ssm_A_log", [1, NSH])
        d["ssm_dt_bias"] = self.dram_in("ssm_dt_bias", [1, NSH])
        d["ssm_D"] = self.dram_in("ssm_D", [1, NSH])
        d["cw_ssm"] = self.dram_in("cw_ssm", [128, 48 * 4])
        d["cb_ssm"] = self.dram_in("cb_ssm", [128, 48])
        d["cw_ffn"] = self.dram_in("cw_ffn", [128, 88 * 3])
        d["cb_ffn"] = self.dram_in("cb_ffn", [128, 88])
        d["biasT"] = self.dram_in("biasT", [NKV, 128, 8 * 2 * 128])
        d["cm_f"] = self.dram_in("cm_f", [128, 256])
        d["cm_b"] = self.dram_in("cm_b", [128, 128 + 512 + 1024])
        self.d = d
        self.out = nc.dram_tensor("out", [1024, D], F32, kind="ExternalOutput").ap()
        self.final_bufs = []

        R1W, R2W, R3W, R4W, CW = 10240, 6144, 18432, 9216, 4200
        import contextlib
        with contextlib.ExitStack() as es:
            def sb(name, shape, dt):
                return es.enter_context(nc.sbuf_tensor(name, shape, dt))
            r1 = Region("R1", sb("R1", [128, R1W], F32), R1W)
            r2 = Region("R2", sb("R2", [128, R2W], F32), R2W)
            r3 = Region("R3", sb("R3", [128, R3W], F32), R3W)
            r4 = Region("R4", sb("R4", [128, R4W], F32), R4W)
            rc = Region("RC", sb("RC", [128, CW], F32), CW)
            self.r1, self.r2, self.r3, self.r4, self.rc = r1, r2, r3, r4, rc
            self.wt = [sb("wt%d" % i, [128, 4096], BF16) for i in range(2)]
            self.wbuf = [Buf("wbuf%d" % i) for i in range(2)]
            self.wsem = [S.new_dma_sem("wsem%d" % i) for i in range(2)]
            self.w_i = 0
            self.ps = [es.enter_context(nc.psum_tensor("ps%d" % i, [128, 512], F32)) for i in range(8)]
            self.pbuf = [Buf("psum%d" % i) for i in range(8)]
            self.bank_i = 0

            phases = [self.setup_consts, self.phase0, self.dt_proj, self.ssm_prefix, self.ssm_main, self.ssm_out,
                      self.attention, self.attn_out, self.wout_residual, self.ffn, self.ple]
            for ph in phases[:self.nphases]:
                ph()

            S.final_wait("sp", self.final_bufs)

            sem_names = list(Sched.ENG) + S.dma_sems
            sems = {}
            for n in sem_names:
                sems[n] = es.enter_context(nc.semaphore(n))
            block = es.enter_context(nc.Block())

            def replay(engname):
                def run(e):
                    for it in S.streams[engname]:
                        if it[0] == "wait":
                            e.wait_ge(sems[it[1]], it[2])
                        else:
                            ins = it[1](e)
                            if it[2] is not None:
                                ins.then_inc(sems[it[2]], it[3])
                return run

            block.sync(replay("sp"))
            block.gpsimd(replay("pool"))
            block.scalar(replay("act"))
            block.vector(replay("dve"))
            block.tensor(replay("pe"))
        return nc

    def setup_consts(self):
        S, d, rc = self.S, self.d, self.rc
        csem = S.new_dma_sem("csem")
        self.cb = Buf("consts")
        cb = self.cb

        def cload(dst, src):
            S.dma("sp", lambda e, dst=dst, src=src: e.dma_start(out=dst, in_=src), csem)

        cmf = rc.alloc([256], F32)
        cload(cmf, d["cm_f"])
        self.tri_f = cmf[:, 0:128]
        self.ones_f = cmf[:, 128:256]
        self.r3.recarve()
        cm = self.r3.alloc([128 + 512 + 1024], F32)
        cmb = self.r3.buf("cm_stage")
        S.dma("sp", lambda e: e.dma_start(out=cm, in_=d["cm_b"]), S.new_dma_sem("cmsem"), writes=(cmb,))
        self.ident_bf = rc.alloc([128], BF16)
        self.neg4 = rc.alloc([512], BF16)
        self.sel = rc.alloc([1024], BF16)
        self.nsel = rc.alloc([1024], BF16)
        self._cm = cm
        self.Ab = rc.alloc([64], F32)
        self.Db = rc.alloc([64], F32)
        self.dtb = rc.alloc([64], F32)
        self.flag = rc.alloc([1], F32)
        self.cw_ssm = rc.alloc([48, 4], F32)
        self.cb_ssm = rc.alloc([48], F32)
        self.cw_ffn = rc.alloc([88, 3], F32)
        self.cb_ffn = rc.alloc([88], F32)
        self.qg = rc.alloc([64], F32)
        self.kg = rc.alloc([64], F32)
        self.esink = rc.alloc([32], F32)
        self.dt_all = rc.alloc([16, 64], F32)
        self.ss16 = rc.alloc([16], F32)
        self.sd16 = rc.alloc([16], F32)
        self.rs16 = rc.alloc([16], F32)
        cload(self.Ab, d["ssm_A_log"].partition_broadcast(128))
        cload(self.Db, d["ssm_D"].partition_broadcast(128))
        cload(self.dtb, d["ssm_dt_bias"].partition_broadcast(128))
        cload(self.flag, d["flag"])
        cload(self.cw_ssm, d["cw_ssm"].rearrange("p (b k) -> p b k", k=4))
        cload(self.cb_ssm, d["cb_ssm"])
        cload(self.cw_ffn, d["cw_ffn"].rearrange("p (b k) -> p b k", k=3))
        cload(self.cb_ffn, d["cb_ffn"])
        cload(self.qg, d["q_norm_w"].partition_broadcast(128))
        cload(self.kg, d["k_norm_w"].partition_broadcast(128))
        cload(self.esink, d["attn_sinks"].partition_broadcast(128))
        cb.w = (csem, S.dcnt[csem])
        S.op("act", [
            lambda e: e.activation(out=self.ident_bf, in_=cm[:, 0:128], func=AF.Copy),
            lambda e: e.activation(out=self.neg4, in_=cm[:, 128:640], func=AF.Copy),
            lambda e: e.activation(out=self.sel, in_=cm[:, 640:1664], func=AF.Copy),
            lambda e: e.activation(out=self.nsel, in_=cm[:, 640:1664], func=AF.Copy, scale=-1.0),
            lambda e: e.activation(out=self.esink, in_=self.esink, func=AF.Exp),
            lambda e: e.activation(out=self.Ab, in_=self.Ab, func=AF.Exp),
        ], reads=(cb, cmb), writes=(cb,))
        S.op("dve", [
            lambda e: e.tensor_scalar(out=self.Ab, in0=self.Ab, scalar1=-1.0, scalar2=None, op0=ALU.mult),
            lambda e: e.tensor_scalar(out=self.qg, in0=self.qg, scalar1=0.125, scalar2=None, op0=ALU.mult),
            lambda e: e.tensor_scalar(out=self.cw_ssm, in0=self.cw_ssm, scalar1=0.5, scalar2=None, op0=ALU.mult),
            lambda e: e.tensor_scalar(out=self.cb_ssm, in0=self.cb_ssm, scalar1=0.5, scalar2=None, op0=ALU.mult),
        ], reads=(cb,), writes=(cb,))

    def hT(self, kc, t0, n):
        if t0 < TP:
            assert t0 + n <= TP
            return self.hTp[:, kc, t0:t0 + n]
        return self.hTm[:, kc, t0 - TP:t0 - TP + n]

    def hT_bufs(self, t0, n):
        c0 = t0 // 128
        c1 = (t0 + n - 1) // 128
        return [self.hTb[c] for c in range(c0, c1 + 1)]

    def rms_tile(self, x_ap, xbufs, gain_bc, gbuf, hb, hbbuf, sidx, scratch_bf, eps=EPS, n=D):
        S = self.S
        ssb = self.small_b[sidx]
        ss, sd, rs = self.ss16[:, sidx:sidx + 1], self.sd16[:, sidx:sidx + 1], self.rs16[:, sidx:sidx + 1]
        S.op("act", lambda e: e.activation(out=scratch_bf, in_=x_ap, func=AF.Square, accum_out=ss),
             reads=list(xbufs), writes=(ssb, hbbuf))
        S.op("act", lambda e: e.activation(out=sd, in_=ss, func=AF.Ln, bias=eps, scale=1.0 / n),
             reads=(ssb,), writes=(ssb,))
        S.op("act", lambda e: e.activation(out=rs, in_=sd, func=AF.Exp, scale=-0.5), reads=(ssb,), writes=(ssb,))
        S.op("dve", lambda e: e.scalar_tensor_tensor(out=hb, in0=x_ap, scalar=rs, in1=gain_bc,
                                                      op0=ALU.mult, op1=ALU.mult),
             reads=list(xbufs) + [ssb, gbuf], writes=(hbbuf,))

    def transpose_to(self, src_tiles, src_bufs, dst_ap_fn, dst_bufs, evac="act"):
        S = self.S
        n = len(src_tiles)
        pb, ps = self.bank()
        psb = ps[:].bitcast(BF16).rearrange("p (a b) -> p a b", b=128)
        fns = []
        for i, t in enumerate(src_tiles):
            fns.append(lambda e, i=i, t=t: e.transpose(out=psb[:, i, :], in_=t, identity=self.ident_bf))
        S.op("pe", fns, reads=list(src_bufs) + [self.cb], writes=(pb,))
        dst = dst_ap_fn(n)
        if evac == "act":
            S.op("act", lambda e: e.activation(out=dst, in_=psb[:, 0:n, :], func=AF.Copy),
                 reads=(pb,), writes=list(dst_bufs))
        else:
            S.op("dve", lambda e: e.tensor_copy(out=dst, in_=psb[:, 0:n, :]), reads=(pb,), writes=list(dst_bufs))

    def proj_fm(self, wview, wb, e0, kcn, act_fn, act_bufs, ranges):
        S = self.S
        banks = [self.bank() for _ in ranges]
        fns = []
        for kc in range(kcn):
            for (pb, ps), (t0, n) in zip(banks, ranges):
                fns.append(lambda e, kc=kc, ps=ps, t0=t0, n=n: e.matmul(
                    ps[:, 0:n], lhsT=wview[:, kc, e0:e0 + 128], rhs=act_fn(kc, t0, n),
                    start=(kc == 0), stop=(kc == kcn - 1)))
        S.op("pe", fns, reads=[wb] + list(act_bufs), writes=[pb for pb, _ in banks])
        return [(pb, ps[:, 0:n]) for (pb, ps), (t0, n) in zip(banks, ranges)]

    def proj_tm(self, wview, wb, c0, ncols, kcn, lhs_fn, act_bufs):
        S = self.S
        pb, ps = self.bank()
        fns = []
        for kc in range(kcn):
            fns.append(lambda e, kc=kc: e.matmul(ps[:, 0:ncols], lhsT=lhs_fn(kc), rhs=wview[:, kc, c0:c0 + ncols],
                                                 start=(kc == 0), stop=(kc == kcn - 1)))
        S.op("pe", fns, reads=[wb] + list(act_bufs), writes=(pb,))
        return pb, ps[:, 0:ncols]

    def phase0(self):
        S, d = self.S, self.d
        r1, r2, r4 = self.r1, self.r2, self.r4
        self.hTm = r1.alloc([16, TM], BF16)
        self.hTp = r2.alloc([16, TP], BF16)
        self.hTb = [Buf("hT%d" % c) for c in range(16)]
        r1.bufs += self.hTb[6:]
        r2.bufs += self.hTb[:6]
        self.small_b = [Buf("small%d" % i) for i in range(16)]
        r4.recarve()
        xt = [r4.alloc([D], F32) for _ in range(2)]
        xb = [r4.buf("xt%d" % i) for i in range(2)]
        xs = [S.new_dma_sem("xsem%d" % i) for i in range(2)]
        self.xt_sems = xs
        hb = [r4.alloc([D], BF16) for _ in range(2)]
        hbb = [r4.buf("hb%d" % i) for i in range(2)]
        gain = r4.alloc([D], F32)
        gb = r4.buf("gain")
        S.dma("sp", lambda e: e.dma_start(out=gain, in_=d["norm_mix_w"].partition_broadcast(128)), S.new_dma_sem("gsem0"), writes=(gb,))
        for ci in range(16):
            s = ci % 2
            src = d["xp"][ci * 128:(ci + 1) * 128, :] if ci < 7 else d["xm"][(ci - 7) * 128:(ci - 6) * 128, :]
            S.dma("sp", lambda e, s=s, src=src: e.dma_start(out=xt[s], in_=src), xs[s], writes=(xb[s],))
            self.rms_tile(xt[s], [xb[s]], gain, gb, hb[s], hbb[s], ci, hb[s])
            t0 = ci * 128
            for half in range(2):
                tiles = [hb[s][:, (half * 8 + i) * 128:(half * 8 + i + 1) * 128] for i in range(8)]
                self.transpose_to(tiles, [hbb[s]],
                                  lambda n, half=half, t0=t0: (self.hTp[:, half * 8:half * 8 + 8, t0:t0 + 128] if t0 < TP
                                                               else self.hTm[:, half * 8:half * 8 + 8, t0 - TP:t0 - TP + 128]),
                                  [self.hTb[ci]])
        self.dump("hTm", self.hTm, self.hTb[6:], [128, 16, TM], BF16)

    def dt_proj(self):
        S, d = self.S, self.d
        r4 = self.r4
        r4.recarve()
        raw = r4.alloc([16, 64], F32)
        rawb = r4.buf("dtraw")
        wv, wb = self.wload([d["w_in"][:, DT0:DT0 + 64]], 16)
        for ci in range(16):
            t0 = ci * 128
            pb, ps = self.proj_tm(wv, wb, 0, 64, 16, lambda kc, t0=t0: self.hT(kc, t0, 128), [self.hTb[ci]])
            S.op("dve", lambda e, ci=ci, ps=ps: e.tensor_tensor(out=raw[:, ci, :], in0=ps, in1=self.dtb, op=ALU.add),
                 reads=(pb, self.cb), writes=(rawb,))
        dtb_ = Buf("dt_all")
        self.dt_buf = dtb_
        S.op("act", lambda e: e.activation(out=raw, in_=raw, func=AF.Exp), reads=(rawb,), writes=(rawb,))
        S.op("act", lambda e: e.activation(out=self.dt_all, in_=raw, func=AF.Ln, bias=1.0, scale=1.0),
             reads=(rawb,), writes=(dtb_,))
        self.dump("dt_all", self.dt_all, [dtb_], [128, 16, 64])

    def conv_silu(self, pre, preb, n_out, blk, acc, accb, tt, ttb, out_bf, outb, lo=0):
        S = self.S
        w = self.cw_ssm
        S.op("dve", [
            lambda e: e.tensor_scalar(out=acc[:, 0:n_out], in0=pre[:, lo + 3:lo + 3 + n_out], scalar1=w[:, blk, 3:4],
                                      scalar2=self.cb_ssm[:, blk:blk + 1], op0=ALU.mult, op1=ALU.add)],
             reads=(preb, self.cb), writes=(accb,))
        for k in (2, 1, 0):
            S.op("dve", lambda e, k=k: e.scalar_tensor_tensor(out=acc[:, 0:n_out], in0=pre[:, lo + k:lo + k + n_out],
                                                             scalar=w[:, blk, k:k + 1], in1=acc[:, 0:n_out],
                                                             op0=ALU.mult, op1=ALU.add),
                 reads=(preb, self.cb, accb), writes=(accb,))
        S.op("act", lambda e: e.activation(out=tt[:, 0:n_out], in_=acc[:, 0:n_out], func=AF.Tanh),
             reads=(accb,), writes=(ttb,))
        S.op("dve", lambda e: e.scalar_tensor_tensor(out=out_bf, in0=tt[:, 0:n_out], scalar=1.0, in1=acc[:, 0:n_out],
                                                      op0=ALU.add, op1=ALU.mult),
             reads=(ttb, accb), writes=(outb,))

    def batch_smalls(self, reg, n, c0, g, suffix=False):
        S = self.S
        n8 = n * 8
        sm = {}
        for nm in ("a", "dwl", "w", "dtw"):
            sm[nm] = reg.alloc([n, 8], F32)
        s2 = reg.alloc([2, n8], F32)
        e2 = reg.alloc([2, n8], F32)
        smb = reg.buf("smalls")
        dt_g = self.dt_all[:, c0:c0 + n, g * 8:(g + 1) * 8]
        sm["dt"] = dt_g
        sm["acum"] = s2[:, 0, :].rearrange("p (c j) -> p c j", j=8)
        sm["atot"] = s2[:, 1, :].rearrange("p (c j) -> p c j", j=8)
        sm["eacum"] = e2[:, 0, :].rearrange("p (c j) -> p c j", j=8)
        sm["eatot"] = e2[:, 1, :].rearrange("p (c j) -> p c j", j=8)
        S.op("dve", lambda e: e.tensor_tensor(out=sm["a"], in0=dt_g,
                                              in1=self.Ab[:, g * 8:(g + 1) * 8].unsqueeze(1).to_broadcast([128, n, 8]), op=ALU.mult),
             reads=(self.dt_buf, self.cb), writes=(smb,))
        pb, ps = self.bank()
        a2 = sm["a"].rearrange("p c j -> p (c j)")
        S.op("pe", [
            lambda e: e.matmul(ps[:, 0:n8], lhsT=self.tri_f, rhs=a2, start=True, stop=True),
            lambda e: e.matmul(ps[:, 128:128 + n8], lhsT=self.ones_f, rhs=a2, start=True, stop=True),
        ], reads=(smb, self.cb), writes=(pb,))
        psv = ps[:, 0:256].rearrange("p (a b) -> p a b", b=128)[:, :, 0:n8]
        S.op("act", [
            lambda e: e.activation(out=s2, in_=psv, func=AF.Copy),
            lambda e: e.activation(out=e2, in_=psv, func=AF.Exp),
        ], reads=(pb,), writes=(smb,))
        S.op("dve", lambda e: e.tensor_tensor(out=sm["dwl"], in0=sm["atot"], in1=sm["acum"], op=ALU.subtract),
             reads=(smb,), writes=(smb,))
        if suffix:
            suf = reg.alloc([n, 8], F32)
            S.op("dve", lambda e: e.memset(suf[:, n - 1, :], 0.0), reads=(smb,), writes=(smb,))
            for c in range(n - 2, -1, -1):
                S.op("dve", lambda e, c=c: e.tensor_tensor(out=suf[:, c, :], in0=suf[:, c + 1, :], in1=sm["atot"][:, c + 1, :], op=ALU.add),
                     reads=(smb,), writes=(smb,))
            S.op("dve", lambda e: e.tensor_tensor(out=sm["dwl"], in0=sm["dwl"], in1=suf, op=ALU.add), reads=(smb,), writes=(smb,))
        if g == 0 and not suffix:
            self.dump("s2", s2, [smb], [128, 2, n8])
            self.dump("sma", sm["a"], [smb], [128, n, 8])
        S.op("act", lambda e: e.activation(out=sm["w"], in_=sm["dwl"], func=AF.Exp), reads=(smb,), writes=(smb,))
        S.op("dve", lambda e: e.tensor_tensor(out=sm["dtw"], in0=sm["w"], in1=dt_g, op=ALU.mult),
             reads=(smb, self.dt_buf), writes=(smb,))
        return sm, smb

    def ssm_prefix(self):
        S, d = self.S, self.d
        r3, r4 = self.r3, self.r4
        r3.recarve()
        self.stash = r3.h[:, :].rearrange("p (g x) -> p g x", g=8)
        self.yTb = [r3.buf("yT%d" % g) for g in range(8)]
        for g in range(NG):
            self._ssm_prefix_group(g)
        self.dump("stash", self.stash[:, :, 0:512], self.yTb, [128, 8, 512])

    def _ssm_prefix_group(self, g):
        S, d = self.S, self.d
        r3, r4 = self.r3, self.r4
        NP = 896
        ranges = [(0, 384), (384, 384), (768, 128)]
        abufs = self.hTb[0:7]
        if True:
            r4.recarve()
            pre = r4.alloc([3 + NP], F32); preb = r4.buf("pre")
            acc = r4.alloc([NP], F32); accb = r4.buf("acc")
            tt = r4.alloc([NP], F32); ttb = r4.buf("tt")
            cvo = [r4.alloc([NP], BF16) for _ in range(2)]; cvob = [r4.buf("cvo%d" % i) for i in range(2)]
            xs_tm = r4.alloc([7, 512], BF16); xsb = r4.buf("xs_tm")
            B_tm = r4.alloc([7, 128], BF16); Btmb = r4.buf("B_tm")
            xw = r4.alloc([7, 512], BF16); xwb = r4.buf("xw")
            S.op("dve", lambda e: e.memset(pre[:, 0:3], 0.0), writes=(preb,))
            sm, smb = self.batch_smalls(r4, 7, 0, g, suffix=True)
            cols = [XBC0 + g * 512 + j * 128 for j in range(4)] + [XBC0 + DI + g * 128]
            pending = []
            wv = None
            for bi, c0 in enumerate(cols):
                blk = (c0 - XBC0) // 128
                if bi in (0, 2):
                    wv, wb = self.wload([d["w_in"][:, c0:c0 + 256]], 16)
                    e0 = 0
                elif bi == 4:
                    wv, wb = self.wload([d["w_in"][:, c0:c0 + 128]], 16)
                    e0 = 0
                else:
                    e0 = 128
                outs = self.proj_fm(wv, wb, e0, 16, self.hT, abufs, ranges)
                for (pb, ps), (t0, n) in zip(outs, ranges):
                    S.op("act", lambda e, ps=ps, t0=t0, n=n: e.activation(out=pre[:, 3 + t0:3 + t0 + n], in_=ps, func=AF.Copy),
                         reads=(pb,), writes=(preb,))
                for f in pending:
                    f()
                pending = []
                cv, cvb = cvo[bi % 2], cvob[bi % 2]
                self.conv_silu(pre, preb, NP, blk, acc, accb, tt, ttb, cv, cvb)
                tiles = [cv[:, c * 128:(c + 1) * 128] for c in range(7)]
                if bi < 4:
                    pending.append(lambda tiles=tiles, cvb=cvb, bi=bi: self.transpose_to(
                        tiles, [cvb], lambda n: xs_tm[:, 0:7, bi * 128:(bi + 1) * 128], [xsb]))
                else:
                    pending.append(lambda tiles=tiles, cvb=cvb: self.transpose_to(tiles, [cvb], lambda n: B_tm[:, 0:7, :], [Btmb]))
            for f in pending:
                f()
            S.op("dve", lambda e: e.tensor_tensor(out=xw.rearrange("p c (j q) -> p c j q", q=64),
                                                  in0=xs_tm.rearrange("p c (j q) -> p c j q", q=64),
                                                  in1=sm["dtw"].unsqueeze(3).to_broadcast([128, 7, 8, 64]), op=ALU.mult),
                 reads=(xsb, smb), writes=(xwb,))
            pb, ps = self.bank()
            fns = [lambda e, c=c, ps=ps: e.matmul(ps[:, 0:512], lhsT=B_tm[:, c, :], rhs=xw[:, c, :], start=(c == 0), stop=(c == 6))
                   for c in range(7)]
            S.op("pe", fns, reads=(Btmb, xwb), writes=(pb,))
            S.op("act", lambda e, g=g, ps=ps: e.activation(out=self.stash[:, g, 0:512], in_=ps[:, 0:512], func=AF.Copy),
                 reads=(pb,), writes=(self.yTb[g],))

    def ssm_main(self):
        S, d = self.S, self.d
        r2, r3, r4 = self.r2, self.r3, self.r4
        yT = r3.h[:, :].bitcast(BF16).rearrange("p (k t) -> p k t", t=NT)
        self.yT = yT
        self.nwsem = S.new_dma_sem("nwsem")
        for g in range(NG):
            self._ssm_main_group(g)
        self.dump("yT", yT, self.yTb, [128, 32, NT], BF16)

    def _ssm_main_group(self, g):
        S, d = self.S, self.d
        r2, r3, r4 = self.r2, self.r3, self.r4
        yT = self.yT
        NPRE = NT + 3
        nsem = self.nwsem
        ranges = [(893 + i * 385, 385) for i in range(3)]
        abufs = self.hTb[6:16]
        if True:
            r2.recarve(); r4.recarve()
            pre = r4.alloc([NPRE], F32); preb = r4.buf("pre")
            acc = r4.alloc([384], F32); accb = r4.buf("acc")
            tt = r4.alloc([384], F32); ttb = r4.buf("tt")
            cvo = [r4.alloc([384], BF16) for _ in range(2)]; cvob = [r4.buf("cvo%d" % i) for i in range(2)]
            BT = r4.alloc([NT], BF16); BTb = r4.buf("BT")
            CT = r4.alloc([NT], BF16); CTb = r4.buf("CT")
            xs_tm = r4.alloc([NMC, 512], BF16); xsb = r4.buf("xs_tm")
            B_tm = r4.alloc([NMC, 128], BF16); Btmb = r4.buf("B_tm")
            zs = [r4.alloc([512], BF16) for _ in range(2)]; zsb = [r4.buf("zs%d" % i) for i in range(2)]
            ztmp = r4.alloc([256], F32); ztb = r4.buf("ztmp")
            hiT = r4.alloc([NMC, 128], BF16); loT = r4.alloc([NMC, 128], BF16); hib = r4.buf("hiloT")
            normw = r2.alloc([512], F32); nwb = r2.buf("normw")
            Scur = r2.alloc([512], F32); Sb = r2.buf("Scur")
            tmpS = r2.alloc([512], F32); tmpSb = r2.buf("tmpS")
            xdt = r2.alloc([512], BF16); xdtb = r2.buf("xdt")
            xw = r2.alloc([512], BF16); xwb = r2.buf("xw")
            Sbf = r2.alloc([512], BF16); Sbfb = r2.buf("Sbf")
            segT = r2.alloc([8, 128], BF16); segb = r2.buf("segT")
            MT = r2.alloc([8, 128], BF16); MTb = r2.buf("MT")
            cbT = r2.alloc([128], BF16); cbTb = r2.buf("cbT")
            t1 = r2.alloc([512], F32); t1b = r2.buf("t1")
            yy = r2.alloc([512], F32); yb = r2.buf("y")
            yn = r2.alloc([512], BF16); ynb = r2.buf("yn")
            ysm = r2.alloc([4], F32); ysmb = r2.buf("ysm")
            hl8 = r2.alloc([2, NMC * 8], BF16); hl8b = r2.buf("hl8")
            S.op("act", lambda e, g=g: e.activation(out=Scur, in_=self.stash[:, g, 0:512], func=AF.Copy),
                 reads=(self.yTb[g],), writes=(Sb,))
            S.dma("sp", lambda e, g=g: e.dma_start(out=normw, in_=d["ssm_norm_w"][:, g * 512:(g + 1) * 512].partition_broadcast(128)),
                  nsem, writes=(nwb,))
            sm, smb = self.batch_smalls(r2, NMC, 7, g)
            acf = sm["acum"].rearrange("p c j -> p (c j)")
            S.op("dve", lambda e: e.tensor_copy(out=hl8[:, 0, :], in_=acf), reads=(smb,), writes=(hl8b,))
            S.op("dve", lambda e: e.tensor_tensor(out=hl8[:, 1, :], in0=acf, in1=hl8[:, 0, :], op=ALU.subtract),
                 reads=(smb, hl8b), writes=(hl8b,))
            if g == 0:
                self.dump("hl8", hl8, [hl8b], [128, 2, NMC * 8], BF16)
            tb = [self.bank() for _ in range(3)]
            tv = [p[1][:].bitcast(BF16).rearrange("p (a b) -> p a b", b=128) for p in tb]
            fns = []
            for c in range(NMC):
                for hl in range(2):
                    if c < 8:
                        dst = tv[hl][0:8, c, :]
                    else:
                        dst = tv[2][0:8, hl, :]
                    fns.append(lambda e, c=c, hl=hl, dst=dst: e.transpose(out=dst, in_=hl8[:, hl, c * 8:(c + 1) * 8], identity=self.ident_bf))
            S.op("pe", fns, reads=(hl8b, self.cb), writes=[p[0] for p in tb])
            S.op("act", [
                lambda e: e.activation(out=hiT[0:8, 0:8, :], in_=tv[0][0:8, 0:8, :], func=AF.Copy),
                lambda e: e.activation(out=loT[0:8, 0:8, :], in_=tv[1][0:8, 0:8, :], func=AF.Copy),
                lambda e: e.activation(out=hiT[0:8, 8, :], in_=tv[2][0:8, 0, :], func=AF.Copy),
                lambda e: e.activation(out=loT[0:8, 8, :], in_=tv[2][0:8, 1, :], func=AF.Copy),
            ], reads=[p[0] for p in tb], writes=(hib,))
            cols = [XBC0 + g * 512 + j * 128 for j in range(4)] + [XBC0 + DI + g * 128, XBC0 + DI + 1024 + g * 128]
            pending = []
            for bi, c0 in enumerate(cols):
                blk = (c0 - XBC0) // 128
                if bi in (0, 2):
                    wv, wb = self.wload([d["w_in"][:, c0:c0 + 256]], 16)
                    e0 = 0
                elif bi == 4:
                    wv, wb = self.wload([d["w_in"][:, c0:c0 + 128], d["w_in"][:, cols[5]:cols[5] + 128]], 16)
                    e0 = 0
                else:
                    e0 = 128
                outs = self.proj_fm(wv, wb, e0, 16, self.hT, abufs, ranges)
                for i, (pb, ps) in enumerate(outs):
                    S.op("act", lambda e, ps=ps, i=i: e.activation(out=pre[:, i * 385:(i + 1) * 385], in_=ps, func=AF.Copy),
                         reads=(pb,), writes=(preb,))
                for th in range(3):
                    for f in pending:
                        f()
                    pending = []
                    cv, cvb = cvo[th % 2], cvob[th % 2]
                    if bi < 4:
                        dst, dstb = cv, cvb
                    elif bi == 4:
                        dst, dstb = BT[:, th * 384:(th + 1) * 384], BTb
                    else:
                        dst, dstb = CT[:, th * 384:(th + 1) * 384], CTb
                    self.conv_silu(pre, preb, 384, blk, acc, accb, tt, ttb, dst, dstb, lo=th * 384)
                    if bi < 4:
                        tiles = [cv[:, c * 128:(c + 1) * 128] for c in range(3)]
                        pending.append(lambda tiles=tiles, cvb=cvb, bi=bi, th=th: self.transpose_to(
                            tiles, [cvb], lambda n: xs_tm[:, th * 3:th * 3 + 3, bi * 128:(bi + 1) * 128], [xsb]))
                    elif bi == 4:
                        tiles = [BT[:, (th * 3 + c) * 128:(th * 3 + c + 1) * 128] for c in range(3)]
                        pending.append(lambda tiles=tiles, th=th: self.transpose_to(
                            tiles, [BTb], lambda n: B_tm[:, th * 3:th * 3 + 3, :], [Btmb]))
            for f in pending:
                f()
            zw = []
            for hv in range(2):
                c0 = Z0 + g * 512 + hv * 256
                zw.append(self.wload([d["w_in"][:, c0:c0 + 256]], 16))
            Dg = self.Db[:, g * 8:(g + 1) * 8]
            for mc in range(NMC):
                tc = slice(mc * 128, (mc + 1) * 128)
                t0 = 896 + mc * 128
                zz, zzb = zs[mc % 2], zsb[mc % 2]
                dbanks = [self.bank() for _ in range(2)]
                fns = []
                for bq, (pbq, psq) in enumerate(dbanks):
                    fns.append(lambda e, psq=psq: e.matmul(psq[:, 0:512], lhsT=self.ident_bf, rhs=self.neg4, start=True, stop=False))
                    for jj in range(4):
                        j = bq * 4 + jj
                        for src in (hiT, loT):
                            fns.append(lambda e, psq=psq, jj=jj, j=j, src=src, mc=mc: e.matmul(
                                psq[:, jj * 128:(jj + 1) * 128], lhsT=self.sel[0:8, j * 128:(j + 1) * 128], rhs=src[0:8, mc, :],
                                start=False, stop=False))
                    for si, src in enumerate((hiT, loT)):
                        fns.append(lambda e, psq=psq, bq=bq, src=src, si=si, mc=mc: e.matmul(
                            psq[:, 0:512], lhsT=src[0:8, mc, :], rhs=self.nsel[0:8, bq * 512:(bq + 1) * 512],
                            start=False, stop=(si == 1)))
                S.op("pe", fns, reads=(hib, self.cb), writes=[p for p, _ in dbanks])
                S.op("act", [lambda e, bq=bq, psq=psq: e.activation(out=segT[:, bq * 4:(bq + 1) * 4, :].rearrange("p a b -> p (a b)"),
                                                                   in_=psq[:, 0:512], func=AF.Exp)
                             for bq, (_, psq) in enumerate(dbanks)],
                     reads=[p for p, _ in dbanks], writes=(segb,))
                pb, ps = self.bank()
                S.op("pe", lambda e, ps=ps, tc=tc: e.matmul(ps[:, 0:128], lhsT=BT[:, tc], rhs=CT[:, tc], start=True, stop=True),
                     reads=(BTb, CTb), writes=(pb,))
                S.op("act", lambda e, ps=ps: e.activation(out=cbT, in_=ps[:, 0:128], func=AF.Copy), reads=(pb,), writes=(cbTb,))
                S.op("dve", lambda e: e.tensor_tensor(out=MT, in0=segT, in1=cbT.unsqueeze(1).to_broadcast([128, 8, 128]), op=ALU.mult),
                     reads=(segb, cbTb), writes=(MTb,))
                xs_c = xs_tm[:, mc, :]
                S.op("dve", lambda e, xs_c=xs_c, mc=mc: e.tensor_tensor(out=xdt.rearrange("p (j q) -> p j q", q=64),
                                                                       in0=xs_c.rearrange("p (j q) -> p j q", q=64),
                                                                       in1=sm["dt"][:, mc, :].unsqueeze(2).to_broadcast([128, 8, 64]), op=ALU.mult),
                     reads=(xsb, self.dt_buf), writes=(xdtb,))
                S.op("dve", lambda e, xs_c=xs_c, mc=mc: e.tensor_tensor(out=xw.rearrange("p (j q) -> p j q", q=64),
                                                                       in0=xs_c.rearrange("p (j q) -> p j q", q=64),
                                                                       in1=sm["dtw"][:, mc, :].unsqueeze(2).to_broadcast([128, 8, 64]), op=ALU.mult),
                     reads=(xsb, smb), writes=(xwb,))
                S.op("act", lambda e: e.activation(out=Sbf, in_=Scur, func=AF.Copy), reads=(Sb,), writes=(Sbfb,))
                pbd, psd = self.bank()
                fns = [lambda e, j=j, psd=psd: e.matmul(psd[:, j * 64:(j + 1) * 64], lhsT=MT[:, j, :], rhs=xdt[:, j * 64:(j + 1) * 64],
                                                        start=True, stop=True) for j in range(8)]
                S.op("pe", fns, reads=(MTb, xdtb), writes=(pbd,))
                pbo, pso = self.bank()
                S.op("pe", lambda e, pso=pso, tc=tc: e.matmul(pso[:, 0:512], lhsT=CT[:, tc], rhs=Sbf, start=True, stop=True),
                     reads=(CTb, Sbfb), writes=(pbo,))
                pbs, pss = self.bank()
                S.op("pe", lambda e, pss=pss, mc=mc: e.matmul(pss[:, 0:512], lhsT=B_tm[:, mc, :], rhs=xw, start=True, stop=True),
                     reads=(Btmb, xwb), writes=(pbs,))
                for hv in range(2):
                    pbz, psz = self.proj_tm(zw[hv][0], zw[hv][1], 0, 256, 16, lambda kc, t0=t0: self.hT(kc, t0, 128), [self.hTb[7 + mc]])
                    S.op("act", lambda e, psz=psz: e.activation(out=ztmp, in_=psz, func=AF.Tanh, scale=0.5), reads=(pbz,), writes=(ztb,))
                    S.op("dve", lambda e, psz=psz, hv=hv, zz=zz: e.scalar_tensor_tensor(
                        out=zz[:, hv * 256:(hv + 1) * 256], in0=ztmp, scalar=1.0, in1=psz, op0=ALU.add, op1=ALU.mult),
                        reads=(ztb, pbz), writes=(zzb,))
                S.op("dve", lambda e, mc=mc: e.tensor_tensor(out=tmpS.rearrange("p (j q) -> p j q", q=64),
                                                            in0=Scur.rearrange("p (j q) -> p j q", q=64),
                                                            in1=sm["eatot"][:, mc, :].unsqueeze(2).to_broadcast([128, 8, 64]), op=ALU.mult),
                     reads=(Sb, smb, Sbfb), writes=(tmpSb,))
                S.op("dve", lambda e, pss=pss: e.tensor_tensor(out=Scur, in0=tmpS, in1=pss[:, 0:512], op=ALU.add),
                     reads=(tmpSb, pbs), writes=(Sb,))
                if mc == 0:
                    S.op("dve", lambda e: e.tensor_scalar(out=Scur, in0=Scur, scalar1=self.flag[:, 0:1], scalar2=None, op0=ALU.mult),
                         reads=(Sb, self.cb), writes=(Sb,))
                S.op("dve", lambda e, pso=pso, mc=mc: e.tensor_tensor(out=t1.rearrange("p (j q) -> p j q", q=64),
                                                                     in0=pso[:, 0:512].rearrange("p (j q) -> p j q", q=64),
                                                                     in1=sm["eacum"][:, mc, :].unsqueeze(2).to_broadcast([128, 8, 64]), op=ALU.mult),
                     reads=(pbo, smb), writes=(t1b,))
                S.op("dve", lambda e, psd=psd: e.tensor_tensor(out=yy, in0=t1, in1=psd[:, 0:512], op=ALU.add),
                     reads=(t1b, pbd), writes=(yb,))
                S.op("dve", lambda e, xs_c=xs_c, Dg=Dg: e.tensor_tensor(out=t1.rearrange("p (j q) -> p j q", q=64),
                                                                       in0=xs_c.rearrange("p (j q) -> p j q", q=64),
                                                                       in1=Dg.unsqueeze(2).to_broadcast([128, 8, 64]), op=ALU.mult),
                     reads=(xsb, self.cb, yb), writes=(t1b,))
                S.op("dve", lambda e: e.tensor_tensor(out=yy, in0=yy, in1=t1, op=ALU.add), reads=(t1b, yb), writes=(yb,))
                S.op("dve", lambda e, zz=zz: e.tensor_tensor(out=yy, in0=yy, in1=zz, op=ALU.mult), reads=(yb, zzb), writes=(yb,))
                S.op("act", lambda e: e.activation(out=yn, in_=yy, func=AF.Square, accum_out=ysm[:, 0:1]), reads=(yb,), writes=(ynb, ysmb))
                S.op("act", lambda e: e.activation(out=ysm[:, 1:2], in_=ysm[:, 0:1], func=AF.Ln, bias=4 * EPS, scale=1.0 / 512),
                     reads=(ysmb,), writes=(ysmb,))
                S.op("act", lambda e: e.activation(out=ysm[:, 2:3], in_=ysm[:, 1:2], func=AF.Exp, scale=-0.5), reads=(ysmb,), writes=(ysmb,))
                S.op("dve", lambda e: e.scalar_tensor_tensor(out=yn, in0=yy, scalar=ysm[:, 2:3], in1=normw, op0=ALU.mult, op1=ALU.mult),
                     reads=(yb, ysmb, nwb), writes=(ynb,))
                tiles = [yn[:, i * 128:(i + 1) * 128] for i in range(4)]
                self.transpose_to(tiles, [ynb], lambda n, g=g, tc=tc: yT[:, 4 * g:4 * g + 4, tc], [self.yTb[g]])

    def ssm_out(self):
        S, d = self.S, self.d
        r2, r4 = self.r2, self.r4
        r2.recarve(); r4.recarve()
        self.mixedT = r4.alloc([16, NT], BF16)
        self.mixb = [r4.buf("mixedT%d" % i) for i in range(16)]
        tg = r2.alloc([3, 384], F32); tgb = r2.buf("tg")
        self.tg, self.tgb = tg, tgb
        ranges = [(896 + i * 384, 384) for i in range(3)]
        mainbufs = self.hTb[7:16]
        yT = self.yT
        for db in range(16):
            if db % 2 == 0:
                wvg, wbg = self.wload([d["w_in"][:, GS0 + db * 128:GS0 + db * 128 + 256]], 16)
            outs = self.proj_fm(wvg, wbg, (db % 2) * 128, 16, self.hT, mainbufs, ranges)
            for i, (pb, ps) in enumerate(outs):
                S.op("act", lambda e, ps=ps, i=i: e.activation(out=tg[:, i, :], in_=ps, func=AF.Tanh, scale=0.5),
                     reads=(pb,), writes=(tgb,))
            wv, wb = self.wload([d["w_ssm_out"][:, db * 128:(db + 1) * 128]], 32)
            outs = self.proj_fm(wv, wb, 0, 32, lambda kc, t0, n: yT[:, kc, t0 - 896:t0 - 896 + n], self.yTb, ranges)
            for i, (pb, ps) in enumerate(outs):
                S.op("dve", lambda e, ps=ps, i=i, db=db: e.scalar_tensor_tensor(
                    out=self.mixedT[:, db, i * 384:(i + 1) * 384], in0=tg[:, i, :], scalar=1.0, in1=ps, op0=ALU.add, op1=ALU.mult),
                    reads=(tgb, pb), writes=(self.mixb[db],))
        self.dump("mixS", self.mixedT, self.mixb, [128, 16, NT], BF16)

    def attention(self):
        S, d = self.S, self.d
        r2, r3 = self.r2, self.r3
        r2.recarve(); r3.recarve()
        self.aoT = r3.alloc([16, NT], BF16)
        self.aoTb = [r3.buf("aoT%d" % i) for i in range(NKV)]
        qT = r3.alloc([4, NT], BF16); qTb = r3.buf("qT")
        kT2 = r3.alloc([TM], BF16); kTb = r3.buf("kT2")
        v1 = r3.alloc([10, 65], BF16); v1b = r3.buf("v1")
        biasT = r3.alloc([8, 2, 128], F32); biasb = r3.buf("biasT")
        q32 = r3.alloc([512], F32); q32b = r3.buf("q32")
        qtmp = r3.alloc([512], F32); qtmpb = r3.buf("qtmp")
        qn = r3.alloc([512], BF16); qnb = r3.buf("qn")
        kv32 = r3.alloc([128], F32); kvb = r3.buf("kv32")
        ktmp = r3.alloc([64], F32); ktmpb = r3.buf("ktmp")
        kdup = r3.alloc([2, 64], BF16); kdupb = r3.buf("kdup")
        qs = r3.alloc([32], F32); qsb = r3.buf("qs")
        ltmp = r2.alloc([512], F32); ltb = r2.buf("ltmp")
        eT = [r2.alloc([2, 2, 128], BF16) for _ in range(2)]; eTb = [r2.buf("eT%d" % i) for i in range(2)]
        den = r2.alloc([16], F32); denb = r2.buf("den")
        ao = r2.alloc([8, 64], BF16); aob = r2.buf("ao")
        bsem = S.new_dma_sem("biassem")
        S.op("dve", lambda e: e.memset(v1[:, :, 64:65], 1.0), writes=(v1b,))
        for kg in range(NKV):
            S.dma("sp", lambda e, kg=kg: e.dma_start(out=biasT, in_=d["biasT"][kg].rearrange("p (h b q) -> p h b q", h=8, b=2)),
                  bsem, writes=(biasb,))
            wv, wb = self.wload([d["w_in"][:, K0 + kg * 64:K0 + kg * 64 + 64], d["w_in"][:, V0 + kg * 64:V0 + kg * 64 + 64]], 16)
            for cj in range(10):
                t0 = 768 + cj * 128
                pb, ps = self.proj_tm(wv, wb, 0, 128, 16, lambda kc, t0=t0: self.hT(kc, t0, 128), [self.hTb[6 + cj]])
                S.op("act", lambda e, ps=ps: e.activation(out=kv32, in_=ps, func=AF.Copy), reads=(pb,), writes=(kvb,))
                S.op("act", lambda e: e.activation(out=ktmp, in_=kv32[:, 0:64], func=AF.Square, accum_out=qs[:, 0:1]),
                     reads=(kvb,), writes=(ktmpb, qsb))
                S.op("act", lambda e: e.activation(out=qs[:, 1:2], in_=qs[:, 0:1], func=AF.Ln, bias=EPS, scale=1.0 / 64),
                     reads=(qsb,), writes=(qsb,))
                S.op("act", lambda e: e.activation(out=qs[:, 2:3], in_=qs[:, 1:2], func=AF.Exp, scale=-0.5), reads=(qsb,), writes=(qsb,))
                S.op("dve", [
                    lambda e: e.scalar_tensor_tensor(out=kdup[:, 0, :], in0=kv32[:, 0:64], scalar=qs[:, 2:3], in1=self.kg,
                                                     op0=ALU.mult, op1=ALU.mult),
                    lambda e: e.scalar_tensor_tensor(out=kdup[:, 1, :], in0=kv32[:, 0:64], scalar=qs[:, 2:3], in1=self.kg,
                                                     op0=ALU.mult, op1=ALU.mult),
                ], reads=(kvb, qsb, self.cb), writes=(kdupb,))
                S.op("act", lambda e, cj=cj: e.activation(out=v1[:, cj, 0:64], in_=kv32[:, 64:128], func=AF.Copy),
                     reads=(kvb,), writes=(v1b,))
                self.transpose_to([kdup.rearrange("p a b -> p (a b)")], [kdupb],
                                  lambda n, cj=cj: kT2[:, cj * 128:(cj + 1) * 128].unsqueeze(1), [kTb])
            wvs = []
            for hv in range(2):
                c0 = Q0 + kg * 512 + hv * 256
                wvs.append(self.wload([d["w_in"][:, c0:c0 + 256]], 16))
            for mc in range(NMC):
                t0 = 896 + mc * 128
                for hv in range(2):
                    pb, ps = self.proj_tm(wvs[hv][0], wvs[hv][1], 0, 256, 16, lambda kc, t0=t0: self.hT(kc, t0, 128), [self.hTb[7 + mc]])
                    S.op("act", lambda e, ps=ps, hv=hv: e.activation(out=q32[:, hv * 256:(hv + 1) * 256], in_=ps, func=AF.Copy),
                         reads=(pb,), writes=(q32b,))
                S.op("dve", lambda e: e.tensor_tensor(out=qtmp, in0=q32, in1=q32, op=ALU.mult), reads=(q32b,), writes=(qtmpb,))
                S.op("dve", lambda e: e.tensor_reduce(out=qs[:, 8:16], in_=qtmp.rearrange("p (h x) -> p h x", x=64), axis=AX.X, op=ALU.add),
                     reads=(qtmpb,), writes=(qsb,))
                S.op("act", lambda e: e.activation(out=qs[:, 16:24], in_=qs[:, 8:16], func=AF.Ln, bias=EPS, scale=1.0 / 64),
                     reads=(qsb,), writes=(qsb,))
                S.op("act", lambda e: e.activation(out=qs[:, 24:32], in_=qs[:, 16:24], func=AF.Exp, scale=-0.5), reads=(qsb,), writes=(qsb,))
                S.op("dve", lambda e: e.tensor_tensor(out=qtmp.rearrange("p (h x) -> p h x", x=64), in0=q32.rearrange("p (h x) -> p h x", x=64),
                                                      in1=qs[:, 24:32].unsqueeze(2).to_broadcast([128, 8, 64]), op=ALU.mult),
                     reads=(q32b, qsb), writes=(qtmpb,))
                S.op("dve", lambda e: e.tensor_tensor(out=qn.rearrange("p (h x) -> p h x", x=64), in0=qtmp.rearrange("p (h x) -> p h x", x=64),
                                                      in1=self.qg.unsqueeze(1).to_broadcast([128, 8, 64]), op=ALU.mult),
                     reads=(qtmpb, self.cb), writes=(qnb,))
                tiles = [qn[:, i * 128:(i + 1) * 128] for i in range(4)]
                self.transpose_to(tiles, [qnb], lambda n, mc=mc: qT[:, 0:4, mc * 128:(mc + 1) * 128], [qTb])
            for mc in range(NMC):
                pvb = [self.bank() for _ in range(2)]
                for qp in range(2):
                    lb = [self.bank() for _ in range(2)]
                    fns = []
                    for hh in range(2):
                        psv = lb[hh][1][:, 0:512].rearrange("p (a b q) -> p a b q", a=2, b=2)
                        for qq in range(2):
                            qt = qp * 2 + qq
                            for blk in range(2):
                                cj = mc + blk
                                fns.append(lambda e, hh=hh, blk=blk, cj=cj, psv=psv, qt=qt, qq=qq, mc=mc: e.matmul(
                                    psv[:, qq, blk, :], lhsT=kT2[hh * 64:(hh + 1) * 64, cj * 128:(cj + 1) * 128],
                                    rhs=qT[hh * 64:(hh + 1) * 64, qt, mc * 128:(mc + 1) * 128], start=True, stop=True))
                    S.op("pe", fns, reads=(kTb, qTb), writes=[lb[0][0], lb[1][0]])
                    for hh in range(2):
                        pb, ps = lb[hh]
                        psv = ps[:, 0:512].rearrange("p (a b q) -> p a b q", a=2, b=2)
                        h0 = qp * 4 + hh
                        S.op("dve", lambda e, psv=psv, h0=h0: e.tensor_tensor(out=ltmp.rearrange("p (a b q) -> p a b q", a=2, b=2), in0=psv,
                                                                             in1=biasT[:, h0:h0 + 3:2, :, :], op=ALU.add),
                             reads=(pb, biasb), writes=(ltb,))
                        S.op("act", lambda e, hh=hh: e.activation(out=eT[hh].rearrange("p a b q -> p (a b q)"), in_=ltmp, func=AF.Exp),
                             reads=(ltb,), writes=(eTb[hh],))
                        if mc == 1:
                            S.op("dve", lambda e, hh=hh: e.tensor_scalar(out=eT[hh][:, :, 0, :], in0=eT[hh][:, :, 0, :],
                                                                         scalar1=self.flag[:, 0:1], scalar2=None, op0=ALU.mult),
                                 reads=(eTb[hh], self.cb), writes=(eTb[hh],))
                        fns = []
                        for qq in range(2):
                            h8 = h0 + 2 * qq
                            pvp, pvs = pvb[h8 // 4]
                            slot = h8 % 4
                            for blk in range(2):
                                cj = mc + blk
                                fns.append(lambda e, hh=hh, qq=qq, blk=blk, cj=cj, pvs=pvs, slot=slot: e.matmul(
                                    pvs[:, slot * 65:(slot + 1) * 65], lhsT=eT[hh][:, qq, blk, :], rhs=v1[:, cj, :],
                                    start=(blk == 0), stop=(blk == 1)))
                        S.op("pe", fns, reads=(eTb[hh], v1b), writes=[pvb[qp][0]])
                for hb_ in range(2):
                    pvp, pvs = pvb[hb_]
                    pv3 = pvs[:, 0:260].rearrange("p (s x) -> p s x", x=65)
                    S.op("dve", lambda e, pv3=pv3, hb_=hb_, kg=kg: e.tensor_tensor(
                        out=den[:, hb_ * 4:hb_ * 4 + 4].unsqueeze(2), in0=pv3[:, :, 64:65],
                        in1=self.esink[:, kg * 8 + hb_ * 4:kg * 8 + hb_ * 4 + 4].unsqueeze(2), op=ALU.add),
                        reads=(pvp, self.cb), writes=(denb,))
                    S.op("dve", lambda e, hb_=hb_: e.reciprocal(out=den[:, 8 + hb_ * 4:8 + hb_ * 4 + 4], in_=den[:, hb_ * 4:hb_ * 4 + 4]),
                         reads=(denb,), writes=(denb,))
                    S.op("dve", lambda e, pv3=pv3, hb_=hb_: e.tensor_tensor(
                        out=ao[:, hb_ * 4:hb_ * 4 + 4, :], in0=pv3[:, :, 0:64],
                        in1=den[:, 8 + hb_ * 4:8 + hb_ * 4 + 4].unsqueeze(2).to_broadcast([128, 4, 64]), op=ALU.mult),
                        reads=(pvp, denb), writes=(aob,))
                aof = ao.rearrange("p h x -> p (h x)")
                tiles = [aof[:, i * 128:(i + 1) * 128] for i in range(4)]
                self.transpose_to(tiles, [aob], lambda n, kg=kg, mc=mc: self.aoT[:, kg * 4:kg * 4 + 4, mc * 128:(mc + 1) * 128],
                                  [self.aoTb[kg]])
        self.dump("aoT", self.aoT, self.aoTb, [128, 16, NT], BF16)

    def attn_out(self):
        S, d = self.S, self.d
        r2 = self.r2
        r2.recarve()
        tg = r2.alloc([3, 384], F32); tgb = r2.buf("tg")
        mt = r2.alloc([384], F32); mtb = r2.buf("mtmp")
        ranges = [(896 + i * 384, 384) for i in range(3)]
        mainbufs = self.hTb[7:16]
        for db in range(16):
            if db % 2 == 0:
                wvg, wbg = self.wload([d["w_in"][:, GA0 + db * 128:GA0 + db * 128 + 256]], 16)
                wva, wba = self.wload([d["w_attn_out"][:, db * 128:db * 128 + 256]], 16)
            outs = self.proj_fm(wvg, wbg, (db % 2) * 128, 16, self.hT, mainbufs, ranges)
            for i, (pb, ps) in enumerate(outs):
                S.op("act", lambda e, ps=ps, i=i: e.activation(out=tg[:, i, :], in_=ps, func=AF.Tanh, scale=0.5),
                     reads=(pb,), writes=(tgb,))
            outs = self.proj_fm(wva, wba, (db % 2) * 128, 16, lambda kc, t0, n: self.aoT[:, kc, t0 - 896:t0 - 896 + n],
                                self.aoTb, ranges)
            for i, (pb, ps) in enumerate(outs):
                S.op("dve", lambda e, ps=ps, i=i: e.scalar_tensor_tensor(out=mt, in0=tg[:, i, :], scalar=1.0, in1=ps,
                                                                        op0=ALU.add, op1=ALU.mult),
                     reads=(tgb, pb), writes=(mtb,))
                S.op("dve", lambda e, i=i, db=db: e.tensor_tensor(out=self.mixedT[:, db, i * 384:(i + 1) * 384],
                                                                  in0=self.mixedT[:, db, i * 384:(i + 1) * 384], in1=mt, op=ALU.add),
                     reads=(mtb, self.mixb[db]), writes=(self.mixb[db],))
        self.dump("mixed", self.mixedT, self.mixb, [128, 16, NT], BF16)

    def wout_residual(self):
        S, d = self.S, self.d
        r1, r2, r3 = self.r1, self.r2, self.r3
        r3.recarve()
        self.x1 = r3.alloc([NMC, D], F32)
        self.x1b = [r3.buf("x1_%d" % i) for i in range(NMC)]
        x1, x1b = self.x1, self.x1b
        xsem = S.new_dma_sem("x1sem")
        for mc in range(NMC):
            S.dma("sp", lambda e, mc=mc: e.dma_start(out=x1[:, mc, :], in_=d["xm"][mc * 128:(mc + 1) * 128, :]), xsem,
                  writes=(x1b[mc],))
        for mc in range(NMC):
            x1b[mc].w = (xsem, S.dcnt[xsem])
        for ct in range(8):
            wv, wb = self.wload([d["w_out"][:, ct * 256:(ct + 1) * 256]], 16)
            for mc in range(NMC):
                pb, ps = self.proj_tm(wv, wb, 0, 256, 16, lambda kc, mc=mc: self.mixedT[:, kc, mc * 128:(mc + 1) * 128], self.mixb)
                S.op("dve", lambda e, ps=ps, mc=mc, ct=ct: e.scalar_tensor_tensor(
                    out=x1[:, mc, ct * 256:(ct + 1) * 256], in0=ps, scalar=0.5, in1=x1[:, mc, ct * 256:(ct + 1) * 256],
                    op0=ALU.mult, op1=ALU.add), reads=(pb, x1b[mc]), writes=(x1b[mc],))
        self.dump("x1", x1, x1b, [128, NMC, D])
        r1.recarve(); r2.recarve()
        self.hfT = r1.alloc([16, NT], BF16)
        self.hfTb = [r1.buf("hfT%d" % i) for i in range(NMC)]
        gain = r2.alloc([D], F32); gb = r2.buf("gain")
        hb = [r2.alloc([D], BF16) for _ in range(2)]
        hbb = [r2.buf("hb%d" % i) for i in range(2)]
        S.dma("sp", lambda e: e.dma_start(out=gain, in_=d["norm_ffn_w"].partition_broadcast(128)), S.new_dma_sem("gsem1"), writes=(gb,))
        for mc in range(NMC):
            s = mc % 2
            self.rms_tile(x1[:, mc, :], [x1b[mc]], gain, gb, hb[s], hbb[s], mc, hb[s])
            for half in range(2):
                tiles = [hb[s][:, (half * 8 + i) * 128:(half * 8 + i + 1) * 128] for i in range(8)]
                self.transpose_to(tiles, [hbb[s]],
                                  lambda n, half=half, mc=mc: self.hfT[:, half * 8:half * 8 + 8, mc * 128:(mc + 1) * 128],
                                  [self.hfTb[mc]])
        self.gain_ap, self.gain_b, self.hb2, self.hbb2 = gain, gb, hb, hbb

    def ffn(self):
        S, d = self.S, self.d
        r2, r4 = self.r2, self.r4
        r4.recarve()
        actT = r4.alloc([11, 1024], BF16); actb = r4.buf("actT")
        pre = [r4.alloc([2 + NT], F32) for _ in range(2)]
        preb = [r4.buf("fpre%d" % i) for i in range(2)]
        acc = [r2.alloc([1024], F32) for _ in range(2)]
        accb = [r2.buf("facc%d" % i) for i in range(2)]
        gl = r4.alloc([1024], F32); glb = r4.buf("gl")
        x1, x1b = self.x1, self.x1b
        ranges = [(i * 384, 384) for i in range(3)]
        for i in range(2):
            S.op("dve", lambda e, i=i: e.memset(pre[i][:, 0:2], 0.0), writes=(preb[i],))
        for fg in range(4):
            for jb in range(11):
                b = fg * 11 + jb
                c0 = b * 128
                wv, wb = self.wload([d["w_ffn_up"][:, c0:c0 + 128], d["w_ffn_up"][:, DFF + c0:DFF + c0 + 128]], 16)
                for gu in range(2):
                    blk = b + gu * 44
                    outs = self.proj_fm(wv, wb, gu * 128, 16, lambda kc, t0, n: self.hfT[:, kc, t0:t0 + n], self.hfTb, ranges)
                    for i, (pb, ps) in enumerate(outs):
                        S.op("act", lambda e, ps=ps, i=i, gu=gu: e.activation(out=pre[gu][:, 2 + i * 384:2 + (i + 1) * 384], in_=ps, func=AF.Copy),
                             reads=(pb,), writes=(preb[gu],))
                    S.op("dve", lambda e, gu=gu: e.tensor_scalar(out=pre[gu][:, 128:130], in0=pre[gu][:, 128:130],
                                                                 scalar1=self.flag[:, 0:1], scalar2=None, op0=ALU.mult),
                         reads=(preb[gu], self.cb), writes=(preb[gu],))
                    w = self.cw_ffn
                    S.op("dve", lambda e, gu=gu, blk=blk: e.tensor_scalar(out=acc[gu], in0=pre[gu][:, 130:130 + 1024], scalar1=w[:, blk, 2:3],
                                                                          scalar2=self.cb_ffn[:, blk:blk + 1], op0=ALU.mult, op1=ALU.add),
                         reads=(preb[gu], self.cb), writes=(accb[gu],))
                    for k in (1, 0):
                        S.op("dve", lambda e, gu=gu, blk=blk, k=k: e.scalar_tensor_tensor(
                            out=acc[gu], in0=pre[gu][:, 128 + k:128 + k + 1024], scalar=w[:, blk, k:k + 1], in1=acc[gu],
                            op0=ALU.mult, op1=ALU.add), reads=(preb[gu], self.cb, accb[gu]), writes=(accb[gu],))
                S.op("act", lambda e: e.activation(out=gl, in_=acc[0], func=AF.Gelu_apprx_tanh), reads=(accb[0],), writes=(glb,))
                S.op("dve", lambda e, jb=jb: e.tensor_tensor(out=actT[:, jb, :], in0=gl, in1=acc[1], op=ALU.mult),
                     reads=(glb, accb[1]), writes=(actb,))
            if fg == 0:
                self.dump("actT0", actT, [actb], [128, 11, 1024], BF16)
            for ct in range(8):
                wv, wb = self.wload([d["w_ffn_down"][fg * 1408:(fg + 1) * 1408, ct * 256:(ct + 1) * 256]], 11)
                for mc in range(1, NMC):
                    pb, ps = self.proj_tm(wv, wb, 0, 256, 11, lambda kc, mc=mc: actT[:, kc, (mc - 1) * 128:mc * 128], [actb])
                    S.op("dve", lambda e, ps=ps, mc=mc, ct=ct: e.tensor_tensor(
                        out=x1[:, mc, ct * 256:(ct + 1) * 256], in0=ps, in1=x1[:, mc, ct * 256:(ct + 1) * 256], op=ALU.add),
                        reads=(pb, x1b[mc]), writes=(x1b[mc],))
        self.dump("x2", x1, x1b, [128, NMC, D])

    def ple(self):
        S, d = self.S, self.d
        r1, r2, r4 = self.r1, self.r2, self.r4
        x1, x1b = self.x1, self.x1b
        r1.recarve(); r4.recarve()
        nT = r1.alloc([16, 1024], BF16)
        nTb = [r1.buf("nT%d" % i) for i in range(8)]
        gain, gb, hb, hbb = self.gain_ap, self.gain_b, self.hb2, self.hbb2
        S.dma("sp", lambda e: e.dma_start(out=gain, in_=d["ple_norm_w"].partition_broadcast(128)), S.new_dma_sem("gsem2"), writes=(gb,))
        pT = r4.alloc([2, 1024], BF16); pTb = r4.buf("pT")
        pt = [r4.alloc([PLE], F32) for _ in range(2)]; ptb = [r4.buf("pt%d" % i) for i in range(2)]
        pbf = [r4.alloc([PLE], BF16) for _ in range(2)]; pbfb = [r4.buf("pbf%d" % i) for i in range(2)]
        psem = [S.new_dma_sem("psem%d" % i) for i in range(2)]
        tgp = r4.alloc([256], F32); tgpb = r4.buf("tgp")
        up = r4.alloc([256], F32); upb = r4.buf("up")
        for mc in range(1, NMC):
            s = mc % 2
            o = mc - 1
            self.rms_tile(x1[:, mc, :], [x1b[mc]], gain, gb, hb[s], hbb[s], mc, hb[s])
            for half in range(2):
                tiles = [hb[s][:, (half * 8 + i) * 128:(half * 8 + i + 1) * 128] for i in range(8)]
                self.transpose_to(tiles, [hbb[s]],
                                  lambda n, half=half, o=o: nT[:, half * 8:half * 8 + 8, o * 128:(o + 1) * 128], [nTb[o]])
            S.dma("sp", lambda e, s=s, o=o: e.dma_start(out=pt[s], in_=d["pp"][o * 128:(o + 1) * 128, :]), psem[s], writes=(ptb[s],))
            S.op("act", lambda e, s=s: e.activation(out=pbf[s], in_=pt[s], func=AF.Copy), reads=(ptb[s],), writes=(pbfb[s],))
            tiles = [pbf[s][:, i * 128:(i + 1) * 128] for i in range(2)]
            self.transpose_to(tiles, [pbfb[s]], lambda n, o=o: pT[:, 0:2, o * 128:(o + 1) * 128], [pTb])
        for ct in range(8):
            wv, wb = self.wload([d["w_ple_gate"][:, ct * 256:(ct + 1) * 256]], 16)
            wv2, wb2 = self.wload([d["w_ple_proj"][:, ct * 256:(ct + 1) * 256]], 2)
            for mc in range(1, NMC):
                o = mc - 1
                pb, ps = self.bank()
                fns = []
                for kc in range(16):
                    fns.append(lambda e, kc=kc, ps=ps, o=o: e.matmul(ps[:, 0:256], lhsT=nT[:, kc, o * 128:(o + 1) * 128], rhs=wv[:, kc, 0:256],
                                                                     start=(kc == 0), stop=(kc == 15)))
                for kc in range(2):
                    fns.append(lambda e, kc=kc, ps=ps, o=o: e.matmul(ps[:, 256:512], lhsT=pT[:, kc, o * 128:(o + 1) * 128], rhs=wv2[:, kc, 0:256],
                                                                     start=(kc == 0), stop=(kc == 1)))
                S.op("pe", fns, reads=(wb, wb2, nTb[o], pTb), writes=(pb,))
                S.op("act", lambda e, ps=ps: e.activation(out=tgp, in_=ps[:, 0:256], func=AF.Tanh, scale=0.5), reads=(pb,), writes=(tgpb,))
                S.op("dve", lambda e, ps=ps: e.scalar_tensor_tensor(out=up, in0=tgp, scalar=1.0, in1=ps[:, 256:512], op0=ALU.add, op1=ALU.mult),
                     reads=(tgpb, pb), writes=(upb,))
                S.op("dve", lambda e, mc=mc, ct=ct: e.scalar_tensor_tensor(
                    out=x1[:, mc, ct * 256:(ct + 1) * 256], in0=up, scalar=0.5, in1=x1[:, mc, ct * 256:(ct + 1) * 256],
                    op0=ALU.mult, op1=ALU.add), reads=(upb, x1b[mc]), writes=(x1b[mc],))
        osem = S.new_dma_sem("osem")
        for mc in range(1, NMC):
            ob = Buf("out%d" % mc)
            S.dma("sp", lambda e, mc=mc: e.dma_start(out=self.out[(mc - 1) * 128:mc * 128, :], in_=x1[:, mc, :]), osem,
                  reads=(x1b[mc],), writes=(ob,))
            self.final_bufs.append(ob)


def _t5_bucket(dist):
    nb, md = 32, 128
    me = nb // 2
    dd = np.maximum(dist, 0)
    lr = np.log(np.maximum(dd, 1).astype(np.float32) / me) / np.log(md / me)
    large = me + (lr * (nb - me)).astype(np.int32)
    large = np.minimum(large, nb - 1)
    return np.where(dd < me, dd, large)


def _const_mats():
    ident = np.eye(128, dtype=np.float32)
    tri = (np.arange(128)[:, None] <= np.arange(128)[None, :]).astype(np.float32)
    ones = np.ones((128, 128), np.float32)
    neg = np.where(np.arange(128)[:, None] > np.arange(128)[None, :], -32768.0, 0.0).astype(np.float32)
    neg4 = np.tile(neg, (1, 4))
    sel = np.zeros((128, 8, 128), np.float32)
    for j in range(8):
        sel[j, j, :] = 1.0
    return (np.ascontiguousarray(np.concatenate([tri, ones], axis=1)),
            np.ascontiguousarray(np.concatenate([ident, neg4, sel.reshape(128, 1024)], axis=1)))


def _bias_tables(table):
    L = 128
    qi = np.arange(L)[:, None]
    kj = np.arange(2 * L)[None, :]
    dist = qi + L - kj
    band = (dist >= 0) & (dist < 128)
    bk = _t5_bucket(dist)
    b = table[bk]
    b = np.where(band[:, :, None], b, np.float32(NEGM)).astype(np.float32)
    b = b.reshape(L, 2, L, NKV, 8)
    b = np.transpose(b, (3, 2, 4, 1, 0))
    return np.ascontiguousarray(b).reshape(NKV, 128, 8 * 2 * 128)


def make_in_maps(inputs):
    x = np.asarray(inputs["x"], np.float32)
    p = np.asarray(inputs["p"], np.float32)[0]
    g = lambda k: np.ascontiguousarray(np.asarray(inputs[k], np.float32)[0])
    shared = {
        "w_in": g("w_in"), "w_attn_out": g("w_attn_out"), "w_ssm_out": g("w_ssm_out"), "w_out": g("w_out"),
        "w_ffn_up": g("w_ffn_up"), "w_ffn_down": g("w_ffn_down"), "w_ple_gate": g("w_ple_gate"),
        "w_ple_proj": g("w_ple_proj"),
        "norm_mix_w": g("norm_mix_w")[None], "norm_ffn_w": g("norm_ffn_w")[None], "ple_norm_w": g("ple_norm_w")[None],
        "ssm_norm_w": g("ssm_norm_w")[None], "q_norm_w": g("q_norm_w")[None], "k_norm_w": g("k_norm_w")[None],
        "attn_sinks": g("attn_sinks")[None], "ssm_A_log": g("ssm_A_log")[None], "ssm_dt_bias": g("ssm_dt_bias")[None],
        "ssm_D": g("ssm_D")[None],
        "cw_ssm": np.ascontiguousarray(g("ssm_conv_w").T.reshape(48, 128, 4).transpose(1, 0, 2)).reshape(128, 192),
        "cb_ssm": np.ascontiguousarray(g("ssm_conv_b").reshape(48, 128).T),
        "cw_ffn": np.ascontiguousarray(g("ffn_conv_w").T.reshape(88, 128, 3).transpose(1, 0, 2)).reshape(128, 264),
        "cb_ffn": np.ascontiguousarray(g("ffn_conv_b").reshape(88, 128).T),
        "biasT": _bias_tables(np.asarray(inputs["rel_bias_table"], np.float32)),
    }
    shared["cm_f"], shared["cm_b"] = _const_mats()
    in_maps = []
    for core in range(8):
        b, hf = core // 2, core % 2
        s0 = hf * 1024
        xm = np.zeros((NT, D), np.float32)
        xp = np.zeros((896, D), np.float32)
        if hf == 1:
            xm[:] = x[b, s0 - 128:s0 + 1024]
            xp[:] = x[b, 0:896]
        else:
            xm[128:] = x[b, 0:1024]
        m = dict(shared)
        m["xm"] = xm
        m["xp"] = xp
        m["pp"] = np.ascontiguousarray(p[b, s0:s0 + 1024])
        m["flag"] = np.full((128, 1), float(hf), np.float32)
        in_maps.append(m)
    return in_maps


def kernel(**inputs):
    in_maps = make_in_maps(inputs)
    dbg = tuple(inputs.get("_debug", ())) if isinstance(inputs.get("_debug", ()), (list, tuple)) else ()
    bld = Builder(debug=dbg)
    nc = bld.build()
    cores = list(range(8))
    if inputs.get("_cores"):
        cores = list(inputs["_cores"])
    res = run_bass_kernel_spmd(nc, [in_maps[c] for c in cores], core_ids=list(range(len(cores))))
    out = np.zeros((BATCH, SEQ, D), np.float32)
    for i, core in enumerate(cores):
        b, hf = core // 2, core % 2
        out[b, hf * 1024:(hf + 1) * 1024] = res.results[i]["out"]
    if dbg:
        kernel.last_debug = [{k: r[v] for k, v in bld.dbg_out.items()} for r in res.results]
    return out
```

```python
import numpy as np
import concourse.bass as bass
import concourse.mybir as mybir
from concourse.bass_utils import run_bass_kernel_spmd

F32 = mybir.dt.float32
BF16 = mybir.dt.bfloat16
AF = mybir.ActivationFunctionType
ALU = mybir.AluOpType
AX = mybir.AxisListType

D = 2048
SEQ = 2048
BATCH = 4
NH = 32
NKV = 4
DH = 64
DI = 4096
NSH = 64
NG = 8
DS = 128
DFF = 5632
PLE = 256
EPS = 1e-6
Q0 = 0
K0 = 2048
V0 = 2304
Z0 = 2560
XBC0 = 6656
DT0 = 12800
GA0 = 12864
GS0 = 14912
IN_DIM = 16960

NMC = 9
NT = NMC * 128
TP = 768
TM = 1280
NEGM = -30000.0


class Buf:
    __slots__ = ("name", "w", "r")

    def __init__(self, name, base=None):
        self.name = name
        self.w = None
        self.r = dict(base) if base else {}


class Sched:
    ENG = ("pe", "act", "dve", "pool", "sp")

    def __init__(self):
        self.streams = {e: [] for e in self.ENG}
        self.cnt = {e: 0 for e in self.ENG}
        self.dcnt = {}
        self.waited = {e: {} for e in self.ENG}
        self.dma_sems = []

    def new_dma_sem(self, name):
        self.dma_sems.append(name)
        self.dcnt[name] = 0
        return name

    def _waits(self, eng, reads, writes):
        deps = {}

        def add(s, v):
            if deps.get(s, 0) < v:
                deps[s] = v

        for b in reads:
            if b.w is not None:
                add(*b.w)
        for b in writes:
            if b.w is not None:
                add(*b.w)
            for s, v in b.r.items():
                add(s, v)
        wd = self.waited[eng]
        st = self.streams[eng]
        for s, v in deps.items():
            if wd.get(s, 0) >= v:
                continue
            wd[s] = v
            st.append(("wait", s, v))

    def op(self, eng, fns, reads=(), writes=()):
        self._waits(eng, reads, writes)
        self.cnt[eng] += 1
        c = self.cnt[eng]
        if not isinstance(fns, (list, tuple)):
            fns = [fns]
        st = self.streams[eng]
        for f in fns[:-1]:
            st.append(("inst", f, None, 0))
        st.append(("inst", fns[-1], eng, 1))
        for b in reads:
            b.r[eng] = c
        for b in writes:
            b.w = (eng, c)
            b.r = {}

    def dma(self, eng, fn, sem, reads=(), writes=()):
        self._waits(eng, reads, writes)
        self.dcnt[sem] += 16
        c = self.dcnt[sem]
        self.streams[eng].append(("inst", fn, sem, 16))
        for b in reads:
            b.r[sem] = c
        for b in writes:
            b.w = (sem, c)
            b.r = {}

    def final_wait(self, eng, bufs):
        self._waits(eng, bufs, bufs)


def collect_tokens(bufs):
    r = {}
    for b in bufs:
        if b.w is not None and r.get(b.w[0], 0) < b.w[1]:
            r[b.w[0]] = b.w[1]
        for s, v in b.r.items():
            if r.get(s, 0) < v:
                r[s] = v
    return r


class Region:
    def __init__(self, name, handle, nwords):
        self.name = name
        self.h = handle
        self.nwords = nwords
        self.off = 0
        self.bufs = []
        self.base = {}

    def recarve(self):
        self.base = collect_tokens(self.bufs)
        self.bufs = []
        self.off = 0

    def buf(self, name):
        b = Buf(name, self.base)
        self.bufs.append(b)
        return b

    def alloc(self, shape, dtype):
        nel = 1
        for s in shape:
            nel *= s
        nbytes = nel * (2 if dtype == BF16 else 4)
        nw = (nbytes + 3) // 4
        assert self.off + nw <= self.nwords, (self.name, self.off, nw, self.nwords)
        ap = self.h[:, self.off:self.off + nw]
        self.off += nw
        if dtype == BF16:
            ap = ap.bitcast(BF16)
            if nel != nw * 2:
                ap = ap[:, 0:nel]
        if len(shape) == 2:
            return ap.rearrange("p (a b) -> p a b", b=shape[1])
        if len(shape) == 3:
            return ap.rearrange("p (a b c) -> p a b c", b=shape[1], c=shape[2])
        return ap


class Builder:
    def __init__(self, debug=(), nphases=99):
        self.nphases = nphases
        self.debug = set(debug)
        self.nc = bass.Bass("TRN2", target_bir_lowering=False)
        self.S = Sched()
        self.dbg_out = {}

    def dram_in(self, name, shape):
        return self.nc.dram_tensor(name, list(shape), F32, kind="ExternalInput").ap()

    def bank(self):
        i = self.bank_i
        self.bank_i = (i + 1) % 8
        return self.pbuf[i], self.ps[i]

    def wslot(self):
        i = self.w_i
        self.w_i = (i + 1) % len(self.wt)
        return i

    def wload(self, srcs, kcn):
        i = self.wslot()
        tot = sum(s.shape[1] for s in srcs)
        assert kcn * tot <= 4096
        view = self.wt[i][:, 0:kcn * tot].rearrange("p (k c) -> p k c", c=tot)
        c0 = 0
        for s in srcs:
            n = s.shape[1]
            src = s.rearrange("(k p) e -> p k e", p=128)
            dst = view[:, :, c0:c0 + n]
            self.S.dma("pool", lambda e, d=dst, s_=src: e.dma_start(out=d, in_=s_), self.wsem[i],
                       reads=(), writes=(self.wbuf[i],))
            c0 += n
        return view, self.wbuf[i]

    def dump(self, name, ap, bufs, shape, dtype=F32):
        if name not in self.debug:
            return
        t = self.nc.dram_tensor("dbg_" + name, list(shape), dtype, kind="ExternalOutput").ap()
        sem = self.S.new_dma_sem("dbgsem_" + name)
        db = Buf("dbg_" + name)
        self.S.dma("sp", lambda e, t=t, ap=ap: e.dma_start(out=t, in_=ap), sem, reads=bufs, writes=(db,))
        self.final_bufs.append(db)
        self.dbg_out[name] = "dbg_" + name

    def build(self):
        nc = self.nc
        S = self.S
        d = {}
        d["xm"] = self.dram_in("xm", [NT, D])
        d["xp"] = self.dram_in("xp", [896, D])
        d["pp"] = self.dram_in("pp", [1024, PLE])
        d["flag"] = self.dram_in("flag", [128, 1])
        d["w_in"] = self.dram_in("w_in", [D, IN_DIM])
        d["w_attn_out"] = self.dram_in("w_attn_out", [D, D])
        d["w_ssm_out"] = self.dram_in("w_ssm_out", [DI, D])
        d["w_out"] = self.dram_in("w_out", [D, D])
        d["w_ffn_up"] = self.dram_in("w_ffn_up", [D, 2 * DFF])
        d["w_ffn_down"] = self.dram_in("w_ffn_down", [DFF, D])
        d["w_ple_gate"] = self.dram_in("w_ple_gate", [D, D])
        d["w_ple_proj"] = self.dram_in("w_ple_proj", [PLE, D])
        d["norm_mix_w"] = self.dram_in("norm_mix_w", [1, D])
        d["norm_ffn_w"] = self.dram_in("norm_ffn_w", [1, D])
        d["ple_norm_w"] = self.dram_in("ple_norm_w", [1, D])
        d["ssm_norm_w"] = self.dram_in("ssm_norm_w", [1, DI])
        d["q_norm_w"] = self.dram_in("q_norm_w", [1, DH])
        d["k_norm_w"] = self.dram_in("k_norm_w", [1, DH])
        d["attn_sinks"] = self.dram_in("attn_sinks", [1, NH])
        d["ssm_A_log"] = self.dram_in("ssm_A_log", [1, NSH])
        d["ssm_dt_bias"] = self.dram_in("ssm_dt_bias", [1, NSH])
        d["ssm_D"] = self.dram_in("ssm_D", [1, NSH])
        d["cw_ssm"] = self.dram_in("cw_ssm", [128, 48 * 4])
        d["cb_ssm"] = self.dram_in("cb_ssm", [128, 48])
        d["cw_ffn"] = self.dram_in("cw_ffn", [128, 88 * 3])
        d["cb_ffn"] = self.dram_in("cb_ffn", [128, 88])
        d["biasT"] = self.dram_in("biasT", [NKV, 128, 8 * 2 * 128])
        d["cm_f"] = self.dram_in("cm_f", [128, 256])
        d["cm_b"] = self.dram_in("cm_b", [128, 128 + 512 + 1024])
        self.d = d
        self.out = nc.dram_tensor("out", [1024, D], F32, kind="ExternalOutput").ap()
        self.final_bufs = []

        R1W, R2W, R3W, R4W, CW = 10240, 6144, 18432, 9216, 4200
        import contextlib
        with contextlib.ExitStack() as es:
            def sb(name, shape, dt):
                return es.enter_context(nc.sbuf_tensor(name, shape, dt))
            r1 = Region("R1", sb("R1", [128, R1W], F32), R1W)
            r2 = Region("R2", sb("R2", [128, R2W], F32), R2W)
            r3 = Region("R3", sb("R3", [128, R3W], F32), R3W)
            r4 = Region("R4", sb("R4", [128, R4W], F32), R4W)
            rc = Region("RC", sb("RC", [128, CW], F32), CW)
            self.r1, self.r2, self.r3, self.r4, self.rc = r1, r2, r3, r4, rc
            self.wt = [sb("wt%d" % i, [128, 4096], BF16) for i in range(2)]
            self.wbuf = [Buf("wbuf%d" % i) for i in range(2)]
            self.wsem = [S.new_dma_sem("wsem%d" % i) for i in range(2)]
            self.w_i = 0
            self.ps = [es.enter_context(nc.psum_tensor("ps%d" % i, [128, 512], F32)) for i in range(8)]
            self.pbuf = [Buf("psum%d" % i) for i in range(8)]
            self.bank_i = 0

            phases = [self.setup_consts, self.phase0, self.dt_proj, self.ssm_prefix, self.ssm_main, self.ssm_out,
                      self.attention, self.attn_out, self.wout_residual, self.ffn, self.ple]
            for ph in phases[:self.nphases]:
                ph()

            S.final_wait("sp", self.final_bufs)

            sem_names = list(Sched.ENG) + S.dma_sems
            sems = {}
            for n in sem_names:
                sems[n] = es.enter_context(nc.semaphore(n))
            block = es.enter_context(nc.Block())

            def replay(engname):
                def run(e):
                    for it in S.streams[engname]:
                        if it[0] == "wait":
                            e.wait_ge(sems[it[1]], it[2])
                        else:
                            ins = it[1](e)
                            if it[2] is not None:
                                ins.then_inc(sems[it[2]], it[3])
                return run

            block.sync(replay("sp"))
            block.gpsimd(replay("pool"))
            block.scalar(replay("act"))
            block.vector(replay("dve"))
            block.tensor(replay("pe"))
        return nc

    def setup_consts(self):
        S, d, rc = self.S, self.d, self.rc
        csem = S.new_dma_sem("csem")
        self.cb = Buf("consts")
        cb = self.cb

        def cload(dst, src):
            S.dma("sp", lambda e, dst=dst, src=src: e.dma_start(out=dst, in_=src), csem)

        cmf = rc.alloc([256], F32)
        cload(cmf, d["cm_f"])
        self.tri_f = cmf[:, 0:128]
        self.ones_f = cmf[:, 128:256]
        self.r3.recarve()
        cm = self.r3.alloc([128 + 512 + 1024], F32)
        cmb = self.r3.buf("cm_stage")
        S.dma("sp", lambda e: e.dma_start(out=cm, in_=d["cm_b"]), S.new_dma_sem("cmsem"), writes=(cmb,))
        self.ident_bf = rc.alloc([128], BF16)
        self.neg4 = rc.alloc([512], BF16)
        self.sel = rc.alloc([1024], BF16)
        self.nsel = rc.alloc([1024], BF16)
        self._cm = cm
        self.Ab = rc.alloc([64], F32)
        self.Db = rc.alloc([64], F32)
        self.dtb = rc.alloc([64], F32)
        self.flag = rc.alloc([1], F32)
        self.cw_ssm = rc.alloc([48, 4], F32)
        self.cb_ssm = rc.alloc([48], F32)
        self.cw_ffn = rc.alloc([88, 3], F32)
        self.cb_ffn = rc.alloc([88], F32)
        self.qg = rc.alloc([64], F32)
        self.kg = rc.alloc([64], F32)
        self.esink = rc.alloc([32], F32)
        self.dt_all = rc.alloc([16, 64], F32)
        self.ss16 = rc.alloc([16], F32)
        self.sd16 = rc.alloc([16], F32)
        self.rs16 = rc.alloc([16], F32)
        cload(self.Ab, d["ssm_A_log"].partition_broadcast(128))
        cload(self.Db, d["ssm_D"].partition_broadcast(128))
        cload(self.dtb, d["ssm_dt_bias"].partition_broadcast(128))
        cload(self.flag, d["flag"])
        cload(self.cw_ssm, d["cw_ssm"].rearrange("p (b k) -> p b k", k=4))
        cload(self.cb_ssm, d["cb_ssm"])
        cload(self.cw_ffn, d["cw_ffn"].rearrange("p (b k) -> p b k", k=3))
        cload(self.cb_ffn, d["cb_ffn"])
        cload(self.qg, d["q_norm_w"].partition_broadcast(128))
        cload(self.kg, d["k_norm_w"].partition_broadcast(128))
        cload(self.esink, d["attn_sinks"].partition_broadcast(128))
        cb.w = (csem, S.dcnt[csem])
        S.op("act", [
            lambda e: e.activation(out=self.ident_bf, in_=cm[:, 0:128], func=AF.Copy),
            lambda e: e.activation(out=self.neg4, in_=cm[:, 128:640], func=AF.Copy),
            lambda e: e.activation(out=self.sel, in_=cm[:, 640:1664], func=AF.Copy),
            lambda e: e.activation(out=self.nsel, in_=cm[:, 640:1664], func=AF.Copy, scale=-1.0),
            lambda e: e.activation(out=self.esink, in_=self.esink, func=AF.Exp),
            lambda e: e.activation(out=self.Ab, in_=self.Ab, func=AF.Exp),
        ], reads=(cb, cmb), writes=(cb,))
        S.op("dve", [
            lambda e: e.tensor_scalar(out=self.Ab, in0=self.Ab, scalar1=-1.0, scalar2=None, op0=ALU.mult),
            lambda e: e.tensor_scalar(out=self.qg, in0=self.qg, scalar1=0.125, scalar2=None, op0=ALU.mult),
            lambda e: e.tensor_scalar(out=self.cw_ssm, in0=self.cw_ssm, scalar1=0.5, scalar2=None, op0=ALU.mult),
            lambda e: e.tensor_scalar(out=self.cb_ssm, in0=self.cb_ssm, scalar1=0.5, scalar2=None, op0=ALU.mult),
        ], reads=(cb,), writes=(cb,))

    def hT(self, kc, t0, n):
        if t0 < TP:
            assert t0 + n <= TP
            return self.hTp[:, kc, t0:t0 + n]
        return self.hTm[:, kc, t0 - TP:t0 - TP + n]

    def hT_bufs(self, t0, n):
        c0 = t0 // 128
        c1 = (t0 + n - 1) // 128
        return [self.hTb[c] for c in range(c0, c1 + 1)]

    def rms_tile(self, x_ap, xbufs, gain_bc, gbuf, hb, hbbuf, sidx, scratch_bf, eps=EPS, n=D):
        S = self.S
        ssb = self.small_b[sidx]
        ss, sd, rs = self.ss16[:, sidx:sidx + 1], self.sd16[:, sidx:sidx + 1], self.rs16[:, sidx:sidx + 1]
        S.op("act", lambda e: e.activation(out=scratch_bf, in_=x_ap, func=AF.Square, accum_out=ss),
             reads=list(xbufs), writes=(ssb, hbbuf))
        S.op("act", lambda e: e.activation(out=sd, in_=ss, func=AF.Ln, bias=eps, scale=1.0 / n),
             reads=(ssb,), writes=(ssb,))
        S.op("act", lambda e: e.activation(out=rs, in_=sd, func=AF.Exp, scale=-0.5), reads=(ssb,), writes=(ssb,))
        S.op("dve", lambda e: e.scalar_tensor_tensor(out=hb, in0=x_ap, scalar=rs, in1=gain_bc,
                                                      op0=ALU.mult, op1=ALU.mult),
             reads=list(xbufs) + [ssb, gbuf], writes=(hbbuf,))

    def transpose_to(self, src_tiles, src_bufs, dst_ap_fn, dst_bufs, evac="act"):
        S = self.S
        n = len(src_tiles)
        pb, ps = self.bank()
        psb = ps[:].bitcast(BF16).rearrange("p (a b) -> p a b", b=128)
        fns = []
        for i, t in enumerate(src_tiles):
            fns.append(lambda e, i=i, t=t: e.transpose(out=psb[:, i, :], in_=t, identity=self.ident_bf))
        S.op("pe", fns, reads=list(src_bufs) + [self.cb], writes=(pb,))
        dst = dst_ap_fn(n)
        if evac == "act":
            S.op("act", lambda e: e.activation(out=dst, in_=psb[:, 0:n, :], func=AF.Copy),
                 reads=(pb,), writes=list(dst_bufs))
        else:
            S.op("dve", lambda e: e.tensor_copy(out=dst, in_=psb[:, 0:n, :]), reads=(pb,), writes=list(dst_bufs))

    def proj_fm(self, wview, wb, e0, kcn, act_fn, act_bufs, ranges):
        S = self.S
        banks = [self.bank() for _ in ranges]
        fns = []
        for kc in range(kcn):
            for (pb, ps), (t0, n) in zip(banks, ranges):
                fns.append(lambda e, kc=kc, ps=ps, t0=t0, n=n: e.matmul(
                    ps[:, 0:n], lhsT=wview[:, kc, e0:e0 + 128], rhs=act_fn(kc, t0, n),
                    start=(kc == 0), stop=(kc == kcn - 1)))
        S.op("pe", fns, reads=[wb] + list(act_bufs), writes=[pb for pb, _ in banks])
        return [(pb, ps[:, 0:n]) for (pb, ps), (t0, n) in zip(banks, ranges)]

    def proj_tm(self, wview, wb, c0, ncols, kcn, lhs_fn, act_bufs):
        S = self.S
        pb, ps = self.bank()
        fns = []
        for kc in range(kcn):
            fns.append(lambda e, kc=kc: e.matmul(ps[:, 0:ncols], lhsT=lhs_fn(kc), rhs=wview[:, kc, c0:c0 + ncols],
                                                 start=(kc == 0), stop=(kc == kcn - 1)))
        S.op("pe", fns, reads=[wb] + list(act_bufs), writes=(pb,))
        return pb, ps[:, 0:ncols]

    def phase0(self):
        S, d = self.S, self.d
        r1, r2, r4 = self.r1, self.r2, self.r4
        self.hTm = r1.alloc([16, TM], BF16)
        self.hTp = r2.alloc([16, TP], BF16)
        self.hTb = [Buf("hT%d" % c) for c in range(16)]
        r1.bufs += self.hTb[6:]
        r2.bufs += self.hTb[:6]
        self.small_b = [Buf("small%d" % i) for i in range(16)]
        r4.recarve()
        xt = [r4.alloc([D], F32) for _ in range(2)]
        xb = [r4.buf("xt%d" % i) for i in range(2)]
        xs = [S.new_dma_sem("xsem%d" % i) for i in range(2)]
        self.xt_sems = xs
        hb = [r4.alloc([D], BF16) for _ in range(2)]
        hbb = [r4.buf("hb%d" % i) for i in range(2)]
        gain = r4.alloc([D], F32)
        gb = r4.buf("gain")
        S.dma("sp", lambda e: e.dma_start(out=gain, in_=d["norm_mix_w"].partition_broadcast(128)), S.new_dma_sem("gsem0"), writes=(gb,))
        for ci in range(16):
            s = ci % 2
            src = d["xp"][ci * 128:(ci + 1) * 128, :] if ci < 7 else d["xm"][(ci - 7) * 128:(ci - 6) * 128, :]
            S.dma("sp", lambda e, s=s, src=src: e.dma_start(out=xt[s], in_=src), xs[s], writes=(xb[s],))
            self.rms_tile(xt[s], [xb[s]], gain, gb, hb[s], hbb[s], ci, hb[s])
            t0 = ci * 128
            for half in range(2):
                tiles = [hb[s][:, (half * 8 + i) * 128:(half * 8 + i + 1) * 128] for i in range(8)]
                self.transpose_to(tiles, [hbb[s]],
                                  lambda n, half=half, t0=t0: (self.hTp[:, half * 8:half * 8 + 8, t0:t0 + 128] if t0 < TP
                                                               else self.hTm[:, half * 8:half * 8 + 8, t0 - TP:t0 - TP + 128]),
                                  [self.hTb[ci]])
        self.dump("hTm", self.hTm, self.hTb[6:], [128, 16, TM], BF16)

    def dt_proj(self):
        S, d = self.S, self.d
        r4 = self.r4
        r4.recarve()
        raw = r4.alloc([16, 64], F32)
        rawb = r4.buf("dtraw")
        wv, wb = self.wload([d["w_in"][:, DT0:DT0 + 64]], 16)
        for ci in range(16):
            t0 = ci * 128
            pb, ps = self.proj_tm(wv, wb, 0, 64, 16, lambda kc, t0=t0: self.hT(kc, t0, 128), [self.hTb[ci]])
            S.op("dve", lambda e, ci=ci, ps=ps: e.tensor_tensor(out=raw[:, ci, :], in0=ps, in1=self.dtb, op=ALU.add),
                 reads=(pb, self.cb), writes=(rawb,))
        dtb_ = Buf("dt_all")
        self.dt_buf = dtb_
        S.op("act", lambda e: e.activation(out=raw, in_=raw, func=AF.Exp), reads=(rawb,), writes=(rawb,))
        S.op("act", lambda e: e.activation(out=self.dt_all, in_=raw, func=AF.Ln, bias=1.0, scale=1.0),
             reads=(rawb,), writes=(dtb_,))
        self.dump("dt_all", self.dt_all, [dtb_], [128, 16, 64])

    def conv_silu(self, pre, preb, n_out, blk, acc, accb, tt, ttb, out_bf, outb, lo=0):
        S = self.S
        w = self.cw_ssm
        S.op("dve", [
            lambda e: e.tensor_scalar(out=acc[:, 0:n_out], in0=pre[:, lo + 3:lo + 3 + n_out], scalar1=w[:, blk, 3:4],
                                      scalar2=self.cb_ssm[:, blk:blk + 1], op0=ALU.mult, op1=ALU.add)],
             reads=(preb, self.cb), writes=(accb,))
        for k in (2, 1, 0):
            S.op("dve", lambda e, k=k: e.scalar_tensor_tensor(out=acc[:, 0:n_out], in0=pre[:, lo + k:lo + k + n_out],
                                                             scalar=w[:, blk, k:k + 1], in1=acc[:, 0:n_out],
                                                             op0=ALU.mult, op1=ALU.add),
                 reads=(preb, self.cb, accb), writes=(accb,))
        S.op("act", lambda e: e.activation(out=tt[:, 0:n_out], in_=acc[:, 0:n_out], func=AF.Tanh),
             reads=(accb,), writes=(ttb,))
        S.op("dve", lambda e: e.scalar_tensor_tensor(out=out_bf, in0=tt[:, 0:n_out], scalar=1.0, in1=acc[:, 0:n_out],
                                                      op0=ALU.add, op1=ALU.mult),
             reads=(ttb, accb), writes=(outb,))

    def batch_smalls(self, reg, n, c0, g, suffix=False):
        S = self.S
        n8 = n * 8
        sm = {}
        for nm in ("a", "dwl", "w", "dtw"):
            sm[nm] = reg.alloc([n, 8], F32)
        s2 = reg.alloc([2, n8], F32)
        e2 = reg.alloc([2, n8], F32)
        smb = reg.buf("smalls")
        dt_g = self.dt_all[:, c0:c0 + n, g * 8:(g + 1) * 8]
        sm["dt"] = dt_g
        sm["acum"] = s2[:, 0, :].rearrange("p (c j) -> p c j", j=8)
        sm["atot"] = s2[:, 1, :].rearrange("p (c j) -> p c j", j=8)
        sm["eacum"] = e2[:, 0, :].rearrange("p (c j) -> p c j", j=8)
        sm["eatot"] = e2[:, 1, :].rearrange("p (c j) -> p c j", j=8)
        S.op("dve", lambda e: e.tensor_tensor(out=sm["a"], in0=dt_g,
                                              in1=self.Ab[:, g * 8:(g + 1) * 8].unsqueeze(1).to_broadcast([128, n, 8]), op=ALU.mult),
             reads=(self.dt_buf, self.cb), writes=(smb,))
        pb, ps = self.bank()
        a2 = sm["a"].rearrange("p c j -> p (c j)")
        S.op("pe", [
            lambda e: e.matmul(ps[:, 0:n8], lhsT=self.tri_f, rhs=a2, start=True, stop=True),
            lambda e: e.matmul(ps[:, 128:128 + n8], lhsT=self.ones_f, rhs=a2, start=True, stop=True),
        ], reads=(smb, self.cb), writes=(pb,))
        psv = ps[:, 0:256].rearrange("p (a b) -> p a b", b=128)[:, :, 0:n8]
        S.op("act", [
            lambda e: e.activation(out=s2, in_=psv, func=AF.Copy),
            lambda e: e.activation(out=e2, in_=psv, func=AF.Exp),
        ], reads=(pb,), writes=(smb,))
        S.op("dve", lambda e: e.tensor_tensor(out=sm["dwl"], in0=sm["atot"], in1=sm["acum"], op=ALU.subtract),
             reads=(smb,), writes=(smb,))
        if suffix:
            suf = reg.alloc([n, 8], F32)
            S.op("dve", lambda e: e.memset(suf[:, n - 1, :], 0.0), reads=(smb,), writes=(smb,))
            for c in range(n - 2, -1, -1):
                S.op("dve", lambda e, c=c: e.tensor_tensor(out=suf[:, c, :], in0=suf[:, c + 1, :], in1=sm["atot"][:, c + 1, :], op=ALU.add),
                     reads=(smb,), writes=(smb,))
            S.op("dve", lambda e: e.tensor_tensor(out=sm["dwl"], in0=sm["dwl"], in1=suf, op=ALU.add), reads=(smb,), writes=(smb,))
        if g == 0 and not suffix:
            self.dump("s2", s2, [smb], [128, 2, n8])
            self.dump("sma", sm["a"], [smb], [128, n, 8])
        S.op("act", lambda e: e.activation(out=sm["w"], in_=sm["dwl"], func=AF.Exp), reads=(smb,), writes=(smb,))
        S.op("dve", lambda e: e.tensor_tensor(out=sm["dtw"], in0=sm["w"], in1=dt_g, op=ALU.mult),
             reads=(smb, self.dt_buf), writes=(smb,))
        return sm, smb

    def ssm_prefix(self):
        S, d = self.S, self.d
        r3, r4 = self.r3, self.r4
        r3.recarve()
        self.stash = r3.h[:, :].rearrange("p (g x) -> p g x", g=8)
        self.yTb = [r3.buf("yT%d" % g) for g in range(8)]
        for g in range(NG):
            self._ssm_prefix_group(g)
        self.dump("stash", self.stash[:, :, 0:512], self.yTb, [128, 8, 512])

    def _ssm_prefix_group(self, g):
        S, d = self.S, self.d
        r3, r4 = self.r3, self.r4
        NP = 896
        ranges = [(0, 384), (384, 384), (768, 128)]
        abufs = self.hTb[0:7]
        if True:
            r4.recarve()
            pre2 = [r4.alloc([3 + NP], F32) for _ in range(2)]; pre2b = [r4.buf("pre%d" % i) for i in range(2)]
            acc = r4.alloc([NP], F32); accb = r4.buf("acc")
            tt = r4.alloc([NP], F32); ttb = r4.buf("tt")
            cvo = [r4.alloc([NP], BF16) for _ in range(2)]; cvob = [r4.buf("cvo%d" % i) for i in range(2)]
            xs_tm = r4.alloc([7, 512], BF16); xsb = r4.buf("xs_tm")
            B_tm = r4.alloc([7, 128], BF16); Btmb = r4.buf("B_tm")
            xw = r4.alloc([7, 512], BF16); xwb = r4.buf("xw")
            for i in range(2):
                S.op("dve", lambda e, i=i: e.memset(pre2[i][:, 0:3], 0.0), writes=(pre2b[i],))
            sm, smb = self.batch_smalls(r4, 7, 0, g, suffix=True)
            cols = [XBC0 + g * 512 + j * 128 for j in range(4)] + [XBC0 + DI + g * 128]
            pending = []
            wv = None
            for bi, c0 in enumerate(cols):
                blk = (c0 - XBC0) // 128
                if bi in (0, 2):
                    wv, wb = self.wload([d["w_in"][:, c0:c0 + 256]], 16)
                    e0 = 0
                elif bi == 4:
                    wv, wb = self.wload([d["w_in"][:, c0:c0 + 128]], 16)
                    e0 = 0
                else:
                    e0 = 128
                outs = self.proj_fm(wv, wb, e0, 16, self.hT, abufs, ranges)
                pre, preb = pre2[bi % 2], pre2b[bi % 2]
                for (pb, ps), (t0, n) in zip(outs, ranges):
                    S.op("act", lambda e, ps=ps, t0=t0, n=n, pre=pre: e.activation(out=pre[:, 3 + t0:3 + t0 + n], in_=ps, func=AF.Copy),
                         reads=(pb,), writes=(preb,))
                for f in pending:
                    f()
                pending = []
                cv, cvb = cvo[bi % 2], cvob[bi % 2]
                self.conv_silu(pre, preb, NP, blk, acc, accb, tt, ttb, cv, cvb)
                tiles = [cv[:, c * 128:(c + 1) * 128] for c in range(7)]
                if bi < 4:
                    pending.append(lambda tiles=tiles, cvb=cvb, bi=bi: self.transpose_to(
                        tiles, [cvb], lambda n: xs_tm[:, 0:7, bi * 128:(bi + 1) * 128], [xsb]))
                else:
                    pending.append(lambda tiles=tiles, cvb=cvb: self.transpose_to(tiles, [cvb], lambda n: B_tm[:, 0:7, :], [Btmb]))
            for f in pending:
                f()
            S.op("dve", lambda e: e.tensor_tensor(out=xw.rearrange("p c (j q) -> p c j q", q=64),
                                                  in0=xs_tm.rearrange("p c (j q) -> p c j q", q=64),
                                                  in1=sm["dtw"].unsqueeze(3).to_broadcast([128, 7, 8, 64]), op=ALU.mult),
                 reads=(xsb, smb), writes=(xwb,))
            pb, ps = self.bank()
            fns = [lambda e, c=c, ps=ps: e.matmul(ps[:, 0:512], lhsT=B_tm[:, c, :], rhs=xw[:, c, :], start=(c == 0), stop=(c == 6))
                   for c in range(7)]
            S.op("pe", fns, reads=(Btmb, xwb), writes=(pb,))
            S.op("act", lambda e, g=g, ps=ps: e.activation(out=self.stash[:, g, 0:512], in_=ps[:, 0:512], func=AF.Copy),
                 reads=(pb,), writes=(self.yTb[g],))

    def ssm_main(self):
        S, d = self.S, self.d
        r2, r3, r4 = self.r2, self.r3, self.r4
        yT = r3.h[:, :].bitcast(BF16).rearrange("p (k t) -> p k t", t=NT)
        self.yT = yT
        self.nwsem = S.new_dma_sem("nwsem")
        for g in range(NG):
            self._ssm_main_group(g)
        self.dump("yT", yT, self.yTb, [128, 32, NT], BF16)

    def _ssm_main_group(self, g):
        S, d = self.S, self.d
        r2, r3, r4 = self.r2, self.r3, self.r4
        yT = self.yT
        NPRE = NT + 3
        nsem = self.nwsem
        ranges = [(893 + i * 385, 385) for i in range(3)]
        abufs = self.hTb[6:16]
        if True:
            r2.recarve(); r4.recarve()
            pre2 = [r4.alloc([NPRE + 1], BF16) for _ in range(2)]; pre2b = [r4.buf("pre%d" % i) for i in range(2)]
            acc = r4.alloc([384], F32); accb = r4.buf("acc")
            tt = r4.alloc([384], F32); ttb = r4.buf("tt")
            cvo = [r4.alloc([384], BF16) for _ in range(2)]; cvob = [r4.buf("cvo%d" % i) for i in range(2)]
            BT = r4.alloc([NT], BF16); BTb = r4.buf("BT")
            CT = r4.alloc([NT], BF16); CTb = r4.buf("CT")
            xs_tm = r4.alloc([NMC, 512], BF16); xsb = r4.buf("xs_tm")
            B_tm = r4.alloc([NMC, 128], BF16); Btmb = r4.buf("B_tm")
            zs = [r4.alloc([512], BF16) for _ in range(2)]; zsb = [r4.buf("zs%d" % i) for i in range(2)]
            ztmp = [r4.alloc([256], F32) for _ in range(2)]; ztb = [r4.buf("ztmp%d" % i) for i in range(2)]
            hiT = r4.alloc([NMC, 128], BF16); loT = r4.alloc([NMC, 128], BF16); hib = r4.buf("hiloT")
            normw = r4.alloc([512], F32); nwb = r4.buf("normw")
            Scur = r2.alloc([512], F32); Sb = r2.buf("Scur")
            xdt = [r2.alloc([512], BF16) for _ in range(2)]; xdtb = [r2.buf("xdt%d" % i) for i in range(2)]
            xw = [r2.alloc([512], BF16) for _ in range(2)]; xwb = [r2.buf("xw%d" % i) for i in range(2)]
            Sbf2 = [r2.alloc([512], BF16) for _ in range(2)]; Sbfb2 = [r2.buf("Sbf%d" % i) for i in range(2)]
            _seg = r2.alloc([8, 128], BF16); _segb = r2.buf("segT"); segT = [_seg, _seg]; segb = [_segb, _segb]
            _mt = r2.alloc([8, 128], BF16); _mtb = r2.buf("MT"); MT = [_mt, _mt]; MTb = [_mtb, _mtb]
            cbT = [r2.alloc([128], BF16) for _ in range(2)]; cbTb = [r2.buf("cbT%d" % i) for i in range(2)]
            t1 = r2.alloc([512], F32); t1b = r2.buf("t1")
            yy = r2.alloc([512], F32); yb = r2.buf("y")
            yn = r2.alloc([512], BF16); ynb = r2.buf("yn")
            ysm = r2.alloc([4], F32); ysmb = r2.buf("ysm")
            hl8 = r2.alloc([2, NMC * 8], BF16); hl8b = r2.buf("hl8")
            S.op("act", lambda e, g=g: e.activation(out=Scur, in_=self.stash[:, g, 0:512], func=AF.Copy),
                 reads=(self.yTb[g],), writes=(Sb,))
            S.dma("sp", lambda e, g=g: e.dma_start(out=normw, in_=d["ssm_norm_w"][:, g * 512:(g + 1) * 512].partition_broadcast(128)),
                  nsem, writes=(nwb,))
            sm, smb = self.batch_smalls(r2, NMC, 7, g)
            acf = sm["acum"].rearrange("p c j -> p (c j)")
            S.op("dve", lambda e: e.tensor_copy(out=hl8[:, 0, :], in_=acf), reads=(smb,), writes=(hl8b,))
            S.op("dve", lambda e: e.tensor_tensor(out=hl8[:, 1, :], in0=acf, in1=hl8[:, 0, :], op=ALU.subtract),
                 reads=(smb, hl8b), writes=(hl8b,))
            if g == 0:
                self.dump("hl8", hl8, [hl8b], [128, 2, NMC * 8], BF16)
            tb = [self.bank() for _ in range(3)]
            tv = [p[1][:].bitcast(BF16).rearrange("p (a b) -> p a b", b=128) for p in tb]
            fns = []
            for c in range(NMC):
                for hl in range(2):
                    if c < 8:
                        dst = tv[hl][0:8, c, :]
                    else:
                        dst = tv[2][0:8, hl, :]
                    fns.append(lambda e, c=c, hl=hl, dst=dst: e.transpose(out=dst, in_=hl8[:, hl, c * 8:(c + 1) * 8], identity=self.ident_bf))
            S.op("pe", fns, reads=(hl8b, self.cb), writes=[p[0] for p in tb])
            S.op("act", [
                lambda e: e.activation(out=hiT[0:8, 0:8, :], in_=tv[0][0:8, 0:8, :], func=AF.Copy),
                lambda e: e.activation(out=loT[0:8, 0:8, :], in_=tv[1][0:8, 0:8, :], func=AF.Copy),
                lambda e: e.activation(out=hiT[0:8, 8, :], in_=tv[2][0:8, 0, :], func=AF.Copy),
                lambda e: e.activation(out=loT[0:8, 8, :], in_=tv[2][0:8, 1, :], func=AF.Copy),
            ], reads=[p[0] for p in tb], writes=(hib,))
            cols = [XBC0 + g * 512 + j * 128 for j in range(4)] + [XBC0 + DI + g * 128, XBC0 + DI + 1024 + g * 128]
            pending = []
            for bi, c0 in enumerate(cols):
                blk = (c0 - XBC0) // 128
                if bi in (0, 2):
                    wv, wb = self.wload([d["w_in"][:, c0:c0 + 256]], 16)
                    e0 = 0
                elif bi == 4:
                    wv, wb = self.wload([d["w_in"][:, c0:c0 + 128], d["w_in"][:, cols[5]:cols[5] + 128]], 16)
                    e0 = 0
                else:
                    e0 = 128
                outs = self.proj_fm(wv, wb, e0, 16, self.hT, abufs, ranges)
                pre, preb = pre2[bi % 2], pre2b[bi % 2]
                for i, (pb, ps) in enumerate(outs):
                    S.op("act", lambda e, ps=ps, i=i, pre=pre: e.activation(out=pre[:, i * 385:(i + 1) * 385], in_=ps, func=AF.Copy),
                         reads=(pb,), writes=(preb,))
                for th in range(3):
                    for f in pending:
                        f()
                    pending = []
                    cv, cvb = cvo[th % 2], cvob[th % 2]
                    if bi < 4:
                        dst, dstb = cv, cvb
                    elif bi == 4:
                        dst, dstb = BT[:, th * 384:(th + 1) * 384], BTb
                    else:
                        dst, dstb = CT[:, th * 384:(th + 1) * 384], CTb
                    self.conv_silu(pre, preb, 384, blk, acc, accb, tt, ttb, dst, dstb, lo=th * 384)
                    if bi < 4:
                        tiles = [cv[:, c * 128:(c + 1) * 128] for c in range(3)]
                        pending.append(lambda tiles=tiles, cvb=cvb, bi=bi, th=th: self.transpose_to(
                            tiles, [cvb], lambda n: xs_tm[:, th * 3:th * 3 + 3, bi * 128:(bi + 1) * 128], [xsb]))
                    elif bi == 4:
                        tiles = [BT[:, (th * 3 + c) * 128:(th * 3 + c + 1) * 128] for c in range(3)]
                        pending.append(lambda tiles=tiles, th=th: self.transpose_to(
                            tiles, [BTb], lambda n: B_tm[:, th * 3:th * 3 + 3, :], [Btmb]))
            for f in pending:
                f()
            zw = []
            for hv in range(2):
                c0 = Z0 + g * 512 + hv * 256
                zw.append(self.wload([d["w_in"][:, c0:c0 + 256]], 16))
            Dg = self.Db[:, g * 8:(g + 1) * 8]
            PB, PS = self.pbuf, self.ps

            def f_pe1(mc):
                k = mc % 2
                tc = slice(mc * 128, (mc + 1) * 128)
                t0 = 896 + mc * 128
                fns = []
                for bq in range(2):
                    psq = PS[bq]
                    fns.append(lambda e, psq=psq: e.matmul(psq[:, 0:512], lhsT=self.ident_bf, rhs=self.neg4, start=True, stop=False))
                    for jj in range(4):
                        j = bq * 4 + jj
                        for src in (hiT, loT):
                            fns.append(lambda e, psq=psq, jj=jj, j=j, src=src: e.matmul(
                                psq[:, jj * 128:(jj + 1) * 128], lhsT=self.sel[0:8, j * 128:(j + 1) * 128], rhs=src[0:8, mc, :],
                                start=False, stop=False))
                    for si, src in enumerate((hiT, loT)):
                        fns.append(lambda e, psq=psq, bq=bq, src=src, si=si: e.matmul(
                            psq[:, 0:512], lhsT=src[0:8, mc, :], rhs=self.nsel[0:8, bq * 512:(bq + 1) * 512],
                            start=False, stop=(si == 1)))
                S.op("pe", fns, reads=(hib, self.cb), writes=[PB[0], PB[1]])
                S.op("pe", lambda e: e.matmul(PS[2][:, 0:128], lhsT=BT[:, tc], rhs=CT[:, tc], start=True, stop=True),
                     reads=(BTb, CTb), writes=(PB[2],))
                for hv in range(2):
                    wv_, wb_ = zw[hv]
                    psz = PS[6 + hv][:, 0:256]
                    S.op("pe", [lambda e, kc=kc, wv_=wv_, psz=psz: e.matmul(psz, lhsT=self.hT(kc, t0, 128), rhs=wv_[:, kc, 0:256],
                                                                            start=(kc == 0), stop=(kc == 15)) for kc in range(16)],
                         reads=[wb_, self.hTb[7 + mc]], writes=(PB[6 + hv],))
                S.op("act", [lambda e, bq=bq: e.activation(out=segT[k][:, bq * 4:(bq + 1) * 4, :].rearrange("p a b -> p (a b)"),
                                                           in_=PS[bq][:, 0:512], func=AF.Exp) for bq in range(2)],
                     reads=[PB[0], PB[1]], writes=(segb[k],))
                S.op("act", lambda e: e.activation(out=cbT[k], in_=PS[2][:, 0:128], func=AF.Copy), reads=(PB[2],), writes=(cbTb[k],))
                for hv in range(2):
                    S.op("act", lambda e, hv=hv: e.activation(out=ztmp[hv], in_=PS[6 + hv][:, 0:256], func=AF.Tanh, scale=0.5),
                         reads=(PB[6 + hv],), writes=(ztb[hv],))

            def f_dve(mc):
                k = mc % 2
                xs_c = xs_tm[:, mc, :]
                for hv in range(2):
                    S.op("dve", lambda e, hv=hv: e.scalar_tensor_tensor(
                        out=zs[k][:, hv * 256:(hv + 1) * 256], in0=ztmp[hv], scalar=1.0, in1=PS[6 + hv][:, 0:256], op0=ALU.add, op1=ALU.mult),
                        reads=(ztb[hv], PB[6 + hv]), writes=(zsb[k],))
                S.op("dve", lambda e: e.tensor_tensor(out=MT[k], in0=segT[k], in1=cbT[k].unsqueeze(1).to_broadcast([128, 8, 128]), op=ALU.mult),
                     reads=(segb[k], cbTb[k]), writes=(MTb[k],))
                S.op("dve", lambda e: e.tensor_tensor(out=xdt[k].rearrange("p (j q) -> p j q", q=64),
                                                      in0=xs_c.rearrange("p (j q) -> p j q", q=64),
                                                      in1=sm["dt"][:, mc, :].unsqueeze(2).to_broadcast([128, 8, 64]), op=ALU.mult),
                     reads=(xsb, self.dt_buf), writes=(xdtb[k],))
                S.op("dve", lambda e: e.tensor_tensor(out=xw[k].rearrange("p (j q) -> p j q", q=64),
                                                      in0=xs_c.rearrange("p (j q) -> p j q", q=64),
                                                      in1=sm["dtw"][:, mc, :].unsqueeze(2).to_broadcast([128, 8, 64]), op=ALU.mult),
                     reads=(xsb, smb), writes=(xwb[k],))
                S.op("pe", [lambda e, j=j: e.matmul(PS[3][:, j * 64:(j + 1) * 64], lhsT=MT[k][:, j, :], rhs=xdt[k][:, j * 64:(j + 1) * 64],
                                                    start=True, stop=True) for j in range(8)],
                     reads=(MTb[k], xdtb[k]), writes=(PB[3],))
                S.op("pe", lambda e: e.matmul(PS[4][:, 0:512], lhsT=B_tm[:, mc, :], rhs=xw[k], start=True, stop=True),
                     reads=(Btmb, xwb[k]), writes=(PB[4],))

            def b_a(mc):
                k = mc % 2
                tc = slice(mc * 128, (mc + 1) * 128)
                xs_c = xs_tm[:, mc, :]
                if mc == 0:
                    S.op("act", lambda e: e.activation(out=Sbf2[0], in_=Scur, func=AF.Copy), reads=(Sb,), writes=(Sbfb2[0],))
                S.op("pe", lambda e: e.matmul(PS[5][:, 0:512], lhsT=CT[:, tc], rhs=Sbf2[k], start=True, stop=True),
                     reads=(CTb, Sbfb2[k]), writes=(PB[5],))
                S.op("dve", lambda e: e.tensor_tensor(out=Scur.rearrange("p (j q) -> p j q", q=64),
                                                      in0=Scur.rearrange("p (j q) -> p j q", q=64),
                                                      in1=sm["eatot"][:, mc, :].unsqueeze(2).to_broadcast([128, 8, 64]), op=ALU.mult),
                     reads=(Sb, smb), writes=(Sb,))
                S.op("dve", lambda e: e.tensor_tensor(out=Scur, in0=Scur, in1=PS[4][:, 0:512], op=ALU.add),
                     reads=(Sb, PB[4]), writes=(Sb,))
                if mc == 0:
                    S.op("dve", lambda e: e.tensor_scalar(out=Scur, in0=Scur, scalar1=self.flag[:, 0:1], scalar2=None, op0=ALU.mult),
                         reads=(Sb, self.cb), writes=(Sb,))
                if mc + 1 < NMC:
                    S.op("act", lambda e: e.activation(out=Sbf2[1 - k], in_=Scur, func=AF.Copy), reads=(Sb,), writes=(Sbfb2[1 - k],))
                S.op("dve", lambda e: e.tensor_tensor(out=t1.rearrange("p (j q) -> p j q", q=64),
                                                      in0=PS[5][:, 0:512].rearrange("p (j q) -> p j q", q=64),
                                                      in1=sm["eacum"][:, mc, :].unsqueeze(2).to_broadcast([128, 8, 64]), op=ALU.mult),
                     reads=(PB[5], smb), writes=(t1b,))
                S.op("dve", lambda e: e.tensor_tensor(out=yy, in0=t1, in1=PS[3][:, 0:512], op=ALU.add),
                     reads=(t1b, PB[3]), writes=(yb,))
                S.op("dve", lambda e: e.tensor_tensor(out=t1.rearrange("p (j q) -> p j q", q=64),
                                                      in0=xs_c.rearrange("p (j q) -> p j q", q=64),
                                                      in1=Dg.unsqueeze(2).to_broadcast([128, 8, 64]), op=ALU.mult),
                     reads=(xsb, self.cb, yb), writes=(t1b,))
                S.op("dve", lambda e: e.tensor_tensor(out=yy, in0=yy, in1=t1, op=ALU.add), reads=(t1b, yb), writes=(yb,))
                S.op("dve", lambda e: e.tensor_tensor(out=yy, in0=yy, in1=zs[k], op=ALU.mult), reads=(yb, zsb[k]), writes=(yb,))

            def b_c(mc):
                S.op("act", lambda e: e.activation(out=yn, in_=yy, func=AF.Square, accum_out=ysm[:, 0:1]), reads=(yb,), writes=(ynb, ysmb))
                S.op("act", lambda e: e.activation(out=ysm[:, 1:2], in_=ysm[:, 0:1], func=AF.Ln, bias=4 * EPS, scale=1.0 / 512),
                     reads=(ysmb,), writes=(ysmb,))
                S.op("act", lambda e: e.activation(out=ysm[:, 2:3], in_=ysm[:, 1:2], func=AF.Exp, scale=-0.5), reads=(ysmb,), writes=(ysmb,))

            def b_e(mc):
                tc = slice(mc * 128, (mc + 1) * 128)
                S.op("dve", lambda e: e.scalar_tensor_tensor(out=yn, in0=yy, scalar=ysm[:, 2:3], in1=normw, op0=ALU.mult, op1=ALU.mult),
                     reads=(yb, ysmb, nwb), writes=(ynb,))
                psb = PS[2][:].bitcast(BF16).rearrange("p (a b) -> p a b", b=128)
                S.op("pe", [lambda e, i=i: e.transpose(out=psb[:, i, :], in_=yn[:, i * 128:(i + 1) * 128], identity=self.ident_bf)
                            for i in range(4)], reads=(ynb, self.cb), writes=(PB[2],))
                S.op("act", lambda e: e.activation(out=yT[:, 4 * g:4 * g + 4, tc], in_=psb[:, 0:4, :], func=AF.Copy),
                     reads=(PB[2],), writes=(self.yTb[g],))

            f_pe1(0)
            f_dve(0)
            for mc in range(NMC):
                b_a(mc)
                if mc + 1 < NMC:
                    f_pe1(mc + 1)
                b_c(mc)
                if mc + 1 < NMC:
                    f_dve(mc + 1)
                b_e(mc)

    def ssm_out(self):
        S, d = self.S, self.d
        r2, r4 = self.r2, self.r4
        r2.recarve(); r4.recarve()
        self.mixedT = r4.alloc([16, NT], BF16)
        self.mixb = [r4.buf("mixedT%d" % i) for i in range(16)]
        tg = r2.alloc([3, 384], F32); tgb = r2.buf("tg")
        self.tg, self.tgb = tg, tgb
        ranges = [(896 + i * 384, 384) for i in range(3)]
        mainbufs = self.hTb[7:16]
        yT = self.yT
        for db in range(16):
            if db % 2 == 0:
                wvg, wbg = self.wload([d["w_in"][:, GS0 + db * 128:GS0 + db * 128 + 256]], 16)
            outs = self.proj_fm(wvg, wbg, (db % 2) * 128, 16, self.hT, mainbufs, ranges)
            for i, (pb, ps) in enumerate(outs):
                S.op("act", lambda e, ps=ps, i=i: e.activation(out=tg[:, i, :], in_=ps, func=AF.Tanh, scale=0.5),
                     reads=(pb,), writes=(tgb,))
            wv, wb = self.wload([d["w_ssm_out"][:, db * 128:(db + 1) * 128]], 32)
            outs = self.proj_fm(wv, wb, 0, 32, lambda kc, t0, n: yT[:, kc, t0 - 896:t0 - 896 + n], self.yTb, ranges)
            for i, (pb, ps) in enumerate(outs):
                S.op("dve", lambda e, ps=ps, i=i, db=db: e.scalar_tensor_tensor(
                    out=self.mixedT[:, db, i * 384:(i + 1) * 384], in0=tg[:, i, :], scalar=1.0, in1=ps, op0=ALU.add, op1=ALU.mult),
                    reads=(tgb, pb), writes=(self.mixb[db],))
        self.dump("mixS", self.mixedT, self.mixb, [128, 16, NT], BF16)

    def attention(self):
        S, d = self.S, self.d
        r2, r3 = self.r2, self.r3
        r2.recarve(); r3.recarve()
        self.aoT = r3.alloc([16, NT], BF16)
        self.aoTb = [r3.buf("aoT%d" % i) for i in range(NKV)]
        qT = r3.alloc([4, NT], BF16); qTb = r3.buf("qT")
        kT2 = r3.alloc([TM], BF16); kTb = r3.buf("kT2")
        v1 = r3.alloc([10, 65], BF16); v1b = r3.buf("v1")
        biasT = r3.alloc([8, 2, 128], F32); biasb = r3.buf("biasT")
        q32 = r3.alloc([512], F32); q32b = r3.buf("q32")
        qtmp = r3.alloc([512], F32); qtmpb = r3.buf("qtmp")
        qn = r3.alloc([512], BF16); qnb = r3.buf("qn")
        kv32 = r3.alloc([128], F32); kvb = r3.buf("kv32")
        ktmp = r3.alloc([64], F32); ktmpb = r3.buf("ktmp")
        kdup = r3.alloc([2, 64], BF16); kdupb = r3.buf("kdup")
        qs = r3.alloc([32], F32); qsb = r3.buf("qs")
        ltmp = r2.alloc([512], F32); ltb = r2.buf("ltmp")
        eT = [r2.alloc([2, 2, 128], BF16) for _ in range(2)]; eTb = [r2.buf("eT%d" % i) for i in range(2)]
        den = r2.alloc([16], F32); denb = r2.buf("den")
        ao = r2.alloc([8, 64], BF16); aob = r2.buf("ao")
        bsem = S.new_dma_sem("biassem")
        S.op("dve", lambda e: e.memset(v1[:, :, 64:65], 1.0), writes=(v1b,))
        for kg in range(NKV):
            S.dma("sp", lambda e, kg=kg: e.dma_start(out=biasT, in_=d["biasT"][kg].rearrange("p (h b q) -> p h b q", h=8, b=2)),
                  bsem, writes=(biasb,))
            wv, wb = self.wload([d["w_in"][:, K0 + kg * 64:K0 + kg * 64 + 64], d["w_in"][:, V0 + kg * 64:V0 + kg * 64 + 64]], 16)
            for cj in range(10):
                t0 = 768 + cj * 128
                pb, ps = self.proj_tm(wv, wb, 0, 128, 16, lambda kc, t0=t0: self.hT(kc, t0, 128), [self.hTb[6 + cj]])
                S.op("act", lambda e, ps=ps: e.activation(out=kv32, in_=ps, func=AF.Copy), reads=(pb,), writes=(kvb,))
                S.op("act", lambda e: e.activation(out=ktmp, in_=kv32[:, 0:64], func=AF.Square, accum_out=qs[:, 0:1]),
                     reads=(kvb,), writes=(ktmpb, qsb))
                S.op("act", lambda e: e.activation(out=qs[:, 1:2], in_=qs[:, 0:1], func=AF.Ln, bias=EPS, scale=1.0 / 64),
                     reads=(qsb,), writes=(qsb,))
                S.op("act", lambda e: e.activation(out=qs[:, 2:3], in_=qs[:, 1:2], func=AF.Exp, scale=-0.5), reads=(qsb,), writes=(qsb,))
                S.op("dve", [
                    lambda e: e.scalar_tensor_tensor(out=kdup[:, 0, :], in0=kv32[:, 0:64], scalar=qs[:, 2:3], in1=self.kg,
                                                     op0=ALU.mult, op1=ALU.mult),
                    lambda e: e.scalar_tensor_tensor(out=kdup[:, 1, :], in0=kv32[:, 0:64], scalar=qs[:, 2:3], in1=self.kg,
                                                     op0=ALU.mult, op1=ALU.mult),
                ], reads=(kvb, qsb, self.cb), writes=(kdupb,))
                S.op("act", lambda e, cj=cj: e.activation(out=v1[:, cj, 0:64], in_=kv32[:, 64:128], func=AF.Copy),
                     reads=(kvb,), writes=(v1b,))
                self.transpose_to([kdup.rearrange("p a b -> p (a b)")], [kdupb],
                                  lambda n, cj=cj: kT2[:, cj * 128:(cj + 1) * 128].unsqueeze(1), [kTb])
            wvs = []
            for hv in range(2):
                c0 = Q0 + kg * 512 + hv * 256
                wvs.append(self.wload([d["w_in"][:, c0:c0 + 256]], 16))
            for mc in range(NMC):
                t0 = 896 + mc * 128
                for hv in range(2):
                    pb, ps = self.proj_tm(wvs[hv][0], wvs[hv][1], 0, 256, 16, lambda kc, t0=t0: self.hT(kc, t0, 128), [self.hTb[7 + mc]])
                    S.op("act", lambda e, ps=ps, hv=hv: e.activation(out=q32[:, hv * 256:(hv + 1) * 256], in_=ps, func=AF.Copy),
                         reads=(pb,), writes=(q32b,))
                S.op("dve", lambda e: e.tensor_tensor(out=qtmp, in0=q32, in1=q32, op=ALU.mult), reads=(q32b,), writes=(qtmpb,))
                S.op("dve", lambda e: e.tensor_reduce(out=qs[:, 8:16], in_=qtmp.rearrange("p (h x) -> p h x", x=64), axis=AX.X, op=ALU.add),
                     reads=(qtmpb,), writes=(qsb,))
                S.op("act", lambda e: e.activation(out=qs[:, 16:24], in_=qs[:, 8:16], func=AF.Ln, bias=EPS, scale=1.0 / 64),
                     reads=(qsb,), writes=(qsb,))
                S.op("act", lambda e: e.activation(out=qs[:, 24:32], in_=qs[:, 16:24], func=AF.Exp, scale=-0.5), reads=(qsb,), writes=(qsb,))
                S.op("dve", lambda e: e.tensor_tensor(out=qtmp.rearrange("p (h x) -> p h x", x=64), in0=q32.rearrange("p (h x) -> p h x", x=64),
                                                      in1=qs[:, 24:32].unsqueeze(2).to_broadcast([128, 8, 64]), op=ALU.mult),
                     reads=(q32b, qsb), writes=(qtmpb,))
                S.op("dve", lambda e: e.tensor_tensor(out=qn.rearrange("p (h x) -> p h x", x=64), in0=qtmp.rearrange("p (h x) -> p h x", x=64),
                                                      in1=self.qg.unsqueeze(1).to_broadcast([128, 8, 64]), op=ALU.mult),
                     reads=(qtmpb, self.cb), writes=(qnb,))
                tiles = [qn[:, i * 128:(i + 1) * 128] for i in range(4)]
                self.transpose_to(tiles, [qnb], lambda n, mc=mc: qT[:, 0:4, mc * 128:(mc + 1) * 128], [qTb])
            for mc in range(NMC):
                pvb = [self.bank() for _ in range(2)]
                for qp in range(2):
                    lb = [self.bank() for _ in range(2)]
                    fns = []
                    for hh in range(2):
                        psv = lb[hh][1][:, 0:512].rearrange("p (a b q) -> p a b q", a=2, b=2)
                        for qq in range(2):
                            qt = qp * 2 + qq
                            for blk in range(2):
                                cj = mc + blk
                                fns.append(lambda e, hh=hh, blk=blk, cj=cj, psv=psv, qt=qt, qq=qq, mc=mc: e.matmul(
                                    psv[:, qq, blk, :], lhsT=kT2[hh * 64:(hh + 1) * 64, cj * 128:(cj + 1) * 128],
                                    rhs=qT[hh * 64:(hh + 1) * 64, qt, mc * 128:(mc + 1) * 128], start=True, stop=True))
                    S.op("pe", fns, reads=(kTb, qTb), writes=[lb[0][0], lb[1][0]])
                    for hh in range(2):
                        pb, ps = lb[hh]
                        psv = ps[:, 0:512].rearrange("p (a b q) -> p a b q", a=2, b=2)
                        h0 = qp * 4 + hh
                        S.op("dve", lambda e, psv=psv, h0=h0: e.tensor_tensor(out=ltmp.rearrange("p (a b q) -> p a b q", a=2, b=2), in0=psv,
                                                                             in1=biasT[:, h0:h0 + 3:2, :, :], op=ALU.add),
                             reads=(pb, biasb), writes=(ltb,))
                        S.op("act", lambda e, hh=hh: e.activation(out=eT[hh].rearrange("p a b q -> p (a b q)"), in_=ltmp, func=AF.Exp),
                             reads=(ltb,), writes=(eTb[hh],))
                        if mc == 1:
                            S.op("dve", lambda e, hh=hh: e.tensor_scalar(out=eT[hh][:, :, 0, :], in0=eT[hh][:, :, 0, :],
                                                                         scalar1=self.flag[:, 0:1], scalar2=None, op0=ALU.mult),
                                 reads=(eTb[hh], self.cb), writes=(eTb[hh],))
                        fns = []
                        for qq in range(2):
                            h8 = h0 + 2 * qq
                            pvp, pvs = pvb[h8 // 4]
                            slot = h8 % 4
                            for blk in range(2):
                                cj = mc + blk
                                fns.append(lambda e, hh=hh, qq=qq, blk=blk, cj=cj, pvs=pvs, slot=slot: e.matmul(
                                    pvs[:, slot * 65:(slot + 1) * 65], lhsT=eT[hh][:, qq, blk, :], rhs=v1[:, cj, :],
                                    start=(blk == 0), stop=(blk == 1)))
                        S.op("pe", fns, reads=(eTb[hh], v1b), writes=[pvb[qp][0]])
                for hb_ in range(2):
                    pvp, pvs = pvb[hb_]
                    pv3 = pvs[:, 0:260].rearrange("p (s x) -> p s x", x=65)
                    S.op("dve", lambda e, pv3=pv3, hb_=hb_, kg=kg: e.tensor_tensor(
                        out=den[:, hb_ * 4:hb_ * 4 + 4].unsqueeze(2), in0=pv3[:, :, 64:65],
                        in1=self.esink[:, kg * 8 + hb_ * 4:kg * 8 + hb_ * 4 + 4].unsqueeze(2), op=ALU.add),
                        reads=(pvp, self.cb), writes=(denb,))
                    S.op("dve", lambda e, hb_=hb_: e.reciprocal(out=den[:, 8 + hb_ * 4:8 + hb_ * 4 + 4], in_=den[:, hb_ * 4:hb_ * 4 + 4]),
                         reads=(denb,), writes=(denb,))
                    S.op("dve", lambda e, pv3=pv3, hb_=hb_: e.tensor_tensor(
                        out=ao[:, hb_ * 4:hb_ * 4 + 4, :], in0=pv3[:, :, 0:64],
                        in1=den[:, 8 + hb_ * 4:8 + hb_ * 4 + 4].unsqueeze(2).to_broadcast([128, 4, 64]), op=ALU.mult),
                        reads=(pvp, denb), writes=(aob,))
                aof = ao.rearrange("p h x -> p (h x)")
                tiles = [aof[:, i * 128:(i + 1) * 128] for i in range(4)]
                self.transpose_to(tiles, [aob], lambda n, kg=kg, mc=mc: self.aoT[:, kg * 4:kg * 4 + 4, mc * 128:(mc + 1) * 128],
                                  [self.aoTb[kg]])
        self.dump("aoT", self.aoT, self.aoTb, [128, 16, NT], BF16)

    def attn_out(self):
        S, d = self.S, self.d
        r2 = self.r2
        r2.recarve()
        tg = r2.alloc([3, 384], F32); tgb = r2.buf("tg")
        mt = r2.alloc([384], F32); mtb = r2.buf("mtmp")
        ranges = [(896 + i * 384, 384) for i in range(3)]
        mainbufs = self.hTb[7:16]
        for db in range(16):
            if db % 2 == 0:
                wvg, wbg = self.wload([d["w_in"][:, GA0 + db * 128:GA0 + db * 128 + 256]], 16)
                wva, wba = self.wload([d["w_attn_out"][:, db * 128:db * 128 + 256]], 16)
            outs = self.proj_fm(wvg, wbg, (db % 2) * 128, 16, self.hT, mainbufs, ranges)
            for i, (pb, ps) in enumerate(outs):
                S.op("act", lambda e, ps=ps, i=i: e.activation(out=tg[:, i, :], in_=ps, func=AF.Tanh, scale=0.5),
                     reads=(pb,), writes=(tgb,))
            outs = self.proj_fm(wva, wba, (db % 2) * 128, 16, lambda kc, t0, n: self.aoT[:, kc, t0 - 896:t0 - 896 + n],
                                self.aoTb, ranges)
            for i, (pb, ps) in enumerate(outs):
                S.op("dve", lambda e, ps=ps, i=i: e.scalar_tensor_tensor(out=mt, in0=tg[:, i, :], scalar=1.0, in1=ps,
                                                                        op0=ALU.add, op1=ALU.mult),
                     reads=(tgb, pb), writes=(mtb,))
                S.op("dve", lambda e, i=i, db=db: e.tensor_tensor(out=self.mixedT[:, db, i * 384:(i + 1) * 384],
                                                                  in0=self.mixedT[:, db, i * 384:(i + 1) * 384], in1=mt, op=ALU.add),
                     reads=(mtb, self.mixb[db]), writes=(self.mixb[db],))
        self.dump("mixed", self.mixedT, self.mixb, [128, 16, NT], BF16)

    def wout_residual(self):
        S, d = self.S, self.d
        r1, r2, r3 = self.r1, self.r2, self.r3
        r3.recarve()
        self.x1 = r3.alloc([NMC, D], F32)
        self.x1b = [r3.buf("x1_%d" % i) for i in range(NMC)]
        x1, x1b = self.x1, self.x1b
        xsem = S.new_dma_sem("x1sem")
        for mc in range(NMC):
            S.dma("sp", lambda e, mc=mc: e.dma_start(out=x1[:, mc, :], in_=d["xm"][mc * 128:(mc + 1) * 128, :]), xsem,
                  writes=(x1b[mc],))
        for mc in range(NMC):
            x1b[mc].w = (xsem, S.dcnt[xsem])
        for ct in range(8):
            wv, wb = self.wload([d["w_out"][:, ct * 256:(ct + 1) * 256]], 16)
            for mc in range(NMC):
                pb, ps = self.proj_tm(wv, wb, 0, 256, 16, lambda kc, mc=mc: self.mixedT[:, kc, mc * 128:(mc + 1) * 128], self.mixb)
                S.op("dve", lambda e, ps=ps, mc=mc, ct=ct: e.scalar_tensor_tensor(
                    out=x1[:, mc, ct * 256:(ct + 1) * 256], in0=ps, scalar=0.5, in1=x1[:, mc, ct * 256:(ct + 1) * 256],
                    op0=ALU.mult, op1=ALU.add), reads=(pb, x1b[mc]), writes=(x1b[mc],))
        self.dump("x1", x1, x1b, [128, NMC, D])
        r1.recarve(); r2.recarve()
        self.hfT = r1.alloc([16, NT], BF16)
        self.hfTb = [r1.buf("hfT%d" % i) for i in range(NMC)]
        gain = r2.alloc([D], F32); gb = r2.buf("gain")
        hb = [r2.alloc([D], BF16) for _ in range(2)]
        hbb = [r2.buf("hb%d" % i) for i in range(2)]
        S.dma("sp", lambda e: e.dma_start(out=gain, in_=d["norm_ffn_w"].partition_broadcast(128)), S.new_dma_sem("gsem1"), writes=(gb,))
        for mc in range(NMC):
            s = mc % 2
            self.rms_tile(x1[:, mc, :], [x1b[mc]], gain, gb, hb[s], hbb[s], mc, hb[s])
            for half in range(2):
                tiles = [hb[s][:, (half * 8 + i) * 128:(half * 8 + i + 1) * 128] for i in range(8)]
                self.transpose_to(tiles, [hbb[s]],
                                  lambda n, half=half, mc=mc: self.hfT[:, half * 8:half * 8 + 8, mc * 128:(mc + 1) * 128],
                                  [self.hfTb[mc]])
        self.gain_ap, self.gain_b, self.hb2, self.hbb2 = gain, gb, hb, hbb

    def ffn(self):
        S, d = self.S, self.d
        r2, r4 = self.r2, self.r4
        r4.recarve()
        actT = r4.alloc([11, 1024], BF16); actb = r4.buf("actT")
        pre = [r4.alloc([2 + NT], F32) for _ in range(2)]
        preb = [r4.buf("fpre%d" % i) for i in range(2)]
        acc = [r2.alloc([1024], F32) for _ in range(2)]
        accb = [r2.buf("facc%d" % i) for i in range(2)]
        gl = r4.alloc([1024], F32); glb = r4.buf("gl")
        x1, x1b = self.x1, self.x1b
        ranges = [(i * 384, 384) for i in range(3)]
        for i in range(2):
            S.op("dve", lambda e, i=i: e.memset(pre[i][:, 0:2], 0.0), writes=(preb[i],))
        for fg in range(4):
            for jb in range(11):
                b = fg * 11 + jb
                c0 = b * 128
                wv, wb = self.wload([d["w_ffn_up"][:, c0:c0 + 128], d["w_ffn_up"][:, DFF + c0:DFF + c0 + 128]], 16)
                for gu in range(2):
                    blk = b + gu * 44
                    outs = self.proj_fm(wv, wb, gu * 128, 16, lambda kc, t0, n: self.hfT[:, kc, t0:t0 + n], self.hfTb, ranges)
                    for i, (pb, ps) in enumerate(outs):
                        S.op("act", lambda e, ps=ps, i=i, gu=gu: e.activation(out=pre[gu][:, 2 + i * 384:2 + (i + 1) * 384], in_=ps, func=AF.Copy),
                             reads=(pb,), writes=(preb[gu],))
                    S.op("dve", lambda e, gu=gu: e.tensor_scalar(out=pre[gu][:, 128:130], in0=pre[gu][:, 128:130],
                                                                 scalar1=self.flag[:, 0:1], scalar2=None, op0=ALU.mult),
                         reads=(preb[gu], self.cb), writes=(preb[gu],))
                    w = self.cw_ffn
                    S.op("dve", lambda e, gu=gu, blk=blk: e.tensor_scalar(out=acc[gu], in0=pre[gu][:, 130:130 + 1024], scalar1=w[:, blk, 2:3],
                                                                          scalar2=self.cb_ffn[:, blk:blk + 1], op0=ALU.mult, op1=ALU.add),
                         reads=(preb[gu], self.cb), writes=(accb[gu],))
                    for k in (1, 0):
                        S.op("dve", lambda e, gu=gu, blk=blk, k=k: e.scalar_tensor_tensor(
                            out=acc[gu], in0=pre[gu][:, 128 + k:128 + k + 1024], scalar=w[:, blk, k:k + 1], in1=acc[gu],
                            op0=ALU.mult, op1=ALU.add), reads=(preb[gu], self.cb, accb[gu]), writes=(accb[gu],))
                S.op("act", lambda e: e.activation(out=gl, in_=acc[0], func=AF.Gelu_apprx_tanh), reads=(accb[0],), writes=(glb,))
                S.op("dve", lambda e, jb=jb: e.tensor_tensor(out=actT[:, jb, :], in0=gl, in1=acc[1], op=ALU.mult),
                     reads=(glb, accb[1]), writes=(actb,))
            if fg == 0:
                self.dump("actT0", actT, [actb], [128, 11, 1024], BF16)
            for ct in range(8):
                wv, wb = self.wload([d["w_ffn_down"][fg * 1408:(fg + 1) * 1408, ct * 256:(ct + 1) * 256]], 11)
                for mc in range(1, NMC):
                    pb, ps = self.proj_tm(wv, wb, 0, 256, 11, lambda kc, mc=mc: actT[:, kc, (mc - 1) * 128:mc * 128], [actb])
                    S.op("dve", lambda e, ps=ps, mc=mc, ct=ct: e.tensor_tensor(
                        out=x1[:, mc, ct * 256:(ct + 1) * 256], in0=ps, in1=x1[:, mc, ct * 256:(ct + 1) * 256], op=ALU.add),
                        reads=(pb, x1b[mc]), writes=(x1b[mc],))
        self.dump("x2", x1, x1b, [128, NMC, D])

    def ple(self):
        S, d = self.S, self.d
        r1, r2, r4 = self.r1, self.r2, self.r4
        x1, x1b = self.x1, self.x1b
        r1.recarve(); r4.recarve()
        nT = r1.alloc([16, 1024], BF16)
        nTb = [r1.buf("nT%d" % i) for i in range(8)]
        gain, gb, hb, hbb = self.gain_ap, self.gain_b, self.hb2, self.hbb2
        S.dma("sp", lambda e: e.dma_start(out=gain, in_=d["ple_norm_w"].partition_broadcast(128)), S.new_dma_sem("gsem2"), writes=(gb,))
        pT = r4.alloc([2, 1024], BF16); pTb = r4.buf("pT")
        pt = [r4.alloc([PLE], F32) for _ in range(2)]; ptb = [r4.buf("pt%d" % i) for i in range(2)]
        pbf = [r4.alloc([PLE], BF16) for _ in range(2)]; pbfb = [r4.buf("pbf%d" % i) for i in range(2)]
        psem = [S.new_dma_sem("psem%d" % i) for i in range(2)]
        tgp = r4.alloc([256], F32); tgpb = r4.buf("tgp")
        up = r4.alloc([256], F32); upb = r4.buf("up")
        for mc in range(1, NMC):
            s = mc % 2
            o = mc - 1
            self.rms_tile(x1[:, mc, :], [x1b[mc]], gain, gb, hb[s], hbb[s], mc, hb[s])
            for half in range(2):
                tiles = [hb[s][:, (half * 8 + i) * 128:(half * 8 + i + 1) * 128] for i in range(8)]
                self.transpose_to(tiles, [hbb[s]],
                                  lambda n, half=half, o=o: nT[:, half * 8:half * 8 + 8, o * 128:(o + 1) * 128], [nTb[o]])
            S.dma("sp", lambda e, s=s, o=o: e.dma_start(out=pt[s], in_=d["pp"][o * 128:(o + 1) * 128, :]), psem[s], writes=(ptb[s],))
            S.op("act", lambda e, s=s: e.activation(out=pbf[s], in_=pt[s], func=AF.Copy), reads=(ptb[s],), writes=(pbfb[s],))
            tiles = [pbf[s][:, i * 128:(i + 1) * 128] for i in range(2)]
            self.transpose_to(tiles, [pbfb[s]], lambda n, o=o: pT[:, 0:2, o * 128:(o + 1) * 128], [pTb])
        for ct in range(8):
            wv, wb = self.wload([d["w_ple_gate"][:, ct * 256:(ct + 1) * 256]], 16)
            wv2, wb2 = self.wload([d["w_ple_proj"][:, ct * 256:(ct + 1) * 256]], 2)
            for mc in range(1, NMC):
                o = mc - 1
                pb, ps = self.bank()
                fns = []
                for kc in range(16):
                    fns.append(lambda e, kc=kc, ps=ps, o=o: e.matmul(ps[:, 0:256], lhsT=nT[:, kc, o * 128:(o + 1) * 128], rhs=wv[:, kc, 0:256],
                                                                     start=(kc == 0), stop=(kc == 15)))
                for kc in range(2):
                    fns.append(lambda e, kc=kc, ps=ps, o=o: e.matmul(ps[:, 256:512], lhsT=pT[:, kc, o * 128:(o + 1) * 128], rhs=wv2[:, kc, 0:256],
                                                                     start=(kc == 0), stop=(kc == 1)))
                S.op("pe", fns, reads=(wb, wb2, nTb[o], pTb), writes=(pb,))
                S.op("act", lambda e, ps=ps: e.activation(out=tgp, in_=ps[:, 0:256], func=AF.Tanh, scale=0.5), reads=(pb,), writes=(tgpb,))
                S.op("dve", lambda e, ps=ps: e.scalar_tensor_tensor(out=up, in0=tgp, scalar=1.0, in1=ps[:, 256:512], op0=ALU.add, op1=ALU.mult),
                     reads=(tgpb, pb), writes=(upb,))
                S.op("dve", lambda e, mc=mc, ct=ct: e.scalar_tensor_tensor(
                    out=x1[:, mc, ct * 256:(ct + 1) * 256], in0=up, scalar=0.5, in1=x1[:, mc, ct * 256:(ct + 1) * 256],
                    op0=ALU.mult, op1=ALU.add), reads=(upb, x1b[mc]), writes=(x1b[mc],))
        osem = S.new_dma_sem("osem")
        for mc in range(1, NMC):
            ob = Buf("out%d" % mc)
            S.dma("sp", lambda e, mc=mc: e.dma_start(out=self.out[(mc - 1) * 128:mc * 128, :], in_=x1[:, mc, :]), osem,
                  reads=(x1b[mc],), writes=(ob,))
            self.final_bufs.append(ob)


def _t5_bucket(dist):
    nb, md = 32, 128
    me = nb // 2
    dd = np.maximum(dist, 0)
    lr = np.log(np.maximum(dd, 1).astype(np.float32) / me) / np.log(md / me)
    large = me + (lr * (nb - me)).astype(np.int32)
    large = np.minimum(large, nb - 1)
    return np.where(dd < me, dd, large)


def _const_mats():
    ident = np.eye(128, dtype=np.float32)
    tri = (np.arange(128)[:, None] <= np.arange(128)[None, :]).astype(np.float32)
    ones = np.ones((128, 128), np.float32)
    neg = np.where(np.arange(128)[:, None] > np.arange(128)[None, :], -32768.0, 0.0).astype(np.float32)
    neg4 = np.tile(neg, (1, 4))
    sel = np.zeros((128, 8, 128), np.float32)
    for j in range(8):
        sel[j, j, :] = 1.0
    return (np.ascontiguousarray(np.concatenate([tri, ones], axis=1)),
            np.ascontiguousarray(np.concatenate([ident, neg4, sel.reshape(128, 1024)], axis=1)))


def _bias_tables(table):
    L = 128
    qi = np.arange(L)[:, None]
    kj = np.arange(2 * L)[None, :]
    dist = qi + L - kj
    band = (dist >= 0) & (dist < 128)
    bk = _t5_bucket(dist)
    b = table[bk]
    b = np.where(band[:, :, None], b, np.float32(NEGM)).astype(np.float32)
    b = b.reshape(L, 2, L, NKV, 8)
    b = np.transpose(b, (3, 2, 4, 1, 0))
    return np.ascontiguousarray(b).reshape(NKV, 128, 8 * 2 * 128)


def make_in_maps(inputs):
    x = np.asarray(inputs["x"], np.float32)
    p = np.asarray(inputs["p"], np.float32)[0]
    g = lambda k: np.ascontiguousarray(np.asarray(inputs[k], np.float32)[0])
    shared = {
        "w_in": g("w_in"), "w_attn_out": g("w_attn_out"), "w_ssm_out": g("w_ssm_out"), "w_out": g("w_out"),
        "w_ffn_up": g("w_ffn_up"), "w_ffn_down": g("w_ffn_down"), "w_ple_gate": g("w_ple_gate"),
        "w_ple_proj": g("w_ple_proj"),
        "norm_mix_w": g("norm_mix_w")[None], "norm_ffn_w": g("norm_ffn_w")[None], "ple_norm_w": g("ple_norm_w")[None],
        "ssm_norm_w": g("ssm_norm_w")[None], "q_norm_w": g("q_norm_w")[None], "k_norm_w": g("k_norm_w")[None],
        "attn_sinks": g("attn_sinks")[None], "ssm_A_log": g("ssm_A_log")[None], "ssm_dt_bias": g("ssm_dt_bias")[None],
        "ssm_D": g("ssm_D")[None],
        "cw_ssm": np.ascontiguousarray(g("ssm_conv_w").T.reshape(48, 128, 4).transpose(1, 0, 2)).reshape(128, 192),
        "cb_ssm": np.ascontiguousarray(g("ssm_conv_b").reshape(48, 128).T),
        "cw_ffn": np.ascontiguousarray(g("ffn_conv_w").T.reshape(88, 128, 3).transpose(1, 0, 2)).reshape(128, 264),
        "cb_ffn": np.ascontiguousarray(g("ffn_conv_b").reshape(88, 128).T),
        "biasT": _bias_tables(np.asarray(inputs["rel_bias_table"], np.float32)),
    }
    shared["cm_f"], shared["cm_b"] = _const_mats()
    in_maps = []
    for core in range(8):
        b, hf = core // 2, core % 2
        s0 = hf * 1024
        xm = np.zeros((NT, D), np.float32)
        xp = np.zeros((896, D), np.float32)
        if hf == 1:
            xm[:] = x[b, s0 - 128:s0 + 1024]
            xp[:] = x[b, 0:896]
        else:
            xm[128:] = x[b, 0:1024]
        m = dict(shared)
        m["xm"] = xm
        m["xp"] = xp
        m["pp"] = np.ascontiguousarray(p[b, s0:s0 + 1024])
        m["flag"] = np.full((128, 1), float(hf), np.float32)
        in_maps.append(m)
    return in_maps


def kernel(**inputs):
    in_maps = make_in_maps(inputs)
    dbg = tuple(inputs.get("_debug", ())) if isinstance(inputs.get("_debug", ()), (list, tuple)) else ()
    bld = Builder(debug=dbg)
    nc = bld.build()
    cores = list(range(8))
    if inputs.get("_cores"):
        cores = list(inputs["_cores"])
    res = run_bass_kernel_spmd(nc, [in_maps[c] for c in cores], core_ids=list(range(len(cores))))
    out = np.zeros((BATCH, SEQ, D), np.float32)
    for i, core in enumerate(cores):
        b, hf = core // 2, core % 2
        out[b, hf * 1024:(hf + 1) * 1024] = res.results[i]["out"]
    if dbg:
        kernel.last_debug = [{k: r[v] for k, v in bld.dbg_out.items()} for r in res.results]
    return out
```

```python
import numpy as np
import concourse.bass as bass
import concourse.mybir as mybir
from concourse.bass_utils import run_bass_kernel_spmd

F32 = mybir.dt.float32
BF16 = mybir.dt.bfloat16
AF = mybir.ActivationFunctionType
ALU = mybir.AluOpType
AX = mybir.AxisListType

D = 2048
SEQ = 2048
BATCH = 4
NH = 32
NKV = 4
DH = 64
DI = 4096
NSH = 64
NG = 8
DS = 128
DFF = 5632
PLE = 256
EPS = 1e-6
Q0 = 0
K0 = 2048
V0 = 2304
Z0 = 2560
XBC0 = 6656
DT0 = 12800
GA0 = 12864
GS0 = 14912
IN_DIM = 16960

NMC = 9
NT = NMC * 128
TP = 768
TM = 1280
NEGM = -30000.0


class Buf:
    __slots__ = ("name", "w", "r")

    def __init__(self, name, base=None):
        self.name = name
        self.w = None
        self.r = dict(base) if base else {}


class Sched:
    ENG = ("pe", "act", "dve", "pool", "sp")

    def __init__(self):
        self.streams = {e: [] for e in self.ENG}
        self.cnt = {e: 0 for e in self.ENG}
        self.dcnt = {}
        self.waited = {e: {} for e in self.ENG}
        self.dma_sems = []

    def new_dma_sem(self, name):
        self.dma_sems.append(name)
        self.dcnt[name] = 0
        return name

    def _waits(self, eng, reads, writes):
        deps = {}

        def add(s, v):
            if deps.get(s, 0) < v:
                deps[s] = v

        for b in reads:
            if b.w is not None:
                add(*b.w)
        for b in writes:
            if b.w is not None:
                add(*b.w)
            for s, v in b.r.items():
                add(s, v)
        wd = self.waited[eng]
        st = self.streams[eng]
        for s, v in deps.items():
            if wd.get(s, 0) >= v:
                continue
            wd[s] = v
            st.append(("wait", s, v))

    def op(self, eng, fns, reads=(), writes=()):
        self._waits(eng, reads, writes)
        self.cnt[eng] += 1
        c = self.cnt[eng]
        if not isinstance(fns, (list, tuple)):
            fns = [fns]
        st = self.streams[eng]
        for f in fns[:-1]:
            st.append(("inst", f, None, 0))
        st.append(("inst", fns[-1], eng, 1))
        for b in reads:
            b.r[eng] = c
        for b in writes:
            b.w = (eng, c)
            b.r = {}

    def dma(self, eng, fn, sem, reads=(), writes=()):
        self._waits(eng, reads, writes)
        self.dcnt[sem] += 16
        c = self.dcnt[sem]
        self.streams[eng].append(("inst", fn, sem, 16))
        for b in reads:
            b.r[sem] = c
        for b in writes:
            b.w = (sem, c)
            b.r = {}

    def final_wait(self, eng, bufs):
        self._waits(eng, bufs, bufs)


def collect_tokens(bufs):
    r = {}
    for b in bufs:
        if b.w is not None and r.get(b.w[0], 0) < b.w[1]:
            r[b.w[0]] = b.w[1]
        for s, v in b.r.items():
            if r.get(s, 0) < v:
                r[s] = v
    return r


class Region:
    def __init__(self, name, handle, nwords):
        self.name = name
        self.h = handle
        self.nwords = nwords
        self.off = 0
        self.bufs = []
        self.base = {}

    def recarve(self):
        self.base = collect_tokens(self.bufs)
        self.bufs = []
        self.off = 0

    def buf(self, name):
        b = Buf(name, self.base)
        self.bufs.append(b)
        return b

    def alloc(self, shape, dtype):
        nel = 1
        for s in shape:
            nel *= s
        nbytes = nel * (2 if dtype == BF16 else 4)
        nw = (nbytes + 3) // 4
        assert self.off + nw <= self.nwords, (self.name, self.off, nw, self.nwords)
        ap = self.h[:, self.off:self.off + nw]
        self.off += nw
        if dtype == BF16:
            ap = ap.bitcast(BF16)
            if nel != nw * 2:
                ap = ap[:, 0:nel]
        if len(shape) == 2:
            return ap.rearrange("p (a b) -> p a b", b=shape[1])
        if len(shape) == 3:
            return ap.rearrange("p (a b c) -> p a b c", b=shape[1], c=shape[2])
        return ap


class Builder:
    def __init__(self, debug=(), nphases=99):
        self.nphases = nphases
        self.debug = set(debug)
        self.nc = bass.Bass("TRN2", target_bir_lowering=False)
        self.S = Sched()
        self.dbg_out = {}

    def dram_in(self, name, shape):
        return self.nc.dram_tensor(name, list(shape), F32, kind="ExternalInput").ap()

    def bank(self):
        i = self.bank_i
        self.bank_i = (i + 1) % 8
        return self.pbuf[i], self.ps[i]

    def wslot(self):
        i = self.w_i
        self.w_i = (i + 1) % len(self.wt)
        return i

    def wload(self, srcs, kcn):
        i = self.wslot()
        tot = sum(s.shape[1] for s in srcs)
        assert kcn * tot <= 4096
        view = self.wt[i][:, 0:kcn * tot].rearrange("p (k c) -> p k c", c=tot)
        c0 = 0
        for s in srcs:
            n = s.shape[1]
            src = s.rearrange("(k p) e -> p k e", p=128)
            dst = view[:, :, c0:c0 + n]
            self.S.dma("pool", lambda e, d=dst, s_=src: e.dma_start(out=d, in_=s_), self.wsem[i],
                       reads=(), writes=(self.wbuf[i],))
            c0 += n
        return view, self.wbuf[i]

    def dump(self, name, ap, bufs, shape, dtype=F32):
        if name not in self.debug:
            return
        t = self.nc.dram_tensor("dbg_" + name, list(shape), dtype, kind="ExternalOutput").ap()
        sem = self.S.new_dma_sem("dbgsem_" + name)
        db = Buf("dbg_" + name)
        self.S.dma("sp", lambda e, t=t, ap=ap: e.dma_start(out=t, in_=ap), sem, reads=bufs, writes=(db,))
        self.final_bufs.append(db)
        self.dbg_out[name] = "dbg_" + name

    def build(self):
        nc = self.nc
        S = self.S
        d = {}
        d["xm"] = self.dram_in("xm", [NT, D])
        d["xp"] = self.dram_in("xp", [896, D])
        d["pp"] = self.dram_in("pp", [1024, PLE])
        d["flag"] = self.dram_in("flag", [128, 1])
        d["w_in"] = self.dram_in("w_in", [D, IN_DIM])
        d["w_attn_out"] = self.dram_in("w_attn_out", [D, D])
        d["w_ssm_out"] = self.dram_in("w_ssm_out", [DI, D])
        d["w_out"] = self.dram_in("w_out", [D, D])
        d["w_ffn_up"] = self.dram_in("w_ffn_up", [D, 2 * DFF])
        d["w_ffn_down"] = self.dram_in("w_ffn_down", [DFF, D])
        d["w_ple_gate"] = self.dram_in("w_ple_gate", [D, D])
        d["w_ple_proj"] = self.dram_in("w_ple_proj", [PLE, D])
        d["norm_mix_w"] = self.dram_in("norm_mix_w", [1, D])
        d["norm_ffn_w"] = self.dram_in("norm_ffn_w", [1, D])
        d["ple_norm_w"] = self.dram_in("ple_norm_w", [1, D])
        d["ssm_norm_w"] = self.dram_in("ssm_norm_w", [1, DI])
        d["q_norm_w"] = self.dram_in("q_norm_w", [1, DH])
        d["k_norm_w"] = self.dram_in("k_norm_w", [1, DH])
        d["attn_sinks"] = self.dram_in("attn_sinks", [1, NH])
        d["ssm_A_log"] = self.dram_in("ssm_A_log", [1, NSH])
        d["ssm_dt_bias"] = self.dram_in("ssm_dt_bias", [1, NSH])
        d["ssm_D"] = self.dram_in("ssm_D", [1, NSH])
        d["cw_ssm"] = self.dram_in("cw_ssm", [128, 48 * 4])
        d["cb_ssm"] = self.dram_in("cb_ssm", [128, 48])
        d["cw_ffn"] = self.dram_in("cw_ffn", [128, 88 * 3])
        d["cb_ffn"] = self.dram_in("cb_ffn", [128, 88])
        d["biasT"] = self.dram_in("biasT", [NKV, 128, 8 * 2 * 128])
        d["cm_f"] = self.dram_in("cm_f", [128, 256])
        d["cm_b"] = self.dram_in("cm_b", [128, 128 + 512 + 1024])
        self.d = d
        self.out = nc.dram_tensor("out", [1024, D], F32, kind="ExternalOutput").ap()
        self.final_bufs = []

        R1W, R2W, R3W, R4W, CW = 10240, 6144, 18432, 9216, 4200
        import contextlib
        with contextlib.ExitStack() as es:
            def sb(name, shape, dt):
                return es.enter_context(nc.sbuf_tensor(name, shape, dt))
            r1 = Region("R1", sb("R1", [128, R1W], F32), R1W)
            r2 = Region("R2", sb("R2", [128, R2W], F32), R2W)
            r3 = Region("R3", sb("R3", [128, R3W], F32), R3W)
            r4 = Region("R4", sb("R4", [128, R4W], F32), R4W)
            rc = Region("RC", sb("RC", [128, CW], F32), CW)
            self.r1, self.r2, self.r3, self.r4, self.rc = r1, r2, r3, r4, rc
            self.wt = [sb("wt%d" % i, [128, 4096], BF16) for i in range(2)]
            self.wbuf = [Buf("wbuf%d" % i) for i in range(2)]
            self.wsem = [S.new_dma_sem("wsem%d" % i) for i in range(2)]
            self.w_i = 0
            self.ps = [es.enter_context(nc.psum_tensor("ps%d" % i, [128, 512], F32)) for i in range(8)]
            self.pbuf = [Buf("psum%d" % i) for i in range(8)]
            self.bank_i = 0

            phases = [self.setup_consts, self.phase0, self.dt_proj, self.ssm_prefix, self.ssm_main, self.ssm_out,
                      self.attention, self.attn_out, self.wout_residual, self.ffn, self.ple]
            for ph in phases[:self.nphases]:
                ph()

            S.final_wait("sp", self.final_bufs)

            sem_names = list(Sched.ENG) + S.dma_sems
            sems = {}
            for n in sem_names:
                sems[n] = es.enter_context(nc.semaphore(n))
            block = es.enter_context(nc.Block())

            def replay(engname):
                def run(e):
                    for it in S.streams[engname]:
                        if it[0] == "wait":
                            e.wait_ge(sems[it[1]], it[2])
                        else:
                            ins = it[1](e)
                            if it[2] is not None:
                                ins.then_inc(sems[it[2]], it[3])
                return run

            block.sync(replay("sp"))
            block.gpsimd(replay("pool"))
            block.scalar(replay("act"))
            block.vector(replay("dve"))
            block.tensor(replay("pe"))
        return nc

    def setup_consts(self):
        S, d, rc = self.S, self.d, self.rc
        csem = S.new_dma_sem("csem")
        self.cb = Buf("consts")
        cb = self.cb

        def cload(dst, src):
            S.dma("sp", lambda e, dst=dst, src=src: e.dma_start(out=dst, in_=src), csem)

        cmf = rc.alloc([256], F32)
        cload(cmf, d["cm_f"])
        self.tri_f = cmf[:, 0:128]
        self.ones_f = cmf[:, 128:256]
        self.r3.recarve()
        cm = self.r3.alloc([128 + 512 + 1024], F32)
        cmb = self.r3.buf("cm_stage")
        S.dma("sp", lambda e: e.dma_start(out=cm, in_=d["cm_b"]), S.new_dma_sem("cmsem"), writes=(cmb,))
        self.ident_bf = rc.alloc([128], BF16)
        self.neg4 = rc.alloc([512], BF16)
        self.sel = rc.alloc([1024], BF16)
        self.nsel = rc.alloc([1024], BF16)
        self._cm = cm
        self.Ab = rc.alloc([64], F32)
        self.Db = rc.alloc([64], F32)
        self.dtb = rc.alloc([64], F32)
        self.flag = rc.alloc([1], F32)
        self.cw_ssm = rc.alloc([48, 4], F32)
        self.cb_ssm = rc.alloc([48], F32)
        self.cw_ffn = rc.alloc([88, 3], F32)
        self.cb_ffn = rc.alloc([88], F32)
        self.qg = rc.alloc([64], F32)
        self.kg = rc.alloc([64], F32)
        self.esink = rc.alloc([32], F32)
        self.dt_all = rc.alloc([16, 64], F32)
        self.ss16 = rc.alloc([16], F32)
        self.sd16 = rc.alloc([16], F32)
        self.rs16 = rc.alloc([16], F32)
        cload(self.Ab, d["ssm_A_log"].partition_broadcast(128))
        cload(self.Db, d["ssm_D"].partition_broadcast(128))
        cload(self.dtb, d["ssm_dt_bias"].partition_broadcast(128))
        cload(self.flag, d["flag"])
        cload(self.cw_ssm, d["cw_ssm"].rearrange("p (b k) -> p b k", k=4))
        cload(self.cb_ssm, d["cb_ssm"])
        cload(self.cw_ffn, d["cw_ffn"].rearrange("p (b k) -> p b k", k=3))
        cload(self.cb_ffn, d["cb_ffn"])
        cload(self.qg, d["q_norm_w"].partition_broadcast(128))
        cload(self.kg, d["k_norm_w"].partition_broadcast(128))
        cload(self.esink, d["attn_sinks"].partition_broadcast(128))
        cb.w = (csem, S.dcnt[csem])
        S.op("act", [
            lambda e: e.activation(out=self.ident_bf, in_=cm[:, 0:128], func=AF.Copy),
            lambda e: e.activation(out=self.neg4, in_=cm[:, 128:640], func=AF.Copy),
            lambda e: e.activation(out=self.sel, in_=cm[:, 640:1664], func=AF.Copy),
            lambda e: e.activation(out=self.nsel, in_=cm[:, 640:1664], func=AF.Copy, scale=-1.0),
            lambda e: e.activation(out=self.esink, in_=self.esink, func=AF.Exp),
            lambda e: e.activation(out=self.Ab, in_=self.Ab, func=AF.Exp),
        ], reads=(cb, cmb), writes=(cb,))
        S.op("dve", [
            lambda e: e.tensor_scalar(out=self.Ab, in0=self.Ab, scalar1=-1.0, scalar2=None, op0=ALU.mult),
            lambda e: e.tensor_scalar(out=self.qg, in0=self.qg, scalar1=0.125, scalar2=None, op0=ALU.mult),
            lambda e: e.tensor_scalar(out=self.cw_ssm, in0=self.cw_ssm, scalar1=0.5, scalar2=None, op0=ALU.mult),
            lambda e: e.tensor_scalar(out=self.cb_ssm, in0=self.cb_ssm, scalar1=0.5, scalar2=None, op0=ALU.mult),
        ], reads=(cb,), writes=(cb,))

    def hT(self, kc, t0, n):
        if t0 < TP:
            assert t0 + n <= TP
            return self.hTp[:, kc, t0:t0 + n]
        return self.hTm[:, kc, t0 - TP:t0 - TP + n]

    def hT_bufs(self, t0, n):
        c0 = t0 // 128
        c1 = (t0 + n - 1) // 128
        return [self.hTb[c] for c in range(c0, c1 + 1)]

    def rms_tile(self, x_ap, xbufs, gain_bc, gbuf, hb, hbbuf, sidx, scratch_bf, eps=EPS, n=D):
        S = self.S
        ssb = self.small_b[sidx]
        ss, sd, rs = self.ss16[:, sidx:sidx + 1], self.sd16[:, sidx:sidx + 1], self.rs16[:, sidx:sidx + 1]
        S.op("act", lambda e: e.activation(out=scratch_bf, in_=x_ap, func=AF.Square, accum_out=ss),
             reads=list(xbufs), writes=(ssb, hbbuf))
        S.op("act", lambda e: e.activation(out=sd, in_=ss, func=AF.Ln, bias=eps, scale=1.0 / n),
             reads=(ssb,), writes=(ssb,))
        S.op("act", lambda e: e.activation(out=rs, in_=sd, func=AF.Exp, scale=-0.5), reads=(ssb,), writes=(ssb,))
        S.op("dve", lambda e: e.scalar_tensor_tensor(out=hb, in0=x_ap, scalar=rs, in1=gain_bc,
                                                      op0=ALU.mult, op1=ALU.mult),
             reads=list(xbufs) + [ssb, gbuf], writes=(hbbuf,))

    def transpose_to(self, src_tiles, src_bufs, dst_ap_fn, dst_bufs, evac="act"):
        S = self.S
        n = len(src_tiles)
        pb, ps = self.bank()
        psb = ps[:].bitcast(BF16).rearrange("p (a b) -> p a b", b=128)
        fns = []
        for i, t in enumerate(src_tiles):
            fns.append(lambda e, i=i, t=t: e.transpose(out=psb[:, i, :], in_=t, identity=self.ident_bf))
        S.op("pe", fns, reads=list(src_bufs) + [self.cb], writes=(pb,))
        dst = dst_ap_fn(n)
        if evac == "act":
            S.op("act", lambda e: e.activation(out=dst, in_=psb[:, 0:n, :], func=AF.Copy),
                 reads=(pb,), writes=list(dst_bufs))
        else:
            S.op("dve", lambda e: e.tensor_copy(out=dst, in_=psb[:, 0:n, :]), reads=(pb,), writes=list(dst_bufs))

    def proj_fm(self, wview, wb, e0, kcn, act_fn, act_bufs, ranges):
        S = self.S
        banks = [self.bank() for _ in ranges]
        fns = []
        for kc in range(kcn):
            for (pb, ps), (t0, n) in zip(banks, ranges):
                fns.append(lambda e, kc=kc, ps=ps, t0=t0, n=n: e.matmul(
                    ps[:, 0:n], lhsT=wview[:, kc, e0:e0 + 128], rhs=act_fn(kc, t0, n),
                    start=(kc == 0), stop=(kc == kcn - 1)))
        S.op("pe", fns, reads=[wb] + list(act_bufs), writes=[pb for pb, _ in banks])
        return [(pb, ps[:, 0:n]) for (pb, ps), (t0, n) in zip(banks, ranges)]

    def proj_tm(self, wview, wb, c0, ncols, kcn, lhs_fn, act_bufs):
        S = self.S
        pb, ps = self.bank()
        fns = []
        for kc in range(kcn):
            fns.append(lambda e, kc=kc: e.matmul(ps[:, 0:ncols], lhsT=lhs_fn(kc), rhs=wview[:, kc, c0:c0 + ncols],
                                                 start=(kc == 0), stop=(kc == kcn - 1)))
        S.op("pe", fns, reads=[wb] + list(act_bufs), writes=(pb,))
        return pb, ps[:, 0:ncols]

    def phase0(self):
        S, d = self.S, self.d
        r1, r2, r4 = self.r1, self.r2, self.r4
        self.hTm = r1.alloc([16, TM], BF16)
        self.hTp = r2.alloc([16, TP], BF16)
        self.hTb = [Buf("hT%d" % c) for c in range(16)]
        r1.bufs += self.hTb[6:]
        r2.bufs += self.hTb[:6]
        self.small_b = [Buf("small%d" % i) for i in range(16)]
        r4.recarve()
        xt = [r4.alloc([D], F32) for _ in range(2)]
        xb = [r4.buf("xt%d" % i) for i in range(2)]
        xs = [S.new_dma_sem("xsem%d" % i) for i in range(2)]
        self.xt_sems = xs
        hb = [r4.alloc([D], BF16) for _ in range(2)]
        hbb = [r4.buf("hb%d" % i) for i in range(2)]
        gain = r4.alloc([D], F32)
        gb = r4.buf("gain")
        S.dma("sp", lambda e: e.dma_start(out=gain, in_=d["norm_mix_w"].partition_broadcast(128)), S.new_dma_sem("gsem0"), writes=(gb,))
        for ci in range(16):
            s = ci % 2
            src = d["xp"][ci * 128:(ci + 1) * 128, :] if ci < 7 else d["xm"][(ci - 7) * 128:(ci - 6) * 128, :]
            S.dma("sp", lambda e, s=s, src=src: e.dma_start(out=xt[s], in_=src), xs[s], writes=(xb[s],))
            self.rms_tile(xt[s], [xb[s]], gain, gb, hb[s], hbb[s], ci, hb[s])
            t0 = ci * 128
            for half in range(2):
                tiles = [hb[s][:, (half * 8 + i) * 128:(half * 8 + i + 1) * 128] for i in range(8)]
                self.transpose_to(tiles, [hbb[s]],
                                  lambda n, half=half, t0=t0: (self.hTp[:, half * 8:half * 8 + 8, t0:t0 + 128] if t0 < TP
                                                               else self.hTm[:, half * 8:half * 8 + 8, t0 - TP:t0 - TP + 128]),
                                  [self.hTb[ci]])
        self.dump("hTm", self.hTm, self.hTb[6:], [128, 16, TM], BF16)

    def dt_proj(self):
        S, d = self.S, self.d
        r4 = self.r4
        r4.recarve()
        raw = r4.alloc([16, 64], F32)
        rawb = r4.buf("dtraw")
        wv, wb = self.wload([d["w_in"][:, DT0:DT0 + 64]], 16)
        for ci in range(16):
            t0 = ci * 128
            pb, ps = self.proj_tm(wv, wb, 0, 64, 16, lambda kc, t0=t0: self.hT(kc, t0, 128), [self.hTb[ci]])
            S.op("dve", lambda e, ci=ci, ps=ps: e.tensor_tensor(out=raw[:, ci, :], in0=ps, in1=self.dtb, op=ALU.add),
                 reads=(pb, self.cb), writes=(rawb,))
        dtb_ = Buf("dt_all")
        self.dt_buf = dtb_
        S.op("act", lambda e: e.activation(out=raw, in_=raw, func=AF.Exp), reads=(rawb,), writes=(rawb,))
        S.op("act", lambda e: e.activation(out=self.dt_all, in_=raw, func=AF.Ln, bias=1.0, scale=1.0),
             reads=(rawb,), writes=(dtb_,))
        self.dump("dt_all", self.dt_all, [dtb_], [128, 16, 64])

    def conv_silu(self, pre, preb, n_out, blk, acc, accb, tt, ttb, out_bf, outb, lo=0):
        S = self.S
        w = self.cw_ssm
        S.op("dve", [
            lambda e: e.tensor_scalar(out=acc[:, 0:n_out], in0=pre[:, lo + 3:lo + 3 + n_out], scalar1=w[:, blk, 3:4],
                                      scalar2=self.cb_ssm[:, blk:blk + 1], op0=ALU.mult, op1=ALU.add)],
             reads=(preb, self.cb), writes=(accb,))
        for k in (2, 1, 0):
            S.op("dve", lambda e, k=k: e.scalar_tensor_tensor(out=acc[:, 0:n_out], in0=pre[:, lo + k:lo + k + n_out],
                                                             scalar=w[:, blk, k:k + 1], in1=acc[:, 0:n_out],
                                                             op0=ALU.mult, op1=ALU.add),
                 reads=(preb, self.cb, accb), writes=(accb,))
        S.op("act", lambda e: e.activation(out=tt[:, 0:n_out], in_=acc[:, 0:n_out], func=AF.Tanh),
             reads=(accb,), writes=(ttb,))
        S.op("dve", lambda e: e.scalar_tensor_tensor(out=out_bf, in0=tt[:, 0:n_out], scalar=1.0, in1=acc[:, 0:n_out],
                                                      op0=ALU.add, op1=ALU.mult),
             reads=(ttb, accb), writes=(outb,))

    def batch_smalls(self, reg, n, c0, g, suffix=False):
        S = self.S
        n8 = n * 8
        sm = {}
        for nm in ("a", "dwl", "w", "dtw"):
            sm[nm] = reg.alloc([n, 8], F32)
        s2 = reg.alloc([2, n8], F32)
        e2 = reg.alloc([2, n8], F32)
        smb = reg.buf("smalls")
        dt_g = self.dt_all[:, c0:c0 + n, g * 8:(g + 1) * 8]
        sm["dt"] = dt_g
        sm["acum"] = s2[:, 0, :].rearrange("p (c j) -> p c j", j=8)
        sm["atot"] = s2[:, 1, :].rearrange("p (c j) -> p c j", j=8)
        sm["eacum"] = e2[:, 0, :].rearrange("p (c j) -> p c j", j=8)
        sm["eatot"] = e2[:, 1, :].rearrange("p (c j) -> p c j", j=8)
        S.op("dve", lambda e: e.tensor_tensor(out=sm["a"], in0=dt_g,
                                              in1=self.Ab[:, g * 8:(g + 1) * 8].unsqueeze(1).to_broadcast([128, n, 8]), op=ALU.mult),
             reads=(self.dt_buf, self.cb), writes=(smb,))
        pb, ps = self.bank()
        a2 = sm["a"].rearrange("p c j -> p (c j)")
        S.op("pe", [
            lambda e: e.matmul(ps[:, 0:n8], lhsT=self.tri_f, rhs=a2, start=True, stop=True),
            lambda e: e.matmul(ps[:, 128:128 + n8], lhsT=self.ones_f, rhs=a2, start=True, stop=True),
        ], reads=(smb, self.cb), writes=(pb,))
        psv = ps[:, 0:256].rearrange("p (a b) -> p a b", b=128)[:, :, 0:n8]
        S.op("act", [
            lambda e: e.activation(out=s2, in_=psv, func=AF.Copy),
            lambda e: e.activation(out=e2, in_=psv, func=AF.Exp),
        ], reads=(pb,), writes=(smb,))
        S.op("dve", lambda e: e.tensor_tensor(out=sm["dwl"], in0=sm["atot"], in1=sm["acum"], op=ALU.subtract),
             reads=(smb,), writes=(smb,))
        if suffix:
            suf = reg.alloc([n, 8], F32)
            S.op("dve", lambda e: e.memset(suf[:, n - 1, :], 0.0), reads=(smb,), writes=(smb,))
            for c in range(n - 2, -1, -1):
                S.op("dve", lambda e, c=c: e.tensor_tensor(out=suf[:, c, :], in0=suf[:, c + 1, :], in1=sm["atot"][:, c + 1, :], op=ALU.add),
                     reads=(smb,), writes=(smb,))
            S.op("dve", lambda e: e.tensor_tensor(out=sm["dwl"], in0=sm["dwl"], in1=suf, op=ALU.add), reads=(smb,), writes=(smb,))
        if g == 0 and not suffix:
            self.dump("s2", s2, [smb], [128, 2, n8])
            self.dump("sma", sm["a"], [smb], [128, n, 8])
        S.op("act", lambda e: e.activation(out=sm["w"], in_=sm["dwl"], func=AF.Exp), reads=(smb,), writes=(smb,))
        S.op("dve", lambda e: e.tensor_tensor(out=sm["dtw"], in0=sm["w"], in1=dt_g, op=ALU.mult),
             reads=(smb, self.dt_buf), writes=(smb,))
        return sm, smb

    def ssm_prefix(self):
        S, d = self.S, self.d
        r3, r4 = self.r3, self.r4
        r3.recarve()
        self.stash = r3.h[:, :].rearrange("p (g x) -> p g x", g=8)
        self.yTb = [r3.buf("yT%d" % g) for g in range(8)]
        for g in range(NG):
            self._ssm_prefix_group(g)
        self.dump("stash", self.stash[:, :, 0:512], self.yTb, [128, 8, 512])

    def _ssm_prefix_group(self, g):
        S, d = self.S, self.d
        r3, r4 = self.r3, self.r4
        NP = 896
        ranges = [(0, 384), (384, 384), (768, 128)]
        abufs = self.hTb[0:7]
        if True:
            r4.recarve()
            pre2 = [r4.alloc([3 + NP], F32) for _ in range(2)]; pre2b = [r4.buf("pre%d" % i) for i in range(2)]
            acc = r4.alloc([NP], F32); accb = r4.buf("acc")
            tt = r4.alloc([NP], F32); ttb = r4.buf("tt")
            cvo = [r4.alloc([NP], BF16) for _ in range(2)]; cvob = [r4.buf("cvo%d" % i) for i in range(2)]
            xs_tm = r4.alloc([7, 512], BF16); xsb = r4.buf("xs_tm")
            B_tm = r4.alloc([7, 128], BF16); Btmb = r4.buf("B_tm")
            xw = r4.alloc([7, 512], BF16); xwb = r4.buf("xw")
            for i in range(2):
                S.op("dve", lambda e, i=i: e.memset(pre2[i][:, 0:3], 0.0), writes=(pre2b[i],))
            sm, smb = self.batch_smalls(r4, 7, 0, g, suffix=True)
            cols = [XBC0 + g * 512 + j * 128 for j in range(4)] + [XBC0 + DI + g * 128]
            pending = []
            wv = None
            for bi, c0 in enumerate(cols):
                blk = (c0 - XBC0) // 128
                if bi in (0, 2):
                    wv, wb = self.wload([d["w_in"][:, c0:c0 + 256]], 16)
                    e0 = 0
                elif bi == 4:
                    wv, wb = self.wload([d["w_in"][:, c0:c0 + 128]], 16)
                    e0 = 0
                else:
                    e0 = 128
                outs = self.proj_fm(wv, wb, e0, 16, self.hT, abufs, ranges)
                pre, preb = pre2[bi % 2], pre2b[bi % 2]
                for (pb, ps), (t0, n) in zip(outs, ranges):
                    S.op("act", lambda e, ps=ps, t0=t0, n=n, pre=pre: e.activation(out=pre[:, 3 + t0:3 + t0 + n], in_=ps, func=AF.Copy),
                         reads=(pb,), writes=(preb,))
                for f in pending:
                    f()
                pending = []
                cv, cvb = cvo[bi % 2], cvob[bi % 2]
                self.conv_silu(pre, preb, NP, blk, acc, accb, tt, ttb, cv, cvb)
                tiles = [cv[:, c * 128:(c + 1) * 128] for c in range(7)]
                if bi < 4:
                    pending.append(lambda tiles=tiles, cvb=cvb, bi=bi: self.transpose_to(
                        tiles, [cvb], lambda n: xs_tm[:, 0:7, bi * 128:(bi + 1) * 128], [xsb]))
                else:
                    pending.append(lambda tiles=tiles, cvb=cvb: self.transpose_to(tiles, [cvb], lambda n: B_tm[:, 0:7, :], [Btmb]))
            for f in pending:
                f()
            S.op("dve", lambda e: e.tensor_tensor(out=xw.rearrange("p c (j q) -> p c j q", q=64),
                                                  in0=xs_tm.rearrange("p c (j q) -> p c j q", q=64),
                                                  in1=sm["dtw"].unsqueeze(3).to_broadcast([128, 7, 8, 64]), op=ALU.mult),
                 reads=(xsb, smb), writes=(xwb,))
            pb, ps = self.bank()
            fns = [lambda e, c=c, ps=ps: e.matmul(ps[:, 0:512], lhsT=B_tm[:, c, :], rhs=xw[:, c, :], start=(c == 0), stop=(c == 6))
                   for c in range(7)]
            S.op("pe", fns, reads=(Btmb, xwb), writes=(pb,))
            S.op("act", lambda e, g=g, ps=ps: e.activation(out=self.stash[:, g, 0:512], in_=ps[:, 0:512], func=AF.Copy),
                 reads=(pb,), writes=(self.yTb[g],))

    def ssm_main(self):
        S, d = self.S, self.d
        r2, r3, r4 = self.r2, self.r3, self.r4
        yT = r3.h[:, :].bitcast(BF16).rearrange("p (k t) -> p k t", t=NT)
        self.yT = yT
        self.nwsem = S.new_dma_sem("nwsem")
        for g in range(NG):
            self._ssm_main_group(g)
        self.dump("yT", yT, self.yTb, [128, 32, NT], BF16)

    def _ssm_main_group(self, g):
        S, d = self.S, self.d
        r2, r3, r4 = self.r2, self.r3, self.r4
        yT = self.yT
        NPRE = NT + 3
        nsem = self.nwsem
        ranges = [(893 + i * 385, 385) for i in range(3)]
        abufs = self.hTb[6:16]
        if True:
            r2.recarve(); r4.recarve()
            pre2 = [r4.alloc([NPRE + 1], BF16) for _ in range(2)]; pre2b = [r4.buf("pre%d" % i) for i in range(2)]
            acc = r4.alloc([384], F32); accb = r4.buf("acc")
            tt = r4.alloc([384], F32); ttb = r4.buf("tt")
            cvo = [r2.alloc([384], BF16) for _ in range(2)]; cvob = [r2.buf("cvo%d" % i) for i in range(2)]
            BT = r4.alloc([NT], BF16); BTb = r4.buf("BT")
            CT = r4.alloc([NT], BF16); CTb = r4.buf("CT")
            xs_tm = r4.alloc([NMC, 512], BF16); xsb = r4.buf("xs_tm")
            B_tm = r4.alloc([NMC, 128], BF16); Btmb = r4.buf("B_tm")
            zs_all = r4.alloc([NMC, 512], BF16); zsab = r4.buf("zs_all")
            ztmp = r4.alloc([256], F32); ztb = r4.buf("ztmp")
            hiT = r2.alloc([NMC, 128], BF16); loT = r4.alloc([NMC, 128], BF16); hib = r4.buf("hiloT")
            normw = r2.alloc([512], F32); nwb = r2.buf("normw")
            Scur = r2.alloc([512], F32); Sb = r2.buf("Scur")
            _x = r2.alloc([512], BF16); _xb = r2.buf("xdt"); xdt = [_x, _x]; xdtb = [_xb, _xb]
            _x2 = r2.alloc([512], BF16); _x2b = r2.buf("xw"); xw = [_x2, _x2]; xwb = [_x2b, _x2b]
            Sbf2 = [r2.alloc([512], BF16) for _ in range(2)]; Sbfb2 = [r2.buf("Sbf%d" % i) for i in range(2)]
            _seg = r2.alloc([8, 128], BF16); _segb = r2.buf("segT"); segT = [_seg, _seg]; segb = [_segb, _segb]
            _mt = r2.alloc([8, 128], BF16); _mtb = r2.buf("MT"); MT = [_mt, _mt]; MTb = [_mtb, _mtb]
            _c = r2.alloc([128], BF16); _cb = r2.buf("cbT"); cbT = [_c, _c]; cbTb = [_cb, _cb]
            t1 = r2.alloc([512], F32); t1b = r2.buf("t1")
            yy = r2.alloc([512], F32); yb = r2.buf("y")
            yn = r2.alloc([512], BF16); ynb = r2.buf("yn")
            ysm = r2.alloc([4], F32); ysmb = r2.buf("ysm")
            hl8 = r2.alloc([2, NMC * 8], BF16); hl8b = r2.buf("hl8")
            S.op("act", lambda e, g=g: e.activation(out=Scur, in_=self.stash[:, g, 0:512], func=AF.Copy),
                 reads=(self.yTb[g],), writes=(Sb,))
            S.dma("sp", lambda e, g=g: e.dma_start(out=normw, in_=d["ssm_norm_w"][:, g * 512:(g + 1) * 512].partition_broadcast(128)),
                  nsem, writes=(nwb,))
            sm, smb = self.batch_smalls(r2, NMC, 7, g)
            acf = sm["acum"].rearrange("p c j -> p (c j)")
            S.op("dve", lambda e: e.tensor_copy(out=hl8[:, 0, :], in_=acf), reads=(smb,), writes=(hl8b,))
            S.op("dve", lambda e: e.tensor_tensor(out=hl8[:, 1, :], in0=acf, in1=hl8[:, 0, :], op=ALU.subtract),
                 reads=(smb, hl8b), writes=(hl8b,))
            if g == 0:
                self.dump("hl8", hl8, [hl8b], [128, 2, NMC * 8], BF16)
            tb = [self.bank() for _ in range(3)]
            tv = [p[1][:].bitcast(BF16).rearrange("p (a b) -> p a b", b=128) for p in tb]
            fns = []
            for c in range(NMC):
                for hl in range(2):
                    if c < 8:
                        dst = tv[hl][0:8, c, :]
                    else:
                        dst = tv[2][0:8, hl, :]
                    fns.append(lambda e, c=c, hl=hl, dst=dst: e.transpose(out=dst, in_=hl8[:, hl, c * 8:(c + 1) * 8], identity=self.ident_bf))
            S.op("pe", fns, reads=(hl8b, self.cb), writes=[p[0] for p in tb])
            S.op("act", [
                lambda e: e.activation(out=hiT[0:8, 0:8, :], in_=tv[0][0:8, 0:8, :], func=AF.Copy),
                lambda e: e.activation(out=loT[0:8, 0:8, :], in_=tv[1][0:8, 0:8, :], func=AF.Copy),
                lambda e: e.activation(out=hiT[0:8, 8, :], in_=tv[2][0:8, 0, :], func=AF.Copy),
                lambda e: e.activation(out=loT[0:8, 8, :], in_=tv[2][0:8, 1, :], func=AF.Copy),
            ], reads=[p[0] for p in tb], writes=(hib,))
            cols = [XBC0 + g * 512 + j * 128 for j in range(4)] + [XBC0 + DI + g * 128, XBC0 + DI + 1024 + g * 128]
            pending = []
            for bi, c0 in enumerate(cols):
                blk = (c0 - XBC0) // 128
                if bi in (0, 2):
                    wv, wb = self.wload([d["w_in"][:, c0:c0 + 256]], 16)
                    e0 = 0
                elif bi == 4:
                    wv, wb = self.wload([d["w_in"][:, c0:c0 + 128], d["w_in"][:, cols[5]:cols[5] + 128]], 16)
                    e0 = 0
                else:
                    e0 = 128
                outs = self.proj_fm(wv, wb, e0, 16, self.hT, abufs, ranges)
                pre, preb = pre2[bi % 2], pre2b[bi % 2]
                for i, (pb, ps) in enumerate(outs):
                    S.op("act", lambda e, ps=ps, i=i, pre=pre: e.activation(out=pre[:, i * 385:(i + 1) * 385], in_=ps, func=AF.Copy),
                         reads=(pb,), writes=(preb,))
                if bi in (1, 3):
                    hv = bi // 2
                    zc0 = Z0 + g * 512 + hv * 256
                    wvz, wbz = self.wload([d["w_in"][:, zc0:zc0 + 256]], 16)
                    for mc in range(NMC):
                        t0z = 896 + mc * 128
                        pbz, psz = self.proj_tm(wvz, wbz, 0, 256, 16, lambda kc, t0z=t0z: self.hT(kc, t0z, 128), [self.hTb[7 + mc]])
                        S.op("act", lambda e, psz=psz: e.activation(out=ztmp, in_=psz, func=AF.Tanh, scale=0.5), reads=(pbz,), writes=(ztb,))
                        S.op("dve", lambda e, psz=psz, mc=mc, hv=hv: e.scalar_tensor_tensor(
                            out=zs_all[:, mc, hv * 256:(hv + 1) * 256], in0=ztmp, scalar=1.0, in1=psz, op0=ALU.add, op1=ALU.mult),
                            reads=(ztb, pbz), writes=(zsab,))
                for th in range(3):
                    for f in pending:
                        f()
                    pending = []
                    cv, cvb = cvo[th % 2], cvob[th % 2]
                    if bi < 4:
                        dst, dstb = cv, cvb
                    elif bi == 4:
                        dst, dstb = BT[:, th * 384:(th + 1) * 384], BTb
                    else:
                        dst, dstb = CT[:, th * 384:(th + 1) * 384], CTb
                    self.conv_silu(pre, preb, 384, blk, acc, accb, tt, ttb, dst, dstb, lo=th * 384)
                    if bi < 4:
                        tiles = [cv[:, c * 128:(c + 1) * 128] for c in range(3)]
                        pending.append(lambda tiles=tiles, cvb=cvb, bi=bi, th=th: self.transpose_to(
                            tiles, [cvb], lambda n: xs_tm[:, th * 3:th * 3 + 3, bi * 128:(bi + 1) * 128], [xsb]))
                    elif bi == 4:
                        tiles = [BT[:, (th * 3 + c) * 128:(th * 3 + c + 1) * 128] for c in range(3)]
                        pending.append(lambda tiles=tiles, th=th: self.transpose_to(
                            tiles, [BTb], lambda n: B_tm[:, th * 3:th * 3 + 3, :], [Btmb]))
            for f in pending:
                f()
            Dg = self.Db[:, g * 8:(g + 1) * 8]
            PB, PS = self.pbuf, self.ps

            def f_pe1(mc):
                k = mc % 2
                tc = slice(mc * 128, (mc + 1) * 128)
                t0 = 896 + mc * 128
                fns = []
                for bq in range(2):
                    psq = PS[bq]
                    fns.append(lambda e, psq=psq: e.matmul(psq[:, 0:512], lhsT=self.ident_bf, rhs=self.neg4, start=True, stop=False))
                    for jj in range(4):
                        j = bq * 4 + jj
                        for src in (hiT, loT):
                            fns.append(lambda e, psq=psq, jj=jj, j=j, src=src: e.matmul(
                                psq[:, jj * 128:(jj + 1) * 128], lhsT=self.sel[0:8, j * 128:(j + 1) * 128], rhs=src[0:8, mc, :],
                                start=False, stop=False))
                    for si, src in enumerate((hiT, loT)):
                        fns.append(lambda e, psq=psq, bq=bq, src=src, si=si: e.matmul(
                            psq[:, 0:512], lhsT=src[0:8, mc, :], rhs=self.nsel[0:8, bq * 512:(bq + 1) * 512],
                            start=False, stop=(si == 1)))
                S.op("pe", fns, reads=(hib, self.cb), writes=[PB[0], PB[1]])
                S.op("pe", lambda e: e.matmul(PS[2][:, 0:128], lhsT=BT[:, tc], rhs=CT[:, tc], start=True, stop=True),
                     reads=(BTb, CTb), writes=(PB[2],))
                S.op("act", [lambda e, bq=bq: e.activation(out=segT[k][:, bq * 4:(bq + 1) * 4, :].rearrange("p a b -> p (a b)"),
                                                           in_=PS[bq][:, 0:512], func=AF.Exp) for bq in range(2)],
                     reads=[PB[0], PB[1]], writes=(segb[k],))
                S.op("act", lambda e: e.activation(out=cbT[k], in_=PS[2][:, 0:128], func=AF.Copy), reads=(PB[2],), writes=(cbTb[k],))

            def f_dve(mc):
                k = mc % 2
                xs_c = xs_tm[:, mc, :]
                S.op("dve", lambda e: e.tensor_tensor(out=MT[k], in0=segT[k], in1=cbT[k].unsqueeze(1).to_broadcast([128, 8, 128]), op=ALU.mult),
                     reads=(segb[k], cbTb[k]), writes=(MTb[k],))
                S.op("dve", lambda e: e.tensor_tensor(out=xdt[k].rearrange("p (j q) -> p j q", q=64),
                                                      in0=xs_c.rearrange("p (j q) -> p j q", q=64),
                                                      in1=sm["dt"][:, mc, :].unsqueeze(2).to_broadcast([128, 8, 64]), op=ALU.mult),
                     reads=(xsb, self.dt_buf), writes=(xdtb[k],))
                S.op("dve", lambda e: e.tensor_tensor(out=xw[k].rearrange("p (j q) -> p j q", q=64),
                                                      in0=xs_c.rearrange("p (j q) -> p j q", q=64),
                                                      in1=sm["dtw"][:, mc, :].unsqueeze(2).to_broadcast([128, 8, 64]), op=ALU.mult),
                     reads=(xsb, smb), writes=(xwb[k],))
                S.op("pe", [lambda e, j=j: e.matmul(PS[3][:, j * 64:(j + 1) * 64], lhsT=MT[k][:, j, :], rhs=xdt[k][:, j * 64:(j + 1) * 64],
                                                    start=True, stop=True) for j in range(8)],
                     reads=(MTb[k], xdtb[k]), writes=(PB[3],))
                S.op("pe", lambda e: e.matmul(PS[4][:, 0:512], lhsT=B_tm[:, mc, :], rhs=xw[k], start=True, stop=True),
                     reads=(Btmb, xwb[k]), writes=(PB[4],))

            def b_a(mc):
                k = mc % 2
                tc = slice(mc * 128, (mc + 1) * 128)
                xs_c = xs_tm[:, mc, :]
                if mc == 0:
                    S.op("act", lambda e: e.activation(out=Sbf2[0], in_=Scur, func=AF.Copy), reads=(Sb,), writes=(Sbfb2[0],))
                S.op("pe", lambda e: e.matmul(PS[5][:, 0:512], lhsT=CT[:, tc], rhs=Sbf2[k], start=True, stop=True),
                     reads=(CTb, Sbfb2[k]), writes=(PB[5],))
                S.op("dve", lambda e: e.tensor_tensor(out=Scur.rearrange("p (j q) -> p j q", q=64),
                                                      in0=Scur.rearrange("p (j q) -> p j q", q=64),
                                                      in1=sm["eatot"][:, mc, :].unsqueeze(2).to_broadcast([128, 8, 64]), op=ALU.mult),
                     reads=(Sb, smb), writes=(Sb,))
                S.op("dve", lambda e: e.tensor_tensor(out=Scur, in0=Scur, in1=PS[4][:, 0:512], op=ALU.add),
                     reads=(Sb, PB[4]), writes=(Sb,))
                if mc == 0:
                    S.op("dve", lambda e: e.tensor_scalar(out=Scur, in0=Scur, scalar1=self.flag[:, 0:1], scalar2=None, op0=ALU.mult),
                         reads=(Sb, self.cb), writes=(Sb,))
                if mc + 1 < NMC:
                    S.op("act", lambda e: e.activation(out=Sbf2[1 - k], in_=Scur, func=AF.Copy), reads=(Sb,), writes=(Sbfb2[1 - k],))
                S.op("dve", lambda e: e.tensor_tensor(out=t1.rearrange("p (j q) -> p j q", q=64),
                                                      in0=PS[5][:, 0:512].rearrange("p (j q) -> p j q", q=64),
                                                      in1=sm["eacum"][:, mc, :].unsqueeze(2).to_broadcast([128, 8, 64]), op=ALU.mult),
                     reads=(PB[5], smb), writes=(t1b,))
                S.op("dve", lambda e: e.tensor_tensor(out=yy, in0=t1, in1=PS[3][:, 0:512], op=ALU.add),
                     reads=(t1b, PB[3]), writes=(yb,))
                S.op("dve", lambda e: e.tensor_tensor(out=t1.rearrange("p (j q) -> p j q", q=64),
                                                      in0=xs_c.rearrange("p (j q) -> p j q", q=64),
                                                      in1=Dg.unsqueeze(2).to_broadcast([128, 8, 64]), op=ALU.mult),
                     reads=(xsb, self.cb, yb), writes=(t1b,))
                S.op("dve", lambda e: e.tensor_tensor(out=yy, in0=yy, in1=t1, op=ALU.add), reads=(t1b, yb), writes=(yb,))
                S.op("dve", lambda e: e.tensor_tensor(out=yy, in0=yy, in1=zs_all[:, mc, :], op=ALU.mult), reads=(yb, zsab), writes=(yb,))

            def b_c(mc):
                S.op("act", lambda e: e.activation(out=yn, in_=yy, func=AF.Square, accum_out=ysm[:, 0:1]), reads=(yb,), writes=(ynb, ysmb))
                S.op("act", lambda e: e.activation(out=ysm[:, 1:2], in_=ysm[:, 0:1], func=AF.Ln, bias=4 * EPS, scale=1.0 / 512),
                     reads=(ysmb,), writes=(ysmb,))
                S.op("act", lambda e: e.activation(out=ysm[:, 2:3], in_=ysm[:, 1:2], func=AF.Exp, scale=-0.5), reads=(ysmb,), writes=(ysmb,))

            def b_e(mc):
                tc = slice(mc * 128, (mc + 1) * 128)
                S.op("dve", lambda e: e.scalar_tensor_tensor(out=yn, in0=yy, scalar=ysm[:, 2:3], in1=normw, op0=ALU.mult, op1=ALU.mult),
                     reads=(yb, ysmb, nwb), writes=(ynb,))
                psb = PS[2][:].bitcast(BF16).rearrange("p (a b) -> p a b", b=128)
                S.op("pe", [lambda e, i=i: e.transpose(out=psb[:, i, :], in_=yn[:, i * 128:(i + 1) * 128], identity=self.ident_bf)
                            for i in range(4)], reads=(ynb, self.cb), writes=(PB[2],))
                S.op("act", lambda e: e.activation(out=yT[:, 4 * g:4 * g + 4, tc], in_=psb[:, 0:4, :], func=AF.Copy),
                     reads=(PB[2],), writes=(self.yTb[g],))

            f_pe1(0)
            f_dve(0)
            for mc in range(NMC):
                b_a(mc)
                if mc + 1 < NMC:
                    f_pe1(mc + 1)
                b_c(mc)
                if mc + 1 < NMC:
                    f_dve(mc + 1)
                b_e(mc)

    def ssm_out(self):
        S, d = self.S, self.d
        r2, r4 = self.r2, self.r4
        r2.recarve(); r4.recarve()
        self.mixedT = r4.alloc([16, NT], BF16)
        self.mixb = [r4.buf("mixedT%d" % i) for i in range(16)]
        tg = r2.alloc([3, 384], F32); tgb = r2.buf("tg")
        self.tg, self.tgb = tg, tgb
        ranges = [(896 + i * 384, 384) for i in range(3)]
        mainbufs = self.hTb[7:16]
        yT = self.yT
        for db in range(16):
            if db % 2 == 0:
                wvg, wbg = self.wload([d["w_in"][:, GS0 + db * 128:GS0 + db * 128 + 256]], 16)
            outs = self.proj_fm(wvg, wbg, (db % 2) * 128, 16, self.hT, mainbufs, ranges)
            for i, (pb, ps) in enumerate(outs):
                S.op("act", lambda e, ps=ps, i=i: e.activation(out=tg[:, i, :], in_=ps, func=AF.Tanh, scale=0.5),
                     reads=(pb,), writes=(tgb,))
            wv, wb = self.wload([d["w_ssm_out"][:, db * 128:(db + 1) * 128]], 32)
            outs = self.proj_fm(wv, wb, 0, 32, lambda kc, t0, n: yT[:, kc, t0 - 896:t0 - 896 + n], self.yTb, ranges)
            for i, (pb, ps) in enumerate(outs):
                S.op("dve", lambda e, ps=ps, i=i, db=db: e.scalar_tensor_tensor(
                    out=self.mixedT[:, db, i * 384:(i + 1) * 384], in0=tg[:, i, :], scalar=1.0, in1=ps, op0=ALU.add, op1=ALU.mult),
                    reads=(tgb, pb), writes=(self.mixb[db],))
        self.dump("mixS", self.mixedT, self.mixb, [128, 16, NT], BF16)

    def attention(self):
        S, d = self.S, self.d
        r2, r3 = self.r2, self.r3
        r2.recarve(); r3.recarve()
        self.aoT = r3.alloc([16, NT], BF16)
        self.aoTb = [r3.buf("aoT%d" % i) for i in range(NKV)]
        qT = r3.alloc([4, NT], BF16); qTb = r3.buf("qT")
        kT2 = r3.alloc([TM], BF16); kTb = r3.buf("kT2")
        v1 = r3.alloc([10, 65], BF16); v1b = r3.buf("v1")
        biasT = r3.alloc([8, 2, 128], F32); biasb = r3.buf("biasT")
        q32 = r3.alloc([512], F32); q32b = r3.buf("q32")
        qtmp = r3.alloc([512], F32); qtmpb = r3.buf("qtmp")
        qn = r3.alloc([512], BF16); qnb = r3.buf("qn")
        kv32 = r3.alloc([128], F32); kvb = r3.buf("kv32")
        ktmp = r3.alloc([64], F32); ktmpb = r3.buf("ktmp")
        kdup = r3.alloc([2, 64], BF16); kdupb = r3.buf("kdup")
        qs = r3.alloc([32], F32); qsb = r3.buf("qs")
        ltmp = r2.alloc([512], F32); ltb = r2.buf("ltmp")
        eT = [r2.alloc([2, 2, 128], BF16) for _ in range(2)]; eTb = [r2.buf("eT%d" % i) for i in range(2)]
        den = r2.alloc([16], F32); denb = r2.buf("den")
        ao = r2.alloc([8, 64], BF16); aob = r2.buf("ao")
        bsem = S.new_dma_sem("biassem")
        S.op("dve", lambda e: e.memset(v1[:, :, 64:65], 1.0), writes=(v1b,))
        for kg in range(NKV):
            S.dma("sp", lambda e, kg=kg: e.dma_start(out=biasT, in_=d["biasT"][kg].rearrange("p (h b q) -> p h b q", h=8, b=2)),
                  bsem, writes=(biasb,))
            wv, wb = self.wload([d["w_in"][:, K0 + kg * 64:K0 + kg * 64 + 64], d["w_in"][:, V0 + kg * 64:V0 + kg * 64 + 64]], 16)
            for cj in range(10):
                t0 = 768 + cj * 128
                pb, ps = self.proj_tm(wv, wb, 0, 128, 16, lambda kc, t0=t0: self.hT(kc, t0, 128), [self.hTb[6 + cj]])
                S.op("act", lambda e, ps=ps: e.activation(out=kv32, in_=ps, func=AF.Copy), reads=(pb,), writes=(kvb,))
                S.op("act", lambda e: e.activation(out=ktmp, in_=kv32[:, 0:64], func=AF.Square, accum_out=qs[:, 0:1]),
                     reads=(kvb,), writes=(ktmpb, qsb))
                S.op("act", lambda e: e.activation(out=qs[:, 1:2], in_=qs[:, 0:1], func=AF.Ln, bias=EPS, scale=1.0 / 64),
                     reads=(qsb,), writes=(qsb,))
                S.op("act", lambda e: e.activation(out=qs[:, 2:3], in_=qs[:, 1:2], func=AF.Exp, scale=-0.5), reads=(qsb,), writes=(qsb,))
                S.op("dve", [
                    lambda e: e.scalar_tensor_tensor(out=kdup[:, 0, :], in0=kv32[:, 0:64], scalar=qs[:, 2:3], in1=self.kg,
                                                     op0=ALU.mult, op1=ALU.mult),
                    lambda e: e.scalar_tensor_tensor(out=kdup[:, 1, :], in0=kv32[:, 0:64], scalar=qs[:, 2:3], in1=self.kg,
                                                     op0=ALU.mult, op1=ALU.mult),
                ], reads=(kvb, qsb, self.cb), writes=(kdupb,))
                S.op("act", lambda e, cj=cj: e.activation(out=v1[:, cj, 0:64], in_=kv32[:, 64:128], func=AF.Copy),
                     reads=(kvb,), writes=(v1b,))
                self.transpose_to([kdup.rearrange("p a b -> p (a b)")], [kdupb],
                                  lambda n, cj=cj: kT2[:, cj * 128:(cj + 1) * 128].unsqueeze(1), [kTb])
            wvs = []
            for hv in range(2):
                c0 = Q0 + kg * 512 + hv * 256
                wvs.append(self.wload([d["w_in"][:, c0:c0 + 256]], 16))
            for mc in range(NMC):
                t0 = 896 + mc * 128
                for hv in range(2):
                    pb, ps = self.proj_tm(wvs[hv][0], wvs[hv][1], 0, 256, 16, lambda kc, t0=t0: self.hT(kc, t0, 128), [self.hTb[7 + mc]])
                    S.op("act", lambda e, ps=ps, hv=hv: e.activation(out=q32[:, hv * 256:(hv + 1) * 256], in_=ps, func=AF.Copy),
                         reads=(pb,), writes=(q32b,))
                S.op("dve", lambda e: e.tensor_tensor(out=qtmp, in0=q32, in1=q32, op=ALU.mult), reads=(q32b,), writes=(qtmpb,))
                S.op("dve", lambda e: e.tensor_reduce(out=qs[:, 8:16], in_=qtmp.rearrange("p (h x) -> p h x", x=64), axis=AX.X, op=ALU.add),
                     reads=(qtmpb,), writes=(qsb,))
                S.op("act", lambda e: e.activation(out=qs[:, 16:24], in_=qs[:, 8:16], func=AF.Ln, bias=EPS, scale=1.0 / 64),
                     reads=(qsb,), writes=(qsb,))
                S.op("act", lambda e: e.activation(out=qs[:, 24:32], in_=qs[:, 16:24], func=AF.Exp, scale=-0.5), reads=(qsb,), writes=(qsb,))
                S.op("dve", lambda e: e.tensor_tensor(out=qtmp.rearrange("p (h x) -> p h x", x=64), in0=q32.rearrange("p (h x) -> p h x", x=64),
                                                      in1=qs[:, 24:32].unsqueeze(2).to_broadcast([128, 8, 64]), op=ALU.mult),
                     reads=(q32b, qsb), writes=(qtmpb,))
                S.op("dve", lambda e: e.tensor_tensor(out=qn.rearrange("p (h x) -> p h x", x=64), in0=qtmp.rearrange("p (h x) -> p h x", x=64),
                                                      in1=self.qg.unsqueeze(1).to_broadcast([128, 8, 64]), op=ALU.mult),
                     reads=(qtmpb, self.cb), writes=(qnb,))
                tiles = [qn[:, i * 128:(i + 1) * 128] for i in range(4)]
                self.transpose_to(tiles, [qnb], lambda n, mc=mc: qT[:, 0:4, mc * 128:(mc + 1) * 128], [qTb])
            for mc in range(NMC):
                pvb = [self.bank() for _ in range(2)]
                for qp in range(2):
                    lb = [self.bank() for _ in range(2)]
                    fns = []
                    for hh in range(2):
                        psv = lb[hh][1][:, 0:512].rearrange("p (a b q) -> p a b q", a=2, b=2)
                        for qq in range(2):
                            qt = qp * 2 + qq
                            for blk in range(2):
                                cj = mc + blk
                                fns.append(lambda e, hh=hh, blk=blk, cj=cj, psv=psv, qt=qt, qq=qq, mc=mc: e.matmul(
                                    psv[:, qq, blk, :], lhsT=kT2[hh * 64:(hh + 1) * 64, cj * 128:(cj + 1) * 128],
                                    rhs=qT[hh * 64:(hh + 1) * 64, qt, mc * 128:(mc + 1) * 128], start=True, stop=True))
                    S.op("pe", fns, reads=(kTb, qTb), writes=[lb[0][0], lb[1][0]])
                    for hh in range(2):
                        pb, ps = lb[hh]
                        psv = ps[:, 0:512].rearrange("p (a b q) -> p a b q", a=2, b=2)
                        h0 = qp * 4 + hh
                        S.op("dve", lambda e, psv=psv, h0=h0: e.tensor_tensor(out=ltmp.rearrange("p (a b q) -> p a b q", a=2, b=2), in0=psv,
                                                                             in1=biasT[:, h0:h0 + 3:2, :, :], op=ALU.add),
                             reads=(pb, biasb), writes=(ltb,))
                        S.op("act", lambda e, hh=hh: e.activation(out=eT[hh].rearrange("p a b q -> p (a b q)"), in_=ltmp, func=AF.Exp),
                             reads=(ltb,), writes=(eTb[hh],))
                        if mc == 1:
                            S.op("dve", lambda e, hh=hh: e.tensor_scalar(out=eT[hh][:, :, 0, :], in0=eT[hh][:, :, 0, :],
                                                                         scalar1=self.flag[:, 0:1], scalar2=None, op0=ALU.mult),
                                 reads=(eTb[hh], self.cb), writes=(eTb[hh],))
                        fns = []
                        for qq in range(2):
                            h8 = h0 + 2 * qq
                            pvp, pvs = pvb[h8 // 4]
                            slot = h8 % 4
                            for blk in range(2):
                                cj = mc + blk
                                fns.append(lambda e, hh=hh, qq=qq, blk=blk, cj=cj, pvs=pvs, slot=slot: e.matmul(
                                    pvs[:, slot * 65:(slot + 1) * 65], lhsT=eT[hh][:, qq, blk, :], rhs=v1[:, cj, :],
                                    start=(blk == 0), stop=(blk == 1)))
                        S.op("pe", fns, reads=(eTb[hh], v1b), writes=[pvb[qp][0]])
                for hb_ in range(2):
                    pvp, pvs = pvb[hb_]
                    pv3 = pvs[:, 0:260].rearrange("p (s x) -> p s x", x=65)
                    S.op("dve", lambda e, pv3=pv3, hb_=hb_, kg=kg: e.tensor_tensor(
                        out=den[:, hb_ * 4:hb_ * 4 + 4].unsqueeze(2), in0=pv3[:, :, 64:65],
                        in1=self.esink[:, kg * 8 + hb_ * 4:kg * 8 + hb_ * 4 + 4].unsqueeze(2), op=ALU.add),
                        reads=(pvp, self.cb), writes=(denb,))
                    S.op("dve", lambda e, hb_=hb_: e.reciprocal(out=den[:, 8 + hb_ * 4:8 + hb_ * 4 + 4], in_=den[:, hb_ * 4:hb_ * 4 + 4]),
                         reads=(denb,), writes=(denb,))
                    S.op("dve", lambda e, pv3=pv3, hb_=hb_: e.tensor_tensor(
                        out=ao[:, hb_ * 4:hb_ * 4 + 4, :], in0=pv3[:, :, 0:64],
                        in1=den[:, 8 + hb_ * 4:8 + hb_ * 4 + 4].unsqueeze(2).to_broadcast([128, 4, 64]), op=ALU.mult),
                        reads=(pvp, denb), writes=(aob,))
                aof = ao.rearrange("p h x -> p (h x)")
                tiles = [aof[:, i * 128:(i + 1) * 128] for i in range(4)]
                self.transpose_to(tiles, [aob], lambda n, kg=kg, mc=mc: self.aoT[:, kg * 4:kg * 4 + 4, mc * 128:(mc + 1) * 128],
                                  [self.aoTb[kg]])
        self.dump("aoT", self.aoT, self.aoTb, [128, 16, NT], BF16)

    def attn_out(self):
        S, d = self.S, self.d
        r2 = self.r2
        r2.recarve()
        tg = r2.alloc([3, 384], F32); tgb = r2.buf("tg")
        mt = r2.alloc([384], F32); mtb = r2.buf("mtmp")
        ranges = [(896 + i * 384, 384) for i in range(3)]
        mainbufs = self.hTb[7:16]
        for db in range(16):
            if db % 2 == 0:
                wvg, wbg = self.wload([d["w_in"][:, GA0 + db * 128:GA0 + db * 128 + 256]], 16)
                wva, wba = self.wload([d["w_attn_out"][:, db * 128:db * 128 + 256]], 16)
            outs = self.proj_fm(wvg, wbg, (db % 2) * 128, 16, self.hT, mainbufs, ranges)
            for i, (pb, ps) in enumerate(outs):
                S.op("act", lambda e, ps=ps, i=i: e.activation(out=tg[:, i, :], in_=ps, func=AF.Tanh, scale=0.5),
                     reads=(pb,), writes=(tgb,))
            outs = self.proj_fm(wva, wba, (db % 2) * 128, 16, lambda kc, t0, n: self.aoT[:, kc, t0 - 896:t0 - 896 + n],
                                self.aoTb, ranges)
            for i, (pb, ps) in enumerate(outs):
                S.op("dve", lambda e, ps=ps, i=i: e.scalar_tensor_tensor(out=mt, in0=tg[:, i, :], scalar=1.0, in1=ps,
                                                                        op0=ALU.add, op1=ALU.mult),
                     reads=(tgb, pb), writes=(mtb,))
                S.op("dve", lambda e, i=i, db=db: e.tensor_tensor(out=self.mixedT[:, db, i * 384:(i + 1) * 384],
                                                                  in0=self.mixedT[:, db, i * 384:(i + 1) * 384], in1=mt, op=ALU.add),
                     reads=(mtb, self.mixb[db]), writes=(self.mixb[db],))
        self.dump("mixed", self.mixedT, self.mixb, [128, 16, NT], BF16)

    def wout_residual(self):
        S, d = self.S, self.d
        r1, r2, r3 = self.r1, self.r2, self.r3
        r3.recarve()
        self.x1 = r3.alloc([NMC, D], F32)
        self.x1b = [r3.buf("x1_%d" % i) for i in range(NMC)]
        x1, x1b = self.x1, self.x1b
        xsem = S.new_dma_sem("x1sem")
        for mc in range(NMC):
            S.dma("sp", lambda e, mc=mc: e.dma_start(out=x1[:, mc, :], in_=d["xm"][mc * 128:(mc + 1) * 128, :]), xsem,
                  writes=(x1b[mc],))
        for mc in range(NMC):
            x1b[mc].w = (xsem, S.dcnt[xsem])
        for ct in range(8):
            wv, wb = self.wload([d["w_out"][:, ct * 256:(ct + 1) * 256]], 16)
            for mc in range(NMC):
                pb, ps = self.proj_tm(wv, wb, 0, 256, 16, lambda kc, mc=mc: self.mixedT[:, kc, mc * 128:(mc + 1) * 128], self.mixb)
                S.op("dve", lambda e, ps=ps, mc=mc, ct=ct: e.scalar_tensor_tensor(
                    out=x1[:, mc, ct * 256:(ct + 1) * 256], in0=ps, scalar=0.5, in1=x1[:, mc, ct * 256:(ct + 1) * 256],
                    op0=ALU.mult, op1=ALU.add), reads=(pb, x1b[mc]), writes=(x1b[mc],))
        self.dump("x1", x1, x1b, [128, NMC, D])
        r1.recarve(); r2.recarve()
        self.hfT = r1.alloc([16, NT], BF16)
        self.hfTb = [r1.buf("hfT%d" % i) for i in range(NMC)]
        gain = r2.alloc([D], F32); gb = r2.buf("gain")
        hb = [r2.alloc([D], BF16) for _ in range(2)]
        hbb = [r2.buf("hb%d" % i) for i in range(2)]
        S.dma("sp", lambda e: e.dma_start(out=gain, in_=d["norm_ffn_w"].partition_broadcast(128)), S.new_dma_sem("gsem1"), writes=(gb,))
        for mc in range(NMC):
            s = mc % 2
            self.rms_tile(x1[:, mc, :], [x1b[mc]], gain, gb, hb[s], hbb[s], mc, hb[s])
            for half in range(2):
                tiles = [hb[s][:, (half * 8 + i) * 128:(half * 8 + i + 1) * 128] for i in range(8)]
                self.transpose_to(tiles, [hbb[s]],
                                  lambda n, half=half, mc=mc: self.hfT[:, half * 8:half * 8 + 8, mc * 128:(mc + 1) * 128],
                                  [self.hfTb[mc]])
        self.gain_ap, self.gain_b, self.hb2, self.hbb2 = gain, gb, hb, hbb

    def ffn(self):
        S, d = self.S, self.d
        r2, r4 = self.r2, self.r4
        r4.recarve()
        actT = r4.alloc([11, 1024], BF16); actb = r4.buf("actT")
        pre = [r4.alloc([2 + NT], F32) for _ in range(2)]
        preb = [r4.buf("fpre%d" % i) for i in range(2)]
        acc = [r2.alloc([1024], F32) for _ in range(2)]
        accb = [r2.buf("facc%d" % i) for i in range(2)]
        gl = r4.alloc([1024], F32); glb = r4.buf("gl")
        x1, x1b = self.x1, self.x1b
        ranges = [(i * 384, 384) for i in range(3)]
        for i in range(2):
            S.op("dve", lambda e, i=i: e.memset(pre[i][:, 0:2], 0.0), writes=(preb[i],))
        for fg in range(4):
            for jb in range(11):
                b = fg * 11 + jb
                c0 = b * 128
                wv, wb = self.wload([d["w_ffn_up"][:, c0:c0 + 128], d["w_ffn_up"][:, DFF + c0:DFF + c0 + 128]], 16)
                for gu in range(2):
                    blk = b + gu * 44
                    outs = self.proj_fm(wv, wb, gu * 128, 16, lambda kc, t0, n: self.hfT[:, kc, t0:t0 + n], self.hfTb, ranges)
                    for i, (pb, ps) in enumerate(outs):
                        S.op("act", lambda e, ps=ps, i=i, gu=gu: e.activation(out=pre[gu][:, 2 + i * 384:2 + (i + 1) * 384], in_=ps, func=AF.Copy),
                             reads=(pb,), writes=(preb[gu],))
                    S.op("dve", lambda e, gu=gu: e.tensor_scalar(out=pre[gu][:, 128:130], in0=pre[gu][:, 128:130],
                                                                 scalar1=self.flag[:, 0:1], scalar2=None, op0=ALU.mult),
                         reads=(preb[gu], self.cb), writes=(preb[gu],))
                    w = self.cw_ffn
                    S.op("dve", lambda e, gu=gu, blk=blk: e.tensor_scalar(out=acc[gu], in0=pre[gu][:, 130:130 + 1024], scalar1=w[:, blk, 2:3],
                                                                          scalar2=self.cb_ffn[:, blk:blk + 1], op0=ALU.mult, op1=ALU.add),
                         reads=(preb[gu], self.cb), writes=(accb[gu],))
                    for k in (1, 0):
                        S.op("dve", lambda e, gu=gu, blk=blk, k=k: e.scalar_tensor_tensor(
                            out=acc[gu], in0=pre[gu][:, 128 + k:128 + k + 1024], scalar=w[:, blk, k:k + 1], in1=acc[gu],
                            op0=ALU.mult, op1=ALU.add), reads=(preb[gu], self.cb, accb[gu]), writes=(accb[gu],))
                S.op("act", lambda e: e.activation(out=gl, in_=acc[0], func=AF.Gelu_apprx_tanh), reads=(accb[0],), writes=(glb,))
                S.op("dve", lambda e, jb=jb: e.tensor_tensor(out=actT[:, jb, :], in0=gl, in1=acc[1], op=ALU.mult),
                     reads=(glb, accb[1]), writes=(actb,))
            if fg == 0:
                self.dump("actT0", actT, [actb], [128, 11, 1024], BF16)
            for ct in range(8):
                wv, wb = self.wload([d["w_ffn_down"][fg * 1408:(fg + 1) * 1408, ct * 256:(ct + 1) * 256]], 11)
                for mc in range(1, NMC):
                    pb, ps = self.proj_tm(wv, wb, 0, 256, 11, lambda kc, mc=mc: actT[:, kc, (mc - 1) * 128:mc * 128], [actb])
                    S.op("dve", lambda e, ps=ps, mc=mc, ct=ct: e.tensor_tensor(
                        out=x1[:, mc, ct * 256:(ct + 1) * 256], in0=ps, in1=x1[:, mc, ct * 256:(ct + 1) * 256], op=ALU.add),
                        reads=(pb, x1b[mc]), writes=(x1b[mc],))
        self.dump("x2", x1, x1b, [128, NMC, D])

    def ple(self):
        S, d = self.S, self.d
        r1, r2, r4 = self.r1, self.r2, self.r4
        x1, x1b = self.x1, self.x1b
        r1.recarve(); r4.recarve()
        nT = r1.alloc([16, 1024], BF16)
        nTb = [r1.buf("nT%d" % i) for i in range(8)]
        gain, gb, hb, hbb = self.gain_ap, self.gain_b, self.hb2, self.hbb2
        S.dma("sp", lambda e: e.dma_start(out=gain, in_=d["ple_norm_w"].partition_broadcast(128)), S.new_dma_sem("gsem2"), writes=(gb,))
        pT = r4.alloc([2, 1024], BF16); pTb = r4.buf("pT")
        pt = [r4.alloc([PLE], F32) for _ in range(2)]; ptb = [r4.buf("pt%d" % i) for i in range(2)]
        pbf = [r4.alloc([PLE], BF16) for _ in range(2)]; pbfb = [r4.buf("pbf%d" % i) for i in range(2)]
        psem = [S.new_dma_sem("psem%d" % i) for i in range(2)]
        tgp = r4.alloc([256], F32); tgpb = r4.buf("tgp")
        up = r4.alloc([256], F32); upb = r4.buf("up")
        for mc in range(1, NMC):
            s = mc % 2
            o = mc - 1
            self.rms_tile(x1[:, mc, :], [x1b[mc]], gain, gb, hb[s], hbb[s], mc, hb[s])
            for half in range(2):
                tiles = [hb[s][:, (half * 8 + i) * 128:(half * 8 + i + 1) * 128] for i in range(8)]
                self.transpose_to(tiles, [hbb[s]],
                                  lambda n, half=half, o=o: nT[:, half * 8:half * 8 + 8, o * 128:(o + 1) * 128], [nTb[o]])
            S.dma("sp", lambda e, s=s, o=o: e.dma_start(out=pt[s], in_=d["pp"][o * 128:(o + 1) * 128, :]), psem[s], writes=(ptb[s],))
            S.op("act", lambda e, s=s: e.activation(out=pbf[s], in_=pt[s], func=AF.Copy), reads=(ptb[s],), writes=(pbfb[s],))
            tiles = [pbf[s][:, i * 128:(i + 1) * 128] for i in range(2)]
            self.transpose_to(tiles, [pbfb[s]], lambda n, o=o: pT[:, 0:2, o * 128:(o + 1) * 128], [pTb])
        for ct in range(8):
            wv, wb = self.wload([d["w_ple_gate"][:, ct * 256:(ct + 1) * 256]], 16)
            wv2, wb2 = self.wload([d["w_ple_proj"][:, ct * 256:(ct + 1) * 256]], 2)
            for mc in range(1, NMC):
                o = mc - 1
                pb, ps = self.bank()
                fns = []
                for kc in range(16):
                    fns.append(lambda e, kc=kc, ps=ps, o=o: e.matmul(ps[:, 0:256], lhsT=nT[:, kc, o * 128:(o + 1) * 128], rhs=wv[:, kc, 0:256],
                                                                     start=(kc == 0), stop=(kc == 15)))
                for kc in range(2):
                    fns.append(lambda e, kc=kc, ps=ps, o=o: e.matmul(ps[:, 256:512], lhsT=pT[:, kc, o * 128:(o + 1) * 128], rhs=wv2[:, kc, 0:256],
                                                                     start=(kc == 0), stop=(kc == 1)))
                S.op("pe", fns, reads=(wb, wb2, nTb[o], pTb), writes=(pb,))
                S.op("act", lambda e, ps=ps: e.activation(out=tgp, in_=ps[:, 0:256], func=AF.Tanh, scale=0.5), reads=(pb,), writes=(tgpb,))
                S.op("dve", lambda e, ps=ps: e.scalar_tensor_tensor(out=up, in0=tgp, scalar=1.0, in1=ps[:, 256:512], op0=ALU.add, op1=ALU.mult),
                     reads=(tgpb, pb), writes=(upb,))
                S.op("dve", lambda e, mc=mc, ct=ct: e.scalar_tensor_tensor(
                    out=x1[:, mc, ct * 256:(ct + 1) * 256], in0=up, scalar=0.5, in1=x1[:, mc, ct * 256:(ct + 1) * 256],
                    op0=ALU.mult, op1=ALU.add), reads=(upb, x1b[mc]), writes=(x1b[mc],))
        osem = S.new_dma_sem("osem")
        for mc in range(1, NMC):
            ob = Buf("out%d" % mc)
            S.dma("sp", lambda e, mc=mc: e.dma_start(out=self.out[(mc - 1) * 128:mc * 128, :], in_=x1[:, mc, :]), osem,
                  reads=(x1b[mc],), writes=(ob,))
            self.final_bufs.append(ob)


def _t5_bucket(dist):
    nb, md = 32, 128
    me = nb // 2
    dd = np.maximum(dist, 0)
    lr = np.log(np.maximum(dd, 1).astype(np.float32) / me) / np.log(md / me)
    large = me + (lr * (nb - me)).astype(np.int32)
    large = np.minimum(large, nb - 1)
    return np.where(dd < me, dd, large)


def _const_mats():
    ident = np.eye(128, dtype=np.float32)
    tri = (np.arange(128)[:, None] <= np.arange(128)[None, :]).astype(np.float32)
    ones = np.ones((128, 128), np.float32)
    neg = np.where(np.arange(128)[:, None] > np.arange(128)[None, :], -32768.0, 0.0).astype(np.float32)
    neg4 = np.tile(neg, (1, 4))
    sel = np.zeros((128, 8, 128), np.float32)
    for j in range(8):
        sel[j, j, :] = 1.0
    return (np.ascontiguousarray(np.concatenate([tri, ones], axis=1)),
            np.ascontiguousarray(np.concatenate([ident, neg4, sel.reshape(128, 1024)], axis=1)))


def _bias_tables(table):
    L = 128
    qi = np.arange(L)[:, None]
    kj = np.arange(2 * L)[None, :]
    dist = qi + L - kj
    band = (dist >= 0) & (dist < 128)
    bk = _t5_bucket(dist)
    b = table[bk]
    b = np.where(band[:, :, None], b, np.float32(NEGM)).astype(np.float32)
    b = b.reshape(L, 2, L, NKV, 8)
    b = np.transpose(b, (3, 2, 4, 1, 0))
    return np.ascontiguousarray(b).reshape(NKV, 128, 8 * 2 * 128)


def make_in_maps(inputs):
    x = np.asarray(inputs["x"], np.float32)
    p = np.asarray(inputs["p"], np.float32)[0]
    g = lambda k: np.ascontiguousarray(np.asarray(inputs[k], np.float32)[0])
    shared = {
        "w_in": g("w_in"), "w_attn_out": g("w_attn_out"), "w_ssm_out": g("w_ssm_out"), "w_out": g("w_out"),
        "w_ffn_up": g("w_ffn_up"), "w_ffn_down": g("w_ffn_down"), "w_ple_gate": g("w_ple_gate"),
        "w_ple_proj": g("w_ple_proj"),
        "norm_mix_w": g("norm_mix_w")[None], "norm_ffn_w": g("norm_ffn_w")[None], "ple_norm_w": g("ple_norm_w")[None],
        "ssm_norm_w": g("ssm_norm_w")[None], "q_norm_w": g("q_norm_w")[None], "k_norm_w": g("k_norm_w")[None],
        "attn_sinks": g("attn_sinks")[None], "ssm_A_log": g("ssm_A_log")[None], "ssm_dt_bias": g("ssm_dt_bias")[None],
        "ssm_D": g("ssm_D")[None],
        "cw_ssm": np.ascontiguousarray(g("ssm_conv_w").T.reshape(48, 128, 4).transpose(1, 0, 2)).reshape(128, 192),
        "cb_ssm": np.ascontiguousarray(g("ssm_conv_b").reshape(48, 128).T),
        "cw_ffn": np.ascontiguousarray(g("ffn_conv_w").T.reshape(88, 128, 3).transpose(1, 0, 2)).reshape(128, 264),
        "cb_ffn": np.ascontiguousarray(g("ffn_conv_b").reshape(88, 128).T),
        "biasT": _bias_tables(np.asarray(inputs["rel_bias_table"], np.float32)),
    }
    shared["cm_f"], shared["cm_b"] = _const_mats()
    in_maps = []
    for core in range(8):
        b, hf = core // 2, core % 2
        s0 = hf * 1024
        xm = np.zeros((NT, D), np.float32)
        xp = np.zeros((896, D), np.float32)
        if hf == 1:
            xm[:] = x[b, s0 - 128:s0 + 1024]
            xp[:] = x[b, 0:896]
        else:
            xm[128:] = x[b, 0:1024]
        m = dict(shared)
        m["xm"] = xm
        m["xp"] = xp
        m["pp"] = np.ascontiguousarray(p[b, s0:s0 + 1024])
        m["flag"] = np.full((128, 1), float(hf), np.float32)
        in_maps.append(m)
    return in_maps


def kernel(**inputs):
    in_maps = make_in_maps(inputs)
    dbg = tuple(inputs.get("_debug", ())) if isinstance(inputs.get("_debug", ()), (list, tuple)) else ()
    bld = Builder(debug=dbg)
    nc = bld.build()
    cores = list(range(8))
    if inputs.get("_cores"):
        cores = list(inputs["_cores"])
    res = run_bass_kernel_spmd(nc, [in_maps[c] for c in cores], core_ids=list(range(len(cores))))
    out = np.zeros((BATCH, SEQ, D), np.float32)
    for i, core in enumerate(cores):
        b, hf = core // 2, core % 2
        out[b, hf * 1024:(hf + 1) * 1024] = res.results[i]["out"]
    if dbg:
        kernel.last_debug = [{k: r[v] for k, v in bld.dbg_out.items()} for r in res.results]
    return out
```

```python
import numpy as np
import concourse.bass as bass
import concourse.mybir as mybir
from concourse.bass_utils import run_bass_kernel_spmd

F32 = mybir.dt.float32
BF16 = mybir.dt.bfloat16
AF = mybir.ActivationFunctionType
ALU = mybir.AluOpType
AX = mybir.AxisListType

D = 2048
SEQ = 2048
BATCH = 4
NH = 32
NKV = 4
DH = 64
DI = 4096
NSH = 64
NG = 8
DS = 128
DFF = 5632
PLE = 256
EPS = 1e-6
Q0 = 0
K0 = 2048
V0 = 2304
Z0 = 2560
XBC0 = 6656
DT0 = 12800
GA0 = 12864
GS0 = 14912
IN_DIM = 16960

NMC = 9
NT = NMC * 128
TP = 768
TM = 1280
NEGM = -30000.0


class Buf:
    __slots__ = ("name", "w", "r")

    def __init__(self, name, base=None):
        self.name = name
        self.w = None
        self.r = dict(base) if base else {}


class Sched:
    ENG = ("pe", "act", "dve", "pool", "sp")

    def __init__(self):
        self.streams = {e: [] for e in self.ENG}
        self.cnt = {e: 0 for e in self.ENG}
        self.dcnt = {}
        self.waited = {e: {} for e in self.ENG}
        self.dma_sems = []

    def new_dma_sem(self, name):
        self.dma_sems.append(name)
        self.dcnt[name] = 0
        return name

    def _waits(self, eng, reads, writes):
        deps = {}

        def add(s, v):
            if deps.get(s, 0) < v:
                deps[s] = v

        for b in reads:
            if b.w is not None:
                add(*b.w)
        for b in writes:
            if b.w is not None:
                add(*b.w)
            for s, v in b.r.items():
                add(s, v)
        wd = self.waited[eng]
        st = self.streams[eng]
        for s, v in deps.items():
            if wd.get(s, 0) >= v:
                continue
            wd[s] = v
            st.append(("wait", s, v))

    def op(self, eng, fns, reads=(), writes=()):
        self._waits(eng, reads, writes)
        self.cnt[eng] += 1
        c = self.cnt[eng]
        if not isinstance(fns, (list, tuple)):
            fns = [fns]
        st = self.streams[eng]
        for f in fns[:-1]:
            st.append(("inst", f, None, 0))
        st.append(("inst", fns[-1], eng, 1))
        for b in reads:
            b.r[eng] = c
        for b in writes:
            b.w = (eng, c)
            b.r = {}

    def dma(self, eng, fn, sem, reads=(), writes=()):
        self._waits(eng, reads, writes)
        self.dcnt[sem] += 16
        c = self.dcnt[sem]
        self.streams[eng].append(("inst", fn, sem, 16))
        for b in reads:
            b.r[sem] = c
        for b in writes:
            b.w = (sem, c)
            b.r = {}

    def final_wait(self, eng, bufs):
        self._waits(eng, bufs, bufs)


def collect_tokens(bufs):
    r = {}
    for b in bufs:
        if b.w is not None and r.get(b.w[0], 0) < b.w[1]:
            r[b.w[0]] = b.w[1]
        for s, v in b.r.items():
            if r.get(s, 0) < v:
                r[s] = v
    return r


class Region:
    def __init__(self, name, handle, nwords):
        self.name = name
        self.h = handle
        self.nwords = nwords
        self.off = 0
        self.bufs = []
        self.base = {}

    def recarve(self):
        self.base = collect_tokens(self.bufs)
        self.bufs = []
        self.off = 0

    def buf(self, name):
        b = Buf(name, self.base)
        self.bufs.append(b)
        return b

    def alloc(self, shape, dtype):
        nel = 1
        for s in shape:
            nel *= s
        nbytes = nel * (2 if dtype == BF16 else 4)
        nw = (nbytes + 3) // 4
        assert self.off + nw <= self.nwords, (self.name, self.off, nw, self.nwords)
        ap = self.h[:, self.off:self.off + nw]
        self.off += nw
        if dtype == BF16:
            ap = ap.bitcast(BF16)
            if nel != nw * 2:
                ap = ap[:, 0:nel]
        if len(shape) == 2:
            return ap.rearrange("p (a b) -> p a b", b=shape[1])
        if len(shape) == 3:
            return ap.rearrange("p (a b c) -> p a b c", b=shape[1], c=shape[2])
        return ap


class Builder:
    def __init__(self, debug=(), nphases=99):
        self.nphases = nphases
        self.debug = set(debug)
        self.nc = bass.Bass("TRN2", target_bir_lowering=False)
        self.S = Sched()
        self.dbg_out = {}

    def dram_in(self, name, shape):
        return self.nc.dram_tensor(name, list(shape), F32, kind="ExternalInput").ap()

    def bank(self):
        i = self.bank_i
        self.bank_i = (i + 1) % 8
        return self.pbuf[i], self.ps[i]

    def wslot(self):
        i = self.w_i
        self.w_i = (i + 1) % len(self.wt)
        return i

    def wload(self, srcs, kcn):
        i = self.wslot()
        tot = sum(s.shape[1] for s in srcs)
        assert kcn * tot <= 4096
        view = self.wt[i][:, 0:kcn * tot].rearrange("p (k c) -> p k c", c=tot)
        c0 = 0
        for s in srcs:
            n = s.shape[1]
            src = s.rearrange("(k p) e -> p k e", p=128)
            dst = view[:, :, c0:c0 + n]
            self.S.dma("pool", lambda e, d=dst, s_=src: e.dma_start(out=d, in_=s_), self.wsem[i],
                       reads=(), writes=(self.wbuf[i],))
            c0 += n
        return view, self.wbuf[i]

    def dump(self, name, ap, bufs, shape, dtype=F32):
        if name not in self.debug:
            return
        t = self.nc.dram_tensor("dbg_" + name, list(shape), dtype, kind="ExternalOutput").ap()
        sem = self.S.new_dma_sem("dbgsem_" + name)
        db = Buf("dbg_" + name)
        self.S.dma("sp", lambda e, t=t, ap=ap: e.dma_start(out=t, in_=ap), sem, reads=bufs, writes=(db,))
        self.final_bufs.append(db)
        self.dbg_out[name] = "dbg_" + name

    def build(self):
        nc = self.nc
        S = self.S
        d = {}
        d["xm"] = self.dram_in("xm", [NT, D])
        d["xp"] = self.dram_in("xp", [896, D])
        d["pp"] = self.dram_in("pp", [1024, PLE])
        d["flag"] = self.dram_in("flag", [128, 1])
        d["w_in"] = self.dram_in("w_in", [D, IN_DIM])
        d["w_attn_out"] = self.dram_in("w_attn_out", [D, D])
        d["w_ssm_out"] = self.dram_in("w_ssm_out", [DI, D])
        d["w_out"] = self.dram_in("w_out", [D, D])
        d["w_ffn_up"] = self.dram_in("w_ffn_up", [D, 2 * DFF])
        d["w_ffn_down"] = self.dram_in("w_ffn_down", [DFF, D])
        d["w_ple_gate"] = self.dram_in("w_ple_gate", [D, D])
        d["w_ple_proj"] = self.dram_in("w_ple_proj", [PLE, D])
        d["norm_mix_w"] = self.dram_in("norm_mix_w", [1, D])
        d["norm_ffn_w"] = self.dram_in("norm_ffn_w", [1, D])
        d["ple_norm_w"] = self.dram_in("ple_norm_w", [1, D])
        d["ssm_norm_w"] = self.dram_in("ssm_norm_w", [1, DI])
        d["q_norm_w"] = self.dram_in("q_norm_w", [1, DH])
        d["k_norm_w"] = self.dram_in("k_norm_w", [1, DH])
        d["attn_sinks"] = self.dram_in("attn_sinks", [1, NH])
        d["ssm_A_log"] = self.dram_in("ssm_A_log", [1, NSH])
        d["ssm_dt_bias"] = self.dram_in("ssm_dt_bias", [1, NSH])
        d["ssm_D"] = self.dram_in("ssm_D", [1, NSH])
        d["cw_ssm"] = self.dram_in("cw_ssm", [128, 48 * 4])
        d["cb_ssm"] = self.dram_in("cb_ssm", [128, 48])
        d["cw_ffn"] = self.dram_in("cw_ffn", [128, 88 * 3])
        d["cb_ffn"] = self.dram_in("cb_ffn", [128, 88])
        d["biasT"] = self.dram_in("biasT", [NKV, 128, 8 * 2 * 128])
        d["cm_f"] = self.dram_in("cm_f", [128, 256])
        d["cm_b"] = self.dram_in("cm_b", [128, 128 + 512 + 1024])
        self.d = d
        self.out = nc.dram_tensor("out", [1024, D], F32, kind="ExternalOutput").ap()
        self.final_bufs = []

        R1W, R2W, R3W, R4W, CW = 10240, 6144, 18432, 9216, 4200
        import contextlib
        with contextlib.ExitStack() as es:
            def sb(name, shape, dt):
                return es.enter_context(nc.sbuf_tensor(name, shape, dt))
            r1 = Region("R1", sb("R1", [128, R1W], F32), R1W)
            r2 = Region("R2", sb("R2", [128, R2W], F32), R2W)
            r3 = Region("R3", sb("R3", [128, R3W], F32), R3W)
            r4 = Region("R4", sb("R4", [128, R4W], F32), R4W)
            rc = Region("RC", sb("RC", [128, CW], F32), CW)
            self.r1, self.r2, self.r3, self.r4, self.rc = r1, r2, r3, r4, rc
            self.wt = [sb("wt%d" % i, [128, 4096], BF16) for i in range(2)]
            self.wbuf = [Buf("wbuf%d" % i) for i in range(2)]
            self.wsem = [S.new_dma_sem("wsem%d" % i) for i in range(2)]
            self.w_i = 0
            self.ps = [es.enter_context(nc.psum_tensor("ps%d" % i, [128, 512], F32)) for i in range(8)]
            self.pbuf = [Buf("psum%d" % i) for i in range(8)]
            self.bank_i = 0

            phases = [self.setup_consts, self.phase0, self.dt_proj, self.ssm_prefix, self.ssm_main, self.ssm_out,
                      self.attention, self.attn_out, self.wout_residual, self.ffn, self.ple]
            for ph in phases[:self.nphases]:
                ph()

            S.final_wait("sp", self.final_bufs)

            sem_names = list(Sched.ENG) + S.dma_sems
            sems = {}
            for n in sem_names:
                sems[n] = es.enter_context(nc.semaphore(n))
            block = es.enter_context(nc.Block())

            def replay(engname):
                def run(e):
                    for it in S.streams[engname]:
                        if it[0] == "wait":
                            e.wait_ge(sems[it[1]], it[2])
                        else:
                            ins = it[1](e)
                            if it[2] is not None:
                                ins.then_inc(sems[it[2]], it[3])
                return run

            block.sync(replay("sp"))
            block.gpsimd(replay("pool"))
            block.scalar(replay("act"))
            block.vector(replay("dve"))
            block.tensor(replay("pe"))
        return nc

    def setup_consts(self):
        S, d, rc = self.S, self.d, self.rc
        csem = S.new_dma_sem("csem")
        self.cb = Buf("consts")
        cb = self.cb

        def cload(dst, src):
            S.dma("sp", lambda e, dst=dst, src=src: e.dma_start(out=dst, in_=src), csem)

        cmf = rc.alloc([256], F32)
        cload(cmf, d["cm_f"])
        self.tri_f = cmf[:, 0:128]
        self.ones_f = cmf[:, 128:256]
        self.r3.recarve()
        cm = self.r3.alloc([128 + 512 + 1024], F32)
        cmb = self.r3.buf("cm_stage")
        S.dma("sp", lambda e: e.dma_start(out=cm, in_=d["cm_b"]), S.new_dma_sem("cmsem"), writes=(cmb,))
        self.ident_bf = rc.alloc([128], BF16)
        self.neg4 = rc.alloc([512], BF16)
        self.sel = rc.alloc([1024], BF16)
        self.nsel = rc.alloc([1024], BF16)
        self._cm = cm
        self.Ab = rc.alloc([64], F32)
        self.Db = rc.alloc([64], F32)
        self.dtb = rc.alloc([64], F32)
        self.flag = rc.alloc([1], F32)
        self.cw_ssm = rc.alloc([48, 4], F32)
        self.cb_ssm = rc.alloc([48], F32)
        self.cw_ffn = rc.alloc([88, 3], F32)
        self.cb_ffn = rc.alloc([88], F32)
        self.qg = rc.alloc([64], F32)
        self.kg = rc.alloc([64], F32)
        self.esink = rc.alloc([32], F32)
        self.dt_all = rc.alloc([16, 64], F32)
        self.ss16 = rc.alloc([16], F32)
        self.sd16 = rc.alloc([16], F32)
        self.rs16 = rc.alloc([16], F32)
        cload(self.Ab, d["ssm_A_log"].partition_broadcast(128))
        cload(self.Db, d["ssm_D"].partition_broadcast(128))
        cload(self.dtb, d["ssm_dt_bias"].partition_broadcast(128))
        cload(self.flag, d["flag"])
        cload(self.cw_ssm, d["cw_ssm"].rearrange("p (b k) -> p b k", k=4))
        cload(self.cb_ssm, d["cb_ssm"])
        cload(self.cw_ffn, d["cw_ffn"].rearrange("p (b k) -> p b k", k=3))
        cload(self.cb_ffn, d["cb_ffn"])
        cload(self.qg, d["q_norm_w"].partition_broadcast(128))
        cload(self.kg, d["k_norm_w"].partition_broadcast(128))
        cload(self.esink, d["attn_sinks"].partition_broadcast(128))
        cb.w = (csem, S.dcnt[csem])
        S.op("act", [
            lambda e: e.activation(out=self.ident_bf, in_=cm[:, 0:128], func=AF.Copy),
            lambda e: e.activation(out=self.neg4, in_=cm[:, 128:640], func=AF.Copy),
            lambda e: e.activation(out=self.sel, in_=cm[:, 640:1664], func=AF.Copy),
            lambda e: e.activation(out=self.nsel, in_=cm[:, 640:1664], func=AF.Copy, scale=-1.0),
            lambda e: e.activation(out=self.esink, in_=self.esink, func=AF.Exp),
            lambda e: e.activation(out=self.Ab, in_=self.Ab, func=AF.Exp),
        ], reads=(cb, cmb), writes=(cb,))
        S.op("dve", [
            lambda e: e.tensor_scalar(out=self.Ab, in0=self.Ab, scalar1=-1.0, scalar2=None, op0=ALU.mult),
            lambda e: e.tensor_scalar(out=self.qg, in0=self.qg, scalar1=0.125, scalar2=None, op0=ALU.mult),
            lambda e: e.tensor_scalar(out=self.cw_ssm, in0=self.cw_ssm, scalar1=0.5, scalar2=None, op0=ALU.mult),
            lambda e: e.tensor_scalar(out=self.cb_ssm, in0=self.cb_ssm, scalar1=0.5, scalar2=None, op0=ALU.mult),
        ], reads=(cb,), writes=(cb,))

    def hT(self, kc, t0, n):
        if t0 < TP:
            assert t0 + n <= TP
            return self.hTp[:, kc, t0:t0 + n]
        return self.hTm[:, kc, t0 - TP:t0 - TP + n]

    def hT_bufs(self, t0, n):
        c0 = t0 // 128
        c1 = (t0 + n - 1) // 128
        return [self.hTb[c] for c in range(c0, c1 + 1)]

    def rms_tile(self, x_ap, xbufs, gain_bc, gbuf, hb, hbbuf, sidx, scratch_bf, eps=EPS, n=D):
        S = self.S
        ssb = self.small_b[sidx]
        ss, sd, rs = self.ss16[:, sidx:sidx + 1], self.sd16[:, sidx:sidx + 1], self.rs16[:, sidx:sidx + 1]
        S.op("act", lambda e: e.activation(out=scratch_bf, in_=x_ap, func=AF.Square, accum_out=ss),
             reads=list(xbufs), writes=(ssb, hbbuf))
        S.op("act", lambda e: e.activation(out=sd, in_=ss, func=AF.Ln, bias=eps, scale=1.0 / n),
             reads=(ssb,), writes=(ssb,))
        S.op("act", lambda e: e.activation(out=rs, in_=sd, func=AF.Exp, scale=-0.5), reads=(ssb,), writes=(ssb,))
        S.op("dve", lambda e: e.scalar_tensor_tensor(out=hb, in0=x_ap, scalar=rs, in1=gain_bc,
                                                      op0=ALU.mult, op1=ALU.mult),
             reads=list(xbufs) + [ssb, gbuf], writes=(hbbuf,))

    def transpose_to(self, src_tiles, src_bufs, dst_ap_fn, dst_bufs, evac="act"):
        S = self.S
        n = len(src_tiles)
        pb, ps = self.bank()
        psb = ps[:].bitcast(BF16).rearrange("p (a b) -> p a b", b=128)
        fns = []
        for i, t in enumerate(src_tiles):
            fns.append(lambda e, i=i, t=t: e.transpose(out=psb[:, i, :], in_=t, identity=self.ident_bf))
        S.op("pe", fns, reads=list(src_bufs) + [self.cb], writes=(pb,))
        dst = dst_ap_fn(n)
        if evac == "act":
            S.op("act", lambda e: e.activation(out=dst, in_=psb[:, 0:n, :], func=AF.Copy),
                 reads=(pb,), writes=list(dst_bufs))
        else:
            S.op("dve", lambda e: e.tensor_copy(out=dst, in_=psb[:, 0:n, :]), reads=(pb,), writes=list(dst_bufs))

    def proj_fm(self, wview, wb, e0, kcn, act_fn, act_bufs, ranges):
        S = self.S
        banks = [self.bank() for _ in ranges]
        fns = []
        for kc in range(kcn):
            for (pb, ps), (t0, n) in zip(banks, ranges):
                fns.append(lambda e, kc=kc, ps=ps, t0=t0, n=n: e.matmul(
                    ps[:, 0:n], lhsT=wview[:, kc, e0:e0 + 128], rhs=act_fn(kc, t0, n),
                    start=(kc == 0), stop=(kc == kcn - 1)))
        S.op("pe", fns, reads=[wb] + list(act_bufs), writes=[pb for pb, _ in banks])
        return [(pb, ps[:, 0:n]) for (pb, ps), (t0, n) in zip(banks, ranges)]

    def proj_tm(self, wview, wb, c0, ncols, kcn, lhs_fn, act_bufs):
        S = self.S
        pb, ps = self.bank()
        fns = []
        for kc in range(kcn):
            fns.append(lambda e, kc=kc: e.matmul(ps[:, 0:ncols], lhsT=lhs_fn(kc), rhs=wview[:, kc, c0:c0 + ncols],
                                                 start=(kc == 0), stop=(kc == kcn - 1)))
        S.op("pe", fns, reads=[wb] + list(act_bufs), writes=(pb,))
        return pb, ps[:, 0:ncols]

    def phase0(self):
        S, d = self.S, self.d
        r1, r2, r4 = self.r1, self.r2, self.r4
        self.hTm = r1.alloc([16, TM], BF16)
        self.hTp = r2.alloc([16, TP], BF16)
        self.hTb = [Buf("hT%d" % c) for c in range(16)]
        r1.bufs += self.hTb[6:]
        r2.bufs += self.hTb[:6]
        self.small_b = [Buf("small%d" % i) for i in range(16)]
        r4.recarve()
        xt = [r4.alloc([D], F32) for _ in range(2)]
        xb = [r4.buf("xt%d" % i) for i in range(2)]
        xs = [S.new_dma_sem("xsem%d" % i) for i in range(2)]
        self.xt_sems = xs
        hb = [r4.alloc([D], BF16) for _ in range(2)]
        hbb = [r4.buf("hb%d" % i) for i in range(2)]
        gain = r4.alloc([D], F32)
        gb = r4.buf("gain")
        S.dma("sp", lambda e: e.dma_start(out=gain, in_=d["norm_mix_w"].partition_broadcast(128)), S.new_dma_sem("gsem0"), writes=(gb,))
        for ci in range(16):
            s = ci % 2
            src = d["xp"][ci * 128:(ci + 1) * 128, :] if ci < 7 else d["xm"][(ci - 7) * 128:(ci - 6) * 128, :]
            S.dma("sp", lambda e, s=s, src=src: e.dma_start(out=xt[s], in_=src), xs[s], writes=(xb[s],))
            self.rms_tile(xt[s], [xb[s]], gain, gb, hb[s], hbb[s], ci, hb[s])
            t0 = ci * 128
            for half in range(2):
                tiles = [hb[s][:, (half * 8 + i) * 128:(half * 8 + i + 1) * 128] for i in range(8)]
                self.transpose_to(tiles, [hbb[s]],
                                  lambda n, half=half, t0=t0: (self.hTp[:, half * 8:half * 8 + 8, t0:t0 + 128] if t0 < TP
                                                               else self.hTm[:, half * 8:half * 8 + 8, t0 - TP:t0 - TP + 128]),
                                  [self.hTb[ci]])
        self.dump("hTm", self.hTm, self.hTb[6:], [128, 16, TM], BF16)

    def dt_proj(self):
        S, d = self.S, self.d
        r4 = self.r4
        r4.recarve()
        raw = r4.alloc([16, 64], F32)
        rawb = r4.buf("dtraw")
        wv, wb = self.wload([d["w_in"][:, DT0:DT0 + 64]], 16)
        for ci in range(16):
            t0 = ci * 128
            pb, ps = self.proj_tm(wv, wb, 0, 64, 16, lambda kc, t0=t0: self.hT(kc, t0, 128), [self.hTb[ci]])
            S.op("dve", lambda e, ci=ci, ps=ps: e.tensor_tensor(out=raw[:, ci, :], in0=ps, in1=self.dtb, op=ALU.add),
                 reads=(pb, self.cb), writes=(rawb,))
        dtb_ = Buf("dt_all")
        self.dt_buf = dtb_
        S.op("act", lambda e: e.activation(out=raw, in_=raw, func=AF.Exp), reads=(rawb,), writes=(rawb,))
        S.op("act", lambda e: e.activation(out=self.dt_all, in_=raw, func=AF.Ln, bias=1.0, scale=1.0),
             reads=(rawb,), writes=(dtb_,))
        self.dump("dt_all", self.dt_all, [dtb_], [128, 16, 64])

    def conv_silu(self, pre, preb, n_out, blk, acc, accb, tt, ttb, out_bf, outb, lo=0):
        S = self.S
        w = self.cw_ssm
        S.op("dve", [
            lambda e: e.tensor_scalar(out=acc[:, 0:n_out], in0=pre[:, lo + 3:lo + 3 + n_out], scalar1=w[:, blk, 3:4],
                                      scalar2=self.cb_ssm[:, blk:blk + 1], op0=ALU.mult, op1=ALU.add)],
             reads=(preb, self.cb), writes=(accb,))
        for k in (2, 1, 0):
            S.op("dve", lambda e, k=k: e.scalar_tensor_tensor(out=acc[:, 0:n_out], in0=pre[:, lo + k:lo + k + n_out],
                                                             scalar=w[:, blk, k:k + 1], in1=acc[:, 0:n_out],
                                                             op0=ALU.mult, op1=ALU.add),
                 reads=(preb, self.cb, accb), writes=(accb,))
        S.op("act", lambda e: e.activation(out=tt[:, 0:n_out], in_=acc[:, 0:n_out], func=AF.Tanh),
             reads=(accb,), writes=(ttb,))
        S.op("dve", lambda e: e.scalar_tensor_tensor(out=out_bf, in0=tt[:, 0:n_out], scalar=1.0, in1=acc[:, 0:n_out],
                                                      op0=ALU.add, op1=ALU.mult),
             reads=(ttb, accb), writes=(outb,))

    def batch_smalls(self, reg, n, c0, g, suffix=False):
        S = self.S
        n8 = n * 8
        sm = {}
        for nm in ("a", "dwl", "w", "dtw"):
            sm[nm] = reg.alloc([n, 8], F32)
        s2 = reg.alloc([2, n8], F32)
        e2 = reg.alloc([2, n8], F32)
        smb = reg.buf("smalls")
        dt_g = self.dt_all[:, c0:c0 + n, g * 8:(g + 1) * 8]
        sm["dt"] = dt_g
        sm["acum"] = s2[:, 0, :].rearrange("p (c j) -> p c j", j=8)
        sm["atot"] = s2[:, 1, :].rearrange("p (c j) -> p c j", j=8)
        sm["eacum"] = e2[:, 0, :].rearrange("p (c j) -> p c j", j=8)
        sm["eatot"] = e2[:, 1, :].rearrange("p (c j) -> p c j", j=8)
        S.op("dve", lambda e: e.tensor_tensor(out=sm["a"], in0=dt_g,
                                              in1=self.Ab[:, g * 8:(g + 1) * 8].unsqueeze(1).to_broadcast([128, n, 8]), op=ALU.mult),
             reads=(self.dt_buf, self.cb), writes=(smb,))
        pb, ps = self.bank()
        a2 = sm["a"].rearrange("p c j -> p (c j)")
        S.op("pe", [
            lambda e: e.matmul(ps[:, 0:n8], lhsT=self.tri_f, rhs=a2, start=True, stop=True),
            lambda e: e.matmul(ps[:, 128:128 + n8], lhsT=self.ones_f, rhs=a2, start=True, stop=True),
        ], reads=(smb, self.cb), writes=(pb,))
        psv = ps[:, 0:256].rearrange("p (a b) -> p a b", b=128)[:, :, 0:n8]
        S.op("act", [
            lambda e: e.activation(out=s2, in_=psv, func=AF.Copy),
            lambda e: e.activation(out=e2, in_=psv, func=AF.Exp),
        ], reads=(pb,), writes=(smb,))
        S.op("dve", lambda e: e.tensor_tensor(out=sm["dwl"], in0=sm["atot"], in1=sm["acum"], op=ALU.subtract),
             reads=(smb,), writes=(smb,))
        if suffix:
            suf = reg.alloc([n, 8], F32)
            S.op("dve", lambda e: e.memset(suf[:, n - 1, :], 0.0), reads=(smb,), writes=(smb,))
            for c in range(n - 2, -1, -1):
                S.op("dve", lambda e, c=c: e.tensor_tensor(out=suf[:, c, :], in0=suf[:, c + 1, :], in1=sm["atot"][:, c + 1, :], op=ALU.add),
                     reads=(smb,), writes=(smb,))
            S.op("dve", lambda e: e.tensor_tensor(out=sm["dwl"], in0=sm["dwl"], in1=suf, op=ALU.add), reads=(smb,), writes=(smb,))
        if g == 0 and not suffix:
            self.dump("s2", s2, [smb], [128, 2, n8])
            self.dump("sma", sm["a"], [smb], [128, n, 8])
        S.op("act", lambda e: e.activation(out=sm["w"], in_=sm["dwl"], func=AF.Exp), reads=(smb,), writes=(smb,))
        S.op("dve", lambda e: e.tensor_tensor(out=sm["dtw"], in0=sm["w"], in1=dt_g, op=ALU.mult),
             reads=(smb, self.dt_buf), writes=(smb,))
        return sm, smb

    def ssm_prefix(self):
        S, d = self.S, self.d
        r3, r4 = self.r3, self.r4
        r3.recarve()
        self.stash = r3.h[:, :].rearrange("p (g x) -> p g x", g=8)
        self.yTb = [r3.buf("yT%d" % g) for g in range(8)]
        for g in range(NG):
            self._ssm_prefix_group(g)
        self.dump("stash", self.stash[:, :, 0:512], self.yTb, [128, 8, 512])

    def _ssm_prefix_group(self, g):
        S, d = self.S, self.d
        r3, r4 = self.r3, self.r4
        NP = 896
        ranges = [(0, 384), (384, 384), (768, 128)]
        abufs = self.hTb[0:7]
        if True:
            r4.recarve()
            pre2 = [r4.alloc([3 + NP], F32) for _ in range(2)]; pre2b = [r4.buf("pre%d" % i) for i in range(2)]
            acc = r4.alloc([NP], F32); accb = r4.buf("acc")
            tt = r4.alloc([NP], F32); ttb = r4.buf("tt")
            cvo = [r4.alloc([NP], BF16) for _ in range(2)]; cvob = [r4.buf("cvo%d" % i) for i in range(2)]
            xs_tm = r4.alloc([7, 512], BF16); xsb = r4.buf("xs_tm")
            B_tm = r4.alloc([7, 128], BF16); Btmb = r4.buf("B_tm")
            xw = r4.alloc([7, 512], BF16); xwb = r4.buf("xw")
            for i in range(2):
                S.op("dve", lambda e, i=i: e.memset(pre2[i][:, 0:3], 0.0), writes=(pre2b[i],))
            sm, smb = self.batch_smalls(r4, 7, 0, g, suffix=True)
            cols = [XBC0 + g * 512 + j * 128 for j in range(4)] + [XBC0 + DI + g * 128]
            pending = []
            wv = None
            for bi, c0 in enumerate(cols):
                blk = (c0 - XBC0) // 128
                if bi in (0, 2):
                    wv, wb = self.wload([d["w_in"][:, c0:c0 + 256]], 16)
                    e0 = 0
                elif bi == 4:
                    wv, wb = self.wload([d["w_in"][:, c0:c0 + 128]], 16)
                    e0 = 0
                else:
                    e0 = 128
                outs = self.proj_fm(wv, wb, e0, 16, self.hT, abufs, ranges)
                pre, preb = pre2[bi % 2], pre2b[bi % 2]
                for (pb, ps), (t0, n) in zip(outs, ranges):
                    S.op("act", lambda e, ps=ps, t0=t0, n=n, pre=pre: e.activation(out=pre[:, 3 + t0:3 + t0 + n], in_=ps, func=AF.Copy),
                         reads=(pb,), writes=(preb,))
                for f in pending:
                    f()
                pending = []
                cv, cvb = cvo[bi % 2], cvob[bi % 2]
                self.conv_silu(pre, preb, NP, blk, acc, accb, tt, ttb, cv, cvb)
                tiles = [cv[:, c * 128:(c + 1) * 128] for c in range(7)]
                if bi < 4:
                    pending.append(lambda tiles=tiles, cvb=cvb, bi=bi: self.transpose_to(
                        tiles, [cvb], lambda n: xs_tm[:, 0:7, bi * 128:(bi + 1) * 128], [xsb]))
                else:
                    pending.append(lambda tiles=tiles, cvb=cvb: self.transpose_to(tiles, [cvb], lambda n: B_tm[:, 0:7, :], [Btmb]))
            for f in pending:
                f()
            S.op("dve", lambda e: e.tensor_tensor(out=xw.rearrange("p c (j q) -> p c j q", q=64),
                                                  in0=xs_tm.rearrange("p c (j q) -> p c j q", q=64),
                                                  in1=sm["dtw"].unsqueeze(3).to_broadcast([128, 7, 8, 64]), op=ALU.mult),
                 reads=(xsb, smb), writes=(xwb,))
            pb, ps = self.bank()
            fns = [lambda e, c=c, ps=ps: e.matmul(ps[:, 0:512], lhsT=B_tm[:, c, :], rhs=xw[:, c, :], start=(c == 0), stop=(c == 6))
                   for c in range(7)]
            S.op("pe", fns, reads=(Btmb, xwb), writes=(pb,))
            S.op("act", lambda e, g=g, ps=ps: e.activation(out=self.stash[:, g, 0:512], in_=ps[:, 0:512], func=AF.Copy),
                 reads=(pb,), writes=(self.yTb[g],))

    def ssm_main(self):
        S, d = self.S, self.d
        r2, r3, r4 = self.r2, self.r3, self.r4
        yT = r3.h[:, :].bitcast(BF16).rearrange("p (k t) -> p k t", t=NT)
        self.yT = yT
        self.nwsem = S.new_dma_sem("nwsem")
        for g in range(NG):
            self._ssm_main_group(g)
        self.dump("yT", yT, self.yTb, [128, 32, NT], BF16)

    def _ssm_main_group(self, g):
        S, d = self.S, self.d
        r2, r3, r4 = self.r2, self.r3, self.r4
        yT = self.yT
        NPRE = NT + 3
        nsem = self.nwsem
        ranges = [(893 + i * 385, 385) for i in range(3)]
        abufs = self.hTb[6:16]
        if True:
            r2.recarve(); r4.recarve()
            pre2 = [r4.alloc([NPRE + 1], BF16) for _ in range(2)]; pre2b = [r4.buf("pre%d" % i) for i in range(2)]
            acc = r4.alloc([384], F32); accb = r4.buf("acc")
            tt = r4.alloc([384], F32); ttb = r4.buf("tt")
            cvo = [r2.alloc([384], BF16) for _ in range(3)]; cvob = [r2.buf("cvo%d" % i) for i in range(3)]
            BT = r4.alloc([NT], BF16); BTb = r4.buf("BT")
            CT = r4.alloc([NT], BF16); CTb = r4.buf("CT")
            xs_tm = r4.alloc([NMC, 512], BF16); xsb = r4.buf("xs_tm")
            B_tm = r4.alloc([NMC, 128], BF16); Btmb = r4.buf("B_tm")
            zs_all = r4.alloc([NMC, 512], BF16); zsab = r4.buf("zs_all")
            ztmp = r4.alloc([256], F32); ztb = r4.buf("ztmp")
            hiT = r2.alloc([NMC, 128], BF16); loT = r4.alloc([NMC, 128], BF16); hib = r4.buf("hiloT")
            normw = r2.alloc([512], F32); nwb = r2.buf("normw")
            Scur = r2.alloc([512], F32); Sb = r2.buf("Scur")
            _x = r2.alloc([512], BF16); _xb = r2.buf("xdt"); xdt = [_x, _x]; xdtb = [_xb, _xb]
            _x2 = r2.alloc([512], BF16); _x2b = r2.buf("xw"); xw = [_x2, _x2]; xwb = [_x2b, _x2b]
            _sb = r2.alloc([512], BF16); _sbb = r2.buf("Sbf"); Sbf2 = [_sb, _sb]; Sbfb2 = [_sbb, _sbb]
            _seg = r2.alloc([8, 128], BF16); _segb = r2.buf("segT"); segT = [_seg, _seg]; segb = [_segb, _segb]
            _mt = r2.alloc([8, 128], BF16); _mtb = r2.buf("MT"); MT = [_mt, _mt]; MTb = [_mtb, _mtb]
            _c = r2.alloc([128], BF16); _cb = r2.buf("cbT"); cbT = [_c, _c]; cbTb = [_cb, _cb]
            t1 = r2.alloc([512], F32); t1b = r2.buf("t1")
            yy = r2.alloc([512], F32); yb = r2.buf("y")
            yn = r2.alloc([512], BF16); ynb = r2.buf("yn")
            ysm = r2.alloc([4], F32); ysmb = r2.buf("ysm")
            hl8 = r2.alloc([2, NMC * 8], BF16); hl8b = r2.buf("hl8")
            S.op("act", lambda e, g=g: e.activation(out=Scur, in_=self.stash[:, g, 0:512], func=AF.Copy),
                 reads=(self.yTb[g],), writes=(Sb,))
            S.dma("sp", lambda e, g=g: e.dma_start(out=normw, in_=d["ssm_norm_w"][:, g * 512:(g + 1) * 512].partition_broadcast(128)),
                  nsem, writes=(nwb,))
            sm, smb = self.batch_smalls(r2, NMC, 7, g)
            acf = sm["acum"].rearrange("p c j -> p (c j)")
            S.op("dve", lambda e: e.tensor_copy(out=hl8[:, 0, :], in_=acf), reads=(smb,), writes=(hl8b,))
            S.op("dve", lambda e: e.tensor_tensor(out=hl8[:, 1, :], in0=acf, in1=hl8[:, 0, :], op=ALU.subtract),
                 reads=(smb, hl8b), writes=(hl8b,))
            if g == 0:
                self.dump("hl8", hl8, [hl8b], [128, 2, NMC * 8], BF16)
            tb = [self.bank() for _ in range(3)]
            tv = [p[1][:].bitcast(BF16).rearrange("p (a b) -> p a b", b=128) for p in tb]
            fns = []
            for c in range(NMC):
                for hl in range(2):
                    if c < 8:
                        dst = tv[hl][0:8, c, :]
                    else:
                        dst = tv[2][0:8, hl, :]
                    fns.append(lambda e, c=c, hl=hl, dst=dst: e.transpose(out=dst, in_=hl8[:, hl, c * 8:(c + 1) * 8], identity=self.ident_bf))
            S.op("pe", fns, reads=(hl8b, self.cb), writes=[p[0] for p in tb])
            S.op("act", [
                lambda e: e.activation(out=hiT[0:8, 0:8, :], in_=tv[0][0:8, 0:8, :], func=AF.Copy),
                lambda e: e.activation(out=loT[0:8, 0:8, :], in_=tv[1][0:8, 0:8, :], func=AF.Copy),
                lambda e: e.activation(out=hiT[0:8, 8, :], in_=tv[2][0:8, 0, :], func=AF.Copy),
                lambda e: e.activation(out=loT[0:8, 8, :], in_=tv[2][0:8, 1, :], func=AF.Copy),
            ], reads=[p[0] for p in tb], writes=(hib,))
            cols = [XBC0 + g * 512 + j * 128 for j in range(4)] + [XBC0 + DI + g * 128, XBC0 + DI + 1024 + g * 128]
            pending = []
            for bi, c0 in enumerate(cols):
                blk = (c0 - XBC0) // 128
                if bi in (0, 2):
                    wv, wb = self.wload([d["w_in"][:, c0:c0 + 256]], 16)
                    e0 = 0
                elif bi == 4:
                    wv, wb = self.wload([d["w_in"][:, c0:c0 + 128], d["w_in"][:, cols[5]:cols[5] + 128]], 16)
                    e0 = 0
                else:
                    e0 = 128
                outs = self.proj_fm(wv, wb, e0, 16, self.hT, abufs, ranges)
                pre, preb = pre2[bi % 2], pre2b[bi % 2]
                for i, (pb, ps) in enumerate(outs):
                    S.op("act", lambda e, ps=ps, i=i, pre=pre: e.activation(out=pre[:, i * 385:(i + 1) * 385], in_=ps, func=AF.Copy),
                         reads=(pb,), writes=(preb,))
                if bi in (1, 3):
                    hv = bi // 2
                    zc0 = Z0 + g * 512 + hv * 256
                    wvz, wbz = self.wload([d["w_in"][:, zc0:zc0 + 256]], 16)
                    for mc in range(NMC):
                        t0z = 896 + mc * 128
                        pbz, psz = self.proj_tm(wvz, wbz, 0, 256, 16, lambda kc, t0z=t0z: self.hT(kc, t0z, 128), [self.hTb[7 + mc]])
                        S.op("act", lambda e, psz=psz: e.activation(out=ztmp, in_=psz, func=AF.Tanh, scale=0.5), reads=(pbz,), writes=(ztb,))
                        S.op("dve", lambda e, psz=psz, mc=mc, hv=hv: e.scalar_tensor_tensor(
                            out=zs_all[:, mc, hv * 256:(hv + 1) * 256], in0=ztmp, scalar=1.0, in1=psz, op0=ALU.add, op1=ALU.mult),
                            reads=(ztb, pbz), writes=(zsab,))
                for f in pending:
                    f()
                pending = []
                for th in range(3):
                    cv, cvb = cvo[th], cvob[th]
                    if bi < 4:
                        dst, dstb = cv, cvb
                    elif bi == 4:
                        dst, dstb = BT[:, th * 384:(th + 1) * 384], BTb
                    else:
                        dst, dstb = CT[:, th * 384:(th + 1) * 384], CTb
                    self.conv_silu(pre, preb, 384, blk, acc, accb, tt, ttb, dst, dstb, lo=th * 384)
                    if bi < 4:
                        tiles = [cv[:, c * 128:(c + 1) * 128] for c in range(3)]
                        pending.append(lambda tiles=tiles, cvb=cvb, bi=bi, th=th: self.transpose_to(
                            tiles, [cvb], lambda n: xs_tm[:, th * 3:th * 3 + 3, bi * 128:(bi + 1) * 128], [xsb]))
                    elif bi == 4:
                        tiles = [BT[:, (th * 3 + c) * 128:(th * 3 + c + 1) * 128] for c in range(3)]
                        pending.append(lambda tiles=tiles, th=th: self.transpose_to(
                            tiles, [BTb], lambda n: B_tm[:, th * 3:th * 3 + 3, :], [Btmb]))
            for f in pending:
                f()
            Dg = self.Db[:, g * 8:(g + 1) * 8]
            PB, PS = self.pbuf, self.ps

            def f_pe1(mc):
                k = mc % 2
                tc = slice(mc * 128, (mc + 1) * 128)
                t0 = 896 + mc * 128
                fns = []
                for bq in range(2):
                    psq = PS[bq]
                    fns.append(lambda e, psq=psq: e.matmul(psq[:, 0:512], lhsT=self.ident_bf, rhs=self.neg4, start=True, stop=False))
                    for jj in range(4):
                        j = bq * 4 + jj
                        for src in (hiT, loT):
                            fns.append(lambda e, psq=psq, jj=jj, j=j, src=src: e.matmul(
                                psq[:, jj * 128:(jj + 1) * 128], lhsT=self.sel[0:8, j * 128:(j + 1) * 128], rhs=src[0:8, mc, :],
                                start=False, stop=False))
                    for si, src in enumerate((hiT, loT)):
                        fns.append(lambda e, psq=psq, bq=bq, src=src, si=si: e.matmul(
                            psq[:, 0:512], lhsT=src[0:8, mc, :], rhs=self.nsel[0:8, bq * 512:(bq + 1) * 512],
                            start=False, stop=(si == 1)))
                S.op("pe", fns, reads=(hib, self.cb), writes=[PB[0], PB[1]])
                S.op("pe", lambda e: e.matmul(PS[2][:, 0:128], lhsT=BT[:, tc], rhs=CT[:, tc], start=True, stop=True),
                     reads=(BTb, CTb), writes=(PB[2],))
                S.op("act", [lambda e, bq=bq: e.activation(out=segT[k][:, bq * 4:(bq + 1) * 4, :].rearrange("p a b -> p (a b)"),
                                                           in_=PS[bq][:, 0:512], func=AF.Exp) for bq in range(2)],
                     reads=[PB[0], PB[1]], writes=(segb[k],))
                S.op("act", lambda e: e.activation(out=cbT[k], in_=PS[2][:, 0:128], func=AF.Copy), reads=(PB[2],), writes=(cbTb[k],))

            def f_dve(mc):
                k = mc % 2
                xs_c = xs_tm[:, mc, :]
                S.op("dve", lambda e: e.tensor_tensor(out=MT[k], in0=segT[k], in1=cbT[k].unsqueeze(1).to_broadcast([128, 8, 128]), op=ALU.mult),
                     reads=(segb[k], cbTb[k]), writes=(MTb[k],))
                S.op("dve", lambda e: e.tensor_tensor(out=xdt[k].rearrange("p (j q) -> p j q", q=64),
                                                      in0=xs_c.rearrange("p (j q) -> p j q", q=64),
                                                      in1=sm["dt"][:, mc, :].unsqueeze(2).to_broadcast([128, 8, 64]), op=ALU.mult),
                     reads=(xsb, self.dt_buf), writes=(xdtb[k],))
                S.op("dve", lambda e: e.tensor_tensor(out=xw[k].rearrange("p (j q) -> p j q", q=64),
                                                      in0=xs_c.rearrange("p (j q) -> p j q", q=64),
                                                      in1=sm["dtw"][:, mc, :].unsqueeze(2).to_broadcast([128, 8, 64]), op=ALU.mult),
                     reads=(xsb, smb), writes=(xwb[k],))
                S.op("pe", [lambda e, j=j: e.matmul(PS[3][:, j * 64:(j + 1) * 64], lhsT=MT[k][:, j, :], rhs=xdt[k][:, j * 64:(j + 1) * 64],
                                                    start=True, stop=True) for j in range(8)],
                     reads=(MTb[k], xdtb[k]), writes=(PB[3],))
                S.op("pe", lambda e: e.matmul(PS[4][:, 0:512], lhsT=B_tm[:, mc, :], rhs=xw[k], start=True, stop=True),
                     reads=(Btmb, xwb[k]), writes=(PB[4],))

            def b_a(mc):
                k = mc % 2
                tc = slice(mc * 128, (mc + 1) * 128)
                xs_c = xs_tm[:, mc, :]
                if mc == 0:
                    S.op("act", lambda e: e.activation(out=Sbf2[0], in_=Scur, func=AF.Copy), reads=(Sb,), writes=(Sbfb2[0],))
                S.op("pe", lambda e: e.matmul(PS[5][:, 0:512], lhsT=CT[:, tc], rhs=Sbf2[k], start=True, stop=True),
                     reads=(CTb, Sbfb2[k]), writes=(PB[5],))
                S.op("dve", lambda e: e.tensor_tensor(out=Scur.rearrange("p (j q) -> p j q", q=64),
                                                      in0=Scur.rearrange("p (j q) -> p j q", q=64),
                                                      in1=sm["eatot"][:, mc, :].unsqueeze(2).to_broadcast([128, 8, 64]), op=ALU.mult),
                     reads=(Sb, smb), writes=(Sb,))
                S.op("dve", lambda e: e.tensor_tensor(out=Scur, in0=Scur, in1=PS[4][:, 0:512], op=ALU.add),
                     reads=(Sb, PB[4]), writes=(Sb,))
                if mc == 0:
                    S.op("dve", lambda e: e.tensor_scalar(out=Scur, in0=Scur, scalar1=self.flag[:, 0:1], scalar2=None, op0=ALU.mult),
                         reads=(Sb, self.cb), writes=(Sb,))
                if mc + 1 < NMC:
                    S.op("act", lambda e: e.activation(out=Sbf2[1 - k], in_=Scur, func=AF.Copy), reads=(Sb,), writes=(Sbfb2[1 - k],))
                S.op("dve", lambda e: e.tensor_tensor(out=t1.rearrange("p (j q) -> p j q", q=64),
                                                      in0=PS[5][:, 0:512].rearrange("p (j q) -> p j q", q=64),
                                                      in1=sm["eacum"][:, mc, :].unsqueeze(2).to_broadcast([128, 8, 64]), op=ALU.mult),
                     reads=(PB[5], smb), writes=(t1b,))
                S.op("dve", lambda e: e.tensor_tensor(out=yy, in0=t1, in1=PS[3][:, 0:512], op=ALU.add),
                     reads=(t1b, PB[3]), writes=(yb,))
                S.op("dve", lambda e: e.tensor_tensor(out=t1.rearrange("p (j q) -> p j q", q=64),
                                                      in0=xs_c.rearrange("p (j q) -> p j q", q=64),
                                                      in1=Dg.unsqueeze(2).to_broadcast([128, 8, 64]), op=ALU.mult),
                     reads=(xsb, self.cb, yb), writes=(t1b,))
                S.op("dve", lambda e: e.tensor_tensor(out=yy, in0=yy, in1=t1, op=ALU.add), reads=(t1b, yb), writes=(yb,))
                S.op("dve", lambda e: e.tensor_tensor(out=yy, in0=yy, in1=zs_all[:, mc, :], op=ALU.mult), reads=(yb, zsab), writes=(yb,))

            def b_c(mc):
                S.op("act", lambda e: e.activation(out=yn, in_=yy, func=AF.Square, accum_out=ysm[:, 0:1]), reads=(yb,), writes=(ynb, ysmb))
                S.op("act", lambda e: e.activation(out=ysm[:, 1:2], in_=ysm[:, 0:1], func=AF.Ln, bias=4 * EPS, scale=1.0 / 512),
                     reads=(ysmb,), writes=(ysmb,))
                S.op("act", lambda e: e.activation(out=ysm[:, 2:3], in_=ysm[:, 1:2], func=AF.Exp, scale=-0.5), reads=(ysmb,), writes=(ysmb,))

            def b_e(mc):
                tc = slice(mc * 128, (mc + 1) * 128)
                S.op("dve", lambda e: e.scalar_tensor_tensor(out=yn, in0=yy, scalar=ysm[:, 2:3], in1=normw, op0=ALU.mult, op1=ALU.mult),
                     reads=(yb, ysmb, nwb), writes=(ynb,))
                psb = PS[2][:].bitcast(BF16).rearrange("p (a b) -> p a b", b=128)
                S.op("pe", [lambda e, i=i: e.transpose(out=psb[:, i, :], in_=yn[:, i * 128:(i + 1) * 128], identity=self.ident_bf)
                            for i in range(4)], reads=(ynb, self.cb), writes=(PB[2],))
                S.op("act", lambda e: e.activation(out=yT[:, 4 * g:4 * g + 4, tc], in_=psb[:, 0:4, :], func=AF.Copy),
                     reads=(PB[2],), writes=(self.yTb[g],))

            f_pe1(0)
            f_dve(0)
            for mc in range(NMC):
                b_a(mc)
                if mc + 1 < NMC:
                    f_pe1(mc + 1)
                b_c(mc)
                if mc + 1 < NMC:
                    f_dve(mc + 1)
                b_e(mc)

    def ssm_out(self):
        S, d = self.S, self.d
        r2, r4 = self.r2, self.r4
        r2.recarve(); r4.recarve()
        self.mixedT = r4.alloc([16, NT], BF16)
        self.mixb = [r4.buf("mixedT%d" % i) for i in range(16)]
        tg = r2.alloc([3, 384], F32); tgb = r2.buf("tg")
        self.tg, self.tgb = tg, tgb
        ranges = [(896 + i * 384, 384) for i in range(3)]
        mainbufs = self.hTb[7:16]
        yT = self.yT
        for db in range(16):
            if db % 2 == 0:
                wvg, wbg = self.wload([d["w_in"][:, GS0 + db * 128:GS0 + db * 128 + 256]], 16)
            outs = self.proj_fm(wvg, wbg, (db % 2) * 128, 16, self.hT, mainbufs, ranges)
            for i, (pb, ps) in enumerate(outs):
                S.op("act", lambda e, ps=ps, i=i: e.activation(out=tg[:, i, :], in_=ps, func=AF.Tanh, scale=0.5),
                     reads=(pb,), writes=(tgb,))
            wv, wb = self.wload([d["w_ssm_out"][:, db * 128:(db + 1) * 128]], 32)
            outs = self.proj_fm(wv, wb, 0, 32, lambda kc, t0, n: yT[:, kc, t0 - 896:t0 - 896 + n], self.yTb, ranges)
            for i, (pb, ps) in enumerate(outs):
                S.op("dve", lambda e, ps=ps, i=i, db=db: e.scalar_tensor_tensor(
                    out=self.mixedT[:, db, i * 384:(i + 1) * 384], in0=tg[:, i, :], scalar=1.0, in1=ps, op0=ALU.add, op1=ALU.mult),
                    reads=(tgb, pb), writes=(self.mixb[db],))
        self.dump("mixS", self.mixedT, self.mixb, [128, 16, NT], BF16)

    def attention(self):
        S, d = self.S, self.d
        r2, r3 = self.r2, self.r3
        r2.recarve(); r3.recarve()
        self.aoT = r3.alloc([16, NT], BF16)
        self.aoTb = [r3.buf("aoT%d" % i) for i in range(NKV)]
        a = {}
        a["qT"] = r3.alloc([4, NT], BF16); a["qTb"] = r3.buf("qT")
        a["kT2"] = r3.alloc([TM], BF16); a["kTb"] = r3.buf("kT2")
        a["v1"] = r3.alloc([10, 65], BF16); a["v1b"] = r3.buf("v1")
        a["biasT"] = r3.alloc([8, 2, 128], F32); a["biasb"] = r3.buf("biasT")
        a["q32"] = [r3.alloc([512], F32) for _ in range(2)]; a["q32b"] = [r3.buf("q32_%d" % i) for i in range(2)]
        a["qtmp"] = [r3.alloc([512], F32) for _ in range(2)]; a["qtmpb"] = [r3.buf("qtmp%d" % i) for i in range(2)]
        a["qn"] = [r3.alloc([512], BF16) for _ in range(2)]; a["qnb"] = [r3.buf("qn%d" % i) for i in range(2)]
        a["kv32"] = [r3.alloc([128], F32) for _ in range(2)]; a["kvb"] = [r3.buf("kv32_%d" % i) for i in range(2)]
        a["ktmp"] = [r3.alloc([64], F32) for _ in range(2)]; a["ktmpb"] = [r3.buf("ktmp%d" % i) for i in range(2)]
        a["kdup"] = [r3.alloc([2, 64], BF16) for _ in range(2)]; a["kdupb"] = [r3.buf("kdup%d" % i) for i in range(2)]
        a["qs"] = [r3.alloc([32], F32) for _ in range(2)]; a["qsb"] = [r3.buf("qs%d" % i) for i in range(2)]
        a["ltmp"] = [r2.alloc([512], F32) for _ in range(2)]; a["ltb"] = [r2.buf("ltmp%d" % i) for i in range(2)]
        a["eT"] = [[r2.alloc([2, 2, 128], BF16) for _ in range(4)] for _ in range(2)]
        a["eTb"] = [[r2.buf("eT%d_%d" % (p, i)) for i in range(4)] for p in range(2)]
        a["den"] = [r2.alloc([16], F32) for _ in range(2)]; a["denb"] = [r2.buf("den%d" % i) for i in range(2)]
        a["ao"] = [r2.alloc([8, 64], BF16) for _ in range(2)]; a["aob"] = [r2.buf("ao%d" % i) for i in range(2)]
        a["bsem"] = S.new_dma_sem("biassem")
        S.op("dve", lambda e: e.memset(a["v1"][:, :, 64:65], 1.0), writes=(a["v1b"],))
        self.at = a
        for kg in range(NKV):
            self._attn_group(kg)
        self.dump("aoT", self.aoT, self.aoTb, [128, 16, NT], BF16)

    def _attn_group(self, kg):
        S, d, a = self.S, self.d, self.at
        qT, qTb, kT2, kTb, v1, v1b, biasT, biasb = a["qT"], a["qTb"], a["kT2"], a["kTb"], a["v1"], a["v1b"], a["biasT"], a["biasb"]
        PB, PS = self.pbuf, self.ps
        S.dma("sp", lambda e: e.dma_start(out=biasT, in_=d["biasT"][kg].rearrange("p (h b q) -> p h b q", h=8, b=2)),
              a["bsem"], writes=(biasb,))
        wv, wb = self.wload([d["w_in"][:, K0 + kg * 64:K0 + kg * 64 + 64], d["w_in"][:, V0 + kg * 64:V0 + kg * 64 + 64]], 16)
        pend = None
        for cj in range(10):
            i = cj % 2
            kv32, kvb, ktmp, ktmpb, kdup, kdupb, qs, qsb = (a["kv32"][i], a["kvb"][i], a["ktmp"][i], a["ktmpb"][i],
                                                            a["kdup"][i], a["kdupb"][i], a["qs"][i], a["qsb"][i])
            t0 = 768 + cj * 128
            pb, ps = self.proj_tm(wv, wb, 0, 128, 16, lambda kc, t0=t0: self.hT(kc, t0, 128), [self.hTb[6 + cj]])
            if pend is not None:
                pend()
            S.op("act", lambda e, ps=ps, kv32=kv32: e.activation(out=kv32, in_=ps, func=AF.Copy), reads=(pb,), writes=(kvb,))
            S.op("act", lambda e, kv32=kv32, ktmp=ktmp, qs=qs: e.activation(out=ktmp, in_=kv32[:, 0:64], func=AF.Square, accum_out=qs[:, 0:1]),
                 reads=(kvb,), writes=(ktmpb, qsb))
            S.op("act", lambda e, qs=qs: e.activation(out=qs[:, 1:2], in_=qs[:, 0:1], func=AF.Ln, bias=EPS, scale=1.0 / 64),
                 reads=(qsb,), writes=(qsb,))
            S.op("act", lambda e, qs=qs: e.activation(out=qs[:, 2:3], in_=qs[:, 1:2], func=AF.Exp, scale=-0.5), reads=(qsb,), writes=(qsb,))
            S.op("dve", [
                lambda e, kv32=kv32, kdup=kdup, qs=qs: e.scalar_tensor_tensor(out=kdup[:, 0, :], in0=kv32[:, 0:64], scalar=qs[:, 2:3], in1=self.kg,
                                                                          op0=ALU.mult, op1=ALU.mult),
                lambda e, kv32=kv32, kdup=kdup, qs=qs: e.scalar_tensor_tensor(out=kdup[:, 1, :], in0=kv32[:, 0:64], scalar=qs[:, 2:3], in1=self.kg,
                                                                          op0=ALU.mult, op1=ALU.mult),
            ], reads=(kvb, qsb, self.cb), writes=(kdupb,))
            S.op("act", lambda e, cj=cj, kv32=kv32: e.activation(out=v1[:, cj, 0:64], in_=kv32[:, 64:128], func=AF.Copy),
                 reads=(kvb,), writes=(v1b,))
            pend = (lambda kdup=kdup, kdupb=kdupb, cj=cj: self.transpose_to(
                [kdup.rearrange("p a b -> p (a b)")], [kdupb], lambda n: kT2[:, cj * 128:(cj + 1) * 128].unsqueeze(1), [kTb]))
        pend()
        wvs = []
        for hv in range(2):
            c0 = Q0 + kg * 512 + hv * 256
            wvs.append(self.wload([d["w_in"][:, c0:c0 + 256]], 16))
        pend = None
        for mc in range(NMC):
            i = mc % 2
            q32, q32b, qtmp, qtmpb, qn, qnb, qs, qsb = (a["q32"][i], a["q32b"][i], a["qtmp"][i], a["qtmpb"][i],
                                                        a["qn"][i], a["qnb"][i], a["qs"][i], a["qsb"][i])
            t0 = 896 + mc * 128
            for hv in range(2):
                pb, ps = self.proj_tm(wvs[hv][0], wvs[hv][1], 0, 256, 16, lambda kc, t0=t0: self.hT(kc, t0, 128), [self.hTb[7 + mc]])
                S.op("act", lambda e, ps=ps, hv=hv, q32=q32: e.activation(out=q32[:, hv * 256:(hv + 1) * 256], in_=ps, func=AF.Copy),
                     reads=(pb,), writes=(q32b,))
            if pend is not None:
                pend()
            S.op("dve", lambda e, q32=q32, qtmp=qtmp: e.tensor_tensor(out=qtmp, in0=q32, in1=q32, op=ALU.mult), reads=(q32b,), writes=(qtmpb,))
            S.op("dve", lambda e, qtmp=qtmp, qs=qs: e.tensor_reduce(out=qs[:, 8:16], in_=qtmp.rearrange("p (h x) -> p h x", x=64), axis=AX.X, op=ALU.add),
                 reads=(qtmpb,), writes=(qsb,))
            S.op("act", lambda e, qs=qs: e.activation(out=qs[:, 16:24], in_=qs[:, 8:16], func=AF.Ln, bias=EPS, scale=1.0 / 64),
                 reads=(qsb,), writes=(qsb,))
            S.op("act", lambda e, qs=qs: e.activation(out=qs[:, 24:32], in_=qs[:, 16:24], func=AF.Exp, scale=-0.5), reads=(qsb,), writes=(qsb,))
            S.op("dve", lambda e, q32=q32, qtmp=qtmp, qs=qs: e.tensor_tensor(
                out=qtmp.rearrange("p (h x) -> p h x", x=64), in0=q32.rearrange("p (h x) -> p h x", x=64),
                in1=qs[:, 24:32].unsqueeze(2).to_broadcast([128, 8, 64]), op=ALU.mult), reads=(q32b, qsb), writes=(qtmpb,))
            S.op("dve", lambda e, qtmp=qtmp, qn=qn: e.tensor_tensor(
                out=qn.rearrange("p (h x) -> p h x", x=64), in0=qtmp.rearrange("p (h x) -> p h x", x=64),
                in1=self.qg.unsqueeze(1).to_broadcast([128, 8, 64]), op=ALU.mult), reads=(qtmpb, self.cb), writes=(qnb,))
            pend = (lambda qn=qn, qnb=qnb, mc=mc: self.transpose_to(
                [qn[:, j * 128:(j + 1) * 128] for j in range(4)], [qnb], lambda n: qT[:, 0:4, mc * 128:(mc + 1) * 128], [qTb]))
        pend()

        def L(mc):
            fns = []
            for qp in range(2):
                for hh in range(2):
                    psv = PS[qp * 2 + hh][:, 0:512].rearrange("p (a b q) -> p a b q", a=2, b=2)
                    for qq in range(2):
                        qt = qp * 2 + qq
                        for blk in range(2):
                            cj = mc + blk
                            fns.append(lambda e, hh=hh, blk=blk, cj=cj, psv=psv, qt=qt, qq=qq: e.matmul(
                                psv[:, qq, blk, :], lhsT=kT2[hh * 64:(hh + 1) * 64, cj * 128:(cj + 1) * 128],
                                rhs=qT[hh * 64:(hh + 1) * 64, qt, mc * 128:(mc + 1) * 128], start=True, stop=True))
            S.op("pe", fns, reads=(kTb, qTb), writes=[PB[0], PB[1], PB[2], PB[3]])

        def Sx(mc):
            par = mc % 2
            for idx in range(4):
                qp, hh = idx // 2, idx % 2
                lt, ltb = a["ltmp"][idx % 2], a["ltb"][idx % 2]
                eT, eTb = a["eT"][par][idx], a["eTb"][par][idx]
                psv = PS[idx][:, 0:512].rearrange("p (a b q) -> p a b q", a=2, b=2)
                h0 = qp * 4 + hh
                S.op("dve", lambda e, psv=psv, h0=h0, lt=lt: e.tensor_tensor(out=lt.rearrange("p (a b q) -> p a b q", a=2, b=2), in0=psv,
                                                                            in1=biasT[:, h0:h0 + 3:2, :, :], op=ALU.add),
                     reads=(PB[idx], biasb), writes=(ltb,))
                S.op("act", lambda e, eT=eT, lt=lt: e.activation(out=eT.rearrange("p a b q -> p (a b q)"), in_=lt, func=AF.Exp),
                     reads=(ltb,), writes=(eTb,))
                if mc == 1:
                    S.op("dve", lambda e, eT=eT: e.tensor_scalar(out=eT[:, :, 0, :], in0=eT[:, :, 0, :],
                                                                 scalar1=self.flag[:, 0:1], scalar2=None, op0=ALU.mult),
                         reads=(eTb, self.cb), writes=(eTb,))

        def P(mc):
            par = mc % 2
            base = 4 + 2 * par
            for qp in range(2):
                pvs = PS[base + qp]
                fns = []
                for hh in range(2):
                    eT = a["eT"][par][qp * 2 + hh]
                    for qq in range(2):
                        slot = (hh + 2 * qq)
                        for blk in range(2):
                            cj = mc + blk
                            fns.append(lambda e, eT=eT, qq=qq, blk=blk, cj=cj, pvs=pvs, slot=slot: e.matmul(
                                pvs[:, slot * 65:(slot + 1) * 65], lhsT=eT[:, qq, blk, :], rhs=v1[:, cj, :],
                                start=(blk == 0), stop=(blk == 1)))
                S.op("pe", fns, reads=(a["eTb"][par][qp * 2], a["eTb"][par][qp * 2 + 1], v1b), writes=[PB[base + qp]])

        def Fd(mc):
            par = mc % 2
            base = 4 + 2 * par
            den, denb, ao, aob = a["den"][par], a["denb"][par], a["ao"][par], a["aob"][par]
            for qp in range(2):
                pv3 = PS[base + qp][:, 0:260].rearrange("p (s x) -> p s x", x=65)
                S.op("dve", lambda e, pv3=pv3, qp=qp: e.tensor_tensor(
                    out=den[:, qp * 4:qp * 4 + 4].unsqueeze(2), in0=pv3[:, :, 64:65],
                    in1=self.esink[:, kg * 8 + qp * 4:kg * 8 + qp * 4 + 4].unsqueeze(2), op=ALU.add),
                    reads=(PB[base + qp], self.cb), writes=(denb,))
                S.op("dve", lambda e, qp=qp: e.reciprocal(out=den[:, 8 + qp * 4:8 + qp * 4 + 4], in_=den[:, qp * 4:qp * 4 + 4]),
                     reads=(denb,), writes=(denb,))
                S.op("dve", lambda e, pv3=pv3, qp=qp: e.tensor_tensor(
                    out=ao[:, qp * 4:qp * 4 + 4, :], in0=pv3[:, :, 0:64],
                    in1=den[:, 8 + qp * 4:8 + qp * 4 + 4].unsqueeze(2).to_broadcast([128, 4, 64]), op=ALU.mult),
                    reads=(PB[base + qp], denb), writes=(aob,))

        def T(mc):
            par = mc % 2
            ao, aob = a["ao"][par], a["aob"][par]
            aof = ao.rearrange("p h x -> p (h x)")
            psb = PS[0][:].bitcast(BF16).rearrange("p (a b) -> p a b", b=128)
            S.op("pe", [lambda e, j=j: e.transpose(out=psb[:, j, :], in_=aof[:, j * 128:(j + 1) * 128], identity=self.ident_bf)
                        for j in range(4)], reads=(aob, self.cb), writes=(PB[0],))
            S.op("act", lambda e: e.activation(out=self.aoT[:, kg * 4:kg * 4 + 4, mc * 128:(mc + 1) * 128], in_=psb[:, 0:4, :], func=AF.Copy),
                 reads=(PB[0],), writes=(self.aoTb[kg],))

        L(0)
        Sx(0)
        for mc in range(NMC):
            if mc >= 1:
                T(mc - 1)
            if mc + 1 < NMC:
                L(mc + 1)
            P(mc)
            if mc + 1 < NMC:
                Sx(mc + 1)
            Fd(mc)
        T(NMC - 1)

    def attn_out(self):
        S, d = self.S, self.d
        r2 = self.r2
        r2.recarve()
        tg = r2.alloc([3, 384], F32); tgb = r2.buf("tg")
        mt = r2.alloc([384], F32); mtb = r2.buf("mtmp")
        ranges = [(896 + i * 384, 384) for i in range(3)]
        mainbufs = self.hTb[7:16]
        for db in range(16):
            if db % 2 == 0:
                wvg, wbg = self.wload([d["w_in"][:, GA0 + db * 128:GA0 + db * 128 + 256]], 16)
                wva, wba = self.wload([d["w_attn_out"][:, db * 128:db * 128 + 256]], 16)
            outs = self.proj_fm(wvg, wbg, (db % 2) * 128, 16, self.hT, mainbufs, ranges)
            for i, (pb, ps) in enumerate(outs):
                S.op("act", lambda e, ps=ps, i=i: e.activation(out=tg[:, i, :], in_=ps, func=AF.Tanh, scale=0.5),
                     reads=(pb,), writes=(tgb,))
            outs = self.proj_fm(wva, wba, (db % 2) * 128, 16, lambda kc, t0, n: self.aoT[:, kc, t0 - 896:t0 - 896 + n],
                                self.aoTb, ranges)
            for i, (pb, ps) in enumerate(outs):
                S.op("dve", lambda e, ps=ps, i=i: e.scalar_tensor_tensor(out=mt, in0=tg[:, i, :], scalar=1.0, in1=ps,
                                                                        op0=ALU.add, op1=ALU.mult),
                     reads=(tgb, pb), writes=(mtb,))
                S.op("dve", lambda e, i=i, db=db: e.tensor_tensor(out=self.mixedT[:, db, i * 384:(i + 1) * 384],
                                                                  in0=self.mixedT[:, db, i * 384:(i + 1) * 384], in1=mt, op=ALU.add),
                     reads=(mtb, self.mixb[db]), writes=(self.mixb[db],))
        self.dump("mixed", self.mixedT, self.mixb, [128, 16, NT], BF16)

    def wout_residual(self):
        S, d = self.S, self.d
        r1, r2, r3 = self.r1, self.r2, self.r3
        r3.recarve()
        self.x1 = r3.alloc([NMC, D], F32)
        self.x1b = [r3.buf("x1_%d" % i) for i in range(NMC)]
        x1, x1b = self.x1, self.x1b
        xsem = S.new_dma_sem("x1sem")
        for mc in range(NMC):
            S.dma("sp", lambda e, mc=mc: e.dma_start(out=x1[:, mc, :], in_=d["xm"][mc * 128:(mc + 1) * 128, :]), xsem,
                  writes=(x1b[mc],))
        for mc in range(NMC):
            x1b[mc].w = (xsem, S.dcnt[xsem])
        for ct in range(8):
            wv, wb = self.wload([d["w_out"][:, ct * 256:(ct + 1) * 256]], 16)
            for mc in range(NMC):
                pb, ps = self.proj_tm(wv, wb, 0, 256, 16, lambda kc, mc=mc: self.mixedT[:, kc, mc * 128:(mc + 1) * 128], self.mixb)
                S.op("dve", lambda e, ps=ps, mc=mc, ct=ct: e.scalar_tensor_tensor(
                    out=x1[:, mc, ct * 256:(ct + 1) * 256], in0=ps, scalar=0.5, in1=x1[:, mc, ct * 256:(ct + 1) * 256],
                    op0=ALU.mult, op1=ALU.add), reads=(pb, x1b[mc]), writes=(x1b[mc],))
        self.dump("x1", x1, x1b, [128, NMC, D])
        r1.recarve(); r2.recarve()
        self.hfT = r1.alloc([16, NT], BF16)
        self.hfTb = [r1.buf("hfT%d" % i) for i in range(NMC)]
        gain = r2.alloc([D], F32); gb = r2.buf("gain")
        hb = [r2.alloc([D], BF16) for _ in range(2)]
        hbb = [r2.buf("hb%d" % i) for i in range(2)]
        S.dma("sp", lambda e: e.dma_start(out=gain, in_=d["norm_ffn_w"].partition_broadcast(128)), S.new_dma_sem("gsem1"), writes=(gb,))
        for mc in range(NMC):
            s = mc % 2
            self.rms_tile(x1[:, mc, :], [x1b[mc]], gain, gb, hb[s], hbb[s], mc, hb[s])
            for half in range(2):
                tiles = [hb[s][:, (half * 8 + i) * 128:(half * 8 + i + 1) * 128] for i in range(8)]
                self.transpose_to(tiles, [hbb[s]],
                                  lambda n, half=half, mc=mc: self.hfT[:, half * 8:half * 8 + 8, mc * 128:(mc + 1) * 128],
                                  [self.hfTb[mc]])
        self.gain_ap, self.gain_b, self.hb2, self.hbb2 = gain, gb, hb, hbb

    def ffn(self):
        S, d = self.S, self.d
        r2, r4 = self.r2, self.r4
        r4.recarve()
        actT = r4.alloc([11, 1024], BF16); actb = r4.buf("actT")
        pre = [r4.alloc([2 + NT], F32) for _ in range(2)]
        preb = [r4.buf("fpre%d" % i) for i in range(2)]
        acc = [r2.alloc([1024], F32) for _ in range(2)]
        accb = [r2.buf("facc%d" % i) for i in range(2)]
        gl = r4.alloc([1024], F32); glb = r4.buf("gl")
        x1, x1b = self.x1, self.x1b
        ranges = [(i * 384, 384) for i in range(3)]
        for i in range(2):
            S.op("dve", lambda e, i=i: e.memset(pre[i][:, 0:2], 0.0), writes=(preb[i],))
        for fg in range(4):
            for jb in range(11):
                b = fg * 11 + jb
                c0 = b * 128
                wv, wb = self.wload([d["w_ffn_up"][:, c0:c0 + 128], d["w_ffn_up"][:, DFF + c0:DFF + c0 + 128]], 16)
                for gu in range(2):
                    blk = b + gu * 44
                    outs = self.proj_fm(wv, wb, gu * 128, 16, lambda kc, t0, n: self.hfT[:, kc, t0:t0 + n], self.hfTb, ranges)
                    for i, (pb, ps) in enumerate(outs):
                        S.op("act", lambda e, ps=ps, i=i, gu=gu: e.activation(out=pre[gu][:, 2 + i * 384:2 + (i + 1) * 384], in_=ps, func=AF.Copy),
                             reads=(pb,), writes=(preb[gu],))
                    S.op("dve", lambda e, gu=gu: e.tensor_scalar(out=pre[gu][:, 128:130], in0=pre[gu][:, 128:130],
                                                                 scalar1=self.flag[:, 0:1], scalar2=None, op0=ALU.mult),
                         reads=(preb[gu], self.cb), writes=(preb[gu],))
                    w = self.cw_ffn
                    S.op("dve", lambda e, gu=gu, blk=blk: e.tensor_scalar(out=acc[gu], in0=pre[gu][:, 130:130 + 1024], scalar1=w[:, blk, 2:3],
                                                                          scalar2=self.cb_ffn[:, blk:blk + 1], op0=ALU.mult, op1=ALU.add),
                         reads=(preb[gu], self.cb), writes=(accb[gu],))
                    for k in (1, 0):
                        S.op("dve", lambda e, gu=gu, blk=blk, k=k: e.scalar_tensor_tensor(
                            out=acc[gu], in0=pre[gu][:, 128 + k:128 + k + 1024], scalar=w[:, blk, k:k + 1], in1=acc[gu],
                            op0=ALU.mult, op1=ALU.add), reads=(preb[gu], self.cb, accb[gu]), writes=(accb[gu],))
                S.op("act", lambda e: e.activation(out=gl, in_=acc[0], func=AF.Gelu_apprx_tanh), reads=(accb[0],), writes=(glb,))
                S.op("dve", lambda e, jb=jb: e.tensor_tensor(out=actT[:, jb, :], in0=gl, in1=acc[1], op=ALU.mult),
                     reads=(glb, accb[1]), writes=(actb,))
            if fg == 0:
                self.dump("actT0", actT, [actb], [128, 11, 1024], BF16)
            for ct in range(8):
                wv, wb = self.wload([d["w_ffn_down"][fg * 1408:(fg + 1) * 1408, ct * 256:(ct + 1) * 256]], 11)
                for mc in range(1, NMC):
                    pb, ps = self.proj_tm(wv, wb, 0, 256, 11, lambda kc, mc=mc: actT[:, kc, (mc - 1) * 128:mc * 128], [actb])
                    S.op("dve", lambda e, ps=ps, mc=mc, ct=ct: e.tensor_tensor(
                        out=x1[:, mc, ct * 256:(ct + 1) * 256], in0=ps, in1=x1[:, mc, ct * 256:(ct + 1) * 256], op=ALU.add),
                        reads=(pb, x1b[mc]), writes=(x1b[mc],))
        self.dump("x2", x1, x1b, [128, NMC, D])

    def ple(self):
        S, d = self.S, self.d
        r1, r2, r4 = self.r1, self.r2, self.r4
        x1, x1b = self.x1, self.x1b
        r1.recarve(); r4.recarve()
        nT = r1.alloc([16, 1024], BF16)
        nTb = [r1.buf("nT%d" % i) for i in range(8)]
        gain, gb, hb, hbb = self.gain_ap, self.gain_b, self.hb2, self.hbb2
        S.dma("sp", lambda e: e.dma_start(out=gain, in_=d["ple_norm_w"].partition_broadcast(128)), S.new_dma_sem("gsem2"), writes=(gb,))
        pT = r4.alloc([2, 1024], BF16); pTb = r4.buf("pT")
        pt = [r4.alloc([PLE], F32) for _ in range(2)]; ptb = [r4.buf("pt%d" % i) for i in range(2)]
        pbf = [r4.alloc([PLE], BF16) for _ in range(2)]; pbfb = [r4.buf("pbf%d" % i) for i in range(2)]
        psem = [S.new_dma_sem("psem%d" % i) for i in range(2)]
        tgp = r4.alloc([256], F32); tgpb = r4.buf("tgp")
        up = r4.alloc([256], F32); upb = r4.buf("up")
        for mc in range(1, NMC):
            s = mc % 2
            o = mc - 1
            self.rms_tile(x1[:, mc, :], [x1b[mc]], gain, gb, hb[s], hbb[s], mc, hb[s])
            for half in range(2):
                tiles = [hb[s][:, (half * 8 + i) * 128:(half * 8 + i + 1) * 128] for i in range(8)]
                self.transpose_to(tiles, [hbb[s]],
                                  lambda n, half=half, o=o: nT[:, half * 8:half * 8 + 8, o * 128:(o + 1) * 128], [nTb[o]])
            S.dma("sp", lambda e, s=s, o=o: e.dma_start(out=pt[s], in_=d["pp"][o * 128:(o + 1) * 128, :]), psem[s], writes=(ptb[s],))
            S.op("act", lambda e, s=s: e.activation(out=pbf[s], in_=pt[s], func=AF.Copy), reads=(ptb[s],), writes=(pbfb[s],))
            tiles = [pbf[s][:, i * 128:(i + 1) * 128] for i in range(2)]
            self.transpose_to(tiles, [pbfb[s]], lambda n, o=o: pT[:, 0:2, o * 128:(o + 1) * 128], [pTb])
        for ct in range(8):
            wv, wb = self.wload([d["w_ple_gate"][:, ct * 256:(ct + 1) * 256]], 16)
            wv2, wb2 = self.wload([d["w_ple_proj"][:, ct * 256:(ct + 1) * 256]], 2)
            for mc in range(1, NMC):
                o = mc - 1
                pb, ps = self.bank()
                fns = []
                for kc in range(16):
                    fns.append(lambda e, kc=kc, ps=ps, o=o: e.matmul(ps[:, 0:256], lhsT=nT[:, kc, o * 128:(o + 1) * 128], rhs=wv[:, kc, 0:256],
                                                                     start=(kc == 0), stop=(kc == 15)))
                for kc in range(2):
                    fns.append(lambda e, kc=kc, ps=ps, o=o: e.matmul(ps[:, 256:512], lhsT=pT[:, kc, o * 128:(o + 1) * 128], rhs=wv2[:, kc, 0:256],
                                                                     start=(kc == 0), stop=(kc == 1)))
                S.op("pe", fns, reads=(wb, wb2, nTb[o], pTb), writes=(pb,))
                S.op("act", lambda e, ps=ps: e.activation(out=tgp, in_=ps[:, 0:256], func=AF.Tanh, scale=0.5), reads=(pb,), writes=(tgpb,))
                S.op("dve", lambda e, ps=ps: e.scalar_tensor_tensor(out=up, in0=tgp, scalar=1.0, in1=ps[:, 256:512], op0=ALU.add, op1=ALU.mult),
                     reads=(tgpb, pb), writes=(upb,))
                S.op("dve", lambda e, mc=mc, ct=ct: e.scalar_tensor_tensor(
                    out=x1[:, mc, ct * 256:(ct + 1) * 256], in0=up, scalar=0.5, in1=x1[:, mc, ct * 256:(ct + 1) * 256],
                    op0=ALU.mult, op1=ALU.add), reads=(upb, x1b[mc]), writes=(x1b[mc],))
        osem = S.new_dma_sem("osem")
        for mc in range(1, NMC):
            ob = Buf("out%d" % mc)
            S.dma("sp", lambda e, mc=mc: e.dma_start(out=self.out[(mc - 1) * 128:mc * 128, :], in_=x1[:, mc, :]), osem,
                  reads=(x1b[mc],), writes=(ob,))
            self.final_bufs.append(ob)


def _t5_bucket(dist):
    nb, md = 32, 128
    me = nb // 2
    dd = np.maximum(dist, 0)
    lr = np.log(np.maximum(dd, 1).astype(np.float32) / me) / np.log(md / me)
    large = me + (lr * (nb - me)).astype(np.int32)
    large = np.minimum(large, nb - 1)
    return np.where(dd < me, dd, large)


def _const_mats():
    ident = np.eye(128, dtype=np.float32)
    tri = (np.arange(128)[:, None] <= np.arange(128)[None, :]).astype(np.float32)
    ones = np.ones((128, 128), np.float32)
    neg = np.where(np.arange(128)[:, None] > np.arange(128)[None, :], -32768.0, 0.0).astype(np.float32)
    neg4 = np.tile(neg, (1, 4))
    sel = np.zeros((128, 8, 128), np.float32)
    for j in range(8):
        sel[j, j, :] = 1.0
    return (np.ascontiguousarray(np.concatenate([tri, ones], axis=1)),
            np.ascontiguousarray(np.concatenate([ident, neg4, sel.reshape(128, 1024)], axis=1)))


def _bias_tables(table):
    L = 128
    qi = np.arange(L)[:, None]
    kj = np.arange(2 * L)[None, :]
    dist = qi + L - kj
    band = (dist >= 0) & (dist < 128)
    bk = _t5_bucket(dist)
    b = table[bk]
    b = np.where(band[:, :, None], b, np.float32(NEGM)).astype(np.float32)
    b = b.reshape(L, 2, L, NKV, 8)
    b = np.transpose(b, (3, 2, 4, 1, 0))
    return np.ascontiguousarray(b).reshape(NKV, 128, 8 * 2 * 128)


def make_in_maps(inputs):
    x = np.asarray(inputs["x"], np.float32)
    p = np.asarray(inputs["p"], np.float32)[0]
    g = lambda k: np.ascontiguousarray(np.asarray(inputs[k], np.float32)[0])
    shared = {
        "w_in": g("w_in"), "w_attn_out": g("w_attn_out"), "w_ssm_out": g("w_ssm_out"), "w_out": g("w_out"),
        "w_ffn_up": g("w_ffn_up"), "w_ffn_down": g("w_ffn_down"), "w_ple_gate": g("w_ple_gate"),
        "w_ple_proj": g("w_ple_proj"),
        "norm_mix_w": g("norm_mix_w")[None], "norm_ffn_w": g("norm_ffn_w")[None], "ple_norm_w": g("ple_norm_w")[None],
        "ssm_norm_w": g("ssm_norm_w")[None], "q_norm_w": g("q_norm_w")[None], "k_norm_w": g("k_norm_w")[None],
        "attn_sinks": g("attn_sinks")[None], "ssm_A_log": g("ssm_A_log")[None], "ssm_dt_bias": g("ssm_dt_bias")[None],
        "ssm_D": g("ssm_D")[None],
        "cw_ssm": np.ascontiguousarray(g("ssm_conv_w").T.reshape(48, 128, 4).transpose(1, 0, 2)).reshape(128, 192),
        "cb_ssm": np.ascontiguousarray(g("ssm_conv_b").reshape(48, 128).T),
        "cw_ffn": np.ascontiguousarray(g("ffn_conv_w").T.reshape(88, 128, 3).transpose(1, 0, 2)).reshape(128, 264),
        "cb_ffn": np.ascontiguousarray(g("ffn_conv_b").reshape(88, 128).T),
        "biasT": _bias_tables(np.asarray(inputs["rel_bias_table"], np.float32)),
    }
    shared["cm_f"], shared["cm_b"] = _const_mats()
    in_maps = []
    for core in range(8):
        b, hf = core // 2, core % 2
        s0 = hf * 1024
        xm = np.zeros((NT, D), np.float32)
        xp = np.zeros((896, D), np.float32)
        if hf == 1:
            xm[:] = x[b, s0 - 128:s0 + 1024]
            xp[:] = x[b, 0:896]
        else:
            xm[128:] = x[b, 0:1024]
        m = dict(shared)
        m["xm"] = xm
        m["xp"] = xp
        m["pp"] = np.ascontiguousarray(p[b, s0:s0 + 1024])
        m["flag"] = np.full((128, 1), float(hf), np.float32)
        in_maps.append(m)
    return in_maps


def kernel(**inputs):
    in_maps = make_in_maps(inputs)
    dbg = tuple(inputs.get("_debug", ())) if isinstance(inputs.get("_debug", ()), (list, tuple)) else ()
    bld = Builder(debug=dbg)
    nc = bld.build()
    cores = list(range(8))
    if inputs.get("_cores"):
        cores = list(inputs["_cores"])
    res = run_bass_kernel_spmd(nc, [in_maps[c] for c in cores], core_ids=list(range(len(cores))))
    out = np.zeros((BATCH, SEQ, D), np.float32)
    for i, core in enumerate(cores):
        b, hf = core // 2, core % 2
        out[b, hf * 1024:(hf + 1) * 1024] = res.results[i]["out"]
    if dbg:
        kernel.last_debug = [{k: r[v] for k, v in bld.dbg_out.items()} for r in res.results]
    return out
```

```python
import numpy as np
import concourse.bass as bass
import concourse.mybir as mybir
from concourse.bass_utils import run_bass_kernel_spmd

F32 = mybir.dt.float32
BF16 = mybir.dt.bfloat16
AF = mybir.ActivationFunctionType
ALU = mybir.AluOpType
AX = mybir.AxisListType

D = 2048
SEQ = 2048
BATCH = 4
NH = 32
NKV = 4
DH = 64
DI = 4096
NSH = 64
NG = 8
DS = 128
DFF = 5632
PLE = 256
EPS = 1e-6
Q0 = 0
K0 = 2048
V0 = 2304
Z0 = 2560
XBC0 = 6656
DT0 = 12800
GA0 = 12864
GS0 = 14912
IN_DIM = 16960

NMC = 9
NT = NMC * 128
TP = 768
TM = 1280
NEGM = -30000.0


class Buf:
    __slots__ = ("name", "w", "r")

    def __init__(self, name, base=None):
        self.name = name
        self.w = None
        self.r = dict(base) if base else {}


class Sched:
    ENG = ("pe", "act", "dve", "pool", "sp")

    def __init__(self):
        self.streams = {e: [] for e in self.ENG}
        self.cnt = {e: 0 for e in self.ENG}
        self.dcnt = {}
        self.waited = {e: {} for e in self.ENG}
        self.dma_sems = []

    def new_dma_sem(self, name):
        self.dma_sems.append(name)
        self.dcnt[name] = 0
        return name

    def _waits(self, eng, reads, writes):
        deps = {}

        def add(s, v):
            if deps.get(s, 0) < v:
                deps[s] = v

        for b in reads:
            if b.w is not None:
                add(*b.w)
        for b in writes:
            if b.w is not None:
                add(*b.w)
            for s, v in b.r.items():
                add(s, v)
        wd = self.waited[eng]
        st = self.streams[eng]
        for s, v in deps.items():
            if wd.get(s, 0) >= v:
                continue
            wd[s] = v
            st.append(("wait", s, v))

    def op(self, eng, fns, reads=(), writes=()):
        self._waits(eng, reads, writes)
        self.cnt[eng] += 1
        c = self.cnt[eng]
        if not isinstance(fns, (list, tuple)):
            fns = [fns]
        st = self.streams[eng]
        for f in fns[:-1]:
            st.append(("inst", f, None, 0))
        st.append(("inst", fns[-1], eng, 1))
        for b in reads:
            b.r[eng] = c
        for b in writes:
            b.w = (eng, c)
            b.r = {}

    def dma(self, eng, fn, sem, reads=(), writes=()):
        self._waits(eng, reads, writes)
        self.dcnt[sem] += 16
        c = self.dcnt[sem]
        self.streams[eng].append(("inst", fn, sem, 16))
        for b in reads:
            b.r[sem] = c
        for b in writes:
            b.w = (sem, c)
            b.r = {}

    def final_wait(self, eng, bufs):
        self._waits(eng, bufs, bufs)


def collect_tokens(bufs):
    r = {}
    for b in bufs:
        if b.w is not None and r.get(b.w[0], 0) < b.w[1]:
            r[b.w[0]] = b.w[1]
        for s, v in b.r.items():
            if r.get(s, 0) < v:
                r[s] = v
    return r


class Region:
    def __init__(self, name, handle, nwords):
        self.name = name
        self.h = handle
        self.nwords = nwords
        self.off = 0
        self.bufs = []
        self.base = {}

    def recarve(self):
        self.base = collect_tokens(self.bufs)
        self.bufs = []
        self.off = 0

    def buf(self, name):
        b = Buf(name, self.base)
        self.bufs.append(b)
        return b

    def alloc(self, shape, dtype):
        nel = 1
        for s in shape:
            nel *= s
        nbytes = nel * (2 if dtype == BF16 else 4)
        nw = (nbytes + 3) // 4
        assert self.off + nw <= self.nwords, (self.name, self.off, nw, self.nwords)
        ap = self.h[:, self.off:self.off + nw]
        self.off += nw
        if dtype == BF16:
            ap = ap.bitcast(BF16)
            if nel != nw * 2:
                ap = ap[:, 0:nel]
        if len(shape) == 2:
            return ap.rearrange("p (a b) -> p a b", b=shape[1])
        if len(shape) == 3:
            return ap.rearrange("p (a b c) -> p a b c", b=shape[1], c=shape[2])
        return ap


class Builder:
    def __init__(self, debug=(), nphases=99):
        self.nphases = nphases
        self.debug = set(debug)
        self.nc = bass.Bass("TRN2", target_bir_lowering=False)
        self.S = Sched()
        self.dbg_out = {}

    def dram_in(self, name, shape):
        return self.nc.dram_tensor(name, list(shape), F32, kind="ExternalInput").ap()

    def bank(self):
        i = self.bank_i
        self.bank_i = (i + 1) % 8
        return self.pbuf[i], self.ps[i]

    def wslot(self):
        i = self.w_i
        self.w_i = (i + 1) % len(self.wt)
        return i

    def wload(self, srcs, kcn):
        i = self.wslot()
        tot = sum(s.shape[1] for s in srcs)
        assert kcn * tot <= 4096
        view = self.wt[i][:, 0:kcn * tot].rearrange("p (k c) -> p k c", c=tot)
        c0 = 0
        for s in srcs:
            n = s.shape[1]
            src = s.rearrange("(k p) e -> p k e", p=128)
            dst = view[:, :, c0:c0 + n]
            self.S.dma("pool", lambda e, d=dst, s_=src: e.dma_start(out=d, in_=s_), self.wsem[i],
                       reads=(), writes=(self.wbuf[i],))
            c0 += n
        return view, self.wbuf[i]

    def dump(self, name, ap, bufs, shape, dtype=F32):
        if name not in self.debug:
            return
        t = self.nc.dram_tensor("dbg_" + name, list(shape), dtype, kind="ExternalOutput").ap()
        sem = self.S.new_dma_sem("dbgsem_" + name)
        db = Buf("dbg_" + name)
        self.S.dma("sp", lambda e, t=t, ap=ap: e.dma_start(out=t, in_=ap), sem, reads=bufs, writes=(db,))
        self.final_bufs.append(db)
        self.dbg_out[name] = "dbg_" + name

    def build(self):
        nc = self.nc
        S = self.S
        d = {}
        d["xm"] = self.dram_in("xm", [NT, D])
        d["xp"] = self.dram_in("xp", [896, D])
        d["pp"] = self.dram_in("pp", [1024, PLE])
        d["flag"] = self.dram_in("flag", [128, 1])
        d["w_in"] = self.dram_in("w_in", [D, IN_DIM])
        d["w_attn_out"] = self.dram_in("w_attn_out", [D, D])
        d["w_ssm_out"] = self.dram_in("w_ssm_out", [DI, D])
        d["w_out"] = self.dram_in("w_out", [D, D])
        d["w_ffn_up"] = self.dram_in("w_ffn_up", [D, 2 * DFF])
        d["w_ffn_down"] = self.dram_in("w_ffn_down", [DFF, D])
        d["w_ple_gate"] = self.dram_in("w_ple_gate", [D, D])
        d["w_ple_proj"] = self.dram_in("w_ple_proj", [PLE, D])
        d["norm_mix_w"] = self.dram_in("norm_mix_w", [1, D])
        d["norm_ffn_w"] = self.dram_in("norm_ffn_w", [1, D])
        d["ple_norm_w"] = self.dram_in("ple_norm_w", [1, D])
        d["ssm_norm_w"] = self.dram_in("ssm_norm_w", [1, DI])
        d["q_norm_w"] = self.dram_in("q_norm_w", [1, DH])
        d["k_norm_w"] = self.dram_in("k_norm_w", [1, DH])
        d["attn_sinks"] = self.dram_in("attn_sinks", [1, NH])
        d["ssm_A_log"] = self.dram_in("ssm_A_log", [1, NSH])
        d["ssm_dt_bias"] = self.dram_in("ssm_dt_bias", [1, NSH])
        d["ssm_D"] = self.dram_in("ssm_D", [1, NSH])
        d["cw_ssm"] = self.dram_in("cw_ssm", [128, 48 * 4])
        d["cb_ssm"] = self.dram_in("cb_ssm", [128, 48])
        d["cw_ffn"] = self.dram_in("cw_ffn", [128, 88 * 3])
        d["cb_ffn"] = self.dram_in("cb_ffn", [128, 88])
        d["biasT"] = self.dram_in("biasT", [NKV, 128, 8 * 2 * 128])
        d["cm_f"] = self.dram_in("cm_f", [128, 256])
        d["cm_b"] = self.dram_in("cm_b", [128, 128 + 512 + 1024])
        self.d = d
        self.out = nc.dram_tensor("out", [1024, D], F32, kind="ExternalOutput").ap()
        self.final_bufs = []

        R1W, R2W, R3W, R4W, CW = 10240, 6144, 18432, 9216, 4200
        import contextlib
        with contextlib.ExitStack() as es:
            def sb(name, shape, dt):
                return es.enter_context(nc.sbuf_tensor(name, shape, dt))
            r1 = Region("R1", sb("R1", [128, R1W], F32), R1W)
            r2 = Region("R2", sb("R2", [128, R2W], F32), R2W)
            r3 = Region("R3", sb("R3", [128, R3W], F32), R3W)
            r4 = Region("R4", sb("R4", [128, R4W], F32), R4W)
            rc = Region("RC", sb("RC", [128, CW], F32), CW)
            self.r1, self.r2, self.r3, self.r4, self.rc = r1, r2, r3, r4, rc
            self.wt = [sb("wt%d" % i, [128, 4096], BF16) for i in range(2)]
            self.wbuf = [Buf("wbuf%d" % i) for i in range(2)]
            self.wsem = [S.new_dma_sem("wsem%d" % i) for i in range(2)]
            self.w_i = 0
            self.ps = [es.enter_context(nc.psum_tensor("ps%d" % i, [128, 512], F32)) for i in range(8)]
            self.pbuf = [Buf("psum%d" % i) for i in range(8)]
            self.bank_i = 0

            phases = [self.setup_consts, self.phase0, self.dt_proj, self.ssm_prefix, self.ssm_main, self.ssm_out,
                      self.attention, self.attn_out, self.wout_residual, self.ffn, self.ple]
            for ph in phases[:self.nphases]:
                ph()

            S.final_wait("sp", self.final_bufs)

            sem_names = list(Sched.ENG) + S.dma_sems
            sems = {}
            for n in sem_names:
                sems[n] = es.enter_context(nc.semaphore(n))
            block = es.enter_context(nc.Block())

            def replay(engname):
                def run(e):
                    for it in S.streams[engname]:
                        if it[0] == "wait":
                            e.wait_ge(sems[it[1]], it[2])
                        else:
                            ins = it[1](e)
                            if it[2] is not None:
                                ins.then_inc(sems[it[2]], it[3])
                return run

            block.sync(replay("sp"))
            block.gpsimd(replay("pool"))
            block.scalar(replay("act"))
            block.vector(replay("dve"))
            block.tensor(replay("pe"))
        return nc

    def setup_consts(self):
        S, d, rc = self.S, self.d, self.rc
        csem = S.new_dma_sem("csem")
        self.cb = Buf("consts")
        cb = self.cb

        def cload(dst, src):
            S.dma("sp", lambda e, dst=dst, src=src: e.dma_start(out=dst, in_=src), csem)

        cmf = rc.alloc([256], F32)
        cload(cmf, d["cm_f"])
        self.tri_f = cmf[:, 0:128]
        self.ones_f = cmf[:, 128:256]
        self.r3.recarve()
        cm = self.r3.alloc([128 + 512 + 1024], F32)
        cmb = self.r3.buf("cm_stage")
        S.dma("sp", lambda e: e.dma_start(out=cm, in_=d["cm_b"]), S.new_dma_sem("cmsem"), writes=(cmb,))
        self.ident_bf = rc.alloc([128], BF16)
        self.neg4 = rc.alloc([512], BF16)
        self.sel = rc.alloc([1024], BF16)
        self.nsel = rc.alloc([1024], BF16)
        self._cm = cm
        self.Ab = rc.alloc([64], F32)
        self.Db = rc.alloc([64], F32)
        self.dtb = rc.alloc([64], F32)
        self.flag = rc.alloc([1], F32)
        self.cw_ssm = rc.alloc([48, 4], F32)
        self.cb_ssm = rc.alloc([48], F32)
        self.cw_ffn = rc.alloc([88, 3], F32)
        self.cb_ffn = rc.alloc([88], F32)
        self.qg = rc.alloc([64], F32)
        self.kg = rc.alloc([64], F32)
        self.esink = rc.alloc([32], F32)
        self.dt_all = rc.alloc([16, 64], F32)
        self.ss16 = rc.alloc([16], F32)
        self.sd16 = rc.alloc([16], F32)
        self.rs16 = rc.alloc([16], F32)
        cload(self.Ab, d["ssm_A_log"].partition_broadcast(128))
        cload(self.Db, d["ssm_D"].partition_broadcast(128))
        cload(self.dtb, d["ssm_dt_bias"].partition_broadcast(128))
        cload(self.flag, d["flag"])
        cload(self.cw_ssm, d["cw_ssm"].rearrange("p (b k) -> p b k", k=4))
        cload(self.cb_ssm, d["cb_ssm"])
        cload(self.cw_ffn, d["cw_ffn"].rearrange("p (b k) -> p b k", k=3))
        cload(self.cb_ffn, d["cb_ffn"])
        cload(self.qg, d["q_norm_w"].partition_broadcast(128))
        cload(self.kg, d["k_norm_w"].partition_broadcast(128))
        cload(self.esink, d["attn_sinks"].partition_broadcast(128))
        cb.w = (csem, S.dcnt[csem])
        S.op("act", [
            lambda e: e.activation(out=self.ident_bf, in_=cm[:, 0:128], func=AF.Copy),
            lambda e: e.activation(out=self.neg4, in_=cm[:, 128:640], func=AF.Copy),
            lambda e: e.activation(out=self.sel, in_=cm[:, 640:1664], func=AF.Copy),
            lambda e: e.activation(out=self.nsel, in_=cm[:, 640:1664], func=AF.Copy, scale=-1.0),
            lambda e: e.activation(out=self.esink, in_=self.esink, func=AF.Exp),
            lambda e: e.activation(out=self.Ab, in_=self.Ab, func=AF.Exp),
        ], reads=(cb, cmb), writes=(cb,))
        S.op("dve", [
            lambda e: e.tensor_scalar(out=self.Ab, in0=self.Ab, scalar1=-1.0, scalar2=None, op0=ALU.mult),
            lambda e: e.tensor_scalar(out=self.qg, in0=self.qg, scalar1=0.125, scalar2=None, op0=ALU.mult),
            lambda e: e.tensor_scalar(out=self.cw_ssm, in0=self.cw_ssm, scalar1=0.5, scalar2=None, op0=ALU.mult),
            lambda e: e.tensor_scalar(out=self.cb_ssm, in0=self.cb_ssm, scalar1=0.5, scalar2=None, op0=ALU.mult),
        ], reads=(cb,), writes=(cb,))

    def hT(self, kc, t0, n):
        if t0 < TP:
            assert t0 + n <= TP
            return self.hTp[:, kc, t0:t0 + n]
        return self.hTm[:, kc, t0 - TP:t0 - TP + n]

    def hT_bufs(self, t0, n):
        c0 = t0 // 128
        c1 = (t0 + n - 1) // 128
        return [self.hTb[c] for c in range(c0, c1 + 1)]

    def rms_tile(self, x_ap, xbufs, gain_bc, gbuf, hb, hbbuf, sidx, scratch_bf, eps=EPS, n=D):
        S = self.S
        ssb = self.small_b[sidx]
        ss, sd, rs = self.ss16[:, sidx:sidx + 1], self.sd16[:, sidx:sidx + 1], self.rs16[:, sidx:sidx + 1]
        S.op("act", lambda e: e.activation(out=scratch_bf, in_=x_ap, func=AF.Square, accum_out=ss),
             reads=list(xbufs), writes=(ssb, hbbuf))
        S.op("act", lambda e: e.activation(out=sd, in_=ss, func=AF.Ln, bias=eps, scale=1.0 / n),
             reads=(ssb,), writes=(ssb,))
        S.op("act", lambda e: e.activation(out=rs, in_=sd, func=AF.Exp, scale=-0.5), reads=(ssb,), writes=(ssb,))
        S.op("dve", lambda e: e.scalar_tensor_tensor(out=hb, in0=x_ap, scalar=rs, in1=gain_bc,
                                                      op0=ALU.mult, op1=ALU.mult),
             reads=list(xbufs) + [ssb, gbuf], writes=(hbbuf,))

    def transpose_to(self, src_tiles, src_bufs, dst_ap_fn, dst_bufs, evac="act"):
        S = self.S
        n = len(src_tiles)
        pb, ps = self.bank()
        psb = ps[:].bitcast(BF16).rearrange("p (a b) -> p a b", b=128)
        fns = []
        for i, t in enumerate(src_tiles):
            fns.append(lambda e, i=i, t=t: e.transpose(out=psb[:, i, :], in_=t, identity=self.ident_bf))
        S.op("pe", fns, reads=list(src_bufs) + [self.cb], writes=(pb,))
        dst = dst_ap_fn(n)
        if evac == "act":
            S.op("act", lambda e: e.activation(out=dst, in_=psb[:, 0:n, :], func=AF.Copy),
                 reads=(pb,), writes=list(dst_bufs))
        else:
            S.op("dve", lambda e: e.tensor_copy(out=dst, in_=psb[:, 0:n, :]), reads=(pb,), writes=list(dst_bufs))

    def proj_fm(self, wview, wb, e0, kcn, act_fn, act_bufs, ranges):
        S = self.S
        banks = [self.bank() for _ in ranges]
        fns = []
        for kc in range(kcn):
            for (pb, ps), (t0, n) in zip(banks, ranges):
                fns.append(lambda e, kc=kc, ps=ps, t0=t0, n=n: e.matmul(
                    ps[:, 0:n], lhsT=wview[:, kc, e0:e0 + 128], rhs=act_fn(kc, t0, n),
                    start=(kc == 0), stop=(kc == kcn - 1)))
        S.op("pe", fns, reads=[wb] + list(act_bufs), writes=[pb for pb, _ in banks])
        return [(pb, ps[:, 0:n]) for (pb, ps), (t0, n) in zip(banks, ranges)]

    def proj_tm(self, wview, wb, c0, ncols, kcn, lhs_fn, act_bufs):
        S = self.S
        pb, ps = self.bank()
        fns = []
        for kc in range(kcn):
            fns.append(lambda e, kc=kc: e.matmul(ps[:, 0:ncols], lhsT=lhs_fn(kc), rhs=wview[:, kc, c0:c0 + ncols],
                                                 start=(kc == 0), stop=(kc == kcn - 1)))
        S.op("pe", fns, reads=[wb] + list(act_bufs), writes=(pb,))
        return pb, ps[:, 0:ncols]

    def phase0(self):
        S, d = self.S, self.d
        r1, r2, r4 = self.r1, self.r2, self.r4
        self.hTm = r1.alloc([16, TM], BF16)
        self.hTp = r2.alloc([16, TP], BF16)
        self.hTb = [Buf("hT%d" % c) for c in range(16)]
        r1.bufs += self.hTb[6:]
        r2.bufs += self.hTb[:6]
        self.small_b = [Buf("small%d" % i) for i in range(16)]
        r4.recarve()
        xt = [r4.alloc([D], F32) for _ in range(2)]
        xb = [r4.buf("xt%d" % i) for i in range(2)]
        xs = [S.new_dma_sem("xsem%d" % i) for i in range(2)]
        self.xt_sems = xs
        hb = [r4.alloc([D], BF16) for _ in range(2)]
        hbb = [r4.buf("hb%d" % i) for i in range(2)]
        gain = r4.alloc([D], F32)
        gb = r4.buf("gain")
        S.dma("sp", lambda e: e.dma_start(out=gain, in_=d["norm_mix_w"].partition_broadcast(128)), S.new_dma_sem("gsem0"), writes=(gb,))
        for ci in range(16):
            s = ci % 2
            src = d["xp"][ci * 128:(ci + 1) * 128, :] if ci < 7 else d["xm"][(ci - 7) * 128:(ci - 6) * 128, :]
            S.dma("sp", lambda e, s=s, src=src: e.dma_start(out=xt[s], in_=src), xs[s], writes=(xb[s],))
            self.rms_tile(xt[s], [xb[s]], gain, gb, hb[s], hbb[s], ci, hb[s])
            t0 = ci * 128
            for half in range(2):
                tiles = [hb[s][:, (half * 8 + i) * 128:(half * 8 + i + 1) * 128] for i in range(8)]
                self.transpose_to(tiles, [hbb[s]],
                                  lambda n, half=half, t0=t0: (self.hTp[:, half * 8:half * 8 + 8, t0:t0 + 128] if t0 < TP
                                                               else self.hTm[:, half * 8:half * 8 + 8, t0 - TP:t0 - TP + 128]),
                                  [self.hTb[ci]])
        self.dump("hTm", self.hTm, self.hTb[6:], [128, 16, TM], BF16)

    def dt_proj(self):
        S, d = self.S, self.d
        r4 = self.r4
        r4.recarve()
        raw = r4.alloc([16, 64], F32)
        rawb = r4.buf("dtraw")
        wv, wb = self.wload([d["w_in"][:, DT0:DT0 + 64]], 16)
        for ci in range(16):
            t0 = ci * 128
            pb, ps = self.proj_tm(wv, wb, 0, 64, 16, lambda kc, t0=t0: self.hT(kc, t0, 128), [self.hTb[ci]])
            S.op("dve", lambda e, ci=ci, ps=ps: e.tensor_tensor(out=raw[:, ci, :], in0=ps, in1=self.dtb, op=ALU.add),
                 reads=(pb, self.cb), writes=(rawb,))
        dtb_ = Buf("dt_all")
        self.dt_buf = dtb_
        S.op("act", lambda e: e.activation(out=raw, in_=raw, func=AF.Exp), reads=(rawb,), writes=(rawb,))
        S.op("act", lambda e: e.activation(out=self.dt_all, in_=raw, func=AF.Ln, bias=1.0, scale=1.0),
             reads=(rawb,), writes=(dtb_,))
        self.dump("dt_all", self.dt_all, [dtb_], [128, 16, 64])

    def conv_silu(self, pre, preb, n_out, blk, acc, accb, tt, ttb, out_bf, outb, lo=0):
        S = self.S
        w = self.cw_ssm
        S.op("dve", [
            lambda e: e.tensor_scalar(out=acc[:, 0:n_out], in0=pre[:, lo + 3:lo + 3 + n_out], scalar1=w[:, blk, 3:4],
                                      scalar2=self.cb_ssm[:, blk:blk + 1], op0=ALU.mult, op1=ALU.add)],
             reads=(preb, self.cb), writes=(accb,))
        for k in (2, 1, 0):
            S.op("dve", lambda e, k=k: e.scalar_tensor_tensor(out=acc[:, 0:n_out], in0=pre[:, lo + k:lo + k + n_out],
                                                             scalar=w[:, blk, k:k + 1], in1=acc[:, 0:n_out],
                                                             op0=ALU.mult, op1=ALU.add),
                 reads=(preb, self.cb, accb), writes=(accb,))
        S.op("act", lambda e: e.activation(out=tt[:, 0:n_out], in_=acc[:, 0:n_out], func=AF.Tanh),
             reads=(accb,), writes=(ttb,))
        S.op("dve", lambda e: e.scalar_tensor_tensor(out=out_bf, in0=tt[:, 0:n_out], scalar=1.0, in1=acc[:, 0:n_out],
                                                      op0=ALU.add, op1=ALU.mult),
             reads=(ttb, accb), writes=(outb,))

    def batch_smalls(self, reg, n, c0, g, suffix=False):
        S = self.S
        n8 = n * 8
        sm = {}
        for nm in ("a", "dwl", "w", "dtw"):
            sm[nm] = reg.alloc([n, 8], F32)
        s2 = reg.alloc([2, n8], F32)
        e2 = reg.alloc([2, n8], F32)
        smb = reg.buf("smalls")
        dt_g = self.dt_all[:, c0:c0 + n, g * 8:(g + 1) * 8]
        sm["dt"] = dt_g
        sm["acum"] = s2[:, 0, :].rearrange("p (c j) -> p c j", j=8)
        sm["atot"] = s2[:, 1, :].rearrange("p (c j) -> p c j", j=8)
        sm["eacum"] = e2[:, 0, :].rearrange("p (c j) -> p c j", j=8)
        sm["eatot"] = e2[:, 1, :].rearrange("p (c j) -> p c j", j=8)
        S.op("dve", lambda e: e.tensor_tensor(out=sm["a"], in0=dt_g,
                                              in1=self.Ab[:, g * 8:(g + 1) * 8].unsqueeze(1).to_broadcast([128, n, 8]), op=ALU.mult),
             reads=(self.dt_buf, self.cb), writes=(smb,))
        pb, ps = self.bank()
        a2 = sm["a"].rearrange("p c j -> p (c j)")
        S.op("pe", [
            lambda e: e.matmul(ps[:, 0:n8], lhsT=self.tri_f, rhs=a2, start=True, stop=True),
            lambda e: e.matmul(ps[:, 128:128 + n8], lhsT=self.ones_f, rhs=a2, start=True, stop=True),
        ], reads=(smb, self.cb), writes=(pb,))
        psv = ps[:, 0:256].rearrange("p (a b) -> p a b", b=128)[:, :, 0:n8]
        S.op("act", [
            lambda e: e.activation(out=s2, in_=psv, func=AF.Copy),
            lambda e: e.activation(out=e2, in_=psv, func=AF.Exp),
        ], reads=(pb,), writes=(smb,))
        S.op("dve", lambda e: e.tensor_tensor(out=sm["dwl"], in0=sm["atot"], in1=sm["acum"], op=ALU.subtract),
             reads=(smb,), writes=(smb,))
        if suffix:
            suf = reg.alloc([n, 8], F32)
            S.op("dve", lambda e: e.memset(suf[:, n - 1, :], 0.0), reads=(smb,), writes=(smb,))
            for c in range(n - 2, -1, -1):
                S.op("dve", lambda e, c=c: e.tensor_tensor(out=suf[:, c, :], in0=suf[:, c + 1, :], in1=sm["atot"][:, c + 1, :], op=ALU.add),
                     reads=(smb,), writes=(smb,))
            S.op("dve", lambda e: e.tensor_tensor(out=sm["dwl"], in0=sm["dwl"], in1=suf, op=ALU.add), reads=(smb,), writes=(smb,))
        if g == 0 and not suffix:
            self.dump("s2", s2, [smb], [128, 2, n8])
            self.dump("sma", sm["a"], [smb], [128, n, 8])
        S.op("act", lambda e: e.activation(out=sm["w"], in_=sm["dwl"], func=AF.Exp), reads=(smb,), writes=(smb,))
        S.op("dve", lambda e: e.tensor_tensor(out=sm["dtw"], in0=sm["w"], in1=dt_g, op=ALU.mult),
             reads=(smb, self.dt_buf), writes=(smb,))
        return sm, smb

    def ssm_prefix(self):
        S, d = self.S, self.d
        r3, r4 = self.r3, self.r4
        r3.recarve()
        self.stash = r3.h[:, :].rearrange("p (g x) -> p g x", g=8)
        self.yTb = [r3.buf("yT%d" % g) for g in range(8)]
        for g in range(NG):
            self._ssm_prefix_group(g)
        self.dump("stash", self.stash[:, :, 0:512], self.yTb, [128, 8, 512])

    def _ssm_prefix_group(self, g):
        S, d = self.S, self.d
        r3, r4 = self.r3, self.r4
        NP = 896
        ranges = [(0, 384), (384, 384), (768, 128)]
        abufs = self.hTb[0:7]
        if True:
            r4.recarve()
            pre2 = [r4.alloc([3 + NP], F32) for _ in range(2)]; pre2b = [r4.buf("pre%d" % i) for i in range(2)]
            acc = r4.alloc([NP], F32); accb = r4.buf("acc")
            tt = r4.alloc([NP], F32); ttb = r4.buf("tt")
            cvo = [r4.alloc([NP], BF16) for _ in range(2)]; cvob = [r4.buf("cvo%d" % i) for i in range(2)]
            xs_tm = r4.alloc([7, 512], BF16); xsb = r4.buf("xs_tm")
            B_tm = r4.alloc([7, 128], BF16); Btmb = r4.buf("B_tm")
            xw = r4.alloc([7, 512], BF16); xwb = r4.buf("xw")
            for i in range(2):
                S.op("dve", lambda e, i=i: e.memset(pre2[i][:, 0:3], 0.0), writes=(pre2b[i],))
            sm, smb = self.batch_smalls(r4, 7, 0, g, suffix=True)
            cols = [XBC0 + g * 512 + j * 128 for j in range(4)] + [XBC0 + DI + g * 128]
            pending = []
            wv = None
            for bi, c0 in enumerate(cols):
                blk = (c0 - XBC0) // 128
                if bi in (0, 2):
                    wv, wb = self.wload([d["w_in"][:, c0:c0 + 256]], 16)
                    e0 = 0
                elif bi == 4:
                    wv, wb = self.wload([d["w_in"][:, c0:c0 + 128]], 16)
                    e0 = 0
                else:
                    e0 = 128
                outs = self.proj_fm(wv, wb, e0, 16, self.hT, abufs, ranges)
                pre, preb = pre2[bi % 2], pre2b[bi % 2]
                for (pb, ps), (t0, n) in zip(outs, ranges):
                    S.op("act", lambda e, ps=ps, t0=t0, n=n, pre=pre: e.activation(out=pre[:, 3 + t0:3 + t0 + n], in_=ps, func=AF.Copy),
                         reads=(pb,), writes=(preb,))
                for f in pending:
                    f()
                pending = []
                cv, cvb = cvo[bi % 2], cvob[bi % 2]
                self.conv_silu(pre, preb, NP, blk, acc, accb, tt, ttb, cv, cvb)
                tiles = [cv[:, c * 128:(c + 1) * 128] for c in range(7)]
                if bi < 4:
                    pending.append(lambda tiles=tiles, cvb=cvb, bi=bi: self.transpose_to(
                        tiles, [cvb], lambda n: xs_tm[:, 0:7, bi * 128:(bi + 1) * 128], [xsb]))
                else:
                    pending.append(lambda tiles=tiles, cvb=cvb: self.transpose_to(tiles, [cvb], lambda n: B_tm[:, 0:7, :], [Btmb]))
            for f in pending:
                f()
            S.op("dve", lambda e: e.tensor_tensor(out=xw.rearrange("p c (j q) -> p c j q", q=64),
                                                  in0=xs_tm.rearrange("p c (j q) -> p c j q", q=64),
                                                  in1=sm["dtw"].unsqueeze(3).to_broadcast([128, 7, 8, 64]), op=ALU.mult),
                 reads=(xsb, smb), writes=(xwb,))
            pb, ps = self.bank()
            fns = [lambda e, c=c, ps=ps: e.matmul(ps[:, 0:512], lhsT=B_tm[:, c, :], rhs=xw[:, c, :], start=(c == 0), stop=(c == 6))
                   for c in range(7)]
            S.op("pe", fns, reads=(Btmb, xwb), writes=(pb,))
            S.op("act", lambda e, g=g, ps=ps: e.activation(out=self.stash[:, g, 0:512], in_=ps[:, 0:512], func=AF.Copy),
                 reads=(pb,), writes=(self.yTb[g],))

    def ssm_main(self):
        S, d = self.S, self.d
        r2, r3, r4 = self.r2, self.r3, self.r4
        yT = r3.h[:, :].bitcast(BF16).rearrange("p (k t) -> p k t", t=NT)
        self.yT = yT
        self.nwsem = S.new_dma_sem("nwsem")
        for g in range(NG):
            self._ssm_main_group(g)
        self.dump("yT", yT, self.yTb, [128, 32, NT], BF16)

    def _ssm_main_group(self, g):
        S, d = self.S, self.d
        r2, r3, r4 = self.r2, self.r3, self.r4
        yT = self.yT
        NPRE = NT + 3
        nsem = self.nwsem
        ranges = [(893 + i * 385, 385) for i in range(3)]
        abufs = self.hTb[6:16]
        if True:
            r2.recarve(); r4.recarve()
            pre2 = [r4.alloc([NPRE + 1], BF16) for _ in range(2)]; pre2b = [r4.buf("pre%d" % i) for i in range(2)]
            acc = r4.alloc([384], F32); accb = r4.buf("acc")
            tt = r4.alloc([384], F32); ttb = r4.buf("tt")
            cvo = [r2.alloc([384], BF16) for _ in range(3)]; cvob = [r2.buf("cvo%d" % i) for i in range(3)]
            BT = r4.alloc([NT], BF16); BTb = r4.buf("BT")
            CT = r4.alloc([NT], BF16); CTb = r4.buf("CT")
            xs_tm = r4.alloc([NMC, 512], BF16); xsb = r4.buf("xs_tm")
            B_tm = r4.alloc([NMC, 128], BF16); Btmb = r4.buf("B_tm")
            zs_all = r4.alloc([NMC, 512], BF16); zsab = r4.buf("zs_all")
            ztmp = r4.alloc([256], F32); ztb = r4.buf("ztmp")
            hiT = r2.alloc([NMC, 128], BF16); loT = r4.alloc([NMC, 128], BF16); hib = r4.buf("hiloT")
            normw = r2.alloc([512], F32); nwb = r2.buf("normw")
            Scur = r2.alloc([512], F32); Sb = r2.buf("Scur")
            _x = r2.alloc([512], BF16); _xb = r2.buf("xdt"); xdt = [_x, _x]; xdtb = [_xb, _xb]
            _x2 = r2.alloc([512], BF16); _x2b = r2.buf("xw"); xw = [_x2, _x2]; xwb = [_x2b, _x2b]
            _sb = r2.alloc([512], BF16); _sbb = r2.buf("Sbf"); Sbf2 = [_sb, _sb]; Sbfb2 = [_sbb, _sbb]
            _seg = r2.alloc([8, 128], BF16); _segb = r2.buf("segT"); segT = [_seg, _seg]; segb = [_segb, _segb]
            _mt = r2.alloc([8, 128], BF16); _mtb = r2.buf("MT"); MT = [_mt, _mt]; MTb = [_mtb, _mtb]
            _c = r2.alloc([128], BF16); _cb = r2.buf("cbT"); cbT = [_c, _c]; cbTb = [_cb, _cb]
            t1 = r2.alloc([512], F32); t1b = r2.buf("t1")
            yy = r2.alloc([512], F32); yb = r2.buf("y")
            yn = r2.alloc([512], BF16); ynb = r2.buf("yn")
            ysm = r2.alloc([4], F32); ysmb = r2.buf("ysm")
            hl8 = r2.alloc([2, NMC * 8], BF16); hl8b = r2.buf("hl8")
            S.op("act", lambda e, g=g: e.activation(out=Scur, in_=self.stash[:, g, 0:512], func=AF.Copy),
                 reads=(self.yTb[g],), writes=(Sb,))
            S.dma("sp", lambda e, g=g: e.dma_start(out=normw, in_=d["ssm_norm_w"][:, g * 512:(g + 1) * 512].partition_broadcast(128)),
                  nsem, writes=(nwb,))
            sm, smb = self.batch_smalls(r2, NMC, 7, g)
            acf = sm["acum"].rearrange("p c j -> p (c j)")
            S.op("dve", lambda e: e.tensor_copy(out=hl8[:, 0, :], in_=acf), reads=(smb,), writes=(hl8b,))
            S.op("dve", lambda e: e.tensor_tensor(out=hl8[:, 1, :], in0=acf, in1=hl8[:, 0, :], op=ALU.subtract),
                 reads=(smb, hl8b), writes=(hl8b,))
            if g == 0:
                self.dump("hl8", hl8, [hl8b], [128, 2, NMC * 8], BF16)
            tb = [self.bank() for _ in range(3)]
            tv = [p[1][:].bitcast(BF16).rearrange("p (a b) -> p a b", b=128) for p in tb]
            fns = []
            for c in range(NMC):
                for hl in range(2):
                    if c < 8:
                        dst = tv[hl][0:8, c, :]
                    else:
                        dst = tv[2][0:8, hl, :]
                    fns.append(lambda e, c=c, hl=hl, dst=dst: e.transpose(out=dst, in_=hl8[:, hl, c * 8:(c + 1) * 8], identity=self.ident_bf))
            S.op("pe", fns, reads=(hl8b, self.cb), writes=[p[0] for p in tb])
            S.op("act", [
                lambda e: e.activation(out=hiT[0:8, 0:8, :], in_=tv[0][0:8, 0:8, :], func=AF.Copy),
                lambda e: e.activation(out=loT[0:8, 0:8, :], in_=tv[1][0:8, 0:8, :], func=AF.Copy),
                lambda e: e.activation(out=hiT[0:8, 8, :], in_=tv[2][0:8, 0, :], func=AF.Copy),
                lambda e: e.activation(out=loT[0:8, 8, :], in_=tv[2][0:8, 1, :], func=AF.Copy),
            ], reads=[p[0] for p in tb], writes=(hib,))
            cols = [XBC0 + g * 512 + j * 128 for j in range(4)] + [XBC0 + DI + g * 128, XBC0 + DI + 1024 + g * 128]
            pending = []
            for bi, c0 in enumerate(cols):
                blk = (c0 - XBC0) // 128
                if bi in (0, 2):
                    wv, wb = self.wload([d["w_in"][:, c0:c0 + 256]], 16)
                    e0 = 0
                elif bi == 4:
                    wv, wb = self.wload([d["w_in"][:, c0:c0 + 128], d["w_in"][:, cols[5]:cols[5] + 128]], 16)
                    e0 = 0
                else:
                    e0 = 128
                outs = self.proj_fm(wv, wb, e0, 16, self.hT, abufs, ranges)
                pre, preb = pre2[bi % 2], pre2b[bi % 2]
                for i, (pb, ps) in enumerate(outs):
                    S.op("act", lambda e, ps=ps, i=i, pre=pre: e.activation(out=pre[:, i * 385:(i + 1) * 385], in_=ps, func=AF.Copy),
                         reads=(pb,), writes=(preb,))
                if bi in (1, 3):
                    hv = bi // 2
                    zc0 = Z0 + g * 512 + hv * 256
                    wvz, wbz = self.wload([d["w_in"][:, zc0:zc0 + 256]], 16)
                    for mc in range(NMC):
                        t0z = 896 + mc * 128
                        pbz, psz = self.proj_tm(wvz, wbz, 0, 256, 16, lambda kc, t0z=t0z: self.hT(kc, t0z, 128), [self.hTb[7 + mc]])
                        S.op("act", lambda e, psz=psz: e.activation(out=ztmp, in_=psz, func=AF.Tanh, scale=0.5), reads=(pbz,), writes=(ztb,))
                        S.op("dve", lambda e, psz=psz, mc=mc, hv=hv: e.scalar_tensor_tensor(
                            out=zs_all[:, mc, hv * 256:(hv + 1) * 256], in0=ztmp, scalar=1.0, in1=psz, op0=ALU.add, op1=ALU.mult),
                            reads=(ztb, pbz), writes=(zsab,))
                for f in pending:
                    f()
                pending = []
                for th in range(3):
                    cv, cvb = cvo[th], cvob[th]
                    if bi < 4:
                        dst, dstb = cv, cvb
                    elif bi == 4:
                        dst, dstb = BT[:, th * 384:(th + 1) * 384], BTb
                    else:
                        dst, dstb = CT[:, th * 384:(th + 1) * 384], CTb
                    self.conv_silu(pre, preb, 384, blk, acc, accb, tt, ttb, dst, dstb, lo=th * 384)
                    if bi < 4:
                        tiles = [cv[:, c * 128:(c + 1) * 128] for c in range(3)]
                        pending.append(lambda tiles=tiles, cvb=cvb, bi=bi, th=th: self.transpose_to(
                            tiles, [cvb], lambda n: xs_tm[:, th * 3:th * 3 + 3, bi * 128:(bi + 1) * 128], [xsb]))
                    elif bi == 4:
                        tiles = [BT[:, (th * 3 + c) * 128:(th * 3 + c + 1) * 128] for c in range(3)]
                        pending.append(lambda tiles=tiles, th=th: self.transpose_to(
                            tiles, [BTb], lambda n: B_tm[:, th * 3:th * 3 + 3, :], [Btmb]))
            for f in pending:
                f()
            Dg = self.Db[:, g * 8:(g + 1) * 8]
            PB, PS = self.pbuf, self.ps

            def f_pe1(mc):
                k = mc % 2
                tc = slice(mc * 128, (mc + 1) * 128)
                t0 = 896 + mc * 128
                fns = []
                for bq in range(2):
                    psq = PS[bq]
                    fns.append(lambda e, psq=psq: e.matmul(psq[:, 0:512], lhsT=self.ident_bf, rhs=self.neg4, start=True, stop=False))
                    for jj in range(4):
                        j = bq * 4 + jj
                        for src in (hiT, loT):
                            fns.append(lambda e, psq=psq, jj=jj, j=j, src=src: e.matmul(
                                psq[:, jj * 128:(jj + 1) * 128], lhsT=self.sel[0:8, j * 128:(j + 1) * 128], rhs=src[0:8, mc, :],
                                start=False, stop=False))
                    for si, src in enumerate((hiT, loT)):
                        fns.append(lambda e, psq=psq, bq=bq, src=src, si=si: e.matmul(
                            psq[:, 0:512], lhsT=src[0:8, mc, :], rhs=self.nsel[0:8, bq * 512:(bq + 1) * 512],
                            start=False, stop=(si == 1)))
                S.op("pe", fns, reads=(hib, self.cb), writes=[PB[0], PB[1]])
                S.op("pe", lambda e: e.matmul(PS[2][:, 0:128], lhsT=BT[:, tc], rhs=CT[:, tc], start=True, stop=True),
                     reads=(BTb, CTb), writes=(PB[2],))
                S.op("act", [lambda e, bq=bq: e.activation(out=segT[k][:, bq * 4:(bq + 1) * 4, :].rearrange("p a b -> p (a b)"),
                                                           in_=PS[bq][:, 0:512], func=AF.Exp) for bq in range(2)],
                     reads=[PB[0], PB[1]], writes=(segb[k],))
                S.op("act", lambda e: e.activation(out=cbT[k], in_=PS[2][:, 0:128], func=AF.Copy), reads=(PB[2],), writes=(cbTb[k],))

            def f_dve(mc):
                k = mc % 2
                xs_c = xs_tm[:, mc, :]
                S.op("dve", lambda e: e.tensor_tensor(out=MT[k], in0=segT[k], in1=cbT[k].unsqueeze(1).to_broadcast([128, 8, 128]), op=ALU.mult),
                     reads=(segb[k], cbTb[k]), writes=(MTb[k],))
                S.op("dve", lambda e: e.tensor_tensor(out=xdt[k].rearrange("p (j q) -> p j q", q=64),
                                                      in0=xs_c.rearrange("p (j q) -> p j q", q=64),
                                                      in1=sm["dt"][:, mc, :].unsqueeze(2).to_broadcast([128, 8, 64]), op=ALU.mult),
                     reads=(xsb, self.dt_buf), writes=(xdtb[k],))
                S.op("dve", lambda e: e.tensor_tensor(out=xw[k].rearrange("p (j q) -> p j q", q=64),
                                                      in0=xs_c.rearrange("p (j q) -> p j q", q=64),
                                                      in1=sm["dtw"][:, mc, :].unsqueeze(2).to_broadcast([128, 8, 64]), op=ALU.mult),
                     reads=(xsb, smb), writes=(xwb[k],))
                S.op("pe", [lambda e, j=j: e.matmul(PS[3][:, j * 64:(j + 1) * 64], lhsT=MT[k][:, j, :], rhs=xdt[k][:, j * 64:(j + 1) * 64],
                                                    start=True, stop=True) for j in range(8)],
                     reads=(MTb[k], xdtb[k]), writes=(PB[3],))
                S.op("pe", lambda e: e.matmul(PS[4][:, 0:512], lhsT=B_tm[:, mc, :], rhs=xw[k], start=True, stop=True),
                     reads=(Btmb, xwb[k]), writes=(PB[4],))

            def b_a(mc):
                k = mc % 2
                tc = slice(mc * 128, (mc + 1) * 128)
                xs_c = xs_tm[:, mc, :]
                if mc == 0:
                    S.op("act", lambda e: e.activation(out=Sbf2[0], in_=Scur, func=AF.Copy), reads=(Sb,), writes=(Sbfb2[0],))
                S.op("pe", lambda e: e.matmul(PS[5][:, 0:512], lhsT=CT[:, tc], rhs=Sbf2[k], start=True, stop=True),
                     reads=(CTb, Sbfb2[k]), writes=(PB[5],))
                S.op("dve", lambda e: e.tensor_tensor(out=Scur.rearrange("p (j q) -> p j q", q=64),
                                                      in0=Scur.rearrange("p (j q) -> p j q", q=64),
                                                      in1=sm["eatot"][:, mc, :].unsqueeze(2).to_broadcast([128, 8, 64]), op=ALU.mult),
                     reads=(Sb, smb), writes=(Sb,))
                S.op("dve", lambda e: e.tensor_tensor(out=Scur, in0=Scur, in1=PS[4][:, 0:512], op=ALU.add),
                     reads=(Sb, PB[4]), writes=(Sb,))
                if mc == 0:
                    S.op("dve", lambda e: e.tensor_scalar(out=Scur, in0=Scur, scalar1=self.flag[:, 0:1], scalar2=None, op0=ALU.mult),
                         reads=(Sb, self.cb), writes=(Sb,))
                if mc + 1 < NMC:
                    S.op("act", lambda e: e.activation(out=Sbf2[1 - k], in_=Scur, func=AF.Copy), reads=(Sb,), writes=(Sbfb2[1 - k],))
                S.op("dve", lambda e: e.tensor_tensor(out=t1.rearrange("p (j q) -> p j q", q=64),
                                                      in0=PS[5][:, 0:512].rearrange("p (j q) -> p j q", q=64),
                                                      in1=sm["eacum"][:, mc, :].unsqueeze(2).to_broadcast([128, 8, 64]), op=ALU.mult),
                     reads=(PB[5], smb), writes=(t1b,))
                S.op("dve", lambda e: e.tensor_tensor(out=yy, in0=t1, in1=PS[3][:, 0:512], op=ALU.add),
                     reads=(t1b, PB[3]), writes=(yb,))
                S.op("dve", lambda e: e.tensor_tensor(out=t1.rearrange("p (j q) -> p j q", q=64),
                                                      in0=xs_c.rearrange("p (j q) -> p j q", q=64),
                                                      in1=Dg.unsqueeze(2).to_broadcast([128, 8, 64]), op=ALU.mult),
                     reads=(xsb, self.cb, yb), writes=(t1b,))
                S.op("dve", lambda e: e.tensor_tensor(out=yy, in0=yy, in1=t1, op=ALU.add), reads=(t1b, yb), writes=(yb,))
                S.op("dve", lambda e: e.tensor_tensor(out=yy, in0=yy, in1=zs_all[:, mc, :], op=ALU.mult), reads=(yb, zsab), writes=(yb,))

            def b_c(mc):
                S.op("act", lambda e: e.activation(out=yn, in_=yy, func=AF.Square, accum_out=ysm[:, 0:1]), reads=(yb,), writes=(ynb, ysmb))
                S.op("act", lambda e: e.activation(out=ysm[:, 1:2], in_=ysm[:, 0:1], func=AF.Ln, bias=4 * EPS, scale=1.0 / 512),
                     reads=(ysmb,), writes=(ysmb,))
                S.op("act", lambda e: e.activation(out=ysm[:, 2:3], in_=ysm[:, 1:2], func=AF.Exp, scale=-0.5), reads=(ysmb,), writes=(ysmb,))

            def b_e(mc):
                tc = slice(mc * 128, (mc + 1) * 128)
                S.op("dve", lambda e: e.scalar_tensor_tensor(out=yn, in0=yy, scalar=ysm[:, 2:3], in1=normw, op0=ALU.mult, op1=ALU.mult),
                     reads=(yb, ysmb, nwb), writes=(ynb,))
                psb = PS[2][:].bitcast(BF16).rearrange("p (a b) -> p a b", b=128)
                S.op("pe", [lambda e, i=i: e.transpose(out=psb[:, i, :], in_=yn[:, i * 128:(i + 1) * 128], identity=self.ident_bf)
                            for i in range(4)], reads=(ynb, self.cb), writes=(PB[2],))
                S.op("act", lambda e: e.activation(out=yT[:, 4 * g:4 * g + 4, tc], in_=psb[:, 0:4, :], func=AF.Copy),
                     reads=(PB[2],), writes=(self.yTb[g],))

            f_pe1(0)
            f_dve(0)
            for mc in range(NMC):
                b_a(mc)
                if mc + 1 < NMC:
                    f_pe1(mc + 1)
                b_c(mc)
                if mc + 1 < NMC:
                    f_dve(mc + 1)
                b_e(mc)

    def ssm_out(self):
        S, d = self.S, self.d
        r2, r4 = self.r2, self.r4
        r2.recarve(); r4.recarve()
        self.mixedT = r4.alloc([16, NT], BF16)
        self.mixb = [r4.buf("mixedT%d" % i) for i in range(16)]
        S.op("dve", lambda e: e.memset(self.mixedT[:, :, 0:126], 0.0), writes=list(self.mixb))
        tg = r2.alloc([3, 384], F32); tgb = r2.buf("tg")
        self.tg, self.tgb = tg, tgb
        mr = [(126 + i * 342, 342) for i in range(3)]
        ranges = [(896 + a_, n_) for a_, n_ in mr]
        mainbufs = self.hTb[7:16]
        yT = self.yT
        for db in range(16):
            if db % 2 == 0:
                wvg, wbg = self.wload([d["w_in"][:, GS0 + db * 128:GS0 + db * 128 + 256]], 16)
            outs = self.proj_fm(wvg, wbg, (db % 2) * 128, 16, self.hT, mainbufs, ranges)
            for i, (pb, ps) in enumerate(outs):
                S.op("act", lambda e, ps=ps, i=i: e.activation(out=tg[:, i, 0:342], in_=ps, func=AF.Tanh, scale=0.5),
                     reads=(pb,), writes=(tgb,))
            wv, wb = self.wload([d["w_ssm_out"][:, db * 128:(db + 1) * 128]], 32)
            outs = self.proj_fm(wv, wb, 0, 32, lambda kc, t0, n: yT[:, kc, t0 - 896:t0 - 896 + n], self.yTb, ranges)
            for i, (pb, ps) in enumerate(outs):
                S.op("dve", lambda e, ps=ps, i=i, db=db: e.scalar_tensor_tensor(
                    out=self.mixedT[:, db, mr[i][0]:mr[i][0] + 342], in0=tg[:, i, 0:342], scalar=1.0, in1=ps, op0=ALU.add, op1=ALU.mult),
                    reads=(tgb, pb), writes=(self.mixb[db],))
        self.dump("mixS", self.mixedT, self.mixb, [128, 16, NT], BF16)

    def attention(self):
        S, d = self.S, self.d
        r2, r3 = self.r2, self.r3
        r2.recarve(); r3.recarve()
        self.aoT = r3.alloc([16, NT], BF16)
        self.aoTb = [r3.buf("aoT%d" % i) for i in range(NKV)]
        a = {}
        a["qT"] = r3.alloc([4, NT], BF16); a["qTb"] = r3.buf("qT")
        a["kT2"] = r3.alloc([TM], BF16); a["kTb"] = r3.buf("kT2")
        a["v1"] = r3.alloc([10, 65], BF16); a["v1b"] = r3.buf("v1")
        a["biasT"] = r3.alloc([8, 2, 128], F32); a["biasb"] = r3.buf("biasT")
        a["q32"] = [r3.alloc([512], F32) for _ in range(2)]; a["q32b"] = [r3.buf("q32_%d" % i) for i in range(2)]
        a["qtmp"] = [r3.alloc([512], F32) for _ in range(2)]; a["qtmpb"] = [r3.buf("qtmp%d" % i) for i in range(2)]
        a["qn"] = [r3.alloc([512], BF16) for _ in range(2)]; a["qnb"] = [r3.buf("qn%d" % i) for i in range(2)]
        a["kv32"] = [r3.alloc([128], F32) for _ in range(2)]; a["kvb"] = [r3.buf("kv32_%d" % i) for i in range(2)]
        a["ktmp"] = [r3.alloc([64], F32) for _ in range(2)]; a["ktmpb"] = [r3.buf("ktmp%d" % i) for i in range(2)]
        a["kdup"] = [r3.alloc([2, 64], BF16) for _ in range(2)]; a["kdupb"] = [r3.buf("kdup%d" % i) for i in range(2)]
        a["qs"] = [r3.alloc([32], F32) for _ in range(2)]; a["qsb"] = [r3.buf("qs%d" % i) for i in range(2)]
        a["ltmp"] = [r2.alloc([512], F32) for _ in range(2)]; a["ltb"] = [r2.buf("ltmp%d" % i) for i in range(2)]
        a["eT"] = [[r2.alloc([2, 2, 128], BF16) for _ in range(4)] for _ in range(2)]
        a["eTb"] = [[r2.buf("eT%d_%d" % (p, i)) for i in range(4)] for p in range(2)]
        a["den"] = [r2.alloc([16], F32) for _ in range(2)]; a["denb"] = [r2.buf("den%d" % i) for i in range(2)]
        a["ao"] = [r2.alloc([8, 64], BF16) for _ in range(2)]; a["aob"] = [r2.buf("ao%d" % i) for i in range(2)]
        a["bsem"] = S.new_dma_sem("biassem")
        S.op("dve", lambda e: e.memset(a["v1"][:, :, 64:65], 1.0), writes=(a["v1b"],))
        self.at = a
        for kg in range(NKV):
            self._attn_group(kg)
        self.dump("aoT", self.aoT, self.aoTb, [128, 16, NT], BF16)

    def _attn_group(self, kg):
        S, d, a = self.S, self.d, self.at
        qT, qTb, kT2, kTb, v1, v1b, biasT, biasb = a["qT"], a["qTb"], a["kT2"], a["kTb"], a["v1"], a["v1b"], a["biasT"], a["biasb"]
        PB, PS = self.pbuf, self.ps
        S.dma("sp", lambda e: e.dma_start(out=biasT, in_=d["biasT"][kg].rearrange("p (h b q) -> p h b q", h=8, b=2)),
              a["bsem"], writes=(biasb,))
        wv, wb = self.wload([d["w_in"][:, K0 + kg * 64:K0 + kg * 64 + 64], d["w_in"][:, V0 + kg * 64:V0 + kg * 64 + 64]], 16)
        pend = None
        for cj in range(10):
            i = cj % 2
            kv32, kvb, ktmp, ktmpb, kdup, kdupb, qs, qsb = (a["kv32"][i], a["kvb"][i], a["ktmp"][i], a["ktmpb"][i],
                                                            a["kdup"][i], a["kdupb"][i], a["qs"][i], a["qsb"][i])
            t0 = 768 + cj * 128
            pb, ps = self.proj_tm(wv, wb, 0, 128, 16, lambda kc, t0=t0: self.hT(kc, t0, 128), [self.hTb[6 + cj]])
            if pend is not None:
                pend()
            S.op("act", lambda e, ps=ps, kv32=kv32: e.activation(out=kv32, in_=ps, func=AF.Copy), reads=(pb,), writes=(kvb,))
            S.op("act", lambda e, kv32=kv32, ktmp=ktmp, qs=qs: e.activation(out=ktmp, in_=kv32[:, 0:64], func=AF.Square, accum_out=qs[:, 0:1]),
                 reads=(kvb,), writes=(ktmpb, qsb))
            S.op("act", lambda e, qs=qs: e.activation(out=qs[:, 1:2], in_=qs[:, 0:1], func=AF.Ln, bias=EPS, scale=1.0 / 64),
                 reads=(qsb,), writes=(qsb,))
            S.op("act", lambda e, qs=qs: e.activation(out=qs[:, 2:3], in_=qs[:, 1:2], func=AF.Exp, scale=-0.5), reads=(qsb,), writes=(qsb,))
            S.op("dve", [
                lambda e, kv32=kv32, kdup=kdup, qs=qs: e.scalar_tensor_tensor(out=kdup[:, 0, :], in0=kv32[:, 0:64], scalar=qs[:, 2:3], in1=self.kg,
                                                                          op0=ALU.mult, op1=ALU.mult),
                lambda e, kv32=kv32, kdup=kdup, qs=qs: e.scalar_tensor_tensor(out=kdup[:, 1, :], in0=kv32[:, 0:64], scalar=qs[:, 2:3], in1=self.kg,
                                                                          op0=ALU.mult, op1=ALU.mult),
            ], reads=(kvb, qsb, self.cb), writes=(kdupb,))
            S.op("act", lambda e, cj=cj, kv32=kv32: e.activation(out=v1[:, cj, 0:64], in_=kv32[:, 64:128], func=AF.Copy),
                 reads=(kvb,), writes=(v1b,))
            pend = (lambda kdup=kdup, kdupb=kdupb, cj=cj: self.transpose_to(
                [kdup.rearrange("p a b -> p (a b)")], [kdupb], lambda n: kT2[:, cj * 128:(cj + 1) * 128].unsqueeze(1), [kTb]))
        pend()
        wvs = []
        for hv in range(2):
            c0 = Q0 + kg * 512 + hv * 256
            wvs.append(self.wload([d["w_in"][:, c0:c0 + 256]], 16))
        pend = None
        for mc in range(NMC):
            i = mc % 2
            q32, q32b, qtmp, qtmpb, qn, qnb, qs, qsb = (a["q32"][i], a["q32b"][i], a["qtmp"][i], a["qtmpb"][i],
                                                        a["qn"][i], a["qnb"][i], a["qs"][i], a["qsb"][i])
            t0 = 896 + mc * 128
            for hv in range(2):
                pb, ps = self.proj_tm(wvs[hv][0], wvs[hv][1], 0, 256, 16, lambda kc, t0=t0: self.hT(kc, t0, 128), [self.hTb[7 + mc]])
                S.op("act", lambda e, ps=ps, hv=hv, q32=q32: e.activation(out=q32[:, hv * 256:(hv + 1) * 256], in_=ps, func=AF.Copy),
                     reads=(pb,), writes=(q32b,))
            if pend is not None:
                pend()
            S.op("dve", lambda e, q32=q32, qtmp=qtmp: e.tensor_tensor(out=qtmp, in0=q32, in1=q32, op=ALU.mult), reads=(q32b,), writes=(qtmpb,))
            S.op("dve", lambda e, qtmp=qtmp, qs=qs: e.tensor_reduce(out=qs[:, 8:16], in_=qtmp.rearrange("p (h x) -> p h x", x=64), axis=AX.X, op=ALU.add),
                 reads=(qtmpb,), writes=(qsb,))
            S.op("act", lambda e, qs=qs: e.activation(out=qs[:, 16:24], in_=qs[:, 8:16], func=AF.Ln, bias=EPS, scale=1.0 / 64),
                 reads=(qsb,), writes=(qsb,))
            S.op("act", lambda e, qs=qs: e.activation(out=qs[:, 24:32], in_=qs[:, 16:24], func=AF.Exp, scale=-0.5), reads=(qsb,), writes=(qsb,))
            S.op("dve", lambda e, q32=q32, qtmp=qtmp, qs=qs: e.tensor_tensor(
                out=qtmp.rearrange("p (h x) -> p h x", x=64), in0=q32.rearrange("p (h x) -> p h x", x=64),
                in1=qs[:, 24:32].unsqueeze(2).to_broadcast([128, 8, 64]), op=ALU.mult), reads=(q32b, qsb), writes=(qtmpb,))
            S.op("dve", lambda e, qtmp=qtmp, qn=qn: e.tensor_tensor(
                out=qn.rearrange("p (h x) -> p h x", x=64), in0=qtmp.rearrange("p (h x) -> p h x", x=64),
                in1=self.qg.unsqueeze(1).to_broadcast([128, 8, 64]), op=ALU.mult), reads=(qtmpb, self.cb), writes=(qnb,))
            pend = (lambda qn=qn, qnb=qnb, mc=mc: self.transpose_to(
                [qn[:, j * 128:(j + 1) * 128] for j in range(4)], [qnb], lambda n: qT[:, 0:4, mc * 128:(mc + 1) * 128], [qTb]))
        pend()

        def L(mc):
            fns = []
            for qp in range(2):
                for hh in range(2):
                    psv = PS[qp * 2 + hh][:, 0:512].rearrange("p (a b q) -> p a b q", a=2, b=2)
                    for qq in range(2):
                        qt = qp * 2 + qq
                        for blk in range(2):
                            cj = mc + blk
                            fns.append(lambda e, hh=hh, blk=blk, cj=cj, psv=psv, qt=qt, qq=qq: e.matmul(
                                psv[:, qq, blk, :], lhsT=kT2[hh * 64:(hh + 1) * 64, cj * 128:(cj + 1) * 128],
                                rhs=qT[hh * 64:(hh + 1) * 64, qt, mc * 128:(mc + 1) * 128], start=True, stop=True))
            S.op("pe", fns, reads=(kTb, qTb), writes=[PB[0], PB[1], PB[2], PB[3]])

        def Sx(mc):
            par = mc % 2
            for idx in range(4):
                qp, hh = idx // 2, idx % 2
                lt, ltb = a["ltmp"][idx % 2], a["ltb"][idx % 2]
                eT, eTb = a["eT"][par][idx], a["eTb"][par][idx]
                psv = PS[idx][:, 0:512].rearrange("p (a b q) -> p a b q", a=2, b=2)
                h0 = qp * 4 + hh
                S.op("dve", lambda e, psv=psv, h0=h0, lt=lt: e.tensor_tensor(out=lt.rearrange("p (a b q) -> p a b q", a=2, b=2), in0=psv,
                                                                            in1=biasT[:, h0:h0 + 3:2, :, :], op=ALU.add),
                     reads=(PB[idx], biasb), writes=(ltb,))
                S.op("act", lambda e, eT=eT, lt=lt: e.activation(out=eT.rearrange("p a b q -> p (a b q)"), in_=lt, func=AF.Exp),
                     reads=(ltb,), writes=(eTb,))
                if mc == 1:
                    S.op("dve", lambda e, eT=eT: e.tensor_scalar(out=eT[:, :, 0, :], in0=eT[:, :, 0, :],
                                                                 scalar1=self.flag[:, 0:1], scalar2=None, op0=ALU.mult),
                         reads=(eTb, self.cb), writes=(eTb,))

        def P(mc):
            par = mc % 2
            base = 4 + 2 * par
            for qp in range(2):
                pvs = PS[base + qp]
                fns = []
                for hh in range(2):
                    eT = a["eT"][par][qp * 2 + hh]
                    for qq in range(2):
                        slot = (hh + 2 * qq)
                        for blk in range(2):
                            cj = mc + blk
                            fns.append(lambda e, eT=eT, qq=qq, blk=blk, cj=cj, pvs=pvs, slot=slot: e.matmul(
                                pvs[:, slot * 65:(slot + 1) * 65], lhsT=eT[:, qq, blk, :], rhs=v1[:, cj, :],
                                start=(blk == 0), stop=(blk == 1)))
                S.op("pe", fns, reads=(a["eTb"][par][qp * 2], a["eTb"][par][qp * 2 + 1], v1b), writes=[PB[base + qp]])

        def Fd(mc):
            par = mc % 2
            base = 4 + 2 * par
            den, denb, ao, aob = a["den"][par], a["denb"][par], a["ao"][par], a["aob"][par]
            for qp in range(2):
                pv3 = PS[base + qp][:, 0:260].rearrange("p (s x) -> p s x", x=65)
                S.op("dve", lambda e, pv3=pv3, qp=qp: e.tensor_tensor(
                    out=den[:, qp * 4:qp * 4 + 4].unsqueeze(2), in0=pv3[:, :, 64:65],
                    in1=self.esink[:, kg * 8 + qp * 4:kg * 8 + qp * 4 + 4].unsqueeze(2), op=ALU.add),
                    reads=(PB[base + qp], self.cb), writes=(denb,))
                S.op("dve", lambda e, qp=qp: e.reciprocal(out=den[:, 8 + qp * 4:8 + qp * 4 + 4], in_=den[:, qp * 4:qp * 4 + 4]),
                     reads=(denb,), writes=(denb,))
                S.op("dve", lambda e, pv3=pv3, qp=qp: e.tensor_tensor(
                    out=ao[:, qp * 4:qp * 4 + 4, :], in0=pv3[:, :, 0:64],
                    in1=den[:, 8 + qp * 4:8 + qp * 4 + 4].unsqueeze(2).to_broadcast([128, 4, 64]), op=ALU.mult),
                    reads=(PB[base + qp], denb), writes=(aob,))

        def T(mc):
            par = mc % 2
            ao, aob = a["ao"][par], a["aob"][par]
            aof = ao.rearrange("p h x -> p (h x)")
            psb = PS[0][:].bitcast(BF16).rearrange("p (a b) -> p a b", b=128)
            S.op("pe", [lambda e, j=j: e.transpose(out=psb[:, j, :], in_=aof[:, j * 128:(j + 1) * 128], identity=self.ident_bf)
                        for j in range(4)], reads=(aob, self.cb), writes=(PB[0],))
            S.op("act", lambda e: e.activation(out=self.aoT[:, kg * 4:kg * 4 + 4, mc * 128:(mc + 1) * 128], in_=psb[:, 0:4, :], func=AF.Copy),
                 reads=(PB[0],), writes=(self.aoTb[kg],))

        L(0)
        Sx(0)
        for mc in range(NMC):
            if mc >= 1:
                T(mc - 1)
            if mc + 1 < NMC:
                L(mc + 1)
            P(mc)
            if mc + 1 < NMC:
                Sx(mc + 1)
            Fd(mc)
        T(NMC - 1)

    def attn_out(self):
        S, d = self.S, self.d
        r2 = self.r2
        r2.recarve()
        tg = r2.alloc([3, 384], F32); tgb = r2.buf("tg")
        mt = r2.alloc([384], F32); mtb = r2.buf("mtmp")
        mr = [(126 + i * 342, 342) for i in range(3)]
        ranges = [(896 + a_, n_) for a_, n_ in mr]
        mainbufs = self.hTb[7:16]
        for db in range(16):
            if db % 2 == 0:
                wvg, wbg = self.wload([d["w_in"][:, GA0 + db * 128:GA0 + db * 128 + 256]], 16)
                wva, wba = self.wload([d["w_attn_out"][:, db * 128:db * 128 + 256]], 16)
            outs = self.proj_fm(wvg, wbg, (db % 2) * 128, 16, self.hT, mainbufs, ranges)
            for i, (pb, ps) in enumerate(outs):
                S.op("act", lambda e, ps=ps, i=i: e.activation(out=tg[:, i, 0:342], in_=ps, func=AF.Tanh, scale=0.5),
                     reads=(pb,), writes=(tgb,))
            outs = self.proj_fm(wva, wba, (db % 2) * 128, 16, lambda kc, t0, n: self.aoT[:, kc, t0 - 896:t0 - 896 + n],
                                self.aoTb, ranges)
            for i, (pb, ps) in enumerate(outs):
                S.op("dve", lambda e, ps=ps, i=i: e.scalar_tensor_tensor(out=mt[:, 0:342], in0=tg[:, i, 0:342], scalar=1.0, in1=ps,
                                                                        op0=ALU.add, op1=ALU.mult),
                     reads=(tgb, pb), writes=(mtb,))
                S.op("dve", lambda e, i=i, db=db: e.tensor_tensor(out=self.mixedT[:, db, mr[i][0]:mr[i][0] + 342],
                                                                  in0=self.mixedT[:, db, mr[i][0]:mr[i][0] + 342], in1=mt[:, 0:342], op=ALU.add),
                     reads=(mtb, self.mixb[db]), writes=(self.mixb[db],))
        self.dump("mixed", self.mixedT, self.mixb, [128, 16, NT], BF16)

    def wout_residual(self):
        S, d = self.S, self.d
        r1, r2, r3 = self.r1, self.r2, self.r3
        r3.recarve()
        self.x1 = r3.alloc([NMC, D], F32)
        self.x1b = [r3.buf("x1_%d" % i) for i in range(NMC)]
        x1, x1b = self.x1, self.x1b
        xsem = S.new_dma_sem("x1sem")
        for mc in range(NMC):
            S.dma("sp", lambda e, mc=mc: e.dma_start(out=x1[:, mc, :], in_=d["xm"][mc * 128:(mc + 1) * 128, :]), xsem,
                  writes=(x1b[mc],))
        for mc in range(NMC):
            x1b[mc].w = (xsem, S.dcnt[xsem])
        for ct in range(8):
            wv, wb = self.wload([d["w_out"][:, ct * 256:(ct + 1) * 256]], 16)
            for mc in range(NMC):
                pb, ps = self.proj_tm(wv, wb, 0, 256, 16, lambda kc, mc=mc: self.mixedT[:, kc, mc * 128:(mc + 1) * 128], self.mixb)
                S.op("dve", lambda e, ps=ps, mc=mc, ct=ct: e.scalar_tensor_tensor(
                    out=x1[:, mc, ct * 256:(ct + 1) * 256], in0=ps, scalar=0.5, in1=x1[:, mc, ct * 256:(ct + 1) * 256],
                    op0=ALU.mult, op1=ALU.add), reads=(pb, x1b[mc]), writes=(x1b[mc],))
        self.dump("x1", x1, x1b, [128, NMC, D])
        r1.recarve(); r2.recarve()
        self.hfT = r1.alloc([16, NT], BF16)
        self.hfTb = [r1.buf("hfT%d" % i) for i in range(NMC)]
        gain = r2.alloc([D], F32); gb = r2.buf("gain")
        hb = [r2.alloc([D], BF16) for _ in range(2)]
        hbb = [r2.buf("hb%d" % i) for i in range(2)]
        S.dma("sp", lambda e: e.dma_start(out=gain, in_=d["norm_ffn_w"].partition_broadcast(128)), S.new_dma_sem("gsem1"), writes=(gb,))
        for mc in range(NMC):
            s = mc % 2
            self.rms_tile(x1[:, mc, :], [x1b[mc]], gain, gb, hb[s], hbb[s], mc, hb[s])
            for half in range(2):
                tiles = [hb[s][:, (half * 8 + i) * 128:(half * 8 + i + 1) * 128] for i in range(8)]
                self.transpose_to(tiles, [hbb[s]],
                                  lambda n, half=half, mc=mc: self.hfT[:, half * 8:half * 8 + 8, mc * 128:(mc + 1) * 128],
                                  [self.hfTb[mc]])
        self.gain_ap, self.gain_b, self.hb2, self.hbb2 = gain, gb, hb, hbb

    def ffn(self):
        S, d = self.S, self.d
        r2, r4 = self.r2, self.r4
        r4.recarve()
        actT = r4.alloc([11, 1024], BF16); actb = r4.buf("actT")
        pre = [r4.alloc([2 + NT], F32) for _ in range(2)]
        preb = [r4.buf("fpre%d" % i) for i in range(2)]
        acc = [r2.alloc([1024], F32) for _ in range(2)]
        accb = [r2.buf("facc%d" % i) for i in range(2)]
        gl = r4.alloc([1024], F32); glb = r4.buf("gl")
        x1, x1b = self.x1, self.x1b
        ranges = [(126 + i * 342, 342) for i in range(3)]
        for i in range(2):
            S.op("dve", lambda e, i=i: e.memset(pre[i][:, 0:2], 0.0), writes=(preb[i],))
        for fg in range(4):
            for jb in range(11):
                b = fg * 11 + jb
                c0 = b * 128
                wv, wb = self.wload([d["w_ffn_up"][:, c0:c0 + 128], d["w_ffn_up"][:, DFF + c0:DFF + c0 + 128]], 16)
                for gu in range(2):
                    blk = b + gu * 44
                    outs = self.proj_fm(wv, wb, gu * 128, 16, lambda kc, t0, n: self.hfT[:, kc, t0:t0 + n], self.hfTb, ranges)
                    for i, (pb, ps) in enumerate(outs):
                        S.op("act", lambda e, ps=ps, i=i, gu=gu: e.activation(out=pre[gu][:, 128 + i * 342:128 + (i + 1) * 342], in_=ps, func=AF.Copy),
                             reads=(pb,), writes=(preb[gu],))
                    S.op("dve", lambda e, gu=gu: e.tensor_scalar(out=pre[gu][:, 128:130], in0=pre[gu][:, 128:130],
                                                                 scalar1=self.flag[:, 0:1], scalar2=None, op0=ALU.mult),
                         reads=(preb[gu], self.cb), writes=(preb[gu],))
                    w = self.cw_ffn
                    S.op("dve", lambda e, gu=gu, blk=blk: e.tensor_scalar(out=acc[gu], in0=pre[gu][:, 130:130 + 1024], scalar1=w[:, blk, 2:3],
                                                                          scalar2=self.cb_ffn[:, blk:blk + 1], op0=ALU.mult, op1=ALU.add),
                         reads=(preb[gu], self.cb), writes=(accb[gu],))
                    for k in (1, 0):
                        S.op("dve", lambda e, gu=gu, blk=blk, k=k: e.scalar_tensor_tensor(
                            out=acc[gu], in0=pre[gu][:, 128 + k:128 + k + 1024], scalar=w[:, blk, k:k + 1], in1=acc[gu],
                            op0=ALU.mult, op1=ALU.add), reads=(preb[gu], self.cb, accb[gu]), writes=(accb[gu],))
                S.op("act", lambda e: e.activation(out=gl, in_=acc[0], func=AF.Gelu_apprx_tanh), reads=(accb[0],), writes=(glb,))
                S.op("dve", lambda e, jb=jb: e.tensor_tensor(out=actT[:, jb, :], in0=gl, in1=acc[1], op=ALU.mult),
                     reads=(glb, accb[1]), writes=(actb,))
            if fg == 0:
                self.dump("actT0", actT, [actb], [128, 11, 1024], BF16)
            for ct in range(8):
                wv, wb = self.wload([d["w_ffn_down"][fg * 1408:(fg + 1) * 1408, ct * 256:(ct + 1) * 256]], 11)
                for mc in range(1, NMC):
                    pb, ps = self.proj_tm(wv, wb, 0, 256, 11, lambda kc, mc=mc: actT[:, kc, (mc - 1) * 128:mc * 128], [actb])
                    S.op("dve", lambda e, ps=ps, mc=mc, ct=ct: e.tensor_tensor(
                        out=x1[:, mc, ct * 256:(ct + 1) * 256], in0=ps, in1=x1[:, mc, ct * 256:(ct + 1) * 256], op=ALU.add),
                        reads=(pb, x1b[mc]), writes=(x1b[mc],))
        self.dump("x2", x1, x1b, [128, NMC, D])

    def ple(self):
        S, d = self.S, self.d
        r1, r2, r4 = self.r1, self.r2, self.r4
        x1, x1b = self.x1, self.x1b
        r1.recarve(); r4.recarve()
        nT = r1.alloc([16, 1024], BF16)
        nTb = [r1.buf("nT%d" % i) for i in range(8)]
        gain, gb, hb, hbb = self.gain_ap, self.gain_b, self.hb2, self.hbb2
        S.dma("sp", lambda e: e.dma_start(out=gain, in_=d["ple_norm_w"].partition_broadcast(128)), S.new_dma_sem("gsem2"), writes=(gb,))
        pT = r4.alloc([2, 1024], BF16); pTb = r4.buf("pT")
        pt = [r4.alloc([PLE], F32) for _ in range(2)]; ptb = [r4.buf("pt%d" % i) for i in range(2)]
        pbf = [r4.alloc([PLE], BF16) for _ in range(2)]; pbfb = [r4.buf("pbf%d" % i) for i in range(2)]
        psem = [S.new_dma_sem("psem%d" % i) for i in range(2)]
        tgp = r4.alloc([256], F32); tgpb = r4.buf("tgp")
        up = r4.alloc([256], F32); upb = r4.buf("up")
        for mc in range(1, NMC):
            s = mc % 2
            o = mc - 1
            self.rms_tile(x1[:, mc, :], [x1b[mc]], gain, gb, hb[s], hbb[s], mc, hb[s])
            for half in range(2):
                tiles = [hb[s][:, (half * 8 + i) * 128:(half * 8 + i + 1) * 128] for i in range(8)]
                self.transpose_to(tiles, [hbb[s]],
                                  lambda n, half=half, o=o: nT[:, half * 8:half * 8 + 8, o * 128:(o + 1) * 128], [nTb[o]])
            S.dma("sp", lambda e, s=s, o=o: e.dma_start(out=pt[s], in_=d["pp"][o * 128:(o + 1) * 128, :]), psem[s], writes=(ptb[s],))
            S.op("act", lambda e, s=s: e.activation(out=pbf[s], in_=pt[s], func=AF.Copy), reads=(ptb[s],), writes=(pbfb[s],))
            tiles = [pbf[s][:, i * 128:(i + 1) * 128] for i in range(2)]
            self.transpose_to(tiles, [pbfb[s]], lambda n, o=o: pT[:, 0:2, o * 128:(o + 1) * 128], [pTb])
        for ct in range(8):
            wv, wb = self.wload([d["w_ple_gate"][:, ct * 256:(ct + 1) * 256]], 16)
            wv2, wb2 = self.wload([d["w_ple_proj"][:, ct * 256:(ct + 1) * 256]], 2)
            for mc in range(1, NMC):
                o = mc - 1
                pb, ps = self.bank()
                fns = []
                for kc in range(16):
                    fns.append(lambda e, kc=kc, ps=ps, o=o: e.matmul(ps[:, 0:256], lhsT=nT[:, kc, o * 128:(o + 1) * 128], rhs=wv[:, kc, 0:256],
                                                                     start=(kc == 0), stop=(kc == 15)))
                for kc in range(2):
                    fns.append(lambda e, kc=kc, ps=ps, o=o: e.matmul(ps[:, 256:512], lhsT=pT[:, kc, o * 128:(o + 1) * 128], rhs=wv2[:, kc, 0:256],
                                                                     start=(kc == 0), stop=(kc == 1)))
                S.op("pe", fns, reads=(wb, wb2, nTb[o], pTb), writes=(pb,))
                S.op("act", lambda e, ps=ps: e.activation(out=tgp, in_=ps[:, 0:256], func=AF.Tanh, scale=0.5), reads=(pb,), writes=(tgpb,))
                S.op("dve", lambda e, ps=ps: e.scalar_tensor_tensor(out=up, in0=tgp, scalar=1.0, in1=ps[:, 256:512], op0=ALU.add, op1=ALU.mult),
                     reads=(tgpb, pb), writes=(upb,))
                S.op("dve", lambda e, mc=mc, ct=ct: e.scalar_tensor_tensor(
                    out=x1[:, mc, ct * 256:(ct + 1) * 256], in0=up, scalar=0.5, in1=x1[:, mc, ct * 256:(ct + 1) * 256],
                    op0=ALU.mult, op1=ALU.add), reads=(upb, x1b[mc]), writes=(x1b[mc],))
        osem = S.new_dma_sem("osem")
        for mc in range(1, NMC):
            ob = Buf("out%d" % mc)
            S.dma("sp", lambda e, mc=mc: e.dma_start(out=self.out[(mc - 1) * 128:mc * 128, :], in_=x1[:, mc, :]), osem,
                  reads=(x1b[mc],), writes=(ob,))
            self.final_bufs.append(ob)


def _t5_bucket(dist):
    nb, md = 32, 128
    me = nb // 2
    dd = np.maximum(dist, 0)
    lr = np.log(np.maximum(dd, 1).astype(np.float32) / me) / np.log(md / me)
    large = me + (lr * (nb - me)).astype(np.int32)
    large = np.minimum(large, nb - 1)
    return np.where(dd < me, dd, large)


def _const_mats():
    ident = np.eye(128, dtype=np.float32)
    tri = (np.arange(128)[:, None] <= np.arange(128)[None, :]).astype(np.float32)
    ones = np.ones((128, 128), np.float32)
    neg = np.where(np.arange(128)[:, None] > np.arange(128)[None, :], -32768.0, 0.0).astype(np.float32)
    neg4 = np.tile(neg, (1, 4))
    sel = np.zeros((128, 8, 128), np.float32)
    for j in range(8):
        sel[j, j, :] = 1.0
    return (np.ascontiguousarray(np.concatenate([tri, ones], axis=1)),
            np.ascontiguousarray(np.concatenate([ident, neg4, sel.reshape(128, 1024)], axis=1)))


def _bias_tables(table):
    L = 128
    qi = np.arange(L)[:, None]
    kj = np.arange(2 * L)[None, :]
    dist = qi + L - kj
    band = (dist >= 0) & (dist < 128)
    bk = _t5_bucket(dist)
    b = table[bk]
    b = np.where(band[:, :, None], b, np.float32(NEGM)).astype(np.float32)
    b = b.reshape(L, 2, L, NKV, 8)
    b = np.transpose(b, (3, 2, 4, 1, 0))
    return np.ascontiguousarray(b).reshape(NKV, 128, 8 * 2 * 128)


def make_in_maps(inputs):
    x = np.asarray(inputs["x"], np.float32)
    p = np.asarray(inputs["p"], np.float32)[0]
    g = lambda k: np.ascontiguousarray(np.asarray(inputs[k], np.float32)[0])
    shared = {
        "w_in": g("w_in"), "w_attn_out": g("w_attn_out"), "w_ssm_out": g("w_ssm_out"), "w_out": g("w_out"),
        "w_ffn_up": g("w_ffn_up"), "w_ffn_down": g("w_ffn_down"), "w_ple_gate": g("w_ple_gate"),
        "w_ple_proj": g("w_ple_proj"),
        "norm_mix_w": g("norm_mix_w")[None], "norm_ffn_w": g("norm_ffn_w")[None], "ple_norm_w": g("ple_norm_w")[None],
        "ssm_norm_w": g("ssm_norm_w")[None], "q_norm_w": g("q_norm_w")[None], "k_norm_w": g("k_norm_w")[None],
        "attn_sinks": g("attn_sinks")[None], "ssm_A_log": g("ssm_A_log")[None], "ssm_dt_bias": g("ssm_dt_bias")[None],
        "ssm_D": g("ssm_D")[None],
        "cw_ssm": np.ascontiguousarray(g("ssm_conv_w").T.reshape(48, 128, 4).transpose(1, 0, 2)).reshape(128, 192),
        "cb_ssm": np.ascontiguousarray(g("ssm_conv_b").reshape(48, 128).T),
        "cw_ffn": np.ascontiguousarray(g("ffn_conv_w").T.reshape(88, 128, 3).transpose(1, 0, 2)).reshape(128, 264),
        "cb_ffn": np.ascontiguousarray(g("ffn_conv_b").reshape(88, 128).T),
        "biasT": _bias_tables(np.asarray(inputs["rel_bias_table"], np.float32)),
    }
    shared["cm_f"], shared["cm_b"] = _const_mats()
    in_maps = []
    for core in range(8):
        b, hf = core // 2, core % 2
        s0 = hf * 1024
        xm = np.zeros((NT, D), np.float32)
        xp = np.zeros((896, D), np.float32)
        if hf == 1:
            xm[:] = x[b, s0 - 128:s0 + 1024]
            xp[:] = x[b, 0:896]
        else:
            xm[128:] = x[b, 0:1024]
        m = dict(shared)
        m["xm"] = xm
        m["xp"] = xp
        m["pp"] = np.ascontiguousarray(p[b, s0:s0 + 1024])
        m["flag"] = np.full((128, 1), float(hf), np.float32)
        in_maps.append(m)
    return in_maps


def kernel(**inputs):
    in_maps = make_in_maps(inputs)
    dbg = tuple(inputs.get("_debug", ())) if isinstance(inputs.get("_debug", ()), (list, tuple)) else ()
    bld = Builder(debug=dbg)
    nc = bld.build()
    cores = list(range(8))
    if inputs.get("_cores"):
        cores = list(inputs["_cores"])
    res = run_bass_kernel_spmd(nc, [in_maps[c] for c in cores], core_ids=list(range(len(cores))))
    out = np.zeros((BATCH, SEQ, D), np.float32)
    for i, core in enumerate(cores):
        b, hf = core // 2, core % 2
        out[b, hf * 1024:(hf + 1) * 1024] = res.results[i]["out"]
    if dbg:
        kernel.last_debug = [{k: r[v] for k, v in bld.dbg_out.items()} for r in res.results]
    return out
```
